# Optimizing a Trainium2 kernel written in Bass

```python
import jax, jax.numpy as jnp
from jax import lax
import numpy as np

D_MODEL = 1024
BATCH = 2
SEQ = 8192
DEPTH = 4
DEC_BATCH = 8
DEC_SEQ = 4096
PAST_LEN = 128

HEAD_DIM = 128
N_HEADS = D_MODEL // HEAD_DIM
N_KV_HEADS = 2
GROUP = N_HEADS // N_KV_HEADS
Q_DIM = N_HEADS * HEAD_DIM
KV_DIM = N_KV_HEADS * HEAD_DIM
QKV_DIM = Q_DIM + 2 * KV_DIM
Q_BLOCK = 128
GRID_W = 64
AXIS_ROT_DIM = HEAD_DIM // 2
ROPE_THETA = 10000.0
N_FGROUPS = 8
FGROUP_DIM = D_MODEL // N_FGROUPS
D_FF = ((8 * D_MODEL + 3 * 256 - 1) // (3 * 256)) * 256
N_FOURIER_LAYERS = (DEPTH + 1) // 2
N_ATTN_LAYERS = DEPTH // 2
EPS = 1e-6

kernel_name = "fnet_gqa_axial_interleaved_encoder"


def rms_norm(x, g):
    xf = x.astype(jnp.float32)
    y = xf * lax.rsqrt(jnp.mean(xf * xf, axis=-1, keepdims=True) + EPS)
    return (y * g.astype(jnp.float32)).astype(x.dtype)


def axial_rope_tables(seq_len):
    rows = seq_len // GRID_W
    row = jnp.repeat(jnp.arange(rows, dtype=jnp.float32), GRID_W)
    col = jnp.tile(jnp.arange(GRID_W, dtype=jnp.float32), rows)
    inv = ROPE_THETA ** (-jnp.arange(0, AXIS_ROT_DIM, 2, dtype=jnp.float32) / AXIS_ROT_DIM)
    ang = jnp.concatenate([row[:, None] * inv, col[:, None] * inv], axis=-1)
    return jnp.cos(ang), jnp.sin(ang)


def apply_rope(x, cos, sin):
    xf = x.astype(jnp.float32).reshape(*x.shape[:-1], HEAD_DIM // 2, 2)
    x0, x1 = xf[..., 0], xf[..., 1]
    c = cos[None, :, None, :]
    s = sin[None, :, None, :]
    out = jnp.stack([x0 * c - x1 * s, x0 * s + x1 * c], axis=-1).reshape(x.shape)
    return out.astype(x.dtype)


def fourier_mixer(x, g, w):
    b, s, d = x.shape
    h = rms_norm(x, g).astype(jnp.float32).reshape(b, s, N_FGROUPS, FGROUP_DIM)
    mixed = jnp.real(jnp.fft.fft2(h, axes=(1, 3), norm="ortho"))
    return mixed.reshape(b, s, d).astype(x.dtype) @ w


def attention_mixer(x, g, w_qkv, q_gain, k_gain, w_o, cos, sin):
    b, s, _ = x.shape
    h = rms_norm(x, g)
    qkv = h @ w_qkv
    q, k, v = jnp.split(qkv, [Q_DIM, Q_DIM + KV_DIM], axis=-1)
    q = apply_rope(rms_norm(q.reshape(b, s, N_HEADS, HEAD_DIM), q_gain), cos, sin)
    k = apply_rope(rms_norm(k.reshape(b, s, N_KV_HEADS, HEAD_DIM), k_gain), cos, sin)
    v = v.reshape(b, s, N_KV_HEADS, HEAD_DIM)
    n_blk = s // Q_BLOCK
    qb = q.reshape(b, n_blk, Q_BLOCK, N_KV_HEADS, GROUP, HEAD_DIM).transpose(1, 0, 2, 3, 4, 5)
    scale = HEAD_DIM ** -0.5

    def block(qi):
        sc = jnp.einsum('bqkgd,bskd->bkgqs', qi, k).astype(jnp.float32) * scale
        p = jax.nn.softmax(sc, axis=-1).astype(v.dtype)
        return jnp.einsum('bkgqs,bskd->bqkgd', p, v)

    o = lax.map(block, qb)
    o = o.transpose(1, 0, 2, 3, 4, 5).reshape(b, s, Q_DIM)
    return o @ w_o


def swiglu_ffn(x, g, w_gate, w_up, w_down):
    h = rms_norm(x, g)
    return (jax.nn.silu(h @ w_gate) * (h @ w_up)) @ w_down


def setup_inputs(seed: int = 0) -> dict:
    key = jax.random.key(seed)
    ks = jax.random.split(key, 12)
    f32 = jnp.float32
    d = D_MODEL
    return {
        "x_prompt": jax.random.normal(ks[0], (BATCH, SEQ, d), f32),
        "x_sample": jax.random.normal(ks[1], (DEC_BATCH, DEC_SEQ, d), f32),
        "norm_mix": 1.0 + 0.02 * jax.random.normal(ks[2], (DEPTH, d), f32),
        "norm_ffn": 1.0 + 0.02 * jax.random.normal(ks[3], (DEPTH, d), f32),
        "fourier_w": jax.random.normal(ks[4], (N_FOURIER_LAYERS, d, d), f32) * d ** -0.5,
        "attn_w_qkv": jax.random.normal(ks[5], (N_ATTN_LAYERS, d, QKV_DIM), f32) * d ** -0.5,
        "attn_q_norm": 1.0 + 0.02 * jax.random.normal(ks[6], (N_ATTN_LAYERS, HEAD_DIM), f32),
        "attn_k_norm": 1.0 + 0.02 * jax.random.normal(ks[7], (N_ATTN_LAYERS, HEAD_DIM), f32),
        "attn_w_o": jax.random.normal(ks[8], (N_ATTN_LAYERS, Q_DIM, d), f32) * Q_DIM ** -0.5,
        "ffn_w_gate": jax.random.normal(ks[9], (DEPTH, d, D_FF), f32) * d ** -0.5,
        "ffn_w_up": jax.random.normal(ks[10], (DEPTH, d, D_FF), f32) * d ** -0.5,
        "ffn_w_down": jax.random.normal(ks[11], (DEPTH, D_FF, d), f32) * D_FF ** -0.5,
    }


def reference(x_prompt, x_sample, norm_mix, norm_ffn, fourier_w, attn_w_qkv, attn_q_norm,
              attn_k_norm, attn_w_o, ffn_w_gate, ffn_w_up, ffn_w_down):
    def run_trunk(x):
        cos, sin = axial_rope_tables(x.shape[1])
        for i in range(DEPTH):
            j = i // 2
            if i % 2 == 0:
                x = x + fourier_mixer(x, norm_mix[i], fourier_w[j])
            else:
                x = x + attention_mixer(x, norm_mix[i], attn_w_qkv[j], attn_q_norm[j],
                                        attn_k_norm[j], attn_w_o[j], cos, sin)
            x = x + swiglu_ffn(x, norm_ffn[i], ffn_w_gate[i], ffn_w_up[i], ffn_w_down[i])
        return x

    y_prompt = run_trunk(x_prompt)
    y_sample = run_trunk(x_sample)
    return (y_prompt, y_sample)
```

```python
import bisect
import contextlib
import numpy as np
import ml_dtypes
import concourse.bass as bass
import concourse.mybir as mybir
from concourse.bass_utils import run_bass_kernel_spmd

F32 = mybir.dt.float32
BF16 = mybir.dt.bfloat16
AF = mybir.ActivationFunctionType
ALU = mybir.AluOpType
AX = mybir.AxisListType

D = 1024
KC = 8
HD = 128
NH = 8
NKV = 2
EPS = 1e-6
NEG = -30000.0


class Prog:
    ENGS = ("sync", "scalar", "vector", "gpsimd", "tensor")

    def __init__(self, nc, stack):
        self.nc = nc
        self.stack = stack
        self.ops = {e: [] for e in self.ENGS}
        self.marked = {e: [] for e in self.ENGS}
        self.cnt = {e: 0 for e in self.ENGS}
        self.seen = {e: {} for e in self.ENGS}
        self.sems = {}
        self.dmacnt = {}
        self.last_write = {}
        self.readers = {}

    def sem(self, name):
        if name not in self.sems:
            self.sems[name] = self.stack.enter_context(self.nc.semaphore(name))
        return self.sems[name]

    def _ticket(self, ref):
        if ref[0] == "dma":
            return (ref[1], ref[2])
        eng, pos = ref
        m = self.marked[eng]
        i = bisect.bisect_left(m, pos)
        if i < len(m):
            p = m[i]
        else:
            p = pos
            self.cnt[eng] += 1
            self.ops[eng][p][1] = self.cnt[eng]
            m.append(p)
        return ("E_" + eng, self.ops[eng][p][1])

    def _wait(self, eng, ticket):
        name, val = ticket
        if self.seen[eng].get(name, 0) >= val:
            return
        self.seen[eng][name] = val
        semh = self.sem(name)
        self.ops[eng].append([lambda e, s=semh, v=val: e.wait_ge(s, v), None, True])

    def _deps(self, eng, reads, writes):
        refs = []
        for r in reads:
            w = self.last_write.get(r)
            if w is not None:
                refs.append(w)
        for w_ in writes:
            w = self.last_write.get(w_)
            if w is not None:
                refs.append(w)
            refs.extend(self.readers.get(w_, {}).values())
        return [r for r in refs if not (r[0] == "tensor" and eng == "tensor")]

    def _update(self, ref, rkey, reads, writes):
        for r in reads:
            self.readers.setdefault(r, {})[rkey] = ref
        for w in writes:
            self.last_write[w] = ref
            self.readers[w] = {}

    def op(self, eng, fn, reads=(), writes=()):
        for ref in self._deps(eng, reads, writes):
            self._wait(eng, self._ticket(ref))
        self.ops[eng].append([fn, None, False])
        ref = (eng, len(self.ops[eng]) - 1)
        self._update(ref, eng, reads, writes)
        return ref

    def dma(self, q, out, in_, semkey, reads=(), writes=(), **kw):
        for ref in self._deps("dma:" + q, reads, writes):
            self._wait(q, self._ticket(ref))
        name = "D_" + semkey
        self.dmacnt[name] = self.dmacnt.get(name, 0) + 16
        val = self.dmacnt[name]
        semh = self.sem(name)
        self.ops[q].append([lambda e, o=out, i=in_, s=semh, k=kw: e.dma_start(out=o, in_=i, **k).then_inc(s, 16),
                            None, True])
        ref = ("dma", name, val)
        self._update(ref, name, reads, writes)
        return ref

    def barrier(self, skip=()):
        tickets = []
        for e in self.ENGS:
            pos = len(self.ops[e]) - 1
            while pos >= 0 and self.ops[e][pos][2]:
                pos -= 1
            if pos >= 0:
                tickets.append(self._ticket((e, pos)))
        for name, val in self.dmacnt.items():
            if name not in skip:
                tickets.append((name, val))
        for e in self.ENGS:
            for t in tickets:
                self._wait(e, t)

    def replay(self, eng, e):
        semh = self.sem("E_" + eng) if self.cnt[eng] else None
        for o in self.ops[eng]:
            ins = o[0](e)
            if o[1] is not None:
                ins.then_inc(semh, 1)


def _coll(P, ins_ap, outs_ap, semkey, reads, writes, groups):
    for ref in P._deps("dma:gpsimd", reads, writes):
        P._wait("gpsimd", P._ticket(ref))
    name = "C_" + semkey
    P.dmacnt[name] = P.dmacnt.get(name, 0) + 1
    val = P.dmacnt[name]
    semh = P.sem(name)
    P.ops["gpsimd"].append([lambda e: e.collective_compute(
        "AllGather", ALU.bypass, replica_groups=groups, ins=[ins_ap.opt()], outs=[outs_ap.opt()]).then_inc(semh),
        None, True])
    ref = ("dma", name, val)
    P._update(ref, name, reads, writes)
    return ref


class Cfg:
    def __init__(self, RP=2048, RS=4096, DFF=2816, layers=("F", "A", "F", "A"), ring=6):
        self.RP, self.RS = RP, RS
        self.NTP, self.NTS = RP // 512, RS // 512
        self.NBP = 4 * RP // 128
        self.KBP = self.NBP // 4
        self.NBS = RS // 128
        assert self.NBS % 16 == 0 and self.NBP % 16 == 0
        self.DFF = DFF
        self.FC = DFF // 128
        assert self.FC % 2 == 0
        self.layers = tuple(layers)
        self.L = len(layers)
        self.ring = ring
        self.slot = {}
        n = 0
        for l, t in enumerate(self.layers):
            if t == "F":
                self.slot[("wf", l)] = n; n += 4
            else:
                self.slot[("wq", l)] = n; n += 4
                self.slot[("wkv", l)] = n; n += 2
                self.slot[("wo", l)] = n; n += 4
            self.slot[("wgu", l)] = n; n += self.FC
            self.slot[("wd", l)] = n; n += self.FC // 2
        self.nslots = n
        self.groups = [[0, 1, 2, 3], [4, 5, 6, 7]]


class Builder:
    def __init__(self, cfg):
        self.c = cfg
        self.nc = bass.Bass("TRN2", target_bir_lowering=False)

    def build(self):
        c, nc = self.c, self.nc
        RP, RS = c.RP, c.RS
        dt = nc.dram_tensor
        ein = lambda name, shape, dtype=F32: dt(name, shape, dtype, kind="ExternalInput").ap()
        itn = lambda name, shape, dtype: dt(name, shape, dtype).ap()
        self.xin = {"P": ein("xp", [RP, D]), "S": ein("xs", [RS, D])}
        self.yout = {"P": dt("yp", [RP, D], F32, kind="ExternalOutput").ap(),
                     "S": dt("ys", [RS, D], F32, kind="ExternalOutput").ap()}
        self.xscr = {"P": itn("xscrp", [RP, D], F32), "S": itn("xscrs", [RS, D], F32)}
        self.wall = ein("wall", [c.nslots * 128, 2048])
        self.gmixT_d = ein("gmixT", [128, c.L * 8])
        self.gffnT_d = ein("gffnT", [128, c.L * 8])
        self.gmixrep_d = ein("gmixrep", [128, c.L, D])
        self.qkg_d = ein("qkg", [128, c.L, 2, 128])
        self.cs_d = {"P": ein("csP", [c.NTP, 128, 4, 128]), "S": ein("csS", [c.NTS, 128, 4, 128])}
        self.dftA_d = ein("dftA", [128, 256])
        self.dftC_d = ein("dftC", [128, 256])
        self.dftB_d = {"P": ein("dftBP", [2, 2 * c.NBP, 128 * c.KBP]), "S": ein("dftBS", [2, 2 * c.NBS, 128 * c.NBS])}
        self.ident_d = ein("ident", [128, 128])
        self.wbf = itn("wbf", [c.nslots * 128, 2048], BF16)
        self.mixT = {"P": itn("mixTP", [8, 128, RP], BF16), "S": itn("mixTS", [8, 128, RS], BF16)}
        self.hloc = [itn("hloc%d" % t, [512, D], BF16) for t in range(c.NTP)]
        self.hall = [itn("hall%d" % t, [4 * 512, D], BF16) for t in range(c.NTP)]
        self.kloc = itn("kloc", [256, RP], BF16)
        self.kall = itn("kall", [1024, RP], BF16)
        self.vloc = itn("vloc", [RP, 256], BF16)
        self.vall = itn("vall", [4 * RP, 256], BF16)
        self.NT = {"P": c.NTP, "S": c.NTS}

        with contextlib.ExitStack() as st:
            self.st = st
            self.P = P = Prog(nc, st)
            sb = lambda name, shape, dtype: st.enter_context(nc.sbuf_tensor("s_" + name, shape, dtype))
            self.pp = [st.enter_context(nc.psum_tensor("pp%d" % i, [128, 1024], F32)) for i in range(4)]
            self.ps = [self.pp[i // 2][:, (i % 2) * 512:(i % 2 + 1) * 512] for i in range(8)]
            self.ident = sb("ident", [128, 128], F32)
            self.identb = sb("identb", [128, 128], BF16)
            self.onesb = sb("onesb", [128, 128], BF16)
            self.gmixT = sb("gmixT", [128, c.L * 8], F32)
            self.gffnT = sb("gffnT", [128, c.L * 8], F32)
            self.xtall = sb("xtall", [128, 8 * D], F32)
            self.xt = [self.xtall[:, i * 4 * D:(i + 1) * 4 * D].rearrange("p (s c) -> p s c", c=D) for i in range(2)]
            self.hn = [sb("hn%d" % i, [128, D], F32) for i in range(2)]
            self.hT = sb("hT", [128, KC, 512], BF16)
            self.act = sb("act", [128, c.FC, 512], BF16)
            self.sg = [sb("sg%d" % i, [128, 512], F32) for i in range(2)]
            self.ringb = [sb("ring%d" % i, [128, 2048], BF16) for i in range(c.ring)]
            self.junk = sb("junk", [128, D], BF16)
            self.ss = sb("ss", [128, 8], F32)
            self.rt = sb("rt", [128, 8], F32)
            self.rstd = sb("rstd", [128, 8], F32)
            self.epsb = sb("epsb", [128, 1], F32)
            self.ring_pos = 0
            self.conv_pos = 0
            self.conv_i = 0
            self.tilecnt = 0

            self.load_consts()
            for l, t in enumerate(c.layers):
                if t == "F":
                    self.fourier_layer(l)
                else:
                    self.attn_layer(l)
                P.barrier()
            P.barrier()

            with nc.Block() as block:
                @block.sync
                def _(e):
                    P.replay("sync", e)

                @block.scalar
                def _(e):
                    P.replay("scalar", e)

                @block.vector
                def _(e):
                    P.replay("vector", e)

                @block.gpsimd
                def _(e):
                    P.replay("gpsimd", e)

                @block.tensor
                def _(e):
                    P.replay("tensor", e)
        return nc

    def src(self, l, part):
        return self.xin[part] if l == 0 else self.xscr[part]

    def dst(self, l, part):
        return self.yout[part] if l == self.c.L - 1 else self.xscr[part]

    def load_consts(self):
        P = self.P
        P.dma("sync", self.ident[:, :], self.ident_d[:, :], "c0", writes=["ident"])
        P.dma("sync", self.gmixT[:, :], self.gmixT_d[:, :], "c1", writes=["gmixT"])
        P.dma("sync", self.gffnT[:, :], self.gffnT_d[:, :], "c2", writes=["gffnT"])
        P.op("vector", lambda e: e.tensor_copy(out=self.identb[:, :], in_=self.ident[:, :]),
             reads=["ident"], writes=["identb"])
        P.op("vector", lambda e: e.memset(self.onesb[:, :], 1.0), writes=["onesb"])
        P.op("vector", lambda e: e.memset(self.epsb[:, :], EPS), writes=["epsb"])

    def conv_some(self, k, after=()):
        c, P = self.c, self.P
        while k > 0 and self.conv_pos < c.nslots:
            s = self.conv_pos
            n = min(4, c.nslots - s)
            P.dma("gpsimd", self.wbf[s * 128:(s + n) * 128, :], self.wall[s * 128:(s + n) * 128, :],
                  "cv%d" % (self.conv_i % 4), reads=list(after), writes=[("wbf", j) for j in range(s, s + n)])
            self.conv_pos += n
            self.conv_i += 1
            k -= 1

    def conv_until(self, slot_end):
        while self.conv_pos < min(slot_end, self.c.nslots):
            self.conv_some(1)

    def convert_weights(self):
        self.conv_until(self.c.nslots)

    def next_xt(self):
        b = self.tilecnt % 2
        self.tilecnt += 1
        return b

    def load_tile(self, l, part, t, b):
        src = self.src(l, part)
        self.P.dma("sync", self.xt[b][:, :, :],
                   src[t * 512:(t + 1) * 512, :].rearrange("(s p) c -> p s c", p=128),
                   "xt%d" % b, reads=[("X", part, l, t)], writes=[(("xt", b), s) for s in range(4)])

    def store_tile(self, l, part, t, b):
        dst = self.dst(l, part)
        self.P.dma("scalar", dst[t * 512:(t + 1) * 512, :].rearrange("(s p) c -> p s c", p=128),
                   self.xt[b][:, :, :], "st%d" % b,
                   reads=[(("xt", b), s) for s in range(4)], writes=[("X", part, l + 1, t)])

    def ring_load(self, slot):
        P = self.P
        b = self.ring_pos % self.c.ring
        self.ring_pos += 1
        key = ("ring", b)
        P.dma("sync", self.ringb[b][:, :], self.wbf[slot * 128:(slot + 1) * 128, :], "ring%d" % b,
              reads=[("wbf", slot)], writes=[key])
        return self.ringb[b], key

    def norm_to_hT(self, xt, xkey, gT, gkey, gcol):
        P = self.P
        for s in range(4):
            P.op("scalar", lambda e, s=s: e.activation(out=self.hn[s % 2][:, :], in_=xt[:, s, :], func=AF.Square,
                                                      accum_out=self.ss[:, s:s + 1]),
                 reads=[(xkey, s)], writes=[("ss", s), ("hn", s % 2)])
        P.op("scalar", lambda e: e.activation(out=self.rt[:, 0:4], in_=self.ss[:, 0:4], func=AF.Sqrt,
                                               bias=self.epsb[:, 0:1], scale=1.0 / D),
             reads=[("ss", s) for s in range(4)] + ["epsb"], writes=["rt"])
        P.op("vector", lambda e: e.reciprocal(out=self.rstd[:, 0:4], in_=self.rt[:, 0:4]),
             reads=["rt"], writes=["rstd"])
        for s in range(4):
            hb = self.hn[s % 2]
            hk = ("hn", s % 2)
            P.op("vector", lambda e, s=s, hb=hb: e.tensor_scalar(out=hb[:, :], in0=xt[:, s, :],
                                                                scalar1=self.rstd[:, s:s + 1], scalar2=None,
                                                                op0=ALU.mult),
                 reads=[(xkey, s), "rstd"], writes=[hk])
            for half in range(2):
                bank = 2 * (s % 2) + half
                pk = ("ps", bank)
                for j in range(4):
                    kc = half * 4 + j
                    P.op("tensor", lambda e, hb=hb, kc=kc, j=j, bank=bank: e.transpose(
                        out=self.ps[bank][:, j * 128:(j + 1) * 128], in_=hb[:, kc * 128:(kc + 1) * 128],
                        identity=self.ident[:, :]),
                        reads=[hk, "ident"], writes=[pk])
                g0 = gcol + half * 4
                P.op("vector", lambda e, s=s, half=half, bank=bank, g0=g0: e.tensor_tensor(
                    out=self.hT[:, half * 4:half * 4 + 4, s * 128:(s + 1) * 128],
                    in0=self.ps[bank][:, :].rearrange("p (j t) -> p j t", j=4),
                    in1=gT[:, g0:g0 + 4].unsqueeze(2).broadcast_to([128, 4, 128]),
                    op=ALU.mult),
                    reads=[pk, gkey], writes=[("hT", s)])

    def ffn_tile(self, l, xt, xkey):
        c, P = self.c, self.P
        self.norm_to_hT(xt, xkey, self.gffnT, "gffnT", l * 8)
        hTr = [("hT", s) for s in range(4)]
        s_gu = c.slot[("wgu", l)]
        s_d = c.slot[("wd", l)]
        for fc in range(c.FC):
            w, wk = self.ring_load(s_gu + fc)
            par = fc % 2
            pg, pu = self.ps[4 + 2 * par], self.ps[5 + 2 * par]
            kg, ku = ("ps", 4 + 2 * par), ("ps", 5 + 2 * par)
            for kc in range(KC):
                P.op("tensor", lambda e, w=w, kc=kc, pg=pg: e.matmul(
                    pg[:, :], lhsT=w[:, kc * 128:(kc + 1) * 128], rhs=self.hT[:, kc, :],
                    start=(kc == 0), stop=(kc == KC - 1)), reads=[wk] + hTr, writes=[kg])
            for kc in range(KC):
                P.op("tensor", lambda e, w=w, kc=kc, pu=pu: e.matmul(
                    pu[:, :], lhsT=w[:, 1024 + kc * 128:1024 + (kc + 1) * 128], rhs=self.hT[:, kc, :],
                    start=(kc == 0), stop=(kc == KC - 1)), reads=[wk] + hTr, writes=[ku])
            sgb = self.sg[par]
            P.op("scalar", lambda e, pg=pg, sgb=sgb: e.activation(out=sgb[:, :], in_=pg[:, :], func=AF.Silu),
                 reads=[kg], writes=[("sg", par)])
            P.op("vector", lambda e, pu=pu, sgb=sgb, fc=fc: e.tensor_tensor(
                out=self.act[:, fc, :], in0=pu[:, :], in1=sgb[:, :], op=ALU.mult),
                reads=[ku, ("sg", par)], writes=[("act", fc)])
        for j in range(c.FC // 2):
            w, wk = self.ring_load(s_d + j)
            for i in range(2):
                fc = 2 * j + i
                for s in range(4):
                    for half in range(2):
                        bank = 2 * s + half
                        P.op("tensor", lambda e, w=w, i=i, fc=fc, s=s, half=half, bank=bank: e.matmul(
                            self.ps[bank][:, :], lhsT=self.act[:, fc, s * 128:(s + 1) * 128],
                            rhs=w[:, i * 1024 + half * 512:i * 1024 + (half + 1) * 512],
                            start=(fc == 0), stop=(fc == c.FC - 1)),
                            reads=[wk, ("act", fc)], writes=[("ps", bank)])
        for s in range(4):
            for half in range(2):
                bank = 2 * s + half
                P.op("vector", lambda e, s=s, half=half, bank=bank: e.tensor_tensor(
                    out=xt[:, s, half * 512:(half + 1) * 512], in0=self.ps[bank][:, :],
                    in1=xt[:, s, half * 512:(half + 1) * 512], op=ALU.add),
                    reads=[("ps", bank), (xkey, s)], writes=[(xkey, s)])

    def add_psum_to_xt(self, xt, xkey):
        for s in range(4):
            for half in range(2):
                bank = 2 * s + half
                self.P.op("vector", lambda e, s=s, half=half, bank=bank: e.tensor_tensor(
                    out=xt[:, s, half * 512:(half + 1) * 512], in0=self.ps[bank][:, :],
                    in1=xt[:, s, half * 512:(half + 1) * 512], op=ALU.add),
                    reads=[("ps", bank), (xkey, s)], writes=[(xkey, s)])


    def fft_part(self, l, part, fb):
        c, P = self.c, self.P
        NB = c.NBP if part == "P" else c.NBS
        KB = c.KBP if part == "P" else c.NBS
        NT = self.NT[part]
        A, C, B, T2, hb, xg = fb["A"], fb["C"], fb["B" + part], fb["T2"], fb["hb"], fb["xg"]
        Bkey = "B" + part
        T1 = self.xtall[:, :].bitcast(BF16)[:, 0:NB * 256].rearrange("p (b k) -> p b k", k=256)
        T1f = self.xtall[:, :].bitcast(BF16)
        YT = self.act[:, :, :].rearrange("p f t -> p (f t)")[:, 0:128 * KB]
        KG = 512 // KB
        if part == "S":
            src = self.src(l, "S")
            allX = [("X", "S", l, t) for t in range(NT)]
            srcv = src.rearrange("(a b) c -> a b c", b=NB)
            ssq, rtq, rsq, grep = fb["ssq"], fb["rtq"], fb["rsq"], fb["grep"]
            nbs = NB // 16
            hbv = lambda h: h[:, :, :].rearrange("p b c -> p (b c)").bitcast(F32)[:, 0:NB * 64].rearrange("p (b c) -> p b c", c=D)
            bufs = [xg[:, :].rearrange("p (b c) -> p b c", c=D), fb["xg2"][:, :].rearrange("p (b c) -> p b c", c=D),
                    hbv(hb[0]), hbv(hb[1])]
            bkeys = [("xg", 0), ("xg", 1), ("hb", 0), ("hb", 1)]
            for i in range(16):
                bf, bk = bufs[i % 4], bkeys[i % 4]
                P.dma("sync", bf[:, 0:nbs, :], srcv[:, i * nbs:(i + 1) * nbs, :], "fs%d" % (i % 4),
                      reads=allX, writes=[bk])
                for q in range(nbs):
                    b = i * nbs + q
                    jk = b % 4
                    P.op("scalar", lambda e, bf=bf, q=q, b=b, jk=jk: e.activation(
                        out=T1f[:, jk * D:(jk + 1) * D], in_=bf[:, q, :], func=AF.Square, accum_out=ssq[:, b:b + 1]),
                        reads=[bk], writes=[("ssq", b), ("jk", jk)])
            P.op("scalar", lambda e: e.activation(out=rtq[:, 0:NB], in_=ssq[:, 0:NB], func=AF.Sqrt,
                                                   bias=self.epsb[:, 0:1], scale=1.0 / D),
                 reads=[("ssq", b) for b in range(NB)] + ["epsb"], writes=["rtq"])
            P.op("vector", lambda e: e.reciprocal(out=rsq[:, 0:NB], in_=rtq[:, 0:NB]), reads=["rtq"], writes=["rsq"])
            hnb = NB // 2
            xgv2 = [xg[:, :].rearrange("p (b c) -> p b c", c=128), fb["xg2"][:, :].rearrange("p (b c) -> p b c", c=128)]
        else:
            npiece = 4 * c.NTP
        def prep(g):
            hbg = hb[g % 2]
            hk = ("hb", g % 2)
            if part == "S":
                for half in range(2):
                    xv = xgv2[half]
                    xk = ("xg", half)
                    P.dma("sync", xv[:, :, :], srcv[:, half * hnb:(half + 1) * hnb, g * 128:(g + 1) * 128],
                          "fx%d" % half, reads=allX, writes=[xk])
                    P.op("vector", lambda e, half=half, xv=xv: e.tensor_tensor(
                        out=xv[:, :, :], in0=xv[:, :, :],
                        in1=rsq[:, half * hnb:(half + 1) * hnb].unsqueeze(2).broadcast_to([128, hnb, 128]),
                        op=ALU.mult), reads=[xk, "rsq"], writes=[xk])
                    P.op("gpsimd", lambda e, half=half, g=g, hbg=hbg, xv=xv: e.tensor_tensor(
                        out=hbg[:, half * hnb:(half + 1) * hnb, :], in0=xv[:, :, :],
                        in1=grep[:, g * 128:(g + 1) * 128].unsqueeze(1).broadcast_to([128, hnb, 128]),
                        op=ALU.mult), reads=[xk, "grep"], writes=[hk])
            else:
                apt, apr = 512 // NB, c.RP // NB
                k = 0
                for r in range(4):
                    for t in range(c.NTP):
                        p0 = r * apr + t * apt
                        P.dma("sync", hbg[p0:p0 + apt, 0:NB, :],
                              self.hall[t][r * 512:(r + 1) * 512, g * 128:(g + 1) * 128].rearrange("(a b) c -> a b c", b=NB),
                              "fq%d" % (k % 4), reads=[("hall", tt) for tt in range(c.NTP)], writes=[(hk, k)])
                        k += 1

        prep(0)
        for g in range(8):
            hbg = hb[g % 2]
            hk = ("hb", g % 2)
            hbreads = [hk] if part == "S" else [(hk, k) for k in range(npiece)]
            for b2 in range(NB // 2):
                bank = b2 % 2
                for q in range(2):
                    b = 2 * b2 + q
                    P.op("tensor", lambda e, b=b, q=q, bank=bank, hbg=hbg: e.matmul(
                        self.ps[bank][:, q * 256:(q + 1) * 256], lhsT=hbg[:, b, :], rhs=A[:, :],
                        start=True, stop=True), reads=hbreads + ["A"], writes=[("ps", bank)])
                dstap = T1[:, 2 * b2:2 * b2 + 2, :]
                srcap = self.ps[bank].rearrange("p (q k) -> p q k", q=2)
                if b2 % 2 == 0:
                    P.op("scalar", lambda e, d=dstap, s_=srcap: e.copy(out=d, in_=s_),
                         reads=[("ps", bank)], writes=["T1"])
                else:
                    P.op("vector", lambda e, d=dstap, s_=srcap: e.tensor_copy(out=d, in_=s_),
                         reads=[("ps", bank)], writes=["T1"])
            if g + 1 < 8:
                prep(g + 1)
            self.conv_some(3, after=[("mixT", part, g - 1)] if g > 0 else [])
            T1v = T1.rearrange("p b (ri k) -> p b ri k", ri=2)
            YTv = YT.rearrange("p (kb ka) -> p ka kb", ka=128)
            def emit_s2(ka2):
                bank = 2 + ka2 % 2
                t2 = T2[ka2 % 2]
                t2k = ("T2", ka2 % 2)
                for q in range(2):
                    ka = 2 * ka2 + q
                    P.op("tensor", lambda e, ka=ka, q=q, bank=bank: e.matmul(
                        self.ps[bank][0:2 * NB, q * 256:(q + 1) * 256], lhsT=T1v[:, :, :, ka], rhs=C[:, :],
                        start=True, stop=True), reads=["T1", "C"], writes=[("ps", bank)])
                if ka2 % 2 == 0:
                    P.op("scalar", lambda e, t2=t2, bank=bank: e.copy(out=t2[0:2 * NB, :], in_=self.ps[bank][0:2 * NB, :]),
                         reads=[("ps", bank)], writes=[t2k])
                else:
                    P.op("vector", lambda e, t2=t2, bank=bank: e.tensor_copy(out=t2[0:2 * NB, :], in_=self.ps[bank][0:2 * NB, :]),
                         reads=[("ps", bank)], writes=[t2k])

            def emit_s3(ka2):
                t2 = T2[ka2 % 2]
                t2k = ("T2", ka2 % 2)
                for q in range(2):
                    ka = 2 * ka2 + q
                    kg = ka // KG
                    bank3 = 4 + kg % 2
                    col = (ka % KG) * KB
                    for w in range(2):
                        P.op("tensor", lambda e, t2=t2, q=q, w=w, ka=ka, bank3=bank3, col=col: e.matmul(
                            self.ps[bank3][:, col:col + KB],
                            lhsT=t2[0:2 * NB, q * 256 + w * 128:q * 256 + (w + 1) * 128],
                            rhs=B[0:2 * NB, w, ka * KB:(ka + 1) * KB],
                            start=(w == 0), stop=(w == 1)),
                            reads=[t2k, Bkey], writes=[("ps", bank3)])
                    if ka % KG == KG - 1:
                        ka0 = ka - KG + 1
                        dstap = YTv[:, ka0:ka0 + KG, :]
                        srcap = self.ps[bank3].rearrange("p (k b) -> p k b", b=KB)
                        if kg % 2 == 0:
                            P.op("vector", lambda e, d=dstap, s_=srcap: e.tensor_copy(out=d, in_=s_),
                                 reads=[("ps", bank3)], writes=["YT"])
                        else:
                            P.op("scalar", lambda e, d=dstap, s_=srcap: e.copy(out=d, in_=s_),
                                 reads=[("ps", bank3)], writes=["YT"])

            emit_s2(0)
            for ka2 in range(64):
                if ka2 + 1 < 64:
                    emit_s2(ka2 + 1)
                emit_s3(ka2)
            P.dma("sync", self.mixT[part][g, :, :], YT, "fy", reads=["YT"], writes=[("mixT", part, g)])

    def fourier_layer(self, l):
        c, P, nc = self.c, self.P, self.nc
        with contextlib.ExitStack() as ls:
            sb = lambda name, shape, dtype: ls.enter_context(nc.sbuf_tensor("f%d_" % l + name, shape, dtype))
            fb = {}
            fb["grep"] = grep = sb("grep", [128, D], F32)
            fb["A"] = A = sb("A", [128, 256], BF16)
            fb["C"] = C = sb("C", [128, 256], BF16)
            fb["BP"] = sb("BP", [128, 2, 128 * c.KBP], BF16)
            fb["BS"] = sb("BS", [128, 2, 128 * c.NBS], BF16)
            fb["xg"] = sb("xg", [128, c.NBS * 64], F32)
            fb["xg2"] = sb("xg2", [128, c.NBS * 64], F32)
            fb["hb"] = [sb("hb%d" % i, [128, max(c.NBP, c.NBS), 128], BF16) for i in range(2)]
            fb["T2"] = [sb("T2_%d" % i, [128, 512], BF16) for i in range(2)]
            fb["ssq"] = sb("ssq", [128, c.NBS], F32)
            fb["rtq"] = sb("rtq", [128, c.NBS], F32)
            fb["rsq"] = sb("rsq", [128, c.NBS], F32)
            mT = sb("mT", [128, 8, 512], BF16)

            P.dma("sync", grep[:, :], self.gmixrep_d[:, l, :], "f0", writes=["grep"])
            P.dma("gpsimd", A[:, :], self.dftA_d[:, :], "f1", writes=["A"])
            P.dma("gpsimd", C[:, :], self.dftC_d[:, :], "f2", writes=["C"])
            k = 0
            for part, nbp, ncol in (("S", c.NBS, 128 * c.NBS), ("P", c.NBP, 128 * c.KBP)):
                ch = min(2048, ncol)
                for w in range(2):
                    for i in range(ncol // ch):
                        P.dma("gpsimd", fb["B" + part][0:2 * nbp, w, i * ch:(i + 1) * ch],
                              self.dftB_d[part][w, :, i * ch:(i + 1) * ch], "fb%d" % (k % 4),
                              writes=["B" + part])
                        k += 1

            hst = mT[:, :, :].rearrange("p g t -> p (g t)").rearrange("p (s c) -> p s c", c=D)
            for t in range(c.NTP):
                b = self.next_xt()
                xt, xkey = self.xt[b], ("xt", b)
                self.load_tile(l, "P", t, b)
                for s in range(4):
                    P.op("scalar", lambda e, s=s, xt=xt: e.activation(out=self.hn[s % 2][:, :], in_=xt[:, s, :], func=AF.Square,
                                                                     accum_out=self.ss[:, s:s + 1]),
                         reads=[(xkey, s)], writes=[("ss", s), ("hn", s % 2)])
                P.op("scalar", lambda e: e.activation(out=self.rt[:, 0:4], in_=self.ss[:, 0:4], func=AF.Sqrt,
                                                       bias=self.epsb[:, 0:1], scale=1.0 / D),
                     reads=[("ss", s) for s in range(4)] + ["epsb"], writes=["rt"])
                P.op("vector", lambda e: e.reciprocal(out=self.rstd[:, 0:4], in_=self.rt[:, 0:4]),
                     reads=["rt"], writes=["rstd"])
                for s in range(4):
                    hb_, hk = self.hn[s % 2], ("hn", s % 2)
                    P.op("vector", lambda e, s=s, hb_=hb_, xt=xt: e.tensor_scalar(
                        out=hb_[:, :], in0=xt[:, s, :], scalar1=self.rstd[:, s:s + 1], scalar2=None, op0=ALU.mult),
                        reads=[(xkey, s), "rstd"], writes=[hk])
                    P.op("vector", lambda e, s=s, hb_=hb_: e.tensor_tensor(
                        out=hst[:, s, :], in0=hb_[:, :], in1=grep[:, :], op=ALU.mult),
                        reads=[hk, "grep"], writes=[("hst", s)])
                P.dma("sync", self.hloc[t].rearrange("(s p) c -> p s c", p=128), hst,
                      "fh", reads=[("hst", s) for s in range(4)], writes=[("hloc", t)])
                _coll(P, self.hloc[t], self.hall[t], "ag", [("hloc", t)], [("hall", t)], c.groups)
            P.barrier(skip=("C_ag",) + tuple("D_cv%d" % i for i in range(4)))

            self.fft_part(l, "S", fb)
            P.barrier(skip=("C_ag",) + tuple("D_cv%d" % i for i in range(4)))
            self.fft_part(l, "P", fb)
            P.barrier(skip=tuple("D_cv%d" % i for i in range(4)))

            self.convert_weights()
            s_wf = c.slot[("wf", l)]
            for part in ("P", "S"):
                mixv = self.mixT[part].rearrange("g k r -> k g r")
                for t in range(self.NT[part]):
                    b = self.next_xt()
                    xt, xkey = self.xt[b], ("xt", b)
                    self.load_tile(l, part, t, b)
                    P.dma("sync", mT[:, :, :], mixv[:, :, t * 512:(t + 1) * 512], "fm",
                          reads=[("mixT", part, g) for g in range(8)], writes=["mT"])
                    ws = [self.ring_load(s_wf + j) for j in range(4)]
                    for s in range(4):
                        for half in range(2):
                            bank = 2 * s + half
                            for g in range(8):
                                w, wk = ws[g // 2]
                                P.op("tensor", lambda e, w=w, g=g, s=s, half=half, bank=bank: e.matmul(
                                    self.ps[bank], lhsT=mT[:, g, s * 128:(s + 1) * 128],
                                    rhs=w[:, (g % 2) * 1024 + half * 512:(g % 2) * 1024 + (half + 1) * 512],
                                    start=(g == 0), stop=(g == 7)),
                                    reads=[wk, "mT"], writes=[("ps", bank)])
                    self.add_psum_to_xt(xt, xkey)
                    self.ffn_tile(l, xt, xkey)
                    self.store_tile(l, part, t, b)
    def qk_norm_rope(self, tmp, nh, pss, gain, gkey, cs, cskey, s):
        P = self.P
        qn, qr = tmp["qn"], tmp["qr"]
        t1, t2, t3, t4 = tmp["t"]
        h0 = 0
        for (psap, bank, n) in pss:
            for i in range(n):
                P.op("scalar", lambda e, psap=psap, i=i, h0=h0: e.activation(
                    out=self.junk[:, (h0 + i) * 128:(h0 + i + 1) * 128], in_=psap[:, i * 128:(i + 1) * 128],
                    func=AF.Square, accum_out=self.ss[:, h0 + i:h0 + i + 1]),
                    reads=[("ps", bank)], writes=[("ssh", h0 + i), ("junk", h0 + i)])
            h0 += n
        P.op("scalar", lambda e: e.activation(out=self.rt[:, 0:nh], in_=self.ss[:, 0:nh], func=AF.Sqrt,
                                               bias=self.epsb[:, 0:1], scale=1.0 / HD),
             reads=[("ssh", i) for i in range(nh)] + ["epsb"], writes=["rth"])
        P.op("vector", lambda e: e.reciprocal(out=self.rstd[:, 0:nh], in_=self.rt[:, 0:nh]),
             reads=["rth"], writes=["rstdh"])
        h0 = 0
        for (psap, bank, n) in pss:
            P.op("vector", lambda e, psap=psap, n=n, h0=h0: e.tensor_tensor(
                out=qn[:, h0:h0 + n, :], in0=psap.rearrange("p (h d) -> p h d", d=128),
                in1=self.rstd[:, h0:h0 + n].unsqueeze(2).broadcast_to([128, n, 128]), op=ALU.mult),
                reads=[("ps", bank), "rstdh"], writes=["qn"] + tmp.get("fence", []))
            h0 += n
        P.op("gpsimd", lambda e: e.tensor_tensor(
            out=qn[:, 0:nh, :], in0=qn[:, 0:nh, :],
            in1=gain.unsqueeze(1).broadcast_to([128, nh, 128]), op=ALU.mult),
            reads=["qn", gkey], writes=["qn"])
        qv = qn[:, 0:nh, :].rearrange("p h (i two) -> p h i two", two=2)
        rv = qr[:, 0:nh, :].rearrange("p h (i two) -> p h i two", two=2)
        x0, x1 = qv[:, :, :, 0], qv[:, :, :, 1]
        cc = cs[:, s, 0:64].unsqueeze(1).broadcast_to([128, nh, 64])
        sn = cs[:, s, 64:128].unsqueeze(1).broadcast_to([128, nh, 64])
        P.op("gpsimd", lambda e: e.tensor_tensor(out=t1[:, 0:nh, :], in0=x0, in1=cc, op=ALU.mult),
             reads=["qn", cskey], writes=["t1"])
        P.op("gpsimd", lambda e: e.tensor_tensor(out=t2[:, 0:nh, :], in0=x1, in1=sn, op=ALU.mult),
             reads=["qn", cskey], writes=["t2"])
        P.op("gpsimd", lambda e: e.tensor_tensor(out=rv[:, :, :, 0], in0=t1[:, 0:nh, :], in1=t2[:, 0:nh, :],
                                                 op=ALU.subtract), reads=["t1", "t2"], writes=["qr0"])
        P.op("vector", lambda e: e.tensor_tensor(out=t3[:, 0:nh, :], in0=x0, in1=sn, op=ALU.mult),
             reads=["qn", cskey], writes=["t3"])
        P.op("vector", lambda e: e.tensor_tensor(out=t4[:, 0:nh, :], in0=x1, in1=cc, op=ALU.mult),
             reads=["qn", cskey], writes=["t4"])
        P.op("vector", lambda e: e.tensor_tensor(out=rv[:, :, :, 1], in0=t3[:, 0:nh, :], in1=t4[:, 0:nh, :],
                                                 op=ALU.add), reads=["t3", "t4"], writes=["qr1"])


    def attn_layer(self, l):
        c, P, nc = self.c, self.P, self.nc
        scale = float(HD) ** -0.5
        NBmax = max(c.NBP, c.NBS)
        with contextlib.ExitStack() as ls:
            sb = lambda name, shape, dtype: ls.enter_context(nc.sbuf_tensor("a%d_" % l + name, shape, dtype))
            KT = sb("KT", [128, 2, NBmax * 128], BF16)
            Vb = sb("Vb", [128, NBmax, 256], BF16)
            QT = sb("QT", [128, 8, 512], BF16)
            OT = sb("OT", [128, 8, 512], BF16)
            NPT = 3
            pT = [sb("pT%d" % i, [128, 1024], BF16) for i in range(NPT)]
            s2 = [sb("s2_%d" % i, [128, 512], BF16) for i in range(NPT)]
            rinv = sb("rinv", [128, 512], F32)
            cs = sb("cs", [128, 4, 128], F32)
            qkg = sb("qkg", [128, 2, 128], F32)
            flat = self.act[:, :, :].rearrange("p f t -> p (f t)").bitcast(F32)
            assert c.FC * 256 >= 3584
            tmp = {
                "fence": [("act", fc) for fc in range(c.FC)],
                "qn": flat[:, 0:1024].rearrange("p (h d) -> p h d", d=128),
                "t": [flat[:, 1024 + 512 * i:1024 + 512 * (i + 1)].rearrange("p (h d) -> p h d", d=64) for i in range(4)],
                "qr": flat[:, 3072:3584].bitcast(BF16).rearrange("p (h d) -> p h d", d=128),
            }
            P.dma("sync", qkg[:, :, :], self.qkg_d[:, l, :, :], "a0", writes=["qkg"])
            self.convert_weights()
            s_q, s_kv, s_o = c.slot[("wq", l)], c.slot[("wkv", l)], c.slot[("wo", l)]

            def pass1(part):
                for ta in range(self.NT[part]):
                    b = self.next_xt()
                    xt, xkey = self.xt[b], ("xt", b)
                    self.load_tile(l, part, ta, b)
                    P.dma("sync", cs[:, :, :], self.cs_d[part][ta], "a1", writes=["cs"])
                    self.norm_to_hT(xt, xkey, self.gmixT, "gmixT", l * 8)
                    ws = [self.ring_load(s_kv + j) for j in range(2)]
                    def mm1(s, ws=ws):
                        bank = 4 + s % 2
                        for kc in range(KC):
                            w, wk = ws[kc // 4]
                            P.op("tensor", lambda e, w=w, kc=kc, s=s, bank=bank: e.matmul(
                                self.ps[bank], lhsT=self.hT[:, kc, s * 128:(s + 1) * 128],
                                rhs=w[:, (kc % 4) * 512:(kc % 4 + 1) * 512],
                                start=(kc == 0), stop=(kc == KC - 1)),
                                reads=[wk, ("hT", s)], writes=[("ps", bank)])
                    mm1(0)
                    for s in range(4):
                        sa = 4 * ta + s
                        bank = 4 + s % 2
                        if s + 1 < 4:
                            mm1(s + 1)
                        P.op("scalar", lambda e, sa=sa, bank=bank: e.copy(out=Vb[:, sa, :], in_=self.ps[bank][:, 256:512]),
                             reads=[("ps", bank)], writes=[("V", sa)])
                        self.qk_norm_rope(tmp, 2, [(self.ps[bank][:, 0:256], bank, 2)], qkg[:, 1, :], "qkg", cs, "cs", s)
                        pb = self.pp[3][:, (s % 2) * 512:(s % 2 + 1) * 512].bitcast(BF16)
                        for kvh in range(2):
                            P.op("tensor", lambda e, kvh=kvh, pb=pb: e.transpose(
                                out=pb[:, kvh * 128:(kvh + 1) * 128], in_=tmp["qr"][:, kvh, :], identity=self.identb[:, :]),
                                reads=["qr0", "qr1", "identb"], writes=[("ps", 6 + s % 2)])
                        P.op("scalar", lambda e, sa=sa, pb=pb: e.copy(
                            out=KT[:, :, sa * 128:(sa + 1) * 128], in_=pb[:, 0:256].rearrange("p (h t) -> p h t", h=2)),
                            reads=[("ps", 6 + s % 2)], writes=[("K", sa)])

            def pass2(part, nbk):
                allK = [("K", sa) for sa in range(nbk)]
                allV = [("V", sa) for sa in range(nbk)]
                for ta in range(self.NT[part]):
                    b = self.next_xt()
                    xt, xkey = self.xt[b], ("xt", b)
                    self.load_tile(l, part, ta, b)
                    P.dma("sync", cs[:, :, :], self.cs_d[part][ta], "a1", writes=["cs"])
                    self.norm_to_hT(xt, xkey, self.gmixT, "gmixT", l * 8)
                    ws = [self.ring_load(s_q + j) for j in range(4)]
                    def mm2(s, ws=ws):
                        banks = (4, 5) if s % 2 == 0 else (2, 3)
                        for half in range(2):
                            for kc in range(KC):
                                w, wk = ws[kc // 2]
                                P.op("tensor", lambda e, w=w, kc=kc, s=s, half=half, banks=banks: e.matmul(
                                    self.ps[banks[half]], lhsT=self.hT[:, kc, s * 128:(s + 1) * 128],
                                    rhs=w[:, (kc % 2) * 1024 + half * 512:(kc % 2) * 1024 + (half + 1) * 512],
                                    start=(kc == 0), stop=(kc == KC - 1)),
                                    reads=[wk, ("hT", s)], writes=[("ps", banks[half])])
                    mm2(0)
                    for s in range(4):
                        banks = (4, 5) if s % 2 == 0 else (2, 3)
                        if s + 1 < 4:
                            mm2(s + 1)
                        self.qk_norm_rope(tmp, 8, [(self.ps[banks[0]], banks[0], 4), (self.ps[banks[1]], banks[1], 4)],
                                          qkg[:, 0, :], "qkg", cs, "cs", s)
                        pb = self.pp[3][:, (s % 2) * 512:(s % 2 + 1) * 512].bitcast(BF16)
                        for hd in range(8):
                            P.op("tensor", lambda e, hd=hd, pb=pb: e.transpose(
                                out=pb[:, hd * 128:(hd + 1) * 128], in_=tmp["qr"][:, hd, :], identity=self.identb[:, :]),
                                reads=["qr0", "qr1", "identb"], writes=[("ps", 6 + s % 2)])
                        P.op("vector", lambda e, s=s, pb=pb: e.tensor_copy(
                            out=QT[:, :, s * 128:(s + 1) * 128], in_=pb[:, :].rearrange("p (h t) -> p h t", h=8)),
                            reads=[("ps", 6 + s % 2)], writes=["QT"])
                    npair = nbk // 2
                    items = [(hd, kp) for hd in range(8) for kp in range(npair)]

                    def emit_score(i):
                        hd, kp = items[i]
                        kvh = hd // 4
                        j = i % 2
                        for q in range(2):
                            kt = 2 * kp + q
                            P.op("tensor", lambda e, hd=hd, kt=kt, kvh=kvh, j=j, q=q: e.matmul(
                                self.ps[2 * j + q], lhsT=KT[:, kvh, kt * 128:(kt + 1) * 128], rhs=QT[:, hd, :],
                                start=True, stop=True), reads=["QT"] + allK, writes=[("ps", 2 * j + q)])
                        jb = i % NPT
                        P.op("scalar", lambda e, j=j, jb=jb: e.activation(
                            out=pT[jb][:, :], in_=self.pp[j][:, :], func=AF.Exp, scale=scale),
                            reads=[("ps", 2 * j), ("ps", 2 * j + 1)], writes=[("pT", jb)])

                    emit_score(0)
                    pending = []
                    for i, (hd, kp) in enumerate(items):
                        if i + 1 < len(items):
                            emit_score(i + 1)
                        kvh = hd // 4
                        jb = i % NPT
                        par = hd % 2
                        po, pr = 4 + par, 6 + par
                        for q in range(2):
                            kt = 2 * kp + q
                            P.op("tensor", lambda e, kt=kt, kvh=kvh, jb=jb, po=po, q=q: e.matmul(
                                self.ps[po], lhsT=Vb[:, kt, kvh * 128:(kvh + 1) * 128],
                                rhs=pT[jb][:, q * 512:(q + 1) * 512],
                                start=(kt == 0), stop=(kt == nbk - 1)),
                                reads=[("pT", jb)] + allV, writes=[("ps", po)])
                        sj = (i // 2) % NPT
                        if kp % 2 == 0:
                            P.op("vector", lambda e, jb=jb, sj=sj: e.tensor_tensor(
                                out=s2[sj][:, :], in0=pT[jb][:, 0:512], in1=pT[jb][:, 512:1024], op=ALU.add),
                                reads=[("pT", jb)], writes=[("s2", sj)])
                        else:
                            P.op("vector", lambda e, jb=jb: e.tensor_tensor(
                                out=rinv[:, :].bitcast(BF16)[:, 0:512], in0=pT[jb][:, 0:512], in1=pT[jb][:, 512:1024], op=ALU.add),
                                reads=[("pT", jb)], writes=["s2tmp", "rinv"])
                            P.op("vector", lambda e, sj=sj: e.tensor_tensor(
                                out=s2[sj][:, :], in0=s2[sj][:, :], in1=rinv[:, :].bitcast(BF16)[:, 0:512], op=ALU.add),
                                reads=["s2tmp", ("s2", sj)], writes=[("s2", sj)])

                            def emit_rs(sj=sj, pr=pr, kp=kp):
                                P.op("tensor", lambda e: e.matmul(
                                    self.ps[pr], lhsT=self.onesb[:, :], rhs=s2[sj][:, :],
                                    start=(kp == 1), stop=(kp == npair - 1)),
                                    reads=[("s2", sj), "onesb"], writes=[("ps", pr)])
                            if pending:
                                pending.pop()()
                            pending.append(emit_rs)
                        if kp == npair - 1:
                            while pending:
                                pending.pop()()
                            P.op("vector", lambda e, pr=pr: e.reciprocal(out=rinv[:, :], in_=self.ps[pr]),
                                 reads=[("ps", pr), "s2tmp"], writes=["rinv", "s2tmp"])
                            P.op("vector", lambda e, po=po, hd=hd: e.tensor_tensor(
                                out=OT[:, hd, :], in0=self.ps[po], in1=rinv[:, :], op=ALU.mult),
                                reads=[("ps", po), "rinv"], writes=[("OT", hd)])
                    ws = [self.ring_load(s_o + j) for j in range(4)]
                    for s in range(4):
                        for half in range(2):
                            bank = 2 * s + half
                            for hd in range(8):
                                w, wk = ws[hd // 2]
                                P.op("tensor", lambda e, w=w, hd=hd, s=s, half=half, bank=bank: e.matmul(
                                    self.ps[bank], lhsT=OT[:, hd, s * 128:(s + 1) * 128],
                                    rhs=w[:, (hd % 2) * 1024 + half * 512:(hd % 2) * 1024 + (half + 1) * 512],
                                    start=(hd == 0), stop=(hd == 7)),
                                    reads=[wk, ("OT", hd)], writes=[("ps", bank)])
                    self.add_psum_to_xt(xt, xkey)
                    self.ffn_tile(l, xt, xkey)
                    self.store_tile(l, part, ta, b)

            nloc = c.RP // 128
            pass1("P")
            P.dma("sync", self.kloc.rearrange("(h d) k -> d h k", h=2), KT[:, :, 0:c.RP], "ak",
                  reads=[("K", sa) for sa in range(nloc)], writes=["kloc"])
            P.dma("sync", self.vloc.rearrange("(s p) c -> p s c", p=128), Vb[:, 0:nloc, :], "av",
                  reads=[("V", sa) for sa in range(nloc)], writes=["vloc"])
            _coll(P, self.kloc, self.kall, "agk", ["kloc"], ["kall"], c.groups)
            _coll(P, self.vloc, self.vall, "agv", ["vloc"], ["vall"], c.groups)
            pass1("S")
            pass2("S", c.NBS)
            kallv = self.kall.rearrange("(r h d) k -> h d r k", r=4, h=2)
            for h in range(2):
                P.dma("sync", KT[:, h, 0:4 * c.RP].rearrange("d (r k) -> d r k", r=4), kallv[h], "ak%d" % h,
                      reads=["kall"], writes=[("K", sa) for sa in range(c.NBP)])
            P.dma("sync", Vb[:, 0:c.NBP, :], self.vall.rearrange("(s p) c -> p s c", p=128), "av2",
                  reads=["vall"], writes=[("V", sa) for sa in range(c.NBP)])
            pass2("P", c.NBP)


def _rope_rows(pos):
    inv = (np.float32(10000.0) ** (-np.arange(0, 64, 2, dtype=np.float32) / np.float32(64))).astype(np.float32)
    rowp = (pos // 64).astype(np.float32)
    colp = (pos % 64).astype(np.float32)
    ang = np.concatenate([rowp[:, None] * inv, colp[:, None] * inv], -1).astype(np.float32)
    return np.concatenate([np.cos(ang), np.sin(ang)], -1).astype(np.float32)


def _cs_tiles(pos):
    cs = _rope_rows(pos)
    nt = cs.shape[0] // 512
    return np.ascontiguousarray(cs.reshape(nt, 4, 128, 128).transpose(0, 2, 1, 3))


def _dft_common():
    a = np.arange(128)
    A = np.exp(-2j * np.pi * np.outer(a, a) / 128.0)
    m = np.concatenate([A.real, A.imag], 1).astype(np.float32)
    return m


def _dft_B(NB, R, kbs):
    b = np.arange(NB).astype(np.float64)
    ka = np.arange(128).astype(np.float64)
    kb = np.asarray(kbs).astype(np.float64)
    M = (np.exp(-2j * np.pi * b[:, None, None] * kb[None, None, :] / NB)
         * np.exp(-2j * np.pi * b[:, None, None] * ka[None, :, None] / float(R))) / np.sqrt(R * 128.0)
    n = 128 * len(kbs)
    Mr, Mi = M.real.reshape(NB, n), M.imag.reshape(NB, n)
    out = np.zeros((2, 2 * NB, n), np.float32)
    out[0, 0::2] = Mr
    out[0, 1::2] = -Mi
    out[1, 0::2] = -Mi
    out[1, 1::2] = -Mr
    return out


def _pack_weights(cfg, fourier_w, attn_w_qkv, attn_w_o, ffn_w_gate, ffn_w_up, ffn_w_down):
    wall = np.zeros((cfg.nslots, 128, 2048), np.float32)
    jF = jA = 0
    for l, t in enumerate(cfg.layers):
        if t == "F":
            w = np.asarray(fourier_w[jF], np.float32)
            s0 = cfg.slot[("wf", l)]
            wall[s0:s0 + 4] = w.reshape(4, 2, 128, D).transpose(0, 2, 1, 3).reshape(4, 128, 2048)
            jF += 1
        else:
            wqkv = np.asarray(attn_w_qkv[jA], np.float32)
            s0 = cfg.slot[("wq", l)]
            wall[s0:s0 + 4] = wqkv[:, 0:1024].reshape(4, 2, 128, 1024).transpose(0, 2, 1, 3).reshape(4, 128, 2048)
            s0 = cfg.slot[("wkv", l)]
            wall[s0:s0 + 2] = wqkv[:, 1024:1536].reshape(2, 4, 128, 512).transpose(0, 2, 1, 3).reshape(2, 128, 2048)
            s0 = cfg.slot[("wo", l)]
            wo = np.asarray(attn_w_o[jA], np.float32)
            wall[s0:s0 + 4] = wo.reshape(4, 2, 128, D).transpose(0, 2, 1, 3).reshape(4, 128, 2048)
            jA += 1
        FC = cfg.FC
        wg = np.asarray(ffn_w_gate[l], np.float32).reshape(KC, 128, FC, 128)
        wu = np.asarray(ffn_w_up[l], np.float32).reshape(KC, 128, FC, 128)
        gu = np.stack([wg, wu], 0)
        s0 = cfg.slot[("wgu", l)]
        wall[s0:s0 + FC] = gu.transpose(3, 2, 0, 1, 4).reshape(FC, 128, 2048)
        wd = np.asarray(ffn_w_down[l], np.float32)
        s0 = cfg.slot[("wd", l)]
        wall[s0:s0 + FC // 2] = wd.reshape(FC // 2, 2, 128, D).transpose(0, 2, 1, 3).reshape(FC // 2, 128, 2048)
    return wall.reshape(cfg.nslots * 128, 2048)


def _common_inputs(cfg, norm_mix, norm_ffn, attn_q_norm, attn_k_norm):
    L = cfg.L
    nm = np.asarray(norm_mix, np.float32)[:L]
    nf = np.asarray(norm_ffn, np.float32)[:L]
    gmixT = nm.reshape(L, KC, 128).transpose(2, 0, 1).reshape(128, L * KC).copy()
    gffnT = nf.reshape(L, KC, 128).transpose(2, 0, 1).reshape(128, L * KC).copy()
    gmixrep = np.broadcast_to(nm[None], (128, L, D)).copy()
    qkg = np.ones((128, L, 2, 128), np.float32)
    jA = 0
    for l, t in enumerate(cfg.layers):
        if t == "A":
            qkg[:, l, 0, :] = np.asarray(attn_q_norm[jA], np.float32)[None]
            qkg[:, l, 1, :] = np.asarray(attn_k_norm[jA], np.float32)[None]
            jA += 1
    dft = _dft_common()
    return dict(gmixT=gmixT, gffnT=gffnT, gmixrep=gmixrep, qkg=qkg, ident=np.eye(128, dtype=np.float32),
                dftA=dft, dftC=dft.copy(),
                dftBS=_dft_B(cfg.NBS, cfg.RS, np.arange(cfg.NBS)),
                csS=_cs_tiles(np.arange(cfg.RS)))


_NC_CACHE = {}


def run_cores(cfg, xp_list, xs_list, weights):
    key = (cfg.RP, cfg.RS, cfg.DFF, cfg.layers, cfg.ring)
    if key not in _NC_CACHE:
        _NC_CACHE[key] = Builder(cfg).build()
    nc = _NC_CACHE[key]
    in_maps = []
    for c in range(8):
        r = c % 4
        m = dict(xp=np.ascontiguousarray(xp_list[c], np.float32), xs=np.ascontiguousarray(xs_list[c], np.float32),
                 wall=weights["wall"])
        m.update(weights["common"])
        m["csP"] = _cs_tiles(cfg.RP * r + np.arange(cfg.RP))
        m["dftBP"] = _dft_B(cfg.NBP, 4 * cfg.RP, cfg.KBP * r + np.arange(cfg.KBP))
        in_maps.append(m)
    res = run_bass_kernel_spmd(nc, in_maps, core_ids=list(range(8)))
    return [(r["yp"], r["ys"]) for r in res.results]


def kernel(x_prompt, x_sample, norm_mix, norm_ffn, fourier_w, attn_w_qkv, attn_q_norm, attn_k_norm,
           attn_w_o, ffn_w_gate, ffn_w_up, ffn_w_down):
    cfg = Cfg()
    xp = np.asarray(x_prompt, np.float32)
    xs = np.asarray(x_sample, np.float32)
    weights = dict(
        wall=_pack_weights(cfg, fourier_w, attn_w_qkv, attn_w_o, ffn_w_gate, ffn_w_up, ffn_w_down),
        common=_common_inputs(cfg, norm_mix, norm_ffn, attn_q_norm, attn_k_norm))
    RP = cfg.RP
    xp_list = [xp[c // 4, RP * (c % 4):RP * (c % 4 + 1)] for c in range(8)]
    xs_list = [xs[c] for c in range(8)]
    outs = run_cores(cfg, xp_list, xs_list, weights)
    y_prompt = np.zeros_like(xp)
    y_sample = np.zeros_like(xs)
    for c in range(8):
        y_prompt[c // 4, RP * (c % 4):RP * (c % 4 + 1)] = outs[c][0]
        y_sample[c] = outs[c][1]
    return (y_prompt, y_sample)
```

```python
import bisect
import contextlib
import numpy as np
import ml_dtypes
import concourse.bass as bass
import concourse.mybir as mybir
from concourse.bass_utils import run_bass_kernel_spmd

F32 = mybir.dt.float32
BF16 = mybir.dt.bfloat16
AF = mybir.ActivationFunctionType
ALU = mybir.AluOpType
AX = mybir.AxisListType

D = 1024
KC = 8
HD = 128
NH = 8
NKV = 2
EPS = 1e-6
NEG = -30000.0


class Prog:
    ENGS = ("sync", "scalar", "vector", "gpsimd", "tensor")

    def __init__(self, nc, stack):
        self.nc = nc
        self.stack = stack
        self.ops = {e: [] for e in self.ENGS}
        self.marked = {e: [] for e in self.ENGS}
        self.cnt = {e: 0 for e in self.ENGS}
        self.seen = {e: {} for e in self.ENGS}
        self.sems = {}
        self.dmacnt = {}
        self.last_write = {}
        self.readers = {}

    def sem(self, name):
        if name not in self.sems:
            self.sems[name] = self.stack.enter_context(self.nc.semaphore(name))
        return self.sems[name]

    def _ticket(self, ref):
        if ref[0] == "dma":
            return (ref[1], ref[2])
        eng, pos = ref
        m = self.marked[eng]
        i = bisect.bisect_left(m, pos)
        if i < len(m):
            p = m[i]
        else:
            p = pos
            self.cnt[eng] += 1
            self.ops[eng][p][1] = self.cnt[eng]
            m.append(p)
        return ("E_" + eng, self.ops[eng][p][1])

    def _wait(self, eng, ticket):
        name, val = ticket
        if self.seen[eng].get(name, 0) >= val:
            return
        self.seen[eng][name] = val
        semh = self.sem(name)
        self.ops[eng].append([lambda e, s=semh, v=val: e.wait_ge(s, v), None, True])

    def _deps(self, eng, reads, writes):
        refs = []
        for r in reads:
            w = self.last_write.get(r)
            if w is not None:
                refs.append(w)
        for w_ in writes:
            w = self.last_write.get(w_)
            if w is not None:
                refs.append(w)
            refs.extend(self.readers.get(w_, {}).values())
        return [r for r in refs if not (r[0] == "tensor" and eng == "tensor")]

    def _update(self, ref, rkey, reads, writes):
        for r in reads:
            self.readers.setdefault(r, {})[rkey] = ref
        for w in writes:
            self.last_write[w] = ref
            self.readers[w] = {}

    def op(self, eng, fn, reads=(), writes=()):
        for ref in self._deps(eng, reads, writes):
            self._wait(eng, self._ticket(ref))
        self.ops[eng].append([fn, None, False])
        ref = (eng, len(self.ops[eng]) - 1)
        self._update(ref, eng, reads, writes)
        return ref

    def dma(self, q, out, in_, semkey, reads=(), writes=(), **kw):
        for ref in self._deps("dma:" + q, reads, writes):
            self._wait(q, self._ticket(ref))
        name = "D_" + semkey
        self.dmacnt[name] = self.dmacnt.get(name, 0) + 16
        val = self.dmacnt[name]
        semh = self.sem(name)
        self.ops[q].append([lambda e, o=out, i=in_, s=semh, k=kw: e.dma_start(out=o, in_=i, **k).then_inc(s, 16),
                            None, True])
        ref = ("dma", name, val)
        self._update(ref, name, reads, writes)
        return ref

    def barrier(self, skip=()):
        tickets = []
        for e in self.ENGS:
            pos = len(self.ops[e]) - 1
            while pos >= 0 and self.ops[e][pos][2]:
                pos -= 1
            if pos >= 0:
                tickets.append(self._ticket((e, pos)))
        for name, val in self.dmacnt.items():
            if name not in skip:
                tickets.append((name, val))
        for e in self.ENGS:
            for t in tickets:
                self._wait(e, t)

    def replay(self, eng, e):
        semh = self.sem("E_" + eng) if self.cnt[eng] else None
        for o in self.ops[eng]:
            ins = o[0](e)
            if o[1] is not None:
                ins.then_inc(semh, 1)


def _coll(P, ins_ap, outs_ap, semkey, reads, writes, groups):
    for ref in P._deps("dma:gpsimd", reads, writes):
        P._wait("gpsimd", P._ticket(ref))
    name = "C_" + semkey
    P.dmacnt[name] = P.dmacnt.get(name, 0) + 1
    val = P.dmacnt[name]
    semh = P.sem(name)
    P.ops["gpsimd"].append([lambda e: e.collective_compute(
        "AllGather", ALU.bypass, replica_groups=groups, ins=[ins_ap.opt()], outs=[outs_ap.opt()]).then_inc(semh),
        None, True])
    ref = ("dma", name, val)
    P._update(ref, name, reads, writes)
    return ref


class Cfg:
    def __init__(self, RP=2048, RS=4096, DFF=2816, layers=("F", "A", "F", "A"), ring=6):
        self.RP, self.RS = RP, RS
        self.NTP, self.NTS = RP // 512, RS // 512
        self.NBP = 4 * RP // 128
        self.KBP = self.NBP // 4
        self.NBS = RS // 128
        assert self.NBS % 16 == 0 and self.NBP % 16 == 0
        self.DFF = DFF
        self.FC = DFF // 128
        assert self.FC % 2 == 0
        self.layers = tuple(layers)
        self.L = len(layers)
        self.ring = ring
        self.slot = {}
        n = 0
        for l, t in enumerate(self.layers):
            if t == "F":
                self.slot[("wf", l)] = n; n += 4
            else:
                self.slot[("wq", l)] = n; n += 4
                self.slot[("wkv", l)] = n; n += 2
                self.slot[("wo", l)] = n; n += 4
            self.slot[("wgu", l)] = n; n += self.FC
            self.slot[("wd", l)] = n; n += self.FC // 2
        self.nslots = n
        self.groups = [[0, 1, 2, 3], [4, 5, 6, 7]]


class Builder:
    def __init__(self, cfg):
        self.c = cfg
        self.nc = bass.Bass("TRN2", target_bir_lowering=False)

    def build(self):
        c, nc = self.c, self.nc
        RP, RS = c.RP, c.RS
        dt = nc.dram_tensor
        ein = lambda name, shape, dtype=F32: dt(name, shape, dtype, kind="ExternalInput").ap()
        itn = lambda name, shape, dtype: dt(name, shape, dtype).ap()
        self.xin = {"P": ein("xp", [RP, D]), "S": ein("xs", [RS, D])}
        self.yout = {"P": dt("yp", [RP, D], F32, kind="ExternalOutput").ap(),
                     "S": dt("ys", [RS, D], F32, kind="ExternalOutput").ap()}
        self.xscr = {"P": itn("xscrp", [RP, D], F32), "S": itn("xscrs", [RS, D], F32)}
        self.wall = ein("wall", [c.nslots * 128, 2048])
        self.gmixT_d = ein("gmixT", [128, c.L * 8])
        self.gffnT_d = ein("gffnT", [128, c.L * 8])
        self.gmixrep_d = ein("gmixrep", [128, c.L, D])
        self.qkg_d = ein("qkg", [128, c.L, 2, 128])
        self.cs_d = {"P": ein("csP", [c.NTP, 128, 4, 128]), "S": ein("csS", [c.NTS, 128, 4, 128])}
        self.dftA_d = ein("dftA", [128, 256])
        self.dftC_d = ein("dftC", [128, 256])
        self.dftB_d = {"P": ein("dftBP", [2, 2 * c.NBP, 128 * c.KBP]), "S": ein("dftBS", [2, 2 * c.NBS, 128 * c.NBS])}
        self.ident_d = ein("ident", [128, 128])
        self.wbf = itn("wbf", [c.nslots * 128, 2048], BF16)
        self.mixT = {"P": itn("mixTP", [8, 128, RP], BF16), "S": itn("mixTS", [8, 128, RS], BF16)}
        self.hloc = [itn("hloc%d" % t, [512, D], BF16) for t in range(c.NTP)]
        self.hall = [itn("hall%d" % t, [4 * 512, D], BF16) for t in range(c.NTP)]
        self.kloc = itn("kloc", [256, RP], BF16)
        self.kall = itn("kall", [1024, RP], BF16)
        self.vloc = itn("vloc", [RP, 256], BF16)
        self.vall = itn("vall", [4 * RP, 256], BF16)
        self.NT = {"P": c.NTP, "S": c.NTS}

        with contextlib.ExitStack() as st:
            self.st = st
            self.P = P = Prog(nc, st)
            sb = lambda name, shape, dtype: st.enter_context(nc.sbuf_tensor("s_" + name, shape, dtype))
            self.pp = [st.enter_context(nc.psum_tensor("pp%d" % i, [128, 1024], F32)) for i in range(4)]
            self.ps = [self.pp[i // 2][:, (i % 2) * 512:(i % 2 + 1) * 512] for i in range(8)]
            self.ident = sb("ident", [128, 128], F32)
            self.identb = sb("identb", [128, 128], BF16)
            self.onesb = sb("onesb", [128, 128], BF16)
            self.gmixT = sb("gmixT", [128, c.L * 8], F32)
            self.gffnT = sb("gffnT", [128, c.L * 8], F32)
            self.xtall = sb("xtall", [128, 8 * D], F32)
            self.xt = [self.xtall[:, i * 4 * D:(i + 1) * 4 * D].rearrange("p (s c) -> p s c", c=D) for i in range(2)]
            self.hn = [sb("hn%d" % i, [128, D], F32) for i in range(2)]
            self.hT = sb("hT", [128, KC, 512], BF16)
            self.act = sb("act", [128, c.FC, 512], BF16)
            self.sg = [sb("sg%d" % i, [128, 512], F32) for i in range(2)]
            self.ringb = [sb("ring%d" % i, [128, 2048], BF16) for i in range(c.ring)]
            self.junk = sb("junk", [128, D], BF16)
            self.ss = sb("ss", [128, 8], F32)
            self.rt = sb("rt", [128, 8], F32)
            self.rstd = sb("rstd", [128, 8], F32)
            self.epsb = sb("epsb", [128, 1], F32)
            self.ring_pos = 0
            self.conv_pos = 0
            self.conv_i = 0
            self.tilecnt = 0

            self.load_consts()
            for l, t in enumerate(c.layers):
                if t == "F":
                    self.fourier_layer(l)
                else:
                    self.attn_layer(l)
                P.barrier()
            P.barrier()

            with nc.Block() as block:
                @block.sync
                def _(e):
                    P.replay("sync", e)

                @block.scalar
                def _(e):
                    P.replay("scalar", e)

                @block.vector
                def _(e):
                    P.replay("vector", e)

                @block.gpsimd
                def _(e):
                    P.replay("gpsimd", e)

                @block.tensor
                def _(e):
                    P.replay("tensor", e)
        return nc

    def src(self, l, part):
        return self.xin[part] if l == 0 else self.xscr[part]

    def dst(self, l, part):
        return self.yout[part] if l == self.c.L - 1 else self.xscr[part]

    def load_consts(self):
        P = self.P
        P.dma("sync", self.ident[:, :], self.ident_d[:, :], "c0", writes=["ident"])
        P.dma("sync", self.gmixT[:, :], self.gmixT_d[:, :], "c1", writes=["gmixT"])
        P.dma("sync", self.gffnT[:, :], self.gffnT_d[:, :], "c2", writes=["gffnT"])
        P.op("vector", lambda e: e.tensor_copy(out=self.identb[:, :], in_=self.ident[:, :]),
             reads=["ident"], writes=["identb"])
        P.op("vector", lambda e: e.memset(self.onesb[:, :], 1.0), writes=["onesb"])
        P.op("vector", lambda e: e.memset(self.epsb[:, :], EPS), writes=["epsb"])

    def conv_some(self, k, after=()):
        c, P = self.c, self.P
        while k > 0 and self.conv_pos < c.nslots:
            s = self.conv_pos
            n = min(4, c.nslots - s)
            P.dma("gpsimd", self.wbf[s * 128:(s + n) * 128, :], self.wall[s * 128:(s + n) * 128, :],
                  "cv%d" % (self.conv_i % 4), reads=list(after), writes=[("wbf", j) for j in range(s, s + n)])
            self.conv_pos += n
            self.conv_i += 1
            k -= 1

    def conv_until(self, slot_end):
        while self.conv_pos < min(slot_end, self.c.nslots):
            self.conv_some(1)

    def convert_weights(self):
        self.conv_until(self.c.nslots)

    def next_xt(self):
        b = self.tilecnt % 2
        self.tilecnt += 1
        return b

    def load_tile(self, l, part, t, b):
        src = self.src(l, part)
        self.P.dma("sync", self.xt[b][:, :, :],
                   src[t * 512:(t + 1) * 512, :].rearrange("(s p) c -> p s c", p=128),
                   "xt%d" % b, reads=[("X", part, l, t)], writes=[(("xt", b), s) for s in range(4)])

    def store_tile(self, l, part, t, b):
        dst = self.dst(l, part)
        self.P.dma("scalar", dst[t * 512:(t + 1) * 512, :].rearrange("(s p) c -> p s c", p=128),
                   self.xt[b][:, :, :], "st%d" % b,
                   reads=[(("xt", b), s) for s in range(4)], writes=[("X", part, l + 1, t)])

    def ring_load(self, slot):
        P = self.P
        b = self.ring_pos % self.c.ring
        self.ring_pos += 1
        key = ("ring", b)
        P.dma("sync", self.ringb[b][:, :], self.wbf[slot * 128:(slot + 1) * 128, :], "ring%d" % b,
              reads=[("wbf", slot)], writes=[key])
        return self.ringb[b], key

    def norm_to_hT(self, xt, xkey, gT, gkey, gcol):
        P = self.P
        for s in range(4):
            P.op("scalar", lambda e, s=s: e.activation(out=self.hn[s % 2][:, :], in_=xt[:, s, :], func=AF.Square,
                                                      accum_out=self.ss[:, s:s + 1]),
                 reads=[(xkey, s)], writes=[("ss", s), ("hn", s % 2)])
        P.op("scalar", lambda e: e.activation(out=self.rt[:, 0:4], in_=self.ss[:, 0:4], func=AF.Sqrt,
                                               bias=self.epsb[:, 0:1], scale=1.0 / D),
             reads=[("ss", s) for s in range(4)] + ["epsb"], writes=["rt"])
        P.op("vector", lambda e: e.reciprocal(out=self.rstd[:, 0:4], in_=self.rt[:, 0:4]),
             reads=["rt"], writes=["rstd"])
        for s in range(4):
            hb = self.hn[s % 2]
            hk = ("hn", s % 2)
            P.op("vector", lambda e, s=s, hb=hb: e.tensor_scalar(out=hb[:, :], in0=xt[:, s, :],
                                                                scalar1=self.rstd[:, s:s + 1], scalar2=None,
                                                                op0=ALU.mult),
                 reads=[(xkey, s), "rstd"], writes=[hk])
            for half in range(2):
                bank = 2 * (s % 2) + half
                pk = ("ps", bank)
                for j in range(4):
                    kc = half * 4 + j
                    P.op("tensor", lambda e, hb=hb, kc=kc, j=j, bank=bank: e.transpose(
                        out=self.ps[bank][:, j * 128:(j + 1) * 128], in_=hb[:, kc * 128:(kc + 1) * 128],
                        identity=self.ident[:, :]),
                        reads=[hk, "ident"], writes=[pk])
                g0 = gcol + half * 4
                P.op("vector", lambda e, s=s, half=half, bank=bank, g0=g0: e.tensor_tensor(
                    out=self.hT[:, half * 4:half * 4 + 4, s * 128:(s + 1) * 128],
                    in0=self.ps[bank][:, :].rearrange("p (j t) -> p j t", j=4),
                    in1=gT[:, g0:g0 + 4].unsqueeze(2).broadcast_to([128, 4, 128]),
                    op=ALU.mult),
                    reads=[pk, gkey], writes=[("hT", s)])

    def ffn_tile(self, l, xt, xkey):
        c, P = self.c, self.P
        self.norm_to_hT(xt, xkey, self.gffnT, "gffnT", l * 8)
        hTr = [("hT", s) for s in range(4)]
        s_gu = c.slot[("wgu", l)]
        s_d = c.slot[("wd", l)]
        for fc in range(c.FC):
            w, wk = self.ring_load(s_gu + fc)
            par = fc % 2
            pg, pu = self.ps[4 + 2 * par], self.ps[5 + 2 * par]
            kg, ku = ("ps", 4 + 2 * par), ("ps", 5 + 2 * par)
            for kc in range(KC):
                P.op("tensor", lambda e, w=w, kc=kc, pg=pg: e.matmul(
                    pg[:, :], lhsT=w[:, kc * 128:(kc + 1) * 128], rhs=self.hT[:, kc, :],
                    start=(kc == 0), stop=(kc == KC - 1)), reads=[wk] + hTr, writes=[kg])
            for kc in range(KC):
                P.op("tensor", lambda e, w=w, kc=kc, pu=pu: e.matmul(
                    pu[:, :], lhsT=w[:, 1024 + kc * 128:1024 + (kc + 1) * 128], rhs=self.hT[:, kc, :],
                    start=(kc == 0), stop=(kc == KC - 1)), reads=[wk] + hTr, writes=[ku])
            sgb = self.sg[par]
            P.op("scalar", lambda e, pg=pg, sgb=sgb: e.activation(out=sgb[:, :], in_=pg[:, :], func=AF.Silu),
                 reads=[kg], writes=[("sg", par)])
            P.op("vector", lambda e, pu=pu, sgb=sgb, fc=fc: e.tensor_tensor(
                out=self.act[:, fc, :], in0=pu[:, :], in1=sgb[:, :], op=ALU.mult),
                reads=[ku, ("sg", par)], writes=[("act", fc)])
        for j in range(c.FC // 2):
            w, wk = self.ring_load(s_d + j)
            for i in range(2):
                fc = 2 * j + i
                for s in range(4):
                    for half in range(2):
                        bank = 2 * s + half
                        P.op("tensor", lambda e, w=w, i=i, fc=fc, s=s, half=half, bank=bank: e.matmul(
                            self.ps[bank][:, :], lhsT=self.act[:, fc, s * 128:(s + 1) * 128],
                            rhs=w[:, i * 1024 + half * 512:i * 1024 + (half + 1) * 512],
                            start=(fc == 0), stop=(fc == c.FC - 1)),
                            reads=[wk, ("act", fc)], writes=[("ps", bank)])
        for s in range(4):
            for half in range(2):
                bank = 2 * s + half
                P.op("vector", lambda e, s=s, half=half, bank=bank: e.tensor_tensor(
                    out=xt[:, s, half * 512:(half + 1) * 512], in0=self.ps[bank][:, :],
                    in1=xt[:, s, half * 512:(half + 1) * 512], op=ALU.add),
                    reads=[("ps", bank), (xkey, s)], writes=[(xkey, s)])

    def add_psum_to_xt(self, xt, xkey):
        for s in range(4):
            for half in range(2):
                bank = 2 * s + half
                self.P.op("vector", lambda e, s=s, half=half, bank=bank: e.tensor_tensor(
                    out=xt[:, s, half * 512:(half + 1) * 512], in0=self.ps[bank][:, :],
                    in1=xt[:, s, half * 512:(half + 1) * 512], op=ALU.add),
                    reads=[("ps", bank), (xkey, s)], writes=[(xkey, s)])


    def fft_part(self, l, part, fb):
        c, P = self.c, self.P
        NB = c.NBP if part == "P" else c.NBS
        KB = c.KBP if part == "P" else c.NBS
        NT = self.NT[part]
        A, C, B, T2, hb, xg = fb["A"], fb["C"], fb["B" + part], fb["T2"], fb["hb"], fb["xg"]
        Bkey = "B" + part
        T1 = self.xtall[:, :].bitcast(BF16)[:, 0:NB * 256].rearrange("p (b k) -> p b k", k=256)
        T1f = self.xtall[:, :].bitcast(BF16)
        YT = self.act[:, :, :].rearrange("p f t -> p (f t)")[:, 0:128 * KB]
        KG = 512 // KB
        if part == "S":
            src = self.src(l, "S")
            allX = [("X", "S", l, t) for t in range(NT)]
            srcv = src.rearrange("(a b) c -> a b c", b=NB)
            ssq, rtq, rsq, grep = fb["ssq"], fb["rtq"], fb["rsq"], fb["grep"]
            nbs = NB // 16
            hbv = lambda h: h[:, :, :].rearrange("p b c -> p (b c)").bitcast(F32)[:, 0:NB * 64].rearrange("p (b c) -> p b c", c=D)
            bufs = [xg[:, :].rearrange("p (b c) -> p b c", c=D), fb["xg2"][:, :].rearrange("p (b c) -> p b c", c=D),
                    hbv(hb[0]), hbv(hb[1])]
            bkeys = [("xg", 0), ("xg", 1), ("hb", 0), ("hb", 1)]
            for i in range(16):
                bf, bk = bufs[i % 4], bkeys[i % 4]
                P.dma("sync", bf[:, 0:nbs, :], srcv[:, i * nbs:(i + 1) * nbs, :], "fs%d" % (i % 4),
                      reads=allX, writes=[bk])
                for q in range(nbs):
                    b = i * nbs + q
                    jk = b % 4
                    P.op("scalar", lambda e, bf=bf, q=q, b=b, jk=jk: e.activation(
                        out=T1f[:, jk * D:(jk + 1) * D], in_=bf[:, q, :], func=AF.Square, accum_out=ssq[:, b:b + 1]),
                        reads=[bk], writes=[("ssq", b), ("jk", jk)])
            P.op("scalar", lambda e: e.activation(out=rtq[:, 0:NB], in_=ssq[:, 0:NB], func=AF.Sqrt,
                                                   bias=self.epsb[:, 0:1], scale=1.0 / D),
                 reads=[("ssq", b) for b in range(NB)] + ["epsb"], writes=["rtq"])
            P.op("vector", lambda e: e.reciprocal(out=rsq[:, 0:NB], in_=rtq[:, 0:NB]), reads=["rtq"], writes=["rsq"])
            hnb = NB // 2
            xgv2 = [xg[:, :].rearrange("p (b c) -> p b c", c=128), fb["xg2"][:, :].rearrange("p (b c) -> p b c", c=128)]
        else:
            npiece = 4 * c.NTP
        def prep(g):
            hbg = hb[g % 2]
            hk = ("hb", g % 2)
            if part == "S":
                for half in range(2):
                    xv = xgv2[half]
                    xk = ("xg", half)
                    P.dma("sync", xv[:, :, :], srcv[:, half * hnb:(half + 1) * hnb, g * 128:(g + 1) * 128],
                          "fx%d" % half, reads=allX, writes=[xk])
                    P.op("vector", lambda e, half=half, xv=xv: e.tensor_tensor(
                        out=xv[:, :, :], in0=xv[:, :, :],
                        in1=rsq[:, half * hnb:(half + 1) * hnb].unsqueeze(2).broadcast_to([128, hnb, 128]),
                        op=ALU.mult), reads=[xk, "rsq"], writes=[xk])
                    P.op("gpsimd", lambda e, half=half, g=g, hbg=hbg, xv=xv: e.tensor_tensor(
                        out=hbg[:, half * hnb:(half + 1) * hnb, :], in0=xv[:, :, :],
                        in1=grep[:, g * 128:(g + 1) * 128].unsqueeze(1).broadcast_to([128, hnb, 128]),
                        op=ALU.mult), reads=[xk, "grep"], writes=[hk])
            else:
                apt, apr = 512 // NB, c.RP // NB
                k = 0
                for r in range(4):
                    for t in range(c.NTP):
                        p0 = r * apr + t * apt
                        P.dma("sync", hbg[p0:p0 + apt, 0:NB, :],
                              self.hall[t][r * 512:(r + 1) * 512, g * 128:(g + 1) * 128].rearrange("(a b) c -> a b c", b=NB),
                              "fq%d" % (k % 4), reads=[("hall", tt) for tt in range(c.NTP)], writes=[(hk, k)])
                        k += 1

        prep(0)
        for g in range(8):
            hbg = hb[g % 2]
            hk = ("hb", g % 2)
            hbreads = [hk] if part == "S" else [(hk, k) for k in range(npiece)]
            for b2 in range(NB // 2):
                bank = b2 % 2
                for q in range(2):
                    b = 2 * b2 + q
                    P.op("tensor", lambda e, b=b, q=q, bank=bank, hbg=hbg: e.matmul(
                        self.ps[bank][:, q * 256:(q + 1) * 256], lhsT=hbg[:, b, :], rhs=A[:, :],
                        start=True, stop=True), reads=hbreads + ["A"], writes=[("ps", bank)])
                dstap = T1[:, 2 * b2:2 * b2 + 2, :]
                srcap = self.ps[bank].rearrange("p (q k) -> p q k", q=2)
                if b2 % 2 == 0:
                    P.op("scalar", lambda e, d=dstap, s_=srcap: e.copy(out=d, in_=s_),
                         reads=[("ps", bank)], writes=["T1"])
                else:
                    P.op("vector", lambda e, d=dstap, s_=srcap: e.tensor_copy(out=d, in_=s_),
                         reads=[("ps", bank)], writes=["T1"])
            if g + 1 < 8:
                prep(g + 1)
            self.conv_some(3, after=[("mixT", part, g - 1)] if g > 0 else [])
            T1v = T1.rearrange("p b (ri k) -> p b ri k", ri=2)
            YTv = YT.rearrange("p (kb ka) -> p ka kb", ka=128)
            def emit_s2(ka2):
                bank = 2 + ka2 % 2
                t2 = T2[ka2 % 2]
                t2k = ("T2", ka2 % 2)
                for q in range(2):
                    ka = 2 * ka2 + q
                    P.op("tensor", lambda e, ka=ka, q=q, bank=bank: e.matmul(
                        self.ps[bank][0:2 * NB, q * 256:(q + 1) * 256], lhsT=T1v[:, :, :, ka], rhs=C[:, :],
                        start=True, stop=True), reads=["T1", "C"], writes=[("ps", bank)])
                if ka2 % 2 == 0:
                    P.op("scalar", lambda e, t2=t2, bank=bank: e.copy(out=t2[0:2 * NB, :], in_=self.ps[bank][0:2 * NB, :]),
                         reads=[("ps", bank)], writes=[t2k])
                else:
                    P.op("vector", lambda e, t2=t2, bank=bank: e.tensor_copy(out=t2[0:2 * NB, :], in_=self.ps[bank][0:2 * NB, :]),
                         reads=[("ps", bank)], writes=[t2k])

            def emit_s3(ka2):
                t2 = T2[ka2 % 2]
                t2k = ("T2", ka2 % 2)
                for q in range(2):
                    ka = 2 * ka2 + q
                    kg = ka // KG
                    bank3 = 4 + kg % 2
                    col = (ka % KG) * KB
                    for w in range(2):
                        P.op("tensor", lambda e, t2=t2, q=q, w=w, ka=ka, bank3=bank3, col=col: e.matmul(
                            self.ps[bank3][:, col:col + KB],
                            lhsT=t2[0:2 * NB, q * 256 + w * 128:q * 256 + (w + 1) * 128],
                            rhs=B[0:2 * NB, w, ka * KB:(ka + 1) * KB],
                            start=(w == 0), stop=(w == 1)),
                            reads=[t2k, Bkey], writes=[("ps", bank3)])
                    if ka % KG == KG - 1:
                        ka0 = ka - KG + 1
                        dstap = YTv[:, ka0:ka0 + KG, :]
                        srcap = self.ps[bank3].rearrange("p (k b) -> p k b", b=KB)
                        if kg % 2 == 0:
                            P.op("vector", lambda e, d=dstap, s_=srcap: e.tensor_copy(out=d, in_=s_),
                                 reads=[("ps", bank3)], writes=["YT"])
                        else:
                            P.op("scalar", lambda e, d=dstap, s_=srcap: e.copy(out=d, in_=s_),
                                 reads=[("ps", bank3)], writes=["YT"])

            emit_s2(0)
            for ka2 in range(64):
                if ka2 + 1 < 64:
                    emit_s2(ka2 + 1)
                emit_s3(ka2)
            P.dma("sync", self.mixT[part][g, :, :], YT, "fy", reads=["YT"], writes=[("mixT", part, g)])

    def fourier_layer(self, l):
        c, P, nc = self.c, self.P, self.nc
        with contextlib.ExitStack() as ls:
            sb = lambda name, shape, dtype: ls.enter_context(nc.sbuf_tensor("f%d_" % l + name, shape, dtype))
            fb = {}
            fb["grep"] = grep = sb("grep", [128, D], F32)
            fb["A"] = A = sb("A", [128, 256], BF16)
            fb["C"] = C = sb("C", [128, 256], BF16)
            fb["BP"] = sb("BP", [128, 2, 128 * c.KBP], BF16)
            fb["BS"] = sb("BS", [128, 2, 128 * c.NBS], BF16)
            fb["xg"] = sb("xg", [128, c.NBS * 64], F32)
            fb["xg2"] = sb("xg2", [128, c.NBS * 64], F32)
            fb["hb"] = [sb("hb%d" % i, [128, max(c.NBP, c.NBS), 128], BF16) for i in range(2)]
            fb["T2"] = [sb("T2_%d" % i, [128, 512], BF16) for i in range(2)]
            fb["ssq"] = sb("ssq", [128, c.NBS], F32)
            fb["rtq"] = sb("rtq", [128, c.NBS], F32)
            fb["rsq"] = sb("rsq", [128, c.NBS], F32)
            mT = sb("mT", [128, 8, 512], BF16)

            P.dma("sync", grep[:, :], self.gmixrep_d[:, l, :], "f0", writes=["grep"])
            P.dma("gpsimd", A[:, :], self.dftA_d[:, :], "f1", writes=["A"])
            P.dma("gpsimd", C[:, :], self.dftC_d[:, :], "f2", writes=["C"])
            k = 0
            for part, nbp, ncol in (("S", c.NBS, 128 * c.NBS), ("P", c.NBP, 128 * c.KBP)):
                ch = min(2048, ncol)
                for w in range(2):
                    for i in range(ncol // ch):
                        P.dma("gpsimd", fb["B" + part][0:2 * nbp, w, i * ch:(i + 1) * ch],
                              self.dftB_d[part][w, :, i * ch:(i + 1) * ch], "fb%d" % (k % 4),
                              writes=["B" + part])
                        k += 1

            hst = mT[:, :, :].rearrange("p g t -> p (g t)").rearrange("p (s c) -> p s c", c=D)
            for t in range(c.NTP):
                b = self.next_xt()
                xt, xkey = self.xt[b], ("xt", b)
                self.load_tile(l, "P", t, b)
                for s in range(4):
                    P.op("scalar", lambda e, s=s, xt=xt: e.activation(out=self.hn[s % 2][:, :], in_=xt[:, s, :], func=AF.Square,
                                                                     accum_out=self.ss[:, s:s + 1]),
                         reads=[(xkey, s)], writes=[("ss", s), ("hn", s % 2)])
                P.op("scalar", lambda e: e.activation(out=self.rt[:, 0:4], in_=self.ss[:, 0:4], func=AF.Sqrt,
                                                       bias=self.epsb[:, 0:1], scale=1.0 / D),
                     reads=[("ss", s) for s in range(4)] + ["epsb"], writes=["rt"])
                P.op("vector", lambda e: e.reciprocal(out=self.rstd[:, 0:4], in_=self.rt[:, 0:4]),
                     reads=["rt"], writes=["rstd"])
                for s in range(4):
                    hb_, hk = self.hn[s % 2], ("hn", s % 2)
                    P.op("vector", lambda e, s=s, hb_=hb_, xt=xt: e.tensor_scalar(
                        out=hb_[:, :], in0=xt[:, s, :], scalar1=self.rstd[:, s:s + 1], scalar2=None, op0=ALU.mult),
                        reads=[(xkey, s), "rstd"], writes=[hk])
                    P.op("vector", lambda e, s=s, hb_=hb_: e.tensor_tensor(
                        out=hst[:, s, :], in0=hb_[:, :], in1=grep[:, :], op=ALU.mult),
                        reads=[hk, "grep"], writes=[("hst", s)])
                P.dma("sync", self.hloc[t].rearrange("(s p) c -> p s c", p=128), hst,
                      "fh", reads=[("hst", s) for s in range(4)], writes=[("hloc", t)])
                _coll(P, self.hloc[t], self.hall[t], "ag", [("hloc", t)], [("hall", t)], c.groups)
            P.barrier(skip=("C_ag",) + tuple("D_cv%d" % i for i in range(4)))

            self.fft_part(l, "S", fb)
            P.barrier(skip=("C_ag",) + tuple("D_cv%d" % i for i in range(4)))
            self.fft_part(l, "P", fb)
            P.barrier(skip=tuple("D_cv%d" % i for i in range(4)))

            self.convert_weights()
            s_wf = c.slot[("wf", l)]
            for part in ("P", "S"):
                mixv = self.mixT[part].rearrange("g k r -> k g r")
                for t in range(self.NT[part]):
                    b = self.next_xt()
                    xt, xkey = self.xt[b], ("xt", b)
                    self.load_tile(l, part, t, b)
                    P.dma("sync", mT[:, :, :], mixv[:, :, t * 512:(t + 1) * 512], "fm",
                          reads=[("mixT", part, g) for g in range(8)], writes=["mT"])
                    ws = [self.ring_load(s_wf + j) for j in range(4)]
                    for s in range(4):
                        for half in range(2):
                            bank = 2 * s + half
                            for g in range(8):
                                w, wk = ws[g // 2]
                                P.op("tensor", lambda e, w=w, g=g, s=s, half=half, bank=bank: e.matmul(
                                    self.ps[bank], lhsT=mT[:, g, s * 128:(s + 1) * 128],
                                    rhs=w[:, (g % 2) * 1024 + half * 512:(g % 2) * 1024 + (half + 1) * 512],
                                    start=(g == 0), stop=(g == 7)),
                                    reads=[wk, "mT"], writes=[("ps", bank)])
                    self.add_psum_to_xt(xt, xkey)
                    self.ffn_tile(l, xt, xkey)
                    self.store_tile(l, part, t, b)
    def qk_norm_rope(self, tmp, nh, pss, gain, gkey, cs, cskey, s):
        P = self.P
        qn, qr = tmp["qn"], tmp["qr"]
        t1, t2, t3, t4 = tmp["t"]
        h0 = 0
        for (psap, bank, n) in pss:
            for i in range(n):
                P.op("scalar", lambda e, psap=psap, i=i, h0=h0: e.activation(
                    out=self.junk[:, (h0 + i) * 128:(h0 + i + 1) * 128], in_=psap[:, i * 128:(i + 1) * 128],
                    func=AF.Square, accum_out=self.ss[:, h0 + i:h0 + i + 1]),
                    reads=[("ps", bank)], writes=[("ssh", h0 + i), ("junk", h0 + i)])
            h0 += n
        P.op("scalar", lambda e: e.activation(out=self.rt[:, 0:nh], in_=self.ss[:, 0:nh], func=AF.Sqrt,
                                               bias=self.epsb[:, 0:1], scale=1.0 / HD),
             reads=[("ssh", i) for i in range(nh)] + ["epsb"], writes=["rth"])
        P.op("vector", lambda e: e.reciprocal(out=self.rstd[:, 0:nh], in_=self.rt[:, 0:nh]),
             reads=["rth"], writes=["rstdh"])
        h0 = 0
        for (psap, bank, n) in pss:
            P.op("vector", lambda e, psap=psap, n=n, h0=h0: e.tensor_tensor(
                out=qn[:, h0:h0 + n, :], in0=psap.rearrange("p (h d) -> p h d", d=128),
                in1=self.rstd[:, h0:h0 + n].unsqueeze(2).broadcast_to([128, n, 128]), op=ALU.mult),
                reads=[("ps", bank), "rstdh"], writes=["qn"] + tmp.get("fence", []))
            h0 += n
        P.op("gpsimd", lambda e: e.tensor_tensor(
            out=qn[:, 0:nh, :], in0=qn[:, 0:nh, :],
            in1=gain.unsqueeze(1).broadcast_to([128, nh, 128]), op=ALU.mult),
            reads=["qn", gkey], writes=["qn"])
        qv = qn[:, 0:nh, :].rearrange("p h (i two) -> p h i two", two=2)
        rv = qr[:, 0:nh, :].rearrange("p h (i two) -> p h i two", two=2)
        x0, x1 = qv[:, :, :, 0], qv[:, :, :, 1]
        cc = cs[:, s, 0:64].unsqueeze(1).broadcast_to([128, nh, 64])
        sn = cs[:, s, 64:128].unsqueeze(1).broadcast_to([128, nh, 64])
        P.op("gpsimd", lambda e: e.tensor_tensor(out=t1[:, 0:nh, :], in0=x0, in1=cc, op=ALU.mult),
             reads=["qn", cskey], writes=["t1"])
        P.op("gpsimd", lambda e: e.tensor_tensor(out=t2[:, 0:nh, :], in0=x1, in1=sn, op=ALU.mult),
             reads=["qn", cskey], writes=["t2"])
        P.op("gpsimd", lambda e: e.tensor_tensor(out=rv[:, :, :, 0], in0=t1[:, 0:nh, :], in1=t2[:, 0:nh, :],
                                                 op=ALU.subtract), reads=["t1", "t2"], writes=["qr0"])
        P.op("vector", lambda e: e.tensor_tensor(out=t3[:, 0:nh, :], in0=x0, in1=sn, op=ALU.mult),
             reads=["qn", cskey], writes=["t3"])
        P.op("vector", lambda e: e.tensor_tensor(out=t4[:, 0:nh, :], in0=x1, in1=cc, op=ALU.mult),
             reads=["qn", cskey], writes=["t4"])
        P.op("vector", lambda e: e.tensor_tensor(out=rv[:, :, :, 1], in0=t3[:, 0:nh, :], in1=t4[:, 0:nh, :],
                                                 op=ALU.add), reads=["t3", "t4"], writes=["qr1"])


    def attn_layer(self, l):
        c, P, nc = self.c, self.P, self.nc
        scale = float(HD) ** -0.5
        NBmax = max(c.NBP, c.NBS)
        with contextlib.ExitStack() as ls:
            sb = lambda name, shape, dtype: ls.enter_context(nc.sbuf_tensor("a%d_" % l + name, shape, dtype))
            KT = sb("KT", [128, 2, NBmax * 128], BF16)
            Vb = sb("Vb", [128, NBmax, 256], BF16)
            QT = sb("QT", [128, 8, 512], BF16)
            OT = sb("OT", [128, 8, 512], BF16)
            NPT = 4
            pT = [sb("pT%d" % i, [128, 1024], BF16) for i in range(NPT)]
            s2 = [sb("s2_%d" % i, [128, 512], BF16) for i in range(NPT)]
            rinv = sb("rinv", [128, 512], F32)
            cs = sb("cs", [128, 4, 128], F32)
            qkg = sb("qkg", [128, 2, 128], F32)
            flat = self.act[:, :, :].rearrange("p f t -> p (f t)").bitcast(F32)
            assert c.FC * 256 >= 3584
            tmp = {
                "fence": [("act", fc) for fc in range(c.FC)],
                "qn": flat[:, 0:1024].rearrange("p (h d) -> p h d", d=128),
                "t": [flat[:, 1024 + 512 * i:1024 + 512 * (i + 1)].rearrange("p (h d) -> p h d", d=64) for i in range(4)],
                "qr": flat[:, 3072:3584].bitcast(BF16).rearrange("p (h d) -> p h d", d=128),
            }
            P.dma("sync", qkg[:, :, :], self.qkg_d[:, l, :, :], "a0", writes=["qkg"])
            self.convert_weights()
            s_q, s_kv, s_o = c.slot[("wq", l)], c.slot[("wkv", l)], c.slot[("wo", l)]

            def pass1(part):
                for ta in range(self.NT[part]):
                    b = self.next_xt()
                    xt, xkey = self.xt[b], ("xt", b)
                    self.load_tile(l, part, ta, b)
                    P.dma("sync", cs[:, :, :], self.cs_d[part][ta], "a1", writes=["cs"])
                    self.norm_to_hT(xt, xkey, self.gmixT, "gmixT", l * 8)
                    ws = [self.ring_load(s_kv + j) for j in range(2)]
                    def mm1(s, ws=ws):
                        bank = 4 + s % 2
                        for kc in range(KC):
                            w, wk = ws[kc // 4]
                            P.op("tensor", lambda e, w=w, kc=kc, s=s, bank=bank: e.matmul(
                                self.ps[bank], lhsT=self.hT[:, kc, s * 128:(s + 1) * 128],
                                rhs=w[:, (kc % 4) * 512:(kc % 4 + 1) * 512],
                                start=(kc == 0), stop=(kc == KC - 1)),
                                reads=[wk, ("hT", s)], writes=[("ps", bank)])
                    mm1(0)
                    for s in range(4):
                        sa = 4 * ta + s
                        bank = 4 + s % 2
                        if s + 1 < 4:
                            mm1(s + 1)
                        P.op("scalar", lambda e, sa=sa, bank=bank: e.copy(out=Vb[:, sa, :], in_=self.ps[bank][:, 256:512]),
                             reads=[("ps", bank)], writes=[("V", sa)])
                        self.qk_norm_rope(tmp, 2, [(self.ps[bank][:, 0:256], bank, 2)], qkg[:, 1, :], "qkg", cs, "cs", s)
                        pb = self.pp[3][:, (s % 2) * 512:(s % 2 + 1) * 512].bitcast(BF16)
                        for kvh in range(2):
                            P.op("tensor", lambda e, kvh=kvh, pb=pb: e.transpose(
                                out=pb[:, kvh * 128:(kvh + 1) * 128], in_=tmp["qr"][:, kvh, :], identity=self.identb[:, :]),
                                reads=["qr0", "qr1", "identb"], writes=[("ps", 6 + s % 2)])
                        P.op("scalar", lambda e, sa=sa, pb=pb: e.copy(
                            out=KT[:, :, sa * 128:(sa + 1) * 128], in_=pb[:, 0:256].rearrange("p (h t) -> p h t", h=2)),
                            reads=[("ps", 6 + s % 2)], writes=[("K", sa)])

            def pass2(part, nbk):
                allK = [("K", sa) for sa in range(nbk)]
                allV = [("V", sa) for sa in range(nbk)]
                for ta in range(self.NT[part]):
                    b = self.next_xt()
                    xt, xkey = self.xt[b], ("xt", b)
                    self.load_tile(l, part, ta, b)
                    P.dma("sync", cs[:, :, :], self.cs_d[part][ta], "a1", writes=["cs"])
                    self.norm_to_hT(xt, xkey, self.gmixT, "gmixT", l * 8)
                    ws = [self.ring_load(s_q + j) for j in range(4)]
                    def mm2(s, ws=ws):
                        banks = (4, 5) if s % 2 == 0 else (2, 3)
                        for half in range(2):
                            for kc in range(KC):
                                w, wk = ws[kc // 2]
                                P.op("tensor", lambda e, w=w, kc=kc, s=s, half=half, banks=banks: e.matmul(
                                    self.ps[banks[half]], lhsT=self.hT[:, kc, s * 128:(s + 1) * 128],
                                    rhs=w[:, (kc % 2) * 1024 + half * 512:(kc % 2) * 1024 + (half + 1) * 512],
                                    start=(kc == 0), stop=(kc == KC - 1)),
                                    reads=[wk, ("hT", s)], writes=[("ps", banks[half])])
                    mm2(0)
                    for s in range(4):
                        banks = (4, 5) if s % 2 == 0 else (2, 3)
                        if s + 1 < 4:
                            mm2(s + 1)
                        self.qk_norm_rope(tmp, 8, [(self.ps[banks[0]], banks[0], 4), (self.ps[banks[1]], banks[1], 4)],
                                          qkg[:, 0, :], "qkg", cs, "cs", s)
                        pb = self.pp[3][:, (s % 2) * 512:(s % 2 + 1) * 512].bitcast(BF16)
                        for hd in range(8):
                            P.op("tensor", lambda e, hd=hd, pb=pb: e.transpose(
                                out=pb[:, hd * 128:(hd + 1) * 128], in_=tmp["qr"][:, hd, :], identity=self.identb[:, :]),
                                reads=["qr0", "qr1", "identb"], writes=[("ps", 6 + s % 2)])
                        P.op("vector", lambda e, s=s, pb=pb: e.tensor_copy(
                            out=QT[:, :, s * 128:(s + 1) * 128], in_=pb[:, :].rearrange("p (h t) -> p h t", h=8)),
                            reads=[("ps", 6 + s % 2)], writes=["QT"])
                    npair = nbk // 2
                    items = [(hd, kp) for hd in range(8) for kp in range(npair)]

                    def emit_score(i):
                        hd, kp = items[i]
                        kvh = hd // 4
                        j = i % 3
                        for q in range(2):
                            kt = 2 * kp + q
                            P.op("tensor", lambda e, hd=hd, kt=kt, kvh=kvh, j=j, q=q: e.matmul(
                                self.ps[2 * j + q], lhsT=KT[:, kvh, kt * 128:(kt + 1) * 128], rhs=QT[:, hd, :],
                                start=True, stop=True), reads=["QT"] + allK, writes=[("ps", 2 * j + q)])
                        jb = i % NPT
                        P.op("scalar", lambda e, j=j, jb=jb: e.activation(
                            out=pT[jb][:, :], in_=self.pp[j][:, :], func=AF.Exp, scale=scale),
                            reads=[("ps", 2 * j), ("ps", 2 * j + 1)], writes=[("pT", jb)])

                    emit_score(0)
                    emit_score(1)
                    pending = []
                    for i, (hd, kp) in enumerate(items):
                        if i + 2 < len(items):
                            emit_score(i + 2)
                        kvh = hd // 4
                        jb = i % NPT
                        par = hd % 2
                        po, pr = 6, 7
                        for q in range(2):
                            kt = 2 * kp + q
                            P.op("tensor", lambda e, kt=kt, kvh=kvh, jb=jb, po=po, q=q: e.matmul(
                                self.ps[po], lhsT=Vb[:, kt, kvh * 128:(kvh + 1) * 128],
                                rhs=pT[jb][:, q * 512:(q + 1) * 512],
                                start=(kt == 0), stop=(kt == nbk - 1)),
                                reads=[("pT", jb)] + allV, writes=[("ps", po)])
                        sj = (i // 2) % NPT
                        if kp % 2 == 0:
                            P.op("vector", lambda e, jb=jb, sj=sj: e.tensor_tensor(
                                out=s2[sj][:, :], in0=pT[jb][:, 0:512], in1=pT[jb][:, 512:1024], op=ALU.add),
                                reads=[("pT", jb)], writes=[("s2", sj)])
                        else:
                            P.op("vector", lambda e, jb=jb: e.tensor_tensor(
                                out=rinv[:, :].bitcast(BF16)[:, 0:512], in0=pT[jb][:, 0:512], in1=pT[jb][:, 512:1024], op=ALU.add),
                                reads=[("pT", jb)], writes=["s2tmp", "rinv"])
                            P.op("vector", lambda e, sj=sj: e.tensor_tensor(
                                out=s2[sj][:, :], in0=s2[sj][:, :], in1=rinv[:, :].bitcast(BF16)[:, 0:512], op=ALU.add),
                                reads=["s2tmp", ("s2", sj)], writes=[("s2", sj)])

                            def emit_rs(sj=sj, pr=pr, kp=kp):
                                P.op("tensor", lambda e: e.matmul(
                                    self.ps[pr], lhsT=self.onesb[:, :], rhs=s2[sj][:, :],
                                    start=(kp == 1), stop=(kp == npair - 1)),
                                    reads=[("s2", sj), "onesb"], writes=[("ps", pr)])
                            if pending:
                                pending.pop()()
                            pending.append(emit_rs)
                        if kp == npair - 1:
                            while pending:
                                pending.pop()()
                            P.op("vector", lambda e, pr=pr: e.reciprocal(out=rinv[:, :], in_=self.ps[pr]),
                                 reads=[("ps", pr), "s2tmp"], writes=["rinv", "s2tmp"])
                            P.op("vector", lambda e, po=po, hd=hd: e.tensor_tensor(
                                out=OT[:, hd, :], in0=self.ps[po], in1=rinv[:, :], op=ALU.mult),
                                reads=[("ps", po), "rinv"], writes=[("OT", hd)])
                    ws = [self.ring_load(s_o + j) for j in range(4)]
                    for s in range(4):
                        for half in range(2):
                            bank = 2 * s + half
                            for hd in range(8):
                                w, wk = ws[hd // 2]
                                P.op("tensor", lambda e, w=w, hd=hd, s=s, half=half, bank=bank: e.matmul(
                                    self.ps[bank], lhsT=OT[:, hd, s * 128:(s + 1) * 128],
                                    rhs=w[:, (hd % 2) * 1024 + half * 512:(hd % 2) * 1024 + (half + 1) * 512],
                                    start=(hd == 0), stop=(hd == 7)),
                                    reads=[wk, ("OT", hd)], writes=[("ps", bank)])
                    self.add_psum_to_xt(xt, xkey)
                    self.ffn_tile(l, xt, xkey)
                    self.store_tile(l, part, ta, b)

            nloc = c.RP // 128
            pass1("P")
            P.dma("sync", self.kloc.rearrange("(h d) k -> d h k", h=2), KT[:, :, 0:c.RP], "ak",
                  reads=[("K", sa) for sa in range(nloc)], writes=["kloc"])
            P.dma("sync", self.vloc.rearrange("(s p) c -> p s c", p=128), Vb[:, 0:nloc, :], "av",
                  reads=[("V", sa) for sa in range(nloc)], writes=["vloc"])
            _coll(P, self.kloc, self.kall, "agk", ["kloc"], ["kall"], c.groups)
            _coll(P, self.vloc, self.vall, "agv", ["vloc"], ["vall"], c.groups)
            pass1("S")
            pass2("S", c.NBS)
            kallv = self.kall.rearrange("(r h d) k -> h d r k", r=4, h=2)
            for h in range(2):
                P.dma("sync", KT[:, h, 0:4 * c.RP].rearrange("d (r k) -> d r k", r=4), kallv[h], "ak%d" % h,
                      reads=["kall"], writes=[("K", sa) for sa in range(c.NBP)])
            P.dma("sync", Vb[:, 0:c.NBP, :], self.vall.rearrange("(s p) c -> p s c", p=128), "av2",
                  reads=["vall"], writes=[("V", sa) for sa in range(c.NBP)])
            pass2("P", c.NBP)


def _rope_rows(pos):
    inv = (np.float32(10000.0) ** (-np.arange(0, 64, 2, dtype=np.float32) / np.float32(64))).astype(np.float32)
    rowp = (pos // 64).astype(np.float32)
    colp = (pos % 64).astype(np.float32)
    ang = np.concatenate([rowp[:, None] * inv, colp[:, None] * inv], -1).astype(np.float32)
    return np.concatenate([np.cos(ang), np.sin(ang)], -1).astype(np.float32)


def _cs_tiles(pos):
    cs = _rope_rows(pos)
    nt = cs.shape[0] // 512
    return np.ascontiguousarray(cs.reshape(nt, 4, 128, 128).transpose(0, 2, 1, 3))


def _dft_common():
    a = np.arange(128)
    A = np.exp(-2j * np.pi * np.outer(a, a) / 128.0)
    m = np.concatenate([A.real, A.imag], 1).astype(np.float32)
    return m


def _dft_B(NB, R, kbs):
    b = np.arange(NB).astype(np.float64)
    ka = np.arange(128).astype(np.float64)
    kb = np.asarray(kbs).astype(np.float64)
    M = (np.exp(-2j * np.pi * b[:, None, None] * kb[None, None, :] / NB)
         * np.exp(-2j * np.pi * b[:, None, None] * ka[None, :, None] / float(R))) / np.sqrt(R * 128.0)
    n = 128 * len(kbs)
    Mr, Mi = M.real.reshape(NB, n), M.imag.reshape(NB, n)
    out = np.zeros((2, 2 * NB, n), np.float32)
    out[0, 0::2] = Mr
    out[0, 1::2] = -Mi
    out[1, 0::2] = -Mi
    out[1, 1::2] = -Mr
    return out


def _pack_weights(cfg, fourier_w, attn_w_qkv, attn_w_o, ffn_w_gate, ffn_w_up, ffn_w_down):
    wall = np.zeros((cfg.nslots, 128, 2048), np.float32)
    jF = jA = 0
    for l, t in enumerate(cfg.layers):
        if t == "F":
            w = np.asarray(fourier_w[jF], np.float32)
            s0 = cfg.slot[("wf", l)]
            wall[s0:s0 + 4] = w.reshape(4, 2, 128, D).transpose(0, 2, 1, 3).reshape(4, 128, 2048)
            jF += 1
        else:
            wqkv = np.asarray(attn_w_qkv[jA], np.float32)
            s0 = cfg.slot[("wq", l)]
            wall[s0:s0 + 4] = wqkv[:, 0:1024].reshape(4, 2, 128, 1024).transpose(0, 2, 1, 3).reshape(4, 128, 2048)
            s0 = cfg.slot[("wkv", l)]
            wall[s0:s0 + 2] = wqkv[:, 1024:1536].reshape(2, 4, 128, 512).transpose(0, 2, 1, 3).reshape(2, 128, 2048)
            s0 = cfg.slot[("wo", l)]
            wo = np.asarray(attn_w_o[jA], np.float32)
            wall[s0:s0 + 4] = wo.reshape(4, 2, 128, D).transpose(0, 2, 1, 3).reshape(4, 128, 2048)
            jA += 1
        FC = cfg.FC
        wg = np.asarray(ffn_w_gate[l], np.float32).reshape(KC, 128, FC, 128)
        wu = np.asarray(ffn_w_up[l], np.float32).reshape(KC, 128, FC, 128)
        gu = np.stack([wg, wu], 0)
        s0 = cfg.slot[("wgu", l)]
        wall[s0:s0 + FC] = gu.transpose(3, 2, 0, 1, 4).reshape(FC, 128, 2048)
        wd = np.asarray(ffn_w_down[l], np.float32)
        s0 = cfg.slot[("wd", l)]
        wall[s0:s0 + FC // 2] = wd.reshape(FC // 2, 2, 128, D).transpose(0, 2, 1, 3).reshape(FC // 2, 128, 2048)
    return wall.reshape(cfg.nslots * 128, 2048)


def _common_inputs(cfg, norm_mix, norm_ffn, attn_q_norm, attn_k_norm):
    L = cfg.L
    nm = np.asarray(norm_mix, np.float32)[:L]
    nf = np.asarray(norm_ffn, np.float32)[:L]
    gmixT = nm.reshape(L, KC, 128).transpose(2, 0, 1).reshape(128, L * KC).copy()
    gffnT = nf.reshape(L, KC, 128).transpose(2, 0, 1).reshape(128, L * KC).copy()
    gmixrep = np.broadcast_to(nm[None], (128, L, D)).copy()
    qkg = np.ones((128, L, 2, 128), np.float32)
    jA = 0
    for l, t in enumerate(cfg.layers):
        if t == "A":
            qkg[:, l, 0, :] = np.asarray(attn_q_norm[jA], np.float32)[None]
            qkg[:, l, 1, :] = np.asarray(attn_k_norm[jA], np.float32)[None]
            jA += 1
    dft = _dft_common()
    return dict(gmixT=gmixT, gffnT=gffnT, gmixrep=gmixrep, qkg=qkg, ident=np.eye(128, dtype=np.float32),
                dftA=dft, dftC=dft.copy(),
                dftBS=_dft_B(cfg.NBS, cfg.RS, np.arange(cfg.NBS)),
                csS=_cs_tiles(np.arange(cfg.RS)))


_NC_CACHE = {}


def run_cores(cfg, xp_list, xs_list, weights):
    key = (cfg.RP, cfg.RS, cfg.DFF, cfg.layers, cfg.ring)
    if key not in _NC_CACHE:
        _NC_CACHE[key] = Builder(cfg).build()
    nc = _NC_CACHE[key]
    in_maps = []
    for c in range(8):
        r = c % 4
        m = dict(xp=np.ascontiguousarray(xp_list[c], np.float32), xs=np.ascontiguousarray(xs_list[c], np.float32),
                 wall=weights["wall"])
        m.update(weights["common"])
        m["csP"] = _cs_tiles(cfg.RP * r + np.arange(cfg.RP))
        m["dftBP"] = _dft_B(cfg.NBP, 4 * cfg.RP, cfg.KBP * r + np.arange(cfg.KBP))
        in_maps.append(m)
    res = run_bass_kernel_spmd(nc, in_maps, core_ids=list(range(8)))
    return [(r["yp"], r["ys"]) for r in res.results]


def kernel(x_prompt, x_sample, norm_mix, norm_ffn, fourier_w, attn_w_qkv, attn_q_norm, attn_k_norm,
           attn_w_o, ffn_w_gate, ffn_w_up, ffn_w_down):
    cfg = Cfg()
    xp = np.asarray(x_prompt, np.float32)
    xs = np.asarray(x_sample, np.float32)
    weights = dict(
        wall=_pack_weights(cfg, fourier_w, attn_w_qkv, attn_w_o, ffn_w_gate, ffn_w_up, ffn_w_down),
        common=_common_inputs(cfg, norm_mix, norm_ffn, attn_q_norm, attn_k_norm))
    RP = cfg.RP
    xp_list = [xp[c // 4, RP * (c % 4):RP * (c % 4 + 1)] for c in range(8)]
    xs_list = [xs[c] for c in range(8)]
    outs = run_cores(cfg, xp_list, xs_list, weights)
    y_prompt = np.zeros_like(xp)
    y_sample = np.zeros_like(xs)
    for c in range(8):
        y_prompt[c // 4, RP * (c % 4):RP * (c % 4 + 1)] = outs[c][0]
        y_sample[c] = outs[c][1]
    return (y_prompt, y_sample)
```

```python
import bisect
import contextlib
import numpy as np
import ml_dtypes
import concourse.bass as bass
import concourse.mybir as mybir
from concourse.bass_utils import run_bass_kernel_spmd

F32 = mybir.dt.float32
BF16 = mybir.dt.bfloat16
AF = mybir.ActivationFunctionType
ALU = mybir.AluOpType
AX = mybir.AxisListType

D = 1024
KC = 8
HD = 128
NH = 8
NKV = 2
EPS = 1e-6
NEG = -30000.0


class Prog:
    ENGS = ("sync", "scalar", "vector", "gpsimd", "tensor")

    def __init__(self, nc, stack):
        self.nc = nc
        self.stack = stack
        self.ops = {e: [] for e in self.ENGS}
        self.marked = {e: [] for e in self.ENGS}
        self.cnt = {e: 0 for e in self.ENGS}
        self.seen = {e: {} for e in self.ENGS}
        self.sems = {}
        self.dmacnt = {}
        self.last_write = {}
        self.readers = {}

    def sem(self, name):
        if name not in self.sems:
            self.sems[name] = self.stack.enter_context(self.nc.semaphore(name))
        return self.sems[name]

    def _ticket(self, ref):
        if ref[0] == "dma":
            return (ref[1], ref[2])
        eng, pos = ref
        m = self.marked[eng]
        i = bisect.bisect_left(m, pos)
        if i < len(m):
            p = m[i]
        else:
            p = pos
            self.cnt[eng] += 1
            self.ops[eng][p][1] = self.cnt[eng]
            m.append(p)
        return ("E_" + eng, self.ops[eng][p][1])

    def _wait(self, eng, ticket):
        name, val = ticket
        if self.seen[eng].get(name, 0) >= val:
            return
        self.seen[eng][name] = val
        semh = self.sem(name)
        self.ops[eng].append([lambda e, s=semh, v=val: e.wait_ge(s, v), None, True])

    def _deps(self, eng, reads, writes):
        refs = []
        for r in reads:
            w = self.last_write.get(r)
            if w is not None:
                refs.append(w)
        for w_ in writes:
            w = self.last_write.get(w_)
            if w is not None:
                refs.append(w)
            refs.extend(self.readers.get(w_, {}).values())
        return [r for r in refs if not (r[0] == "tensor" and eng == "tensor")]

    def _update(self, ref, rkey, reads, writes):
        for r in reads:
            self.readers.setdefault(r, {})[rkey] = ref
        for w in writes:
            self.last_write[w] = ref
            self.readers[w] = {}

    def op(self, eng, fn, reads=(), writes=()):
        for ref in self._deps(eng, reads, writes):
            self._wait(eng, self._ticket(ref))
        self.ops[eng].append([fn, None, False])
        ref = (eng, len(self.ops[eng]) - 1)
        self._update(ref, eng, reads, writes)
        return ref

    def dma(self, q, out, in_, semkey, reads=(), writes=(), **kw):
        for ref in self._deps("dma:" + q, reads, writes):
            self._wait(q, self._ticket(ref))
        name = "D_" + semkey
        self.dmacnt[name] = self.dmacnt.get(name, 0) + 16
        val = self.dmacnt[name]
        semh = self.sem(name)
        self.ops[q].append([lambda e, o=out, i=in_, s=semh, k=kw: e.dma_start(out=o, in_=i, **k).then_inc(s, 16),
                            None, True])
        ref = ("dma", name, val)
        self._update(ref, name, reads, writes)
        return ref

    def barrier(self, skip=()):
        tickets = []
        for e in self.ENGS:
            pos = len(self.ops[e]) - 1
            while pos >= 0 and self.ops[e][pos][2]:
                pos -= 1
            if pos >= 0:
                tickets.append(self._ticket((e, pos)))
        for name, val in self.dmacnt.items():
            if name not in skip:
                tickets.append((name, val))
        for e in self.ENGS:
            for t in tickets:
                self._wait(e, t)

    def replay(self, eng, e):
        semh = self.sem("E_" + eng) if self.cnt[eng] else None
        for o in self.ops[eng]:
            ins = o[0](e)
            if o[1] is not None:
                ins.then_inc(semh, 1)


def _coll(P, ins_ap, outs_ap, semkey, reads, writes, groups):
    for ref in P._deps("dma:gpsimd", reads, writes):
        P._wait("gpsimd", P._ticket(ref))
    name = "C_" + semkey
    P.dmacnt[name] = P.dmacnt.get(name, 0) + 1
    val = P.dmacnt[name]
    semh = P.sem(name)
    P.ops["gpsimd"].append([lambda e: e.collective_compute(
        "AllGather", ALU.bypass, replica_groups=groups, ins=[ins_ap.opt()], outs=[outs_ap.opt()]).then_inc(semh),
        None, True])
    ref = ("dma", name, val)
    P._update(ref, name, reads, writes)
    return ref


class Cfg:
    def __init__(self, RP=2048, RS=4096, DFF=2816, layers=("F", "A", "F", "A"), ring=6):
        self.RP, self.RS = RP, RS
        self.NTP, self.NTS = RP // 512, RS // 512
        self.NBP = 4 * RP // 128
        self.KBP = self.NBP // 4
        self.NBS = RS // 128
        assert self.NBS % 16 == 0 and self.NBP % 16 == 0
        self.DFF = DFF
        self.FC = DFF // 128
        assert self.FC % 2 == 0
        self.layers = tuple(layers)
        self.L = len(layers)
        self.ring = ring
        self.slot = {}
        n = 0
        for l, t in enumerate(self.layers):
            if t == "F":
                self.slot[("wf", l)] = n; n += 4
            else:
                self.slot[("wq", l)] = n; n += 4
                self.slot[("wkv", l)] = n; n += 2
                self.slot[("wo", l)] = n; n += 4
            self.slot[("wgu", l)] = n; n += self.FC
            self.slot[("wd", l)] = n; n += self.FC // 2
        self.nslots = n
        self.groups = [[0, 1, 2, 3], [4, 5, 6, 7]]


class Builder:
    def __init__(self, cfg):
        self.c = cfg
        self.nc = bass.Bass("TRN2", target_bir_lowering=False)

    def build(self):
        c, nc = self.c, self.nc
        RP, RS = c.RP, c.RS
        dt = nc.dram_tensor
        ein = lambda name, shape, dtype=F32: dt(name, shape, dtype, kind="ExternalInput").ap()
        itn = lambda name, shape, dtype: dt(name, shape, dtype).ap()
        self.xin = {"P": ein("xp", [RP, D]), "S": ein("xs", [RS, D])}
        self.yout = {"P": dt("yp", [RP, D], F32, kind="ExternalOutput").ap(),
                     "S": dt("ys", [RS, D], F32, kind="ExternalOutput").ap()}
        self.xscr = {"P": itn("xscrp", [RP, D], F32), "S": itn("xscrs", [RS, D], F32)}
        self.wall = ein("wall", [c.nslots * 128, 2048])
        self.gmixT_d = ein("gmixT", [128, c.L * 8])
        self.gffnT_d = ein("gffnT", [128, c.L * 8])
        self.gmixrep_d = ein("gmixrep", [128, c.L, D])
        self.qkg_d = ein("qkg", [128, c.L, 2, 128])
        self.cs_d = {"P": ein("csP", [c.NTP, 128, 4, 128]), "S": ein("csS", [c.NTS, 128, 4, 128])}
        self.dftA_d = ein("dftA", [128, 256])
        self.dftC_d = ein("dftC", [128, 256])
        self.dftB_d = {"P": ein("dftBP", [2, 2 * c.NBP, 128 * c.KBP]), "S": ein("dftBS", [2, 2 * c.NBS, 128 * c.NBS])}
        self.ident_d = ein("ident", [128, 128])
        self.wbf = itn("wbf", [c.nslots * 128, 2048], BF16)
        self.mixT = {"P": itn("mixTP", [8, 128, RP], BF16), "S": itn("mixTS", [8, 128, RS], BF16)}
        self.hloc = [itn("hloc%d" % t, [512, D], BF16) for t in range(c.NTP)]
        self.hall = [itn("hall%d" % t, [4 * 512, D], BF16) for t in range(c.NTP)]
        self.kloc = itn("kloc", [256, RP], BF16)
        self.kall = itn("kall", [1024, RP], BF16)
        self.vloc = itn("vloc", [RP, 256], BF16)
        self.vall = itn("vall", [4 * RP, 256], BF16)
        self.NT = {"P": c.NTP, "S": c.NTS}

        with contextlib.ExitStack() as st:
            self.st = st
            self.P = P = Prog(nc, st)
            sb = lambda name, shape, dtype: st.enter_context(nc.sbuf_tensor("s_" + name, shape, dtype))
            self.pp = [st.enter_context(nc.psum_tensor("pp%d" % i, [128, 1024], F32)) for i in range(4)]
            self.ps = [self.pp[i // 2][:, (i % 2) * 512:(i % 2 + 1) * 512] for i in range(8)]
            self.ident = sb("ident", [128, 128], F32)
            self.identb = sb("identb", [128, 128], BF16)
            self.onesb = sb("onesb", [128, 128], BF16)
            self.gmixT = sb("gmixT", [128, c.L * 8], F32)
            self.gffnT = sb("gffnT", [128, c.L * 8], F32)
            self.xtall = sb("xtall", [128, 8 * D], F32)
            self.xt = [self.xtall[:, i * 4 * D:(i + 1) * 4 * D].rearrange("p (s c) -> p s c", c=D) for i in range(2)]
            self.hn = [sb("hn%d" % i, [128, D], F32) for i in range(2)]
            self.hT = sb("hT", [128, KC, 512], BF16)
            self.act = sb("act", [128, c.FC, 512], BF16)
            self.sg = [sb("sg%d" % i, [128, 512], F32) for i in range(2)]
            self.ringb = [sb("ring%d" % i, [128, 2048], BF16) for i in range(c.ring)]
            self.junk = sb("junk", [128, D], BF16)
            self.ss = sb("ss", [128, 8], F32)
            self.rt = sb("rt", [128, 8], F32)
            self.rstd = sb("rstd", [128, 8], F32)
            self.epsb = sb("epsb", [128, 1], F32)
            self.ring_pos = 0
            self.conv_pos = 0
            self.conv_i = 0
            self.tilecnt = 0

            self.load_consts()
            for l, t in enumerate(c.layers):
                if t == "F":
                    self.fourier_layer(l)
                else:
                    self.attn_layer(l)
                P.barrier()
            P.barrier()

            with nc.Block() as block:
                @block.sync
                def _(e):
                    P.replay("sync", e)

                @block.scalar
                def _(e):
                    P.replay("scalar", e)

                @block.vector
                def _(e):
                    P.replay("vector", e)

                @block.gpsimd
                def _(e):
                    P.replay("gpsimd", e)

                @block.tensor
                def _(e):
                    P.replay("tensor", e)
        return nc

    def src(self, l, part):
        return self.xin[part] if l == 0 else self.xscr[part]

    def dst(self, l, part):
        return self.yout[part] if l == self.c.L - 1 else self.xscr[part]

    def load_consts(self):
        P = self.P
        P.dma("sync", self.ident[:, :], self.ident_d[:, :], "c0", writes=["ident"])
        P.dma("sync", self.gmixT[:, :], self.gmixT_d[:, :], "c1", writes=["gmixT"])
        P.dma("sync", self.gffnT[:, :], self.gffnT_d[:, :], "c2", writes=["gffnT"])
        P.op("vector", lambda e: e.tensor_copy(out=self.identb[:, :], in_=self.ident[:, :]),
             reads=["ident"], writes=["identb"])
        P.op("vector", lambda e: e.memset(self.onesb[:, :], 1.0), writes=["onesb"])
        P.op("vector", lambda e: e.memset(self.epsb[:, :], EPS), writes=["epsb"])

    def conv_some(self, k, after=()):
        c, P = self.c, self.P
        while k > 0 and self.conv_pos < c.nslots:
            s = self.conv_pos
            n = min(4, c.nslots - s)
            P.dma("gpsimd", self.wbf[s * 128:(s + n) * 128, :], self.wall[s * 128:(s + n) * 128, :],
                  "cv%d" % (self.conv_i % 4), reads=list(after), writes=[("wbf", j) for j in range(s, s + n)])
            self.conv_pos += n
            self.conv_i += 1
            k -= 1

    def conv_until(self, slot_end):
        while self.conv_pos < min(slot_end, self.c.nslots):
            self.conv_some(1)

    def convert_weights(self):
        self.conv_until(self.c.nslots)

    def next_xt(self):
        b = self.tilecnt % 2
        self.tilecnt += 1
        return b

    def load_tile(self, l, part, t, b):
        src = self.src(l, part)
        self.P.dma("sync", self.xt[b][:, :, :],
                   src[t * 512:(t + 1) * 512, :].rearrange("(s p) c -> p s c", p=128),
                   "xt%d" % b, reads=[("X", part, l, t)], writes=[(("xt", b), s) for s in range(4)])

    def store_tile(self, l, part, t, b):
        dst = self.dst(l, part)
        self.P.dma("scalar", dst[t * 512:(t + 1) * 512, :].rearrange("(s p) c -> p s c", p=128),
                   self.xt[b][:, :, :], "st%d" % b,
                   reads=[(("xt", b), s) for s in range(4)], writes=[("X", part, l + 1, t)])

    def ring_load(self, slot):
        P = self.P
        b = self.ring_pos % self.c.ring
        self.ring_pos += 1
        key = ("ring", b)
        P.dma("sync", self.ringb[b][:, :], self.wbf[slot * 128:(slot + 1) * 128, :], "ring%d" % b,
              reads=[("wbf", slot)], writes=[key])
        return self.ringb[b], key

    def norm_to_hT(self, xt, xkey, gT, gkey, gcol):
        P = self.P
        for s in range(4):
            P.op("scalar", lambda e, s=s: e.activation(out=self.hn[s % 2][:, :], in_=xt[:, s, :], func=AF.Square,
                                                      accum_out=self.ss[:, s:s + 1]),
                 reads=[(xkey, s)], writes=[("ss", s), ("hn", s % 2)])
        P.op("scalar", lambda e: e.activation(out=self.rt[:, 0:4], in_=self.ss[:, 0:4], func=AF.Sqrt,
                                               bias=self.epsb[:, 0:1], scale=1.0 / D),
             reads=[("ss", s) for s in range(4)] + ["epsb"], writes=["rt"])
        P.op("vector", lambda e: e.reciprocal(out=self.rstd[:, 0:4], in_=self.rt[:, 0:4]),
             reads=["rt"], writes=["rstd"])
        for s in range(4):
            hb = self.hn[s % 2]
            hk = ("hn", s % 2)
            P.op("vector", lambda e, s=s, hb=hb: e.tensor_scalar(out=hb[:, :], in0=xt[:, s, :],
                                                                scalar1=self.rstd[:, s:s + 1], scalar2=None,
                                                                op0=ALU.mult),
                 reads=[(xkey, s), "rstd"], writes=[hk])
            for half in range(2):
                bank = 2 * (s % 2) + half
                pk = ("ps", bank)
                for j in range(4):
                    kc = half * 4 + j
                    P.op("tensor", lambda e, hb=hb, kc=kc, j=j, bank=bank: e.transpose(
                        out=self.ps[bank][:, j * 128:(j + 1) * 128], in_=hb[:, kc * 128:(kc + 1) * 128],
                        identity=self.ident[:, :]),
                        reads=[hk, "ident"], writes=[pk])
                g0 = gcol + half * 4
                P.op("vector", lambda e, s=s, half=half, bank=bank, g0=g0: e.tensor_tensor(
                    out=self.hT[:, half * 4:half * 4 + 4, s * 128:(s + 1) * 128],
                    in0=self.ps[bank][:, :].rearrange("p (j t) -> p j t", j=4),
                    in1=gT[:, g0:g0 + 4].unsqueeze(2).broadcast_to([128, 4, 128]),
                    op=ALU.mult),
                    reads=[pk, gkey], writes=[("hT", s)])

    def ffn_tile(self, l, xt, xkey):
        c, P = self.c, self.P
        self.norm_to_hT(xt, xkey, self.gffnT, "gffnT", l * 8)
        hTr = [("hT", s) for s in range(4)]
        s_gu = c.slot[("wgu", l)]
        s_d = c.slot[("wd", l)]
        for fc in range(c.FC):
            w, wk = self.ring_load(s_gu + fc)
            par = fc % 2
            pg, pu = self.ps[4 + 2 * par], self.ps[5 + 2 * par]
            kg, ku = ("ps", 4 + 2 * par), ("ps", 5 + 2 * par)
            for kc in range(KC):
                P.op("tensor", lambda e, w=w, kc=kc, pg=pg: e.matmul(
                    pg[:, :], lhsT=w[:, kc * 128:(kc + 1) * 128], rhs=self.hT[:, kc, :],
                    start=(kc == 0), stop=(kc == KC - 1)), reads=[wk] + hTr, writes=[kg])
            for kc in range(KC):
                P.op("tensor", lambda e, w=w, kc=kc, pu=pu: e.matmul(
                    pu[:, :], lhsT=w[:, 1024 + kc * 128:1024 + (kc + 1) * 128], rhs=self.hT[:, kc, :],
                    start=(kc == 0), stop=(kc == KC - 1)), reads=[wk] + hTr, writes=[ku])
            sgb = self.sg[par]
            P.op("scalar", lambda e, pg=pg, sgb=sgb: e.activation(out=sgb[:, :], in_=pg[:, :], func=AF.Silu),
                 reads=[kg], writes=[("sg", par)])
            P.op("vector", lambda e, pu=pu, sgb=sgb, fc=fc: e.tensor_tensor(
                out=self.act[:, fc, :], in0=pu[:, :], in1=sgb[:, :], op=ALU.mult),
                reads=[ku, ("sg", par)], writes=[("act", fc)])
        for j in range(c.FC // 2):
            w, wk = self.ring_load(s_d + j)
            for i in range(2):
                fc = 2 * j + i
                for s in range(4):
                    for half in range(2):
                        bank = 2 * s + half
                        P.op("tensor", lambda e, w=w, i=i, fc=fc, s=s, half=half, bank=bank: e.matmul(
                            self.ps[bank][:, :], lhsT=self.act[:, fc, s * 128:(s + 1) * 128],
                            rhs=w[:, i * 1024 + half * 512:i * 1024 + (half + 1) * 512],
                            start=(fc == 0), stop=(fc == c.FC - 1)),
                            reads=[wk, ("act", fc)], writes=[("ps", bank)])
        for s in range(4):
            for half in range(2):
                bank = 2 * s + half
                P.op("vector", lambda e, s=s, half=half, bank=bank: e.tensor_tensor(
                    out=xt[:, s, half * 512:(half + 1) * 512], in0=self.ps[bank][:, :],
                    in1=xt[:, s, half * 512:(half + 1) * 512], op=ALU.add),
                    reads=[("ps", bank), (xkey, s)], writes=[(xkey, s)])

    def add_psum_to_xt(self, xt, xkey):
        for s in range(4):
            for half in range(2):
                bank = 2 * s + half
                self.P.op("vector", lambda e, s=s, half=half, bank=bank: e.tensor_tensor(
                    out=xt[:, s, half * 512:(half + 1) * 512], in0=self.ps[bank][:, :],
                    in1=xt[:, s, half * 512:(half + 1) * 512], op=ALU.add),
                    reads=[("ps", bank), (xkey, s)], writes=[(xkey, s)])


    def fft_part(self, l, part, fb):
        c, P = self.c, self.P
        NB = c.NBP if part == "P" else c.NBS
        KB = c.KBP if part == "P" else c.NBS
        NT = self.NT[part]
        A, C, B, T2, hb, xg = fb["A"], fb["C"], fb["B" + part], fb["T2"], fb["hb"], fb["xg"]
        Bkey = "B" + part
        T1 = self.xtall[:, :].bitcast(BF16)[:, 0:NB * 256].rearrange("p (b k) -> p b k", k=256)
        T1f = self.xtall[:, :].bitcast(BF16)
        YT = self.act[:, :, :].rearrange("p f t -> p (f t)")[:, 0:128 * KB]
        KG = 512 // KB
        if part == "S":
            src = self.src(l, "S")
            allX = [("X", "S", l, t) for t in range(NT)]
            srcv = src.rearrange("(a b) c -> a b c", b=NB)
            ssq, rtq, rsq, grep = fb["ssq"], fb["rtq"], fb["rsq"], fb["grep"]
            nbs = NB // 16
            hbv = lambda h: h[:, :, :].rearrange("p b c -> p (b c)").bitcast(F32)[:, 0:NB * 64].rearrange("p (b c) -> p b c", c=D)
            bufs = [xg[:, :].rearrange("p (b c) -> p b c", c=D), fb["xg2"][:, :].rearrange("p (b c) -> p b c", c=D),
                    hbv(hb[0]), hbv(hb[1])]
            bkeys = [("xg", 0), ("xg", 1), ("hb", 0), ("hb", 1)]
            for i in range(16):
                bf, bk = bufs[i % 4], bkeys[i % 4]
                P.dma("sync", bf[:, 0:nbs, :], srcv[:, i * nbs:(i + 1) * nbs, :], "fs%d" % (i % 4),
                      reads=allX, writes=[bk])
                for q in range(nbs):
                    b = i * nbs + q
                    jk = b % 4
                    P.op("scalar", lambda e, bf=bf, q=q, b=b, jk=jk: e.activation(
                        out=T1f[:, jk * D:(jk + 1) * D], in_=bf[:, q, :], func=AF.Square, accum_out=ssq[:, b:b + 1]),
                        reads=[bk], writes=[("ssq", b), ("jk", jk)])
            P.op("scalar", lambda e: e.activation(out=rtq[:, 0:NB], in_=ssq[:, 0:NB], func=AF.Sqrt,
                                                   bias=self.epsb[:, 0:1], scale=1.0 / D),
                 reads=[("ssq", b) for b in range(NB)] + ["epsb"], writes=["rtq"])
            P.op("vector", lambda e: e.reciprocal(out=rsq[:, 0:NB], in_=rtq[:, 0:NB]), reads=["rtq"], writes=["rsq"])
            hnb = NB // 2
            xgv2 = [xg[:, :].rearrange("p (b c) -> p b c", c=128), fb["xg2"][:, :].rearrange("p (b c) -> p b c", c=128)]
        else:
            npiece = 4 * c.NTP
        def prep(g):
            hbg = hb[g % 2]
            hk = ("hb", g % 2)
            if part == "S":
                for half in range(2):
                    xv = xgv2[half]
                    xk = ("xg", half)
                    P.dma("sync", xv[:, :, :], srcv[:, half * hnb:(half + 1) * hnb, g * 128:(g + 1) * 128],
                          "fx%d" % half, reads=allX, writes=[xk])
                    P.op("vector", lambda e, half=half, xv=xv: e.tensor_tensor(
                        out=xv[:, :, :], in0=xv[:, :, :],
                        in1=rsq[:, half * hnb:(half + 1) * hnb].unsqueeze(2).broadcast_to([128, hnb, 128]),
                        op=ALU.mult), reads=[xk, "rsq"], writes=[xk])
                    P.op("gpsimd", lambda e, half=half, g=g, hbg=hbg, xv=xv: e.tensor_tensor(
                        out=hbg[:, half * hnb:(half + 1) * hnb, :], in0=xv[:, :, :],
                        in1=grep[:, g * 128:(g + 1) * 128].unsqueeze(1).broadcast_to([128, hnb, 128]),
                        op=ALU.mult), reads=[xk, "grep"], writes=[hk])
            else:
                apt, apr = 512 // NB, c.RP // NB
                k = 0
                for r in range(4):
                    for t in range(c.NTP):
                        p0 = r * apr + t * apt
                        P.dma("sync", hbg[p0:p0 + apt, 0:NB, :],
                              self.hall[t][r * 512:(r + 1) * 512, g * 128:(g + 1) * 128].rearrange("(a b) c -> a b c", b=NB),
                              "fq%d" % (k % 4), reads=[("hall", tt) for tt in range(c.NTP)], writes=[(hk, k)])
                        k += 1

        prep(0)
        for g in range(8):
            hbg = hb[g % 2]
            hk = ("hb", g % 2)
            hbreads = [hk] if part == "S" else [(hk, k) for k in range(npiece)]
            for b2 in range(NB // 2):
                bank = b2 % 2
                for q in range(2):
                    b = 2 * b2 + q
                    P.op("tensor", lambda e, b=b, q=q, bank=bank, hbg=hbg: e.matmul(
                        self.ps[bank][:, q * 256:(q + 1) * 256], lhsT=hbg[:, b, :], rhs=A[:, :],
                        start=True, stop=True), reads=hbreads + ["A"], writes=[("ps", bank)])
                dstap = T1[:, 2 * b2:2 * b2 + 2, :]
                srcap = self.ps[bank].rearrange("p (q k) -> p q k", q=2)
                if b2 % 2 == 0:
                    P.op("scalar", lambda e, d=dstap, s_=srcap: e.copy(out=d, in_=s_),
                         reads=[("ps", bank)], writes=["T1"])
                else:
                    P.op("vector", lambda e, d=dstap, s_=srcap: e.tensor_copy(out=d, in_=s_),
                         reads=[("ps", bank)], writes=["T1"])
            if g + 1 < 8:
                prep(g + 1)
            self.conv_some(3, after=[("mixT", part, g - 1)] if g > 0 else [])
            T1v = T1.rearrange("p b (ri k) -> p b ri k", ri=2)
            YTv = YT.rearrange("p (kb ka) -> p ka kb", ka=128)
            def emit_s2(ka2):
                bank = 2 + ka2 % 2
                t2 = T2[ka2 % 2]
                t2k = ("T2", ka2 % 2)
                for q in range(2):
                    ka = 2 * ka2 + q
                    P.op("tensor", lambda e, ka=ka, q=q, bank=bank: e.matmul(
                        self.ps[bank][0:2 * NB, q * 256:(q + 1) * 256], lhsT=T1v[:, :, :, ka], rhs=C[:, :],
                        start=True, stop=True), reads=["T1", "C"], writes=[("ps", bank)])
                if ka2 % 2 == 0:
                    P.op("scalar", lambda e, t2=t2, bank=bank: e.copy(out=t2[0:2 * NB, :], in_=self.ps[bank][0:2 * NB, :]),
                         reads=[("ps", bank)], writes=[t2k])
                else:
                    P.op("vector", lambda e, t2=t2, bank=bank: e.tensor_copy(out=t2[0:2 * NB, :], in_=self.ps[bank][0:2 * NB, :]),
                         reads=[("ps", bank)], writes=[t2k])

            def emit_s3(ka2):
                t2 = T2[ka2 % 2]
                t2k = ("T2", ka2 % 2)
                for q in range(2):
                    ka = 2 * ka2 + q
                    kg = ka // KG
                    bank3 = 4 + kg % 2
                    col = (ka % KG) * KB
                    for w in range(2):
                        P.op("tensor", lambda e, t2=t2, q=q, w=w, ka=ka, bank3=bank3, col=col: e.matmul(
                            self.ps[bank3][:, col:col + KB],
                            lhsT=t2[0:2 * NB, q * 256 + w * 128:q * 256 + (w + 1) * 128],
                            rhs=B[0:2 * NB, w, ka * KB:(ka + 1) * KB],
                            start=(w == 0), stop=(w == 1)),
                            reads=[t2k, Bkey], writes=[("ps", bank3)])
                    if ka % KG == KG - 1:
                        ka0 = ka - KG + 1
                        dstap = YTv[:, ka0:ka0 + KG, :]
                        srcap = self.ps[bank3].rearrange("p (k b) -> p k b", b=KB)
                        if kg % 2 == 0:
                            P.op("vector", lambda e, d=dstap, s_=srcap: e.tensor_copy(out=d, in_=s_),
                                 reads=[("ps", bank3)], writes=["YT"])
                        else:
                            P.op("scalar", lambda e, d=dstap, s_=srcap: e.copy(out=d, in_=s_),
                                 reads=[("ps", bank3)], writes=["YT"])

            emit_s2(0)
            for ka2 in range(64):
                if ka2 + 1 < 64:
                    emit_s2(ka2 + 1)
                emit_s3(ka2)
            P.dma("sync", self.mixT[part][g, :, :], YT, "fy", reads=["YT"], writes=[("mixT", part, g)])

    def fourier_layer(self, l):
        c, P, nc = self.c, self.P, self.nc
        with contextlib.ExitStack() as ls:
            sb = lambda name, shape, dtype: ls.enter_context(nc.sbuf_tensor("f%d_" % l + name, shape, dtype))
            fb = {}
            fb["grep"] = grep = sb("grep", [128, D], F32)
            fb["A"] = A = sb("A", [128, 256], BF16)
            fb["C"] = C = sb("C", [128, 256], BF16)
            fb["BP"] = sb("BP", [128, 2, 128 * c.KBP], BF16)
            fb["BS"] = sb("BS", [128, 2, 128 * c.NBS], BF16)
            fb["xg"] = sb("xg", [128, c.NBS * 64], F32)
            fb["xg2"] = sb("xg2", [128, c.NBS * 64], F32)
            fb["hb"] = [sb("hb%d" % i, [128, max(c.NBP, c.NBS), 128], BF16) for i in range(2)]
            fb["T2"] = [sb("T2_%d" % i, [128, 512], BF16) for i in range(2)]
            fb["ssq"] = sb("ssq", [128, c.NBS], F32)
            fb["rtq"] = sb("rtq", [128, c.NBS], F32)
            fb["rsq"] = sb("rsq", [128, c.NBS], F32)
            mT = sb("mT", [128, 8, 512], BF16)

            P.dma("sync", grep[:, :], self.gmixrep_d[:, l, :], "f0", writes=["grep"])
            P.dma("gpsimd", A[:, :], self.dftA_d[:, :], "f1", writes=["A"])
            P.dma("gpsimd", C[:, :], self.dftC_d[:, :], "f2", writes=["C"])
            k = 0
            for part, nbp, ncol in (("S", c.NBS, 128 * c.NBS), ("P", c.NBP, 128 * c.KBP)):
                ch = min(2048, ncol)
                for w in range(2):
                    for i in range(ncol // ch):
                        P.dma("gpsimd", fb["B" + part][0:2 * nbp, w, i * ch:(i + 1) * ch],
                              self.dftB_d[part][w, :, i * ch:(i + 1) * ch], "fb%d" % (k % 4),
                              writes=["B" + part])
                        k += 1

            hst = mT[:, :, :].rearrange("p g t -> p (g t)").rearrange("p (s c) -> p s c", c=D)
            for t in range(c.NTP):
                b = self.next_xt()
                xt, xkey = self.xt[b], ("xt", b)
                self.load_tile(l, "P", t, b)
                for s in range(4):
                    P.op("scalar", lambda e, s=s, xt=xt: e.activation(out=self.hn[s % 2][:, :], in_=xt[:, s, :], func=AF.Square,
                                                                     accum_out=self.ss[:, s:s + 1]),
                         reads=[(xkey, s)], writes=[("ss", s), ("hn", s % 2)])
                P.op("scalar", lambda e: e.activation(out=self.rt[:, 0:4], in_=self.ss[:, 0:4], func=AF.Sqrt,
                                                       bias=self.epsb[:, 0:1], scale=1.0 / D),
                     reads=[("ss", s) for s in range(4)] + ["epsb"], writes=["rt"])
                P.op("vector", lambda e: e.reciprocal(out=self.rstd[:, 0:4], in_=self.rt[:, 0:4]),
                     reads=["rt"], writes=["rstd"])
                for s in range(4):
                    hb_, hk = self.hn[s % 2], ("hn", s % 2)
                    P.op("vector", lambda e, s=s, hb_=hb_, xt=xt: e.tensor_scalar(
                        out=hb_[:, :], in0=xt[:, s, :], scalar1=self.rstd[:, s:s + 1], scalar2=None, op0=ALU.mult),
                        reads=[(xkey, s), "rstd"], writes=[hk])
                    P.op("vector", lambda e, s=s, hb_=hb_: e.tensor_tensor(
                        out=hst[:, s, :], in0=hb_[:, :], in1=grep[:, :], op=ALU.mult),
                        reads=[hk, "grep"], writes=[("hst", s)])
                P.dma("sync", self.hloc[t].rearrange("(s p) c -> p s c", p=128), hst,
                      "fh", reads=[("hst", s) for s in range(4)], writes=[("hloc", t)])
                _coll(P, self.hloc[t], self.hall[t], "ag", [("hloc", t)], [("hall", t)], c.groups)
            P.barrier(skip=("C_ag",) + tuple("D_cv%d" % i for i in range(4)))

            self.fft_part(l, "S", fb)
            P.barrier(skip=("C_ag",) + tuple("D_cv%d" % i for i in range(4)))
            self.fft_part(l, "P", fb)
            P.barrier(skip=tuple("D_cv%d" % i for i in range(4)))

            self.convert_weights()
            s_wf = c.slot[("wf", l)]
            for part in ("P", "S"):
                mixv = self.mixT[part].rearrange("g k r -> k g r")
                for t in range(self.NT[part]):
                    b = self.next_xt()
                    xt, xkey = self.xt[b], ("xt", b)
                    self.load_tile(l, part, t, b)
                    P.dma("sync", mT[:, :, :], mixv[:, :, t * 512:(t + 1) * 512], "fm",
                          reads=[("mixT", part, g) for g in range(8)], writes=["mT"])
                    ws = [self.ring_load(s_wf + j) for j in range(4)]
                    for s in range(4):
                        for half in range(2):
                            bank = 2 * s + half
                            for g in range(8):
                                w, wk = ws[g // 2]
                                P.op("tensor", lambda e, w=w, g=g, s=s, half=half, bank=bank: e.matmul(
                                    self.ps[bank], lhsT=mT[:, g, s * 128:(s + 1) * 128],
                                    rhs=w[:, (g % 2) * 1024 + half * 512:(g % 2) * 1024 + (half + 1) * 512],
                                    start=(g == 0), stop=(g == 7)),
                                    reads=[wk, "mT"], writes=[("ps", bank)])
                    self.add_psum_to_xt(xt, xkey)
                    self.ffn_tile(l, xt, xkey)
                    self.store_tile(l, part, t, b)
    def qk_norm_rope(self, tmp, nh, pss, gain, gkey, cs, cskey, s):
        P = self.P
        qn, qr = tmp["qn"], tmp["qr"]
        t1, t2, t3, t4 = tmp["t"]
        h0 = 0
        for (psap, bank, n) in pss:
            for i in range(n):
                P.op("scalar", lambda e, psap=psap, i=i, h0=h0: e.activation(
                    out=self.junk[:, (h0 + i) * 128:(h0 + i + 1) * 128], in_=psap[:, i * 128:(i + 1) * 128],
                    func=AF.Square, accum_out=self.ss[:, h0 + i:h0 + i + 1]),
                    reads=[("ps", bank)], writes=[("ssh", h0 + i), ("junk", h0 + i)])
            h0 += n
        P.op("scalar", lambda e: e.activation(out=self.rt[:, 0:nh], in_=self.ss[:, 0:nh], func=AF.Sqrt,
                                               bias=self.epsb[:, 0:1], scale=1.0 / HD),
             reads=[("ssh", i) for i in range(nh)] + ["epsb"], writes=["rth"])
        P.op("vector", lambda e: e.reciprocal(out=self.rstd[:, 0:nh], in_=self.rt[:, 0:nh]),
             reads=["rth"], writes=["rstdh"])
        h0 = 0
        for (psap, bank, n) in pss:
            P.op("vector", lambda e, psap=psap, n=n, h0=h0: e.tensor_tensor(
                out=qn[:, h0:h0 + n, :], in0=psap.rearrange("p (h d) -> p h d", d=128),
                in1=self.rstd[:, h0:h0 + n].unsqueeze(2).broadcast_to([128, n, 128]), op=ALU.mult),
                reads=[("ps", bank), "rstdh"], writes=["qn"] + tmp.get("fence", []))
            h0 += n
        P.op("gpsimd", lambda e: e.tensor_tensor(
            out=qn[:, 0:nh, :], in0=qn[:, 0:nh, :],
            in1=gain.unsqueeze(1).broadcast_to([128, nh, 128]), op=ALU.mult),
            reads=["qn", gkey], writes=["qn"])
        qv = qn[:, 0:nh, :].rearrange("p h (i two) -> p h i two", two=2)
        rv = qr[:, 0:nh, :].rearrange("p h (i two) -> p h i two", two=2)
        x0, x1 = qv[:, :, :, 0], qv[:, :, :, 1]
        cc = cs[:, s, 0:64].unsqueeze(1).broadcast_to([128, nh, 64])
        sn = cs[:, s, 64:128].unsqueeze(1).broadcast_to([128, nh, 64])
        P.op("gpsimd", lambda e: e.tensor_tensor(out=t1[:, 0:nh, :], in0=x0, in1=cc, op=ALU.mult),
             reads=["qn", cskey], writes=["t1"])
        P.op("gpsimd", lambda e: e.tensor_tensor(out=t2[:, 0:nh, :], in0=x1, in1=sn, op=ALU.mult),
             reads=["qn", cskey], writes=["t2"])
        P.op("gpsimd", lambda e: e.tensor_tensor(out=rv[:, :, :, 0], in0=t1[:, 0:nh, :], in1=t2[:, 0:nh, :],
                                                 op=ALU.subtract), reads=["t1", "t2"], writes=["qr0"])
        P.op("vector", lambda e: e.tensor_tensor(out=t3[:, 0:nh, :], in0=x0, in1=sn, op=ALU.mult),
             reads=["qn", cskey], writes=["t3"])
        P.op("vector", lambda e: e.tensor_tensor(out=t4[:, 0:nh, :], in0=x1, in1=cc, op=ALU.mult),
             reads=["qn", cskey], writes=["t4"])
        P.op("vector", lambda e: e.tensor_tensor(out=rv[:, :, :, 1], in0=t3[:, 0:nh, :], in1=t4[:, 0:nh, :],
                                                 op=ALU.add), reads=["t3", "t4"], writes=["qr1"])


    def attn_layer(self, l):
        c, P, nc = self.c, self.P, self.nc
        scale = float(HD) ** -0.5
        NBmax = max(c.NBP, c.NBS)
        with contextlib.ExitStack() as ls:
            sb = lambda name, shape, dtype: ls.enter_context(nc.sbuf_tensor("a%d_" % l + name, shape, dtype))
            KT = sb("KT", [128, 2, NBmax * 128], BF16)
            Vb = sb("Vb", [128, NBmax, 256], BF16)
            QT = sb("QT", [128, 8, 512], BF16)
            OT = sb("OT", [128, 8, 512], BF16)
            NPT = 4
            pT = [sb("pT%d" % i, [128, 1024], BF16) for i in range(NPT)]
            s2 = [sb("s2_%d" % i, [128, 512], BF16) for i in range(NPT)]
            rinv = sb("rinv", [128, 512], F32)
            cs = sb("cs", [128, 4, 128], F32)
            qkg = sb("qkg", [128, 2, 128], F32)
            flat = self.act[:, :, :].rearrange("p f t -> p (f t)").bitcast(F32)
            assert c.FC * 256 >= 3584
            tmp = {
                "fence": [("act", fc) for fc in range(c.FC)],
                "qn": flat[:, 0:1024].rearrange("p (h d) -> p h d", d=128),
                "t": [flat[:, 1024 + 512 * i:1024 + 512 * (i + 1)].rearrange("p (h d) -> p h d", d=64) for i in range(4)],
                "qr": flat[:, 3072:3584].bitcast(BF16).rearrange("p (h d) -> p h d", d=128),
            }
            P.dma("sync", qkg[:, :, :], self.qkg_d[:, l, :, :], "a0", writes=["qkg"])
            self.convert_weights()
            s_q, s_kv, s_o = c.slot[("wq", l)], c.slot[("wkv", l)], c.slot[("wo", l)]

            def pass1(part):
                for ta in range(self.NT[part]):
                    b = self.next_xt()
                    xt, xkey = self.xt[b], ("xt", b)
                    self.load_tile(l, part, ta, b)
                    P.dma("sync", cs[:, :, :], self.cs_d[part][ta], "a1", writes=["cs"])
                    self.norm_to_hT(xt, xkey, self.gmixT, "gmixT", l * 8)
                    ws = [self.ring_load(s_kv + j) for j in range(2)]
                    def mm1(s, ws=ws):
                        bank = 4 + s % 2
                        for kc in range(KC):
                            w, wk = ws[kc // 4]
                            P.op("tensor", lambda e, w=w, kc=kc, s=s, bank=bank: e.matmul(
                                self.ps[bank], lhsT=self.hT[:, kc, s * 128:(s + 1) * 128],
                                rhs=w[:, (kc % 4) * 512:(kc % 4 + 1) * 512],
                                start=(kc == 0), stop=(kc == KC - 1)),
                                reads=[wk, ("hT", s)], writes=[("ps", bank)])
                    mm1(0)
                    for s in range(4):
                        sa = 4 * ta + s
                        bank = 4 + s % 2
                        if s + 1 < 4:
                            mm1(s + 1)
                        P.op("scalar", lambda e, sa=sa, bank=bank: e.copy(out=Vb[:, sa, :], in_=self.ps[bank][:, 256:512]),
                             reads=[("ps", bank)], writes=[("V", sa)])
                        self.qk_norm_rope(tmp, 2, [(self.ps[bank][:, 0:256], bank, 2)], qkg[:, 1, :], "qkg", cs, "cs", s)
                        pb = self.pp[3][:, (s % 2) * 512:(s % 2 + 1) * 512].bitcast(BF16)
                        for kvh in range(2):
                            P.op("tensor", lambda e, kvh=kvh, pb=pb: e.transpose(
                                out=pb[:, kvh * 128:(kvh + 1) * 128], in_=tmp["qr"][:, kvh, :], identity=self.identb[:, :]),
                                reads=["qr0", "qr1", "identb"], writes=[("ps", 6 + s % 2)])
                        P.op("scalar", lambda e, sa=sa, pb=pb: e.copy(
                            out=KT[:, :, sa * 128:(sa + 1) * 128], in_=pb[:, 0:256].rearrange("p (h t) -> p h t", h=2)),
                            reads=[("ps", 6 + s % 2)], writes=[("K", sa)])

            def pass2(part, nbk):
                allK = [("K", sa) for sa in range(nbk)]
                allV = [("V", sa) for sa in range(nbk)]
                for ta in range(self.NT[part]):
                    b = self.next_xt()
                    xt, xkey = self.xt[b], ("xt", b)
                    self.load_tile(l, part, ta, b)
                    P.dma("sync", cs[:, :, :], self.cs_d[part][ta], "a1", writes=["cs"])
                    self.norm_to_hT(xt, xkey, self.gmixT, "gmixT", l * 8)
                    ws = [self.ring_load(s_q + j) for j in range(4)]
                    def mm2(s, ws=ws):
                        banks = (4, 5) if s % 2 == 0 else (2, 3)
                        for half in range(2):
                            for kc in range(KC):
                                w, wk = ws[kc // 2]
                                P.op("tensor", lambda e, w=w, kc=kc, s=s, half=half, banks=banks: e.matmul(
                                    self.ps[banks[half]], lhsT=self.hT[:, kc, s * 128:(s + 1) * 128],
                                    rhs=w[:, (kc % 2) * 1024 + half * 512:(kc % 2) * 1024 + (half + 1) * 512],
                                    start=(kc == 0), stop=(kc == KC - 1)),
                                    reads=[wk, ("hT", s)], writes=[("ps", banks[half])])
                    mm2(0)
                    for s in range(4):
                        banks = (4, 5) if s % 2 == 0 else (2, 3)
                        if s + 1 < 4:
                            mm2(s + 1)
                        self.qk_norm_rope(tmp, 8, [(self.ps[banks[0]], banks[0], 4), (self.ps[banks[1]], banks[1], 4)],
                                          qkg[:, 0, :], "qkg", cs, "cs", s)
                        pb = self.pp[3][:, (s % 2) * 512:(s % 2 + 1) * 512].bitcast(BF16)
                        for hd in range(8):
                            P.op("tensor", lambda e, hd=hd, pb=pb: e.transpose(
                                out=pb[:, hd * 128:(hd + 1) * 128], in_=tmp["qr"][:, hd, :], identity=self.identb[:, :]),
                                reads=["qr0", "qr1", "identb"], writes=[("ps", 6 + s % 2)])
                        P.op("vector", lambda e, s=s, pb=pb: e.tensor_copy(
                            out=QT[:, :, s * 128:(s + 1) * 128], in_=pb[:, :].rearrange("p (h t) -> p h t", h=8)),
                            reads=[("ps", 6 + s % 2)], writes=["QT"])
                    npair = nbk // 2
                    items = [(hd, kp) for hd in range(8) for kp in range(npair)]

                    def emit_score(i):
                        hd, kp = items[i]
                        kvh = hd // 4
                        j = i % 3
                        for q in range(2):
                            kt = 2 * kp + q
                            P.op("tensor", lambda e, hd=hd, kt=kt, kvh=kvh, j=j, q=q: e.matmul(
                                self.ps[2 * j + q], lhsT=KT[:, kvh, kt * 128:(kt + 1) * 128], rhs=QT[:, hd, :],
                                start=True, stop=True), reads=["QT"] + allK, writes=[("ps", 2 * j + q)])
                        jb = i % NPT
                        P.op("scalar", lambda e, j=j, jb=jb: e.activation(
                            out=pT[jb][:, :], in_=self.pp[j][:, :], func=AF.Exp, scale=scale),
                            reads=[("ps", 2 * j), ("ps", 2 * j + 1)], writes=[("pT", jb)])

                    emit_score(0)
                    emit_score(1)
                    pending = []
                    for i, (hd, kp) in enumerate(items):
                        if i + 2 < len(items):
                            emit_score(i + 2)
                        kvh = hd // 4
                        jb = i % NPT
                        par = hd % 2
                        po, pr = 6, 7
                        for q in range(2):
                            kt = 2 * kp + q
                            P.op("tensor", lambda e, kt=kt, kvh=kvh, jb=jb, po=po, q=q: e.matmul(
                                self.ps[po], lhsT=Vb[:, kt, kvh * 128:(kvh + 1) * 128],
                                rhs=pT[jb][:, q * 512:(q + 1) * 512],
                                start=(kt == 0), stop=(kt == nbk - 1)),
                                reads=[("pT", jb)] + allV, writes=[("ps", po)])
                        GS = 4 if npair % 4 == 0 else 2
                        sj = (i // GS) % NPT
                        if kp % GS == 0:
                            P.op("vector", lambda e, jb=jb, sj=sj: e.tensor_tensor(
                                out=s2[sj][:, :], in0=pT[jb][:, 0:512], in1=pT[jb][:, 512:1024], op=ALU.add),
                                reads=[("pT", jb)], writes=[("s2", sj)])
                        else:
                            P.op("vector", lambda e, jb=jb: e.tensor_tensor(
                                out=rinv[:, :].bitcast(BF16)[:, 0:512], in0=pT[jb][:, 0:512], in1=pT[jb][:, 512:1024], op=ALU.add),
                                reads=[("pT", jb)], writes=["s2tmp", "rinv"])
                            P.op("vector", lambda e, sj=sj: e.tensor_tensor(
                                out=s2[sj][:, :], in0=s2[sj][:, :], in1=rinv[:, :].bitcast(BF16)[:, 0:512], op=ALU.add),
                                reads=["s2tmp", ("s2", sj)], writes=[("s2", sj)])

                            def emit_rs(sj=sj, pr=pr, kp=kp, GS=GS):
                                P.op("tensor", lambda e: e.matmul(
                                    self.ps[pr], lhsT=self.onesb[:, :], rhs=s2[sj][:, :],
                                    start=(kp == GS - 1), stop=(kp == npair - 1)),
                                    reads=[("s2", sj), "onesb"], writes=[("ps", pr)])
                            if kp % GS == GS - 1:
                                if pending:
                                    pending.pop()()
                                pending.append(emit_rs)
                        if kp == npair - 1:
                            while pending:
                                pending.pop()()
                            P.op("vector", lambda e, pr=pr: e.reciprocal(out=rinv[:, :], in_=self.ps[pr]),
                                 reads=[("ps", pr), "s2tmp"], writes=["rinv", "s2tmp"])
                            P.op("vector", lambda e, po=po, hd=hd: e.tensor_tensor(
                                out=OT[:, hd, :], in0=self.ps[po], in1=rinv[:, :], op=ALU.mult),
                                reads=[("ps", po), "rinv"], writes=[("OT", hd)])
                    ws = [self.ring_load(s_o + j) for j in range(4)]
                    for s in range(4):
                        for half in range(2):
                            bank = 2 * s + half
                            for hd in range(8):
                                w, wk = ws[hd // 2]
                                P.op("tensor", lambda e, w=w, hd=hd, s=s, half=half, bank=bank: e.matmul(
                                    self.ps[bank], lhsT=OT[:, hd, s * 128:(s + 1) * 128],
                                    rhs=w[:, (hd % 2) * 1024 + half * 512:(hd % 2) * 1024 + (half + 1) * 512],
                                    start=(hd == 0), stop=(hd == 7)),
                                    reads=[wk, ("OT", hd)], writes=[("ps", bank)])
                    self.add_psum_to_xt(xt, xkey)
                    self.ffn_tile(l, xt, xkey)
                    self.store_tile(l, part, ta, b)

            nloc = c.RP // 128
            pass1("P")
            P.dma("sync", self.kloc.rearrange("(h d) k -> d h k", h=2), KT[:, :, 0:c.RP], "ak",
                  reads=[("K", sa) for sa in range(nloc)], writes=["kloc"])
            P.dma("sync", self.vloc.rearrange("(s p) c -> p s c", p=128), Vb[:, 0:nloc, :], "av",
                  reads=[("V", sa) for sa in range(nloc)], writes=["vloc"])
            _coll(P, self.kloc, self.kall, "agk", ["kloc"], ["kall"], c.groups)
            _coll(P, self.vloc, self.vall, "agv", ["vloc"], ["vall"], c.groups)
            pass1("S")
            pass2("S", c.NBS)
            kallv = self.kall.rearrange("(r h d) k -> h d r k", r=4, h=2)
            for h in range(2):
                P.dma("sync", KT[:, h, 0:4 * c.RP].rearrange("d (r k) -> d r k", r=4), kallv[h], "ak%d" % h,
                      reads=["kall"], writes=[("K", sa) for sa in range(c.NBP)])
            P.dma("sync", Vb[:, 0:c.NBP, :], self.vall.rearrange("(s p) c -> p s c", p=128), "av2",
                  reads=["vall"], writes=[("V", sa) for sa in range(c.NBP)])
            pass2("P", c.NBP)


def _rope_rows(pos):
    inv = (np.float32(10000.0) ** (-np.arange(0, 64, 2, dtype=np.float32) / np.float32(64))).astype(np.float32)
    rowp = (pos // 64).astype(np.float32)
    colp = (pos % 64).astype(np.float32)
    ang = np.concatenate([rowp[:, None] * inv, colp[:, None] * inv], -1).astype(np.float32)
    return np.concatenate([np.cos(ang), np.sin(ang)], -1).astype(np.float32)


def _cs_tiles(pos):
    cs = _rope_rows(pos)
    nt = cs.shape[0] // 512
    return np.ascontiguousarray(cs.reshape(nt, 4, 128, 128).transpose(0, 2, 1, 3))


def _dft_common():
    a = np.arange(128)
    A = np.exp(-2j * np.pi * np.outer(a, a) / 128.0)
    m = np.concatenate([A.real, A.imag], 1).astype(np.float32)
    return m


def _dft_B(NB, R, kbs):
    b = np.arange(NB).astype(np.float64)
    ka = np.arange(128).astype(np.float64)
    kb = np.asarray(kbs).astype(np.float64)
    M = (np.exp(-2j * np.pi * b[:, None, None] * kb[None, None, :] / NB)
         * np.exp(-2j * np.pi * b[:, None, None] * ka[None, :, None] / float(R))) / np.sqrt(R * 128.0)
    n = 128 * len(kbs)
    Mr, Mi = M.real.reshape(NB, n), M.imag.reshape(NB, n)
    out = np.zeros((2, 2 * NB, n), np.float32)
    out[0, 0::2] = Mr
    out[0, 1::2] = -Mi
    out[1, 0::2] = -Mi
    out[1, 1::2] = -Mr
    return out


def _pack_weights(cfg, fourier_w, attn_w_qkv, attn_w_o, ffn_w_gate, ffn_w_up, ffn_w_down):
    wall = np.zeros((cfg.nslots, 128, 2048), np.float32)
    jF = jA = 0
    for l, t in enumerate(cfg.layers):
        if t == "F":
            w = np.asarray(fourier_w[jF], np.float32)
            s0 = cfg.slot[("wf", l)]
            wall[s0:s0 + 4] = w.reshape(4, 2, 128, D).transpose(0, 2, 1, 3).reshape(4, 128, 2048)
            jF += 1
        else:
            wqkv = np.asarray(attn_w_qkv[jA], np.float32)
            s0 = cfg.slot[("wq", l)]
            wall[s0:s0 + 4] = wqkv[:, 0:1024].reshape(4, 2, 128, 1024).transpose(0, 2, 1, 3).reshape(4, 128, 2048)
            s0 = cfg.slot[("wkv", l)]
            wall[s0:s0 + 2] = wqkv[:, 1024:1536].reshape(2, 4, 128, 512).transpose(0, 2, 1, 3).reshape(2, 128, 2048)
            s0 = cfg.slot[("wo", l)]
            wo = np.asarray(attn_w_o[jA], np.float32)
            wall[s0:s0 + 4] = wo.reshape(4, 2, 128, D).transpose(0, 2, 1, 3).reshape(4, 128, 2048)
            jA += 1
        FC = cfg.FC
        wg = np.asarray(ffn_w_gate[l], np.float32).reshape(KC, 128, FC, 128)
        wu = np.asarray(ffn_w_up[l], np.float32).reshape(KC, 128, FC, 128)
        gu = np.stack([wg, wu], 0)
        s0 = cfg.slot[("wgu", l)]
        wall[s0:s0 + FC] = gu.transpose(3, 2, 0, 1, 4).reshape(FC, 128, 2048)
        wd = np.asarray(ffn_w_down[l], np.float32)
        s0 = cfg.slot[("wd", l)]
        wall[s0:s0 + FC // 2] = wd.reshape(FC // 2, 2, 128, D).transpose(0, 2, 1, 3).reshape(FC // 2, 128, 2048)
    return wall.reshape(cfg.nslots * 128, 2048)


def _common_inputs(cfg, norm_mix, norm_ffn, attn_q_norm, attn_k_norm):
    L = cfg.L
    nm = np.asarray(norm_mix, np.float32)[:L]
    nf = np.asarray(norm_ffn, np.float32)[:L]
    gmixT = nm.reshape(L, KC, 128).transpose(2, 0, 1).reshape(128, L * KC).copy()
    gffnT = nf.reshape(L, KC, 128).transpose(2, 0, 1).reshape(128, L * KC).copy()
    gmixrep = np.broadcast_to(nm[None], (128, L, D)).copy()
    qkg = np.ones((128, L, 2, 128), np.float32)
    jA = 0
    for l, t in enumerate(cfg.layers):
        if t == "A":
            qkg[:, l, 0, :] = np.asarray(attn_q_norm[jA], np.float32)[None]
            qkg[:, l, 1, :] = np.asarray(attn_k_norm[jA], np.float32)[None]
            jA += 1
    dft = _dft_common()
    return dict(gmixT=gmixT, gffnT=gffnT, gmixrep=gmixrep, qkg=qkg, ident=np.eye(128, dtype=np.float32),
                dftA=dft, dftC=dft.copy(),
                dftBS=_dft_B(cfg.NBS, cfg.RS, np.arange(cfg.NBS)),
                csS=_cs_tiles(np.arange(cfg.RS)))


_NC_CACHE = {}


def run_cores(cfg, xp_list, xs_list, weights):
    key = (cfg.RP, cfg.RS, cfg.DFF, cfg.layers, cfg.ring)
    if key not in _NC_CACHE:
        _NC_CACHE[key] = Builder(cfg).build()
    nc = _NC_CACHE[key]
    in_maps = []
    for c in range(8):
        r = c % 4
        m = dict(xp=np.ascontiguousarray(xp_list[c], np.float32), xs=np.ascontiguousarray(xs_list[c], np.float32),
                 wall=weights["wall"])
        m.update(weights["common"])
        m["csP"] = _cs_tiles(cfg.RP * r + np.arange(cfg.RP))
        m["dftBP"] = _dft_B(cfg.NBP, 4 * cfg.RP, cfg.KBP * r + np.arange(cfg.KBP))
        in_maps.append(m)
    res = run_bass_kernel_spmd(nc, in_maps, core_ids=list(range(8)))
    return [(r["yp"], r["ys"]) for r in res.results]


def kernel(x_prompt, x_sample, norm_mix, norm_ffn, fourier_w, attn_w_qkv, attn_q_norm, attn_k_norm,
           attn_w_o, ffn_w_gate, ffn_w_up, ffn_w_down):
    cfg = Cfg()
    xp = np.asarray(x_prompt, np.float32)
    xs = np.asarray(x_sample, np.float32)
    weights = dict(
        wall=_pack_weights(cfg, fourier_w, attn_w_qkv, attn_w_o, ffn_w_gate, ffn_w_up, ffn_w_down),
        common=_common_inputs(cfg, norm_mix, norm_ffn, attn_q_norm, attn_k_norm))
    RP = cfg.RP
    xp_list = [xp[c // 4, RP * (c % 4):RP * (c % 4 + 1)] for c in range(8)]
    xs_list = [xs[c] for c in range(8)]
    outs = run_cores(cfg, xp_list, xs_list, weights)
    y_prompt = np.zeros_like(xp)
    y_sample = np.zeros_like(xs)
    for c in range(8):
        y_prompt[c // 4, RP * (c % 4):RP * (c % 4 + 1)] = outs[c][0]
        y_sample[c] = outs[c][1]
    return (y_prompt, y_sample)
```

```python
import bisect
import contextlib
import numpy as np
import ml_dtypes
import concourse.bass as bass
import concourse.mybir as mybir
from concourse.bass_utils import run_bass_kernel_spmd

F32 = mybir.dt.float32
BF16 = mybir.dt.bfloat16
AF = mybir.ActivationFunctionType
ALU = mybir.AluOpType
AX = mybir.AxisListType

D = 1024
KC = 8
HD = 128
NH = 8
NKV = 2
EPS = 1e-6
NEG = -30000.0


class Prog:
    ENGS = ("sync", "scalar", "vector", "gpsimd", "tensor")

    def __init__(self, nc, stack):
        self.nc = nc
        self.stack = stack
        self.ops = {e: [] for e in self.ENGS}
        self.marked = {e: [] for e in self.ENGS}
        self.cnt = {e: 0 for e in self.ENGS}
        self.seen = {e: {} for e in self.ENGS}
        self.sems = {}
        self.dmacnt = {}
        self.last_write = {}
        self.readers = {}

    def sem(self, name):
        if name not in self.sems:
            self.sems[name] = self.stack.enter_context(self.nc.semaphore(name))
        return self.sems[name]

    def _ticket(self, ref):
        if ref[0] == "dma":
            return (ref[1], ref[2])
        eng, pos = ref
        m = self.marked[eng]
        i = bisect.bisect_left(m, pos)
        if i < len(m):
            p = m[i]
        else:
            p = pos
            self.cnt[eng] += 1
            self.ops[eng][p][1] = self.cnt[eng]
            m.append(p)
        return ("E_" + eng, self.ops[eng][p][1])

    def _wait(self, eng, ticket):
        name, val = ticket
        if self.seen[eng].get(name, 0) >= val:
            return
        self.seen[eng][name] = val
        semh = self.sem(name)
        self.ops[eng].append([lambda e, s=semh, v=val: e.wait_ge(s, v), None, True])

    def _deps(self, eng, reads, writes):
        refs = []
        for r in reads:
            w = self.last_write.get(r)
            if w is not None:
                refs.append(w)
        for w_ in writes:
            w = self.last_write.get(w_)
            if w is not None:
                refs.append(w)
            refs.extend(self.readers.get(w_, {}).values())
        return [r for r in refs if not (r[0] == "tensor" and eng == "tensor")]

    def _update(self, ref, rkey, reads, writes):
        for r in reads:
            self.readers.setdefault(r, {})[rkey] = ref
        for w in writes:
            self.last_write[w] = ref
            self.readers[w] = {}

    def op(self, eng, fn, reads=(), writes=()):
        for ref in self._deps(eng, reads, writes):
            self._wait(eng, self._ticket(ref))
        self.ops[eng].append([fn, None, False])
        ref = (eng, len(self.ops[eng]) - 1)
        self._update(ref, eng, reads, writes)
        return ref

    def dma(self, q, out, in_, semkey, reads=(), writes=(), **kw):
        for ref in self._deps("dma:" + q, reads, writes):
            self._wait(q, self._ticket(ref))
        name = "D_" + semkey
        self.dmacnt[name] = self.dmacnt.get(name, 0) + 16
        val = self.dmacnt[name]
        semh = self.sem(name)
        self.ops[q].append([lambda e, o=out, i=in_, s=semh, k=kw: e.dma_start(out=o, in_=i, **k).then_inc(s, 16),
                            None, True])
        ref = ("dma", name, val)
        self._update(ref, name, reads, writes)
        return ref

    def barrier(self, skip=()):
        tickets = []
        for e in self.ENGS:
            pos = len(self.ops[e]) - 1
            while pos >= 0 and self.ops[e][pos][2]:
                pos -= 1
            if pos >= 0:
                tickets.append(self._ticket((e, pos)))
        for name, val in self.dmacnt.items():
            if name not in skip:
                tickets.append((name, val))
        for e in self.ENGS:
            for t in tickets:
                self._wait(e, t)

    def replay(self, eng, e):
        semh = self.sem("E_" + eng) if self.cnt[eng] else None
        for o in self.ops[eng]:
            ins = o[0](e)
            if o[1] is not None:
                ins.then_inc(semh, 1)


def _coll(P, ins_ap, outs_ap, semkey, reads, writes, groups):
    for ref in P._deps("dma:gpsimd", reads, writes):
        P._wait("gpsimd", P._ticket(ref))
    name = "C_" + semkey
    P.dmacnt[name] = P.dmacnt.get(name, 0) + 1
    val = P.dmacnt[name]
    semh = P.sem(name)
    P.ops["gpsimd"].append([lambda e: e.collective_compute(
        "AllGather", ALU.bypass, replica_groups=groups, ins=[ins_ap.opt()], outs=[outs_ap.opt()]).then_inc(semh),
        None, True])
    ref = ("dma", name, val)
    P._update(ref, name, reads, writes)
    return ref


class Cfg:
    def __init__(self, RP=2048, RS=4096, DFF=2816, layers=("F", "A", "F", "A"), ring=6):
        self.RP, self.RS = RP, RS
        self.NTP, self.NTS = RP // 512, RS // 512
        self.NBP = 4 * RP // 128
        self.KBP = self.NBP // 4
        self.NBS = RS // 128
        assert self.NBS % 16 == 0 and self.NBP % 16 == 0
        self.DFF = DFF
        self.FC = DFF // 128
        assert self.FC % 2 == 0
        self.layers = tuple(layers)
        self.L = len(layers)
        self.ring = ring
        self.slot = {}
        n = 0
        for l, t in enumerate(self.layers):
            if t == "F":
                self.slot[("wf", l)] = n; n += 4
            else:
                self.slot[("wq", l)] = n; n += 4
                self.slot[("wkv", l)] = n; n += 2
                self.slot[("wo", l)] = n; n += 4
            self.slot[("wgu", l)] = n; n += self.FC
            self.slot[("wd", l)] = n; n += self.FC // 2
        self.nslots = n
        self.groups = [[0, 1, 2, 3], [4, 5, 6, 7]]


class Builder:
    def __init__(self, cfg):
        self.c = cfg
        self.nc = bass.Bass("TRN2", target_bir_lowering=False)

    def build(self):
        c, nc = self.c, self.nc
        RP, RS = c.RP, c.RS
        dt = nc.dram_tensor
        ein = lambda name, shape, dtype=F32: dt(name, shape, dtype, kind="ExternalInput").ap()
        itn = lambda name, shape, dtype: dt(name, shape, dtype).ap()
        self.xin = {"P": ein("xp", [RP, D]), "S": ein("xs", [RS, D])}
        self.yout = {"P": dt("yp", [RP, D], F32, kind="ExternalOutput").ap(),
                     "S": dt("ys", [RS, D], F32, kind="ExternalOutput").ap()}
        self.xscr = {"P": itn("xscrp", [RP, D], F32), "S": itn("xscrs", [RS, D], F32)}
        self.wall = ein("wall", [c.nslots * 128, 2048])
        self.gmixT_d = ein("gmixT", [128, c.L * 8])
        self.gffnT_d = ein("gffnT", [128, c.L * 8])
        self.gmixrep_d = ein("gmixrep", [128, c.L, D])
        self.qkg_d = ein("qkg", [128, c.L, 2, 128])
        self.cs_d = {"P": ein("csP", [c.NTP, 128, 4, 128]), "S": ein("csS", [c.NTS, 128, 4, 128])}
        self.dftA_d = ein("dftA", [128, 256])
        self.dftC_d = ein("dftC", [128, 256])
        self.dftB_d = {"P": ein("dftBP", [2, 2 * c.NBP, 128 * c.KBP]), "S": ein("dftBS", [2, 2 * c.NBS, 128 * c.NBS])}
        self.ident_d = ein("ident", [128, 128])
        self.wbf = itn("wbf", [c.nslots * 128, 2048], BF16)
        self.mixT = {"P": itn("mixTP", [8, 128, RP], BF16), "S": itn("mixTS", [8, 128, RS], BF16)}
        self.hloc = [itn("hloc%d" % t, [512, D], BF16) for t in range(c.NTP)]
        self.hall = [itn("hall%d" % t, [4 * 512, D], BF16) for t in range(c.NTP)]
        self.kloc = itn("kloc", [256, RP], BF16)
        self.kall = itn("kall", [1024, RP], BF16)
        self.vloc = itn("vloc", [RP, 256], BF16)
        self.vall = itn("vall", [4 * RP, 256], BF16)
        self.NT = {"P": c.NTP, "S": c.NTS}

        with contextlib.ExitStack() as st:
            self.st = st
            self.P = P = Prog(nc, st)
            sb = lambda name, shape, dtype: st.enter_context(nc.sbuf_tensor("s_" + name, shape, dtype))
            self.pp = [st.enter_context(nc.psum_tensor("pp%d" % i, [128, 1024], F32)) for i in range(4)]
            self.ps = [self.pp[i // 2][:, (i % 2) * 512:(i % 2 + 1) * 512] for i in range(8)]
            self.ident = sb("ident", [128, 128], F32)
            self.identb = sb("identb", [128, 128], BF16)
            self.onesb = sb("onesb", [128, 128], BF16)
            self.gmixT = sb("gmixT", [128, c.L * 8], F32)
            self.gffnT = sb("gffnT", [128, c.L * 8], F32)
            self.xtall = sb("xtall", [128, 8 * D], F32)
            self.xt = [self.xtall[:, i * 4 * D:(i + 1) * 4 * D].rearrange("p (s c) -> p s c", c=D) for i in range(2)]
            self.hn = [sb("hn%d" % i, [128, D], F32) for i in range(2)]
            self.hT = sb("hT", [128, KC, 512], BF16)
            self.act = sb("act", [128, c.FC, 512], BF16)
            self.sg = [sb("sg%d" % i, [128, 512], F32) for i in range(2)]
            self.ringb = [sb("ring%d" % i, [128, 2048], BF16) for i in range(c.ring)]
            self.junk = sb("junk", [128, D], BF16)
            self.ss = sb("ss", [128, 8], F32)
            self.rt = sb("rt", [128, 8], F32)
            self.rstd = sb("rstd", [128, 8], F32)
            self.epsb = sb("epsb", [128, 1], F32)
            self.ring_pos = 0
            self.conv_pos = 0
            self.conv_i = 0
            self.tilecnt = 0

            self.load_consts()
            for l, t in enumerate(c.layers):
                if t == "F":
                    self.fourier_layer(l)
                else:
                    self.attn_layer(l)
                P.barrier()
            P.barrier()

            with nc.Block() as block:
                @block.sync
                def _(e):
                    P.replay("sync", e)

                @block.scalar
                def _(e):
                    P.replay("scalar", e)

                @block.vector
                def _(e):
                    P.replay("vector", e)

                @block.gpsimd
                def _(e):
                    P.replay("gpsimd", e)

                @block.tensor
                def _(e):
                    P.replay("tensor", e)
        return nc

    def src(self, l, part):
        return self.xin[part] if l == 0 else self.xscr[part]

    def dst(self, l, part):
        return self.yout[part] if l == self.c.L - 1 else self.xscr[part]

    def load_consts(self):
        P = self.P
        P.dma("sync", self.ident[:, :], self.ident_d[:, :], "c0", writes=["ident"])
        P.dma("sync", self.gmixT[:, :], self.gmixT_d[:, :], "c1", writes=["gmixT"])
        P.dma("sync", self.gffnT[:, :], self.gffnT_d[:, :], "c2", writes=["gffnT"])
        P.op("vector", lambda e: e.tensor_copy(out=self.identb[:, :], in_=self.ident[:, :]),
             reads=["ident"], writes=["identb"])
        P.op("vector", lambda e: e.memset(self.onesb[:, :], 1.0), writes=["onesb"])
        P.op("vector", lambda e: e.memset(self.epsb[:, :], EPS), writes=["epsb"])

    def conv_some(self, k, after=()):
        c, P = self.c, self.P
        while k > 0 and self.conv_pos < c.nslots:
            s = self.conv_pos
            n = min(4, c.nslots - s)
            P.dma("gpsimd", self.wbf[s * 128:(s + n) * 128, :], self.wall[s * 128:(s + n) * 128, :],
                  "cv%d" % (self.conv_i % 4), reads=list(after), writes=[("wbf", j) for j in range(s, s + n)])
            self.conv_pos += n
            self.conv_i += 1
            k -= 1

    def conv_until(self, slot_end):
        while self.conv_pos < min(slot_end, self.c.nslots):
            self.conv_some(1)

    def convert_weights(self):
        self.conv_until(self.c.nslots)

    def next_xt(self):
        b = self.tilecnt % 2
        self.tilecnt += 1
        return b

    def load_tile(self, l, part, t, b):
        src = self.src(l, part)
        self.P.dma("sync", self.xt[b][:, :, :],
                   src[t * 512:(t + 1) * 512, :].rearrange("(s p) c -> p s c", p=128),
                   "xt%d" % b, reads=[("X", part, l, t)], writes=[(("xt", b), s) for s in range(4)])

    def store_tile(self, l, part, t, b):
        dst = self.dst(l, part)
        self.P.dma("scalar", dst[t * 512:(t + 1) * 512, :].rearrange("(s p) c -> p s c", p=128),
                   self.xt[b][:, :, :], "st%d" % b,
                   reads=[(("xt", b), s) for s in range(4)], writes=[("X", part, l + 1, t)])

    def ring_load(self, slot):
        P = self.P
        b = self.ring_pos % self.c.ring
        self.ring_pos += 1
        key = ("ring", b)
        P.dma("sync", self.ringb[b][:, :], self.wbf[slot * 128:(slot + 1) * 128, :], "ring%d" % b,
              reads=[("wbf", slot)], writes=[key])
        return self.ringb[b], key

    def norm_to_hT(self, xt, xkey, gT, gkey, gcol):
        P = self.P
        for s in range(4):
            P.op("scalar", lambda e, s=s: e.activation(out=self.hn[s % 2][:, :], in_=xt[:, s, :], func=AF.Square,
                                                      accum_out=self.ss[:, s:s + 1]),
                 reads=[(xkey, s)], writes=[("ss", s), ("hn", s % 2)])
        P.op("scalar", lambda e: e.activation(out=self.rt[:, 0:4], in_=self.ss[:, 0:4], func=AF.Sqrt,
                                               bias=self.epsb[:, 0:1], scale=1.0 / D),
             reads=[("ss", s) for s in range(4)] + ["epsb"], writes=["rt"])
        P.op("vector", lambda e: e.reciprocal(out=self.rstd[:, 0:4], in_=self.rt[:, 0:4]),
             reads=["rt"], writes=["rstd"])
        for s in range(4):
            hb = self.hn[s % 2]
            hk = ("hn", s % 2)
            P.op("vector", lambda e, s=s, hb=hb: e.tensor_scalar(out=hb[:, :], in0=xt[:, s, :],
                                                                scalar1=self.rstd[:, s:s + 1], scalar2=None,
                                                                op0=ALU.mult),
                 reads=[(xkey, s), "rstd"], writes=[hk])
            for half in range(2):
                bank = 2 * (s % 2) + half
                pk = ("ps", bank)
                for j in range(4):
                    kc = half * 4 + j
                    P.op("tensor", lambda e, hb=hb, kc=kc, j=j, bank=bank: e.transpose(
                        out=self.ps[bank][:, j * 128:(j + 1) * 128], in_=hb[:, kc * 128:(kc + 1) * 128],
                        identity=self.ident[:, :]),
                        reads=[hk, "ident"], writes=[pk])
                g0 = gcol + half * 4
                P.op("vector", lambda e, s=s, half=half, bank=bank, g0=g0: e.tensor_tensor(
                    out=self.hT[:, half * 4:half * 4 + 4, s * 128:(s + 1) * 128],
                    in0=self.ps[bank][:, :].rearrange("p (j t) -> p j t", j=4),
                    in1=gT[:, g0:g0 + 4].unsqueeze(2).broadcast_to([128, 4, 128]),
                    op=ALU.mult),
                    reads=[pk, gkey], writes=[("hT", s)])

    def ffn_tile(self, l, xt, xkey):
        c, P = self.c, self.P
        self.norm_to_hT(xt, xkey, self.gffnT, "gffnT", l * 8)
        hTr = [("hT", s) for s in range(4)]
        s_gu = c.slot[("wgu", l)]
        s_d = c.slot[("wd", l)]
        for fc in range(c.FC):
            w, wk = self.ring_load(s_gu + fc)
            par = fc % 2
            pg, pu = self.ps[4 + 2 * par], self.ps[5 + 2 * par]
            kg, ku = ("ps", 4 + 2 * par), ("ps", 5 + 2 * par)
            for kc in range(KC):
                P.op("tensor", lambda e, w=w, kc=kc, pg=pg: e.matmul(
                    pg[:, :], lhsT=w[:, kc * 128:(kc + 1) * 128], rhs=self.hT[:, kc, :],
                    start=(kc == 0), stop=(kc == KC - 1)), reads=[wk] + hTr, writes=[kg])
            for kc in range(KC):
                P.op("tensor", lambda e, w=w, kc=kc, pu=pu: e.matmul(
                    pu[:, :], lhsT=w[:, 1024 + kc * 128:1024 + (kc + 1) * 128], rhs=self.hT[:, kc, :],
                    start=(kc == 0), stop=(kc == KC - 1)), reads=[wk] + hTr, writes=[ku])
            sgb = self.sg[par]
            P.op("scalar", lambda e, pg=pg, sgb=sgb: e.activation(out=sgb[:, :], in_=pg[:, :], func=AF.Silu),
                 reads=[kg], writes=[("sg", par)])
            P.op("vector", lambda e, pu=pu, sgb=sgb, fc=fc: e.tensor_tensor(
                out=self.act[:, fc, :], in0=pu[:, :], in1=sgb[:, :], op=ALU.mult),
                reads=[ku, ("sg", par)], writes=[("act", fc)])
        for j in range(c.FC // 2):
            w, wk = self.ring_load(s_d + j)
            for i in range(2):
                fc = 2 * j + i
                for s in range(4):
                    for half in range(2):
                        bank = 2 * s + half
                        P.op("tensor", lambda e, w=w, i=i, fc=fc, s=s, half=half, bank=bank: e.matmul(
                            self.ps[bank][:, :], lhsT=self.act[:, fc, s * 128:(s + 1) * 128],
                            rhs=w[:, i * 1024 + half * 512:i * 1024 + (half + 1) * 512],
                            start=(fc == 0), stop=(fc == c.FC - 1)),
                            reads=[wk, ("act", fc)], writes=[("ps", bank)])
        for s in range(4):
            for half in range(2):
                bank = 2 * s + half
                P.op("vector", lambda e, s=s, half=half, bank=bank: e.tensor_tensor(
                    out=xt[:, s, half * 512:(half + 1) * 512], in0=self.ps[bank][:, :],
                    in1=xt[:, s, half * 512:(half + 1) * 512], op=ALU.add),
                    reads=[("ps", bank), (xkey, s)], writes=[(xkey, s)])

    def add_psum_to_xt(self, xt, xkey):
        for s in range(4):
            for half in range(2):
                bank = 2 * s + half
                self.P.op("vector", lambda e, s=s, half=half, bank=bank: e.tensor_tensor(
                    out=xt[:, s, half * 512:(half + 1) * 512], in0=self.ps[bank][:, :],
                    in1=xt[:, s, half * 512:(half + 1) * 512], op=ALU.add),
                    reads=[("ps", bank), (xkey, s)], writes=[(xkey, s)])


    def fft_part(self, l, part, fb):
        c, P = self.c, self.P
        NB = c.NBP if part == "P" else c.NBS
        KB = c.KBP if part == "P" else c.NBS
        NT = self.NT[part]
        A, C, B, T2, hb, xg = fb["A"], fb["C"], fb["B" + part], fb["T2"], fb["hb"], fb["xg"]
        Bkey = "B" + part
        T1 = self.xtall[:, :].bitcast(BF16)[:, 0:NB * 256].rearrange("p (b k) -> p b k", k=256)
        T1f = self.xtall[:, :].bitcast(BF16)
        YT = self.act[:, :, :].rearrange("p f t -> p (f t)")[:, 0:128 * KB]
        KG = 512 // KB
        if part == "S":
            src = self.src(l, "S")
            allX = [("X", "S", l, t) for t in range(NT)]
            srcv = src.rearrange("(a b) c -> a b c", b=NB)
            ssq, rtq, rsq, grep = fb["ssq"], fb["rtq"], fb["rsq"], fb["grep"]
            nbs = NB // 16
            hbv = lambda h: h[:, :, :].rearrange("p b c -> p (b c)").bitcast(F32)[:, 0:NB * 64].rearrange("p (b c) -> p b c", c=D)
            bufs = [xg[:, :].rearrange("p (b c) -> p b c", c=D), fb["xg2"][:, :].rearrange("p (b c) -> p b c", c=D),
                    hbv(hb[0]), hbv(hb[1])]
            bkeys = [("xg", 0), ("xg", 1), ("hb", 0), ("hb", 1)]
            for i in range(16):
                bf, bk = bufs[i % 4], bkeys[i % 4]
                P.dma("sync", bf[:, 0:nbs, :], srcv[:, i * nbs:(i + 1) * nbs, :], "fs%d" % (i % 4),
                      reads=allX, writes=[bk])
                for q in range(nbs):
                    b = i * nbs + q
                    jk = b % 4
                    P.op("scalar", lambda e, bf=bf, q=q, b=b, jk=jk: e.activation(
                        out=T1f[:, jk * D:(jk + 1) * D], in_=bf[:, q, :], func=AF.Square, accum_out=ssq[:, b:b + 1]),
                        reads=[bk], writes=[("ssq", b), ("jk", jk)])
            P.op("scalar", lambda e: e.activation(out=rtq[:, 0:NB], in_=ssq[:, 0:NB], func=AF.Sqrt,
                                                   bias=self.epsb[:, 0:1], scale=1.0 / D),
                 reads=[("ssq", b) for b in range(NB)] + ["epsb"], writes=["rtq"])
            P.op("vector", lambda e: e.reciprocal(out=rsq[:, 0:NB], in_=rtq[:, 0:NB]), reads=["rtq"], writes=["rsq"])
            hnb = NB // 2
            xgv2 = [xg[:, :].rearrange("p (b c) -> p b c", c=128), fb["xg2"][:, :].rearrange("p (b c) -> p b c", c=128)]
        else:
            npiece = 4 * c.NTP
        def prep(g):
            hbg = hb[g % 2]
            hk = ("hb", g % 2)
            if part == "S":
                for half in range(2):
                    xv = xgv2[half]
                    xk = ("xg", half)
                    P.dma("sync", xv[:, :, :], srcv[:, half * hnb:(half + 1) * hnb, g * 128:(g + 1) * 128],
                          "fx%d" % half, reads=allX, writes=[xk])
                    P.op("vector", lambda e, half=half, xv=xv: e.tensor_tensor(
                        out=xv[:, :, :], in0=xv[:, :, :],
                        in1=rsq[:, half * hnb:(half + 1) * hnb].unsqueeze(2).broadcast_to([128, hnb, 128]),
                        op=ALU.mult), reads=[xk, "rsq"], writes=[xk])
                    P.op("gpsimd", lambda e, half=half, g=g, hbg=hbg, xv=xv: e.tensor_tensor(
                        out=hbg[:, half * hnb:(half + 1) * hnb, :], in0=xv[:, :, :],
                        in1=grep[:, g * 128:(g + 1) * 128].unsqueeze(1).broadcast_to([128, hnb, 128]),
                        op=ALU.mult), reads=[xk, "grep"], writes=[hk])
            else:
                apt, apr = 512 // NB, c.RP // NB
                k = 0
                for r in range(4):
                    for t in range(c.NTP):
                        p0 = r * apr + t * apt
                        P.dma("sync", hbg[p0:p0 + apt, 0:NB, :],
                              self.hall[t][r * 512:(r + 1) * 512, g * 128:(g + 1) * 128].rearrange("(a b) c -> a b c", b=NB),
                              "fq%d" % (k % 4), reads=[("hall", tt) for tt in range(c.NTP)], writes=[(hk, k)])
                        k += 1

        prep(0)
        for g in range(8):
            hbg = hb[g % 2]
            hk = ("hb", g % 2)
            hbreads = [hk] if part == "S" else [(hk, k) for k in range(npiece)]
            for b2 in range(NB // 2):
                bank = b2 % 2
                for q in range(2):
                    b = 2 * b2 + q
                    P.op("tensor", lambda e, b=b, q=q, bank=bank, hbg=hbg: e.matmul(
                        self.ps[bank][:, q * 256:(q + 1) * 256], lhsT=hbg[:, b, :], rhs=A[:, :],
                        start=True, stop=True), reads=hbreads + ["A"], writes=[("ps", bank)])
                dstap = T1[:, 2 * b2:2 * b2 + 2, :]
                srcap = self.ps[bank].rearrange("p (q k) -> p q k", q=2)
                if b2 % 2 == 0:
                    P.op("scalar", lambda e, d=dstap, s_=srcap: e.copy(out=d, in_=s_),
                         reads=[("ps", bank)], writes=["T1"])
                else:
                    P.op("vector", lambda e, d=dstap, s_=srcap: e.tensor_copy(out=d, in_=s_),
                         reads=[("ps", bank)], writes=["T1"])
            if g + 1 < 8:
                prep(g + 1)
            self.conv_some(3, after=[("mixT", part, g - 1)] if g > 0 else [])
            T1v = T1.rearrange("p b (ri k) -> p b ri k", ri=2)
            YTv = YT.rearrange("p (kb ka) -> p ka kb", ka=128)
            def emit_s2(ka2):
                bank = 2 + ka2 % 2
                t2 = T2[ka2 % 2]
                t2k = ("T2", ka2 % 2)
                for q in range(2):
                    ka = 2 * ka2 + q
                    P.op("tensor", lambda e, ka=ka, q=q, bank=bank: e.matmul(
                        self.ps[bank][0:2 * NB, q * 256:(q + 1) * 256], lhsT=T1v[:, :, :, ka], rhs=C[:, :],
                        start=True, stop=True), reads=["T1", "C"], writes=[("ps", bank)])
                if ka2 % 2 == 0:
                    P.op("scalar", lambda e, t2=t2, bank=bank: e.copy(out=t2[0:2 * NB, :], in_=self.ps[bank][0:2 * NB, :]),
                         reads=[("ps", bank)], writes=[t2k])
                else:
                    P.op("vector", lambda e, t2=t2, bank=bank: e.tensor_copy(out=t2[0:2 * NB, :], in_=self.ps[bank][0:2 * NB, :]),
                         reads=[("ps", bank)], writes=[t2k])

            def emit_s3(ka2):
                t2 = T2[ka2 % 2]
                t2k = ("T2", ka2 % 2)
                for q in range(2):
                    ka = 2 * ka2 + q
                    kg = ka // KG
                    bank3 = 4 + kg % 2
                    col = (ka % KG) * KB
                    for w in range(2):
                        P.op("tensor", lambda e, t2=t2, q=q, w=w, ka=ka, bank3=bank3, col=col: e.matmul(
                            self.ps[bank3][:, col:col + KB],
                            lhsT=t2[0:2 * NB, q * 256 + w * 128:q * 256 + (w + 1) * 128],
                            rhs=B[0:2 * NB, w, ka * KB:(ka + 1) * KB],
                            start=(w == 0), stop=(w == 1)),
                            reads=[t2k, Bkey], writes=[("ps", bank3)])
                    if ka % KG == KG - 1:
                        ka0 = ka - KG + 1
                        dstap = YTv[:, ka0:ka0 + KG, :]
                        srcap = self.ps[bank3].rearrange("p (k b) -> p k b", b=KB)
                        if kg % 2 == 0:
                            P.op("vector", lambda e, d=dstap, s_=srcap: e.tensor_copy(out=d, in_=s_),
                                 reads=[("ps", bank3)], writes=["YT"])
                        else:
                            P.op("scalar", lambda e, d=dstap, s_=srcap: e.copy(out=d, in_=s_),
                                 reads=[("ps", bank3)], writes=["YT"])

            emit_s2(0)
            for ka2 in range(64):
                if ka2 + 1 < 64:
                    emit_s2(ka2 + 1)
                emit_s3(ka2)
            P.dma("sync", self.mixT[part][g, :, :], YT, "fy", reads=["YT"], writes=[("mixT", part, g)])

    def fourier_layer(self, l):
        c, P, nc = self.c, self.P, self.nc
        with contextlib.ExitStack() as ls:
            sb = lambda name, shape, dtype: ls.enter_context(nc.sbuf_tensor("f%d_" % l + name, shape, dtype))
            fb = {}
            fb["grep"] = grep = sb("grep", [128, D], F32)
            fb["A"] = A = sb("A", [128, 256], BF16)
            fb["C"] = C = sb("C", [128, 256], BF16)
            fb["BP"] = sb("BP", [128, 2, 128 * c.KBP], BF16)
            fb["BS"] = sb("BS", [128, 2, 128 * c.NBS], BF16)
            fb["xg"] = sb("xg", [128, c.NBS * 64], F32)
            fb["xg2"] = sb("xg2", [128, c.NBS * 64], F32)
            fb["hb"] = [sb("hb%d" % i, [128, max(c.NBP, c.NBS), 128], BF16) for i in range(2)]
            fb["T2"] = [sb("T2_%d" % i, [128, 512], BF16) for i in range(2)]
            fb["ssq"] = sb("ssq", [128, c.NBS], F32)
            fb["rtq"] = sb("rtq", [128, c.NBS], F32)
            fb["rsq"] = sb("rsq", [128, c.NBS], F32)
            mT = sb("mT", [128, 8, 512], BF16)

            P.dma("sync", grep[:, :], self.gmixrep_d[:, l, :], "f0", writes=["grep"])
            P.dma("gpsimd", A[:, :], self.dftA_d[:, :], "f1", writes=["A"])
            P.dma("gpsimd", C[:, :], self.dftC_d[:, :], "f2", writes=["C"])
            k = 0
            for part, nbp, ncol in (("S", c.NBS, 128 * c.NBS), ("P", c.NBP, 128 * c.KBP)):
                ch = min(2048, ncol)
                for w in range(2):
                    for i in range(ncol // ch):
                        P.dma("gpsimd", fb["B" + part][0:2 * nbp, w, i * ch:(i + 1) * ch],
                              self.dftB_d[part][w, :, i * ch:(i + 1) * ch], "fb%d" % (k % 4),
                              writes=["B" + part])
                        k += 1

            hst = mT[:, :, :].rearrange("p g t -> p (g t)").rearrange("p (s c) -> p s c", c=D)
            for t in range(c.NTP):
                b = self.next_xt()
                xt, xkey = self.xt[b], ("xt", b)
                self.load_tile(l, "P", t, b)
                for s in range(4):
                    P.op("scalar", lambda e, s=s, xt=xt: e.activation(out=self.hn[s % 2][:, :], in_=xt[:, s, :], func=AF.Square,
                                                                     accum_out=self.ss[:, s:s + 1]),
                         reads=[(xkey, s)], writes=[("ss", s), ("hn", s % 2)])
                P.op("scalar", lambda e: e.activation(out=self.rt[:, 0:4], in_=self.ss[:, 0:4], func=AF.Sqrt,
                                                       bias=self.epsb[:, 0:1], scale=1.0 / D),
                     reads=[("ss", s) for s in range(4)] + ["epsb"], writes=["rt"])
                P.op("vector", lambda e: e.reciprocal(out=self.rstd[:, 0:4], in_=self.rt[:, 0:4]),
                     reads=["rt"], writes=["rstd"])
                for s in range(4):
                    hb_, hk = self.hn[s % 2], ("hn", s % 2)
                    P.op("vector", lambda e, s=s, hb_=hb_, xt=xt: e.tensor_scalar(
                        out=hb_[:, :], in0=xt[:, s, :], scalar1=self.rstd[:, s:s + 1], scalar2=None, op0=ALU.mult),
                        reads=[(xkey, s), "rstd"], writes=[hk])
                    P.op("vector", lambda e, s=s, hb_=hb_: e.tensor_tensor(
                        out=hst[:, s, :], in0=hb_[:, :], in1=grep[:, :], op=ALU.mult),
                        reads=[hk, "grep"], writes=[("hst", s)])
                P.dma("sync", self.hloc[t].rearrange("(s p) c -> p s c", p=128), hst,
                      "fh", reads=[("hst", s) for s in range(4)], writes=[("hloc", t)])
                _coll(P, self.hloc[t], self.hall[t], "ag", [("hloc", t)], [("hall", t)], c.groups)
            P.barrier(skip=("C_ag",) + tuple("D_cv%d" % i for i in range(4)))

            self.fft_part(l, "S", fb)
            P.barrier(skip=("C_ag",) + tuple("D_cv%d" % i for i in range(4)))
            self.fft_part(l, "P", fb)
            P.barrier(skip=tuple("D_cv%d" % i for i in range(4)))

            self.convert_weights()
            s_wf = c.slot[("wf", l)]
            for part in ("P", "S"):
                mixv = self.mixT[part].rearrange("g k r -> k g r")
                for t in range(self.NT[part]):
                    b = self.next_xt()
                    xt, xkey = self.xt[b], ("xt", b)
                    self.load_tile(l, part, t, b)
                    P.dma("sync", mT[:, :, :], mixv[:, :, t * 512:(t + 1) * 512], "fm",
                          reads=[("mixT", part, g) for g in range(8)], writes=["mT"])
                    ws = [self.ring_load(s_wf + j) for j in range(4)]
                    for s in range(4):
                        for half in range(2):
                            bank = 2 * s + half
                            for g in range(8):
                                w, wk = ws[g // 2]
                                P.op("tensor", lambda e, w=w, g=g, s=s, half=half, bank=bank: e.matmul(
                                    self.ps[bank], lhsT=mT[:, g, s * 128:(s + 1) * 128],
                                    rhs=w[:, (g % 2) * 1024 + half * 512:(g % 2) * 1024 + (half + 1) * 512],
                                    start=(g == 0), stop=(g == 7)),
                                    reads=[wk, "mT"], writes=[("ps", bank)])
                    self.add_psum_to_xt(xt, xkey)
                    self.ffn_tile(l, xt, xkey)
                    self.store_tile(l, part, t, b)
    def qk_norm_rope(self, tmp, nh, pss, gain, gkey, cs, cskey, s):
        P = self.P
        qn, qr = tmp["qn"], tmp["qr"]
        t1, t2, t3, t4 = tmp["t"]
        h0 = 0
        for (psap, bank, n) in pss:
            for i in range(n):
                P.op("scalar", lambda e, psap=psap, i=i, h0=h0: e.activation(
                    out=self.junk[:, (h0 + i) * 128:(h0 + i + 1) * 128], in_=psap[:, i * 128:(i + 1) * 128],
                    func=AF.Square, accum_out=self.ss[:, h0 + i:h0 + i + 1]),
                    reads=[("ps", bank)], writes=[("ssh", h0 + i), ("junk", h0 + i)])
            h0 += n
        P.op("scalar", lambda e: e.activation(out=self.rt[:, 0:nh], in_=self.ss[:, 0:nh], func=AF.Sqrt,
                                               bias=self.epsb[:, 0:1], scale=1.0 / HD),
             reads=[("ssh", i) for i in range(nh)] + ["epsb"], writes=["rth"])
        P.op("vector", lambda e: e.reciprocal(out=self.rstd[:, 0:nh], in_=self.rt[:, 0:nh]),
             reads=["rth"], writes=["rstdh"])
        h0 = 0
        for (psap, bank, n) in pss:
            P.op("vector", lambda e, psap=psap, n=n, h0=h0: e.tensor_tensor(
                out=qn[:, h0:h0 + n, :], in0=psap.rearrange("p (h d) -> p h d", d=128),
                in1=self.rstd[:, h0:h0 + n].unsqueeze(2).broadcast_to([128, n, 128]), op=ALU.mult),
                reads=[("ps", bank), "rstdh"], writes=["qn"] + tmp.get("fence", []))
            h0 += n
        P.op("gpsimd", lambda e: e.tensor_tensor(
            out=qn[:, 0:nh, :], in0=qn[:, 0:nh, :],
            in1=gain.unsqueeze(1).broadcast_to([128, nh, 128]), op=ALU.mult),
            reads=["qn", gkey], writes=["qn"])
        qv = qn[:, 0:nh, :].rearrange("p h (i two) -> p h i two", two=2)
        rv = qr[:, 0:nh, :].rearrange("p h (i two) -> p h i two", two=2)
        x0, x1 = qv[:, :, :, 0], qv[:, :, :, 1]
        cc = cs[:, s, 0:64].unsqueeze(1).broadcast_to([128, nh, 64])
        sn = cs[:, s, 64:128].unsqueeze(1).broadcast_to([128, nh, 64])
        P.op("gpsimd", lambda e: e.tensor_tensor(out=t1[:, 0:nh, :], in0=x0, in1=cc, op=ALU.mult),
             reads=["qn", cskey], writes=["t1"])
        P.op("gpsimd", lambda e: e.tensor_tensor(out=t2[:, 0:nh, :], in0=x1, in1=sn, op=ALU.mult),
             reads=["qn", cskey], writes=["t2"])
        P.op("gpsimd", lambda e: e.tensor_tensor(out=rv[:, :, :, 0], in0=t1[:, 0:nh, :], in1=t2[:, 0:nh, :],
                                                 op=ALU.subtract), reads=["t1", "t2"], writes=["qr0"])
        P.op("vector", lambda e: e.tensor_tensor(out=t3[:, 0:nh, :], in0=x0, in1=sn, op=ALU.mult),
             reads=["qn", cskey], writes=["t3"])
        P.op("vector", lambda e: e.tensor_tensor(out=t4[:, 0:nh, :], in0=x1, in1=cc, op=ALU.mult),
             reads=["qn", cskey], writes=["t4"])
        P.op("vector", lambda e: e.tensor_tensor(out=rv[:, :, :, 1], in0=t3[:, 0:nh, :], in1=t4[:, 0:nh, :],
                                                 op=ALU.add), reads=["t3", "t4"], writes=["qr1"])


    def attn_layer(self, l):
        c, P, nc = self.c, self.P, self.nc
        scale = float(HD) ** -0.5
        NBmax = max(c.NBP, c.NBS)
        with contextlib.ExitStack() as ls:
            sb = lambda name, shape, dtype: ls.enter_context(nc.sbuf_tensor("a%d_" % l + name, shape, dtype))
            KT = sb("KT", [128, 2, NBmax * 128], BF16)
            Vb = sb("Vb", [128, NBmax, 256], BF16)
            QT = sb("QT", [128, 8, 512], BF16)
            OT = sb("OT", [128, 8, 512], BF16)
            NPT = 4
            pT = [sb("pT%d" % i, [128, 1024], BF16) for i in range(NPT)]
            s2 = [sb("s2_%d" % i, [128, 512], BF16) for i in range(NPT)]
            rinv = sb("rinv", [128, 512], F32)
            ou = sb("ou", [128, 512], F32)
            cs = sb("cs", [128, 4, 128], F32)
            qkg = sb("qkg", [128, 2, 128], F32)
            flat = self.act[:, :, :].rearrange("p f t -> p (f t)").bitcast(F32)
            assert c.FC * 256 >= 3584
            tmp = {
                "fence": [("act", fc) for fc in range(c.FC)],
                "qn": flat[:, 0:1024].rearrange("p (h d) -> p h d", d=128),
                "t": [flat[:, 1024 + 512 * i:1024 + 512 * (i + 1)].rearrange("p (h d) -> p h d", d=64) for i in range(4)],
                "qr": flat[:, 3072:3584].bitcast(BF16).rearrange("p (h d) -> p h d", d=128),
            }
            P.dma("sync", qkg[:, :, :], self.qkg_d[:, l, :, :], "a0", writes=["qkg"])
            self.convert_weights()
            s_q, s_kv, s_o = c.slot[("wq", l)], c.slot[("wkv", l)], c.slot[("wo", l)]

            def pass1(part):
                for ta in range(self.NT[part]):
                    b = self.next_xt()
                    xt, xkey = self.xt[b], ("xt", b)
                    self.load_tile(l, part, ta, b)
                    P.dma("sync", cs[:, :, :], self.cs_d[part][ta], "a1", writes=["cs"])
                    self.norm_to_hT(xt, xkey, self.gmixT, "gmixT", l * 8)
                    ws = [self.ring_load(s_kv + j) for j in range(2)]
                    def mm1(s, ws=ws):
                        bank = 4 + s % 2
                        for kc in range(KC):
                            w, wk = ws[kc // 4]
                            P.op("tensor", lambda e, w=w, kc=kc, s=s, bank=bank: e.matmul(
                                self.ps[bank], lhsT=self.hT[:, kc, s * 128:(s + 1) * 128],
                                rhs=w[:, (kc % 4) * 512:(kc % 4 + 1) * 512],
                                start=(kc == 0), stop=(kc == KC - 1)),
                                reads=[wk, ("hT", s)], writes=[("ps", bank)])
                    mm1(0)
                    for s in range(4):
                        sa = 4 * ta + s
                        bank = 4 + s % 2
                        if s + 1 < 4:
                            mm1(s + 1)
                        P.op("scalar", lambda e, sa=sa, bank=bank: e.copy(out=Vb[:, sa, :], in_=self.ps[bank][:, 256:512]),
                             reads=[("ps", bank)], writes=[("V", sa)])
                        self.qk_norm_rope(tmp, 2, [(self.ps[bank][:, 0:256], bank, 2)], qkg[:, 1, :], "qkg", cs, "cs", s)
                        pb = self.pp[3][:, (s % 2) * 512:(s % 2 + 1) * 512].bitcast(BF16)
                        for kvh in range(2):
                            P.op("tensor", lambda e, kvh=kvh, pb=pb: e.transpose(
                                out=pb[:, kvh * 128:(kvh + 1) * 128], in_=tmp["qr"][:, kvh, :], identity=self.identb[:, :]),
                                reads=["qr0", "qr1", "identb"], writes=[("ps", 6 + s % 2)])
                        P.op("scalar", lambda e, sa=sa, pb=pb: e.copy(
                            out=KT[:, :, sa * 128:(sa + 1) * 128], in_=pb[:, 0:256].rearrange("p (h t) -> p h t", h=2)),
                            reads=[("ps", 6 + s % 2)], writes=[("K", sa)])

            def pass2(part, nbk):
                allK = [("K", sa) for sa in range(nbk)]
                allV = [("V", sa) for sa in range(nbk)]
                for ta in range(self.NT[part]):
                    b = self.next_xt()
                    xt, xkey = self.xt[b], ("xt", b)
                    self.load_tile(l, part, ta, b)
                    P.dma("sync", cs[:, :, :], self.cs_d[part][ta], "a1", writes=["cs"])
                    self.norm_to_hT(xt, xkey, self.gmixT, "gmixT", l * 8)
                    ws = [self.ring_load(s_q + j) for j in range(4)]
                    def mm2(s, ws=ws):
                        banks = (4, 5) if s % 2 == 0 else (2, 3)
                        for half in range(2):
                            for kc in range(KC):
                                w, wk = ws[kc // 2]
                                P.op("tensor", lambda e, w=w, kc=kc, s=s, half=half, banks=banks: e.matmul(
                                    self.ps[banks[half]], lhsT=self.hT[:, kc, s * 128:(s + 1) * 128],
                                    rhs=w[:, (kc % 2) * 1024 + half * 512:(kc % 2) * 1024 + (half + 1) * 512],
                                    start=(kc == 0), stop=(kc == KC - 1)),
                                    reads=[wk, ("hT", s)], writes=[("ps", banks[half])])
                    mm2(0)
                    for s in range(4):
                        banks = (4, 5) if s % 2 == 0 else (2, 3)
                        if s + 1 < 4:
                            mm2(s + 1)
                        self.qk_norm_rope(tmp, 8, [(self.ps[banks[0]], banks[0], 4), (self.ps[banks[1]], banks[1], 4)],
                                          qkg[:, 0, :], "qkg", cs, "cs", s)
                        pb = self.pp[3][:, (s % 2) * 512:(s % 2 + 1) * 512].bitcast(BF16)
                        for hd in range(8):
                            P.op("tensor", lambda e, hd=hd, pb=pb: e.transpose(
                                out=pb[:, hd * 128:(hd + 1) * 128], in_=tmp["qr"][:, hd, :], identity=self.identb[:, :]),
                                reads=["qr0", "qr1", "identb"], writes=[("ps", 6 + s % 2)])
                        P.op("vector", lambda e, s=s, pb=pb: e.tensor_copy(
                            out=QT[:, :, s * 128:(s + 1) * 128], in_=pb[:, :].rearrange("p (h t) -> p h t", h=8)),
                            reads=[("ps", 6 + s % 2)], writes=["QT"])
                    npair = nbk // 2
                    items = [(hd, kp) for hd in range(8) for kp in range(npair)]

                    def emit_score(i):
                        hd, kp = items[i]
                        kvh = hd // 4
                        j = i % 3
                        for q in range(2):
                            kt = 2 * kp + q
                            P.op("tensor", lambda e, hd=hd, kt=kt, kvh=kvh, j=j, q=q: e.matmul(
                                self.ps[2 * j + q], lhsT=KT[:, kvh, kt * 128:(kt + 1) * 128], rhs=QT[:, hd, :],
                                start=True, stop=True), reads=["QT"] + allK, writes=[("ps", 2 * j + q)])
                        jb = i % NPT
                        P.op("scalar", lambda e, j=j, jb=jb: e.activation(
                            out=pT[jb][:, :], in_=self.pp[j][:, :], func=AF.Exp, scale=scale),
                            reads=[("ps", 2 * j), ("ps", 2 * j + 1)], writes=[("pT", jb)])

                    emit_score(0)
                    emit_score(1)
                    pending = []
                    for i, (hd, kp) in enumerate(items):
                        if i + 2 < len(items):
                            emit_score(i + 2)
                        kvh = hd // 4
                        jb = i % NPT
                        par = hd % 2
                        po, pr = 6, 7
                        for q in range(2):
                            kt = 2 * kp + q
                            P.op("tensor", lambda e, kt=kt, kvh=kvh, jb=jb, po=po, q=q: e.matmul(
                                self.ps[po], lhsT=Vb[:, kt, kvh * 128:(kvh + 1) * 128],
                                rhs=pT[jb][:, q * 512:(q + 1) * 512],
                                start=(kt == 0), stop=(kt == nbk - 1)),
                                reads=[("pT", jb)] + allV, writes=[("ps", po)])
                        GS = 4 if npair % 4 == 0 else 2
                        sj = (i // GS) % NPT
                        if kp % GS == 0:
                            P.op("vector", lambda e, jb=jb, sj=sj: e.tensor_tensor(
                                out=s2[sj][:, :], in0=pT[jb][:, 0:512], in1=pT[jb][:, 512:1024], op=ALU.add),
                                reads=[("pT", jb)], writes=[("s2", sj)])
                        else:
                            P.op("vector", lambda e, jb=jb: e.tensor_tensor(
                                out=rinv[:, :].bitcast(BF16)[:, 0:512], in0=pT[jb][:, 0:512], in1=pT[jb][:, 512:1024], op=ALU.add),
                                reads=[("pT", jb)], writes=["s2tmp", "rinv"])
                            P.op("vector", lambda e, sj=sj: e.tensor_tensor(
                                out=s2[sj][:, :], in0=s2[sj][:, :], in1=rinv[:, :].bitcast(BF16)[:, 0:512], op=ALU.add),
                                reads=["s2tmp", ("s2", sj)], writes=[("s2", sj)])

                            def emit_rs(sj=sj, pr=pr, kp=kp, GS=GS):
                                P.op("tensor", lambda e: e.matmul(
                                    self.ps[pr], lhsT=self.onesb[:, :], rhs=s2[sj][:, :],
                                    start=(kp == GS - 1), stop=(kp == npair - 1)),
                                    reads=[("s2", sj), "onesb"], writes=[("ps", pr)])
                            if kp % GS == GS - 1:
                                if pending:
                                    pending.pop()()
                                pending.append(emit_rs)
                        if kp == npair - 1:
                            while pending:
                                pending.pop()()
                            P.op("vector", lambda e, po=po: e.tensor_copy(out=ou[:, :], in_=self.ps[po]),
                                 reads=[("ps", po)], writes=["ou"])
                            P.op("vector", lambda e, pr=pr: e.reciprocal(out=rinv[:, :], in_=self.ps[pr]),
                                 reads=[("ps", pr), "s2tmp"], writes=["rinv", "s2tmp"])
                            P.op("vector", lambda e, hd=hd: e.tensor_tensor(
                                out=OT[:, hd, :], in0=ou[:, :], in1=rinv[:, :], op=ALU.mult),
                                reads=["ou", "rinv"], writes=[("OT", hd)])
                    ws = [self.ring_load(s_o + j) for j in range(4)]
                    for s in range(4):
                        for half in range(2):
                            bank = 2 * s + half
                            for hd in range(8):
                                w, wk = ws[hd // 2]
                                P.op("tensor", lambda e, w=w, hd=hd, s=s, half=half, bank=bank: e.matmul(
                                    self.ps[bank], lhsT=OT[:, hd, s * 128:(s + 1) * 128],
                                    rhs=w[:, (hd % 2) * 1024 + half * 512:(hd % 2) * 1024 + (half + 1) * 512],
                                    start=(hd == 0), stop=(hd == 7)),
                                    reads=[wk, ("OT", hd)], writes=[("ps", bank)])
                    self.add_psum_to_xt(xt, xkey)
                    self.ffn_tile(l, xt, xkey)
                    self.store_tile(l, part, ta, b)

            nloc = c.RP // 128
            pass1("P")
            P.dma("sync", self.kloc.rearrange("(h d) k -> d h k", h=2), KT[:, :, 0:c.RP], "ak",
                  reads=[("K", sa) for sa in range(nloc)], writes=["kloc"])
            P.dma("sync", self.vloc.rearrange("(s p) c -> p s c", p=128), Vb[:, 0:nloc, :], "av",
                  reads=[("V", sa) for sa in range(nloc)], writes=["vloc"])
            _coll(P, self.kloc, self.kall, "agk", ["kloc"], ["kall"], c.groups)
            _coll(P, self.vloc, self.vall, "agv", ["vloc"], ["vall"], c.groups)
            pass1("S")
            pass2("S", c.NBS)
            kallv = self.kall.rearrange("(r h d) k -> h d r k", r=4, h=2)
            for h in range(2):
                P.dma("sync", KT[:, h, 0:4 * c.RP].rearrange("d (r k) -> d r k", r=4), kallv[h], "ak%d" % h,
                      reads=["kall"], writes=[("K", sa) for sa in range(c.NBP)])
            P.dma("sync", Vb[:, 0:c.NBP, :], self.vall.rearrange("(s p) c -> p s c", p=128), "av2",
                  reads=["vall"], writes=[("V", sa) for sa in range(c.NBP)])
            pass2("P", c.NBP)


def _rope_rows(pos):
    inv = (np.float32(10000.0) ** (-np.arange(0, 64, 2, dtype=np.float32) / np.float32(64))).astype(np.float32)
    rowp = (pos // 64).astype(np.float32)
    colp = (pos % 64).astype(np.float32)
    ang = np.concatenate([rowp[:, None] * inv, colp[:, None] * inv], -1).astype(np.float32)
    return np.concatenate([np.cos(ang), np.sin(ang)], -1).astype(np.float32)


def _cs_tiles(pos):
    cs = _rope_rows(pos)
    nt = cs.shape[0] // 512
    return np.ascontiguousarray(cs.reshape(nt, 4, 128, 128).transpose(0, 2, 1, 3))


def _dft_common():
    a = np.arange(128)
    A = np.exp(-2j * np.pi * np.outer(a, a) / 128.0)
    m = np.concatenate([A.real, A.imag], 1).astype(np.float32)
    return m


def _dft_B(NB, R, kbs):
    b = np.arange(NB).astype(np.float64)
    ka = np.arange(128).astype(np.float64)
    kb = np.asarray(kbs).astype(np.float64)
    M = (np.exp(-2j * np.pi * b[:, None, None] * kb[None, None, :] / NB)
         * np.exp(-2j * np.pi * b[:, None, None] * ka[None, :, None] / float(R))) / np.sqrt(R * 128.0)
    n = 128 * len(kbs)
    Mr, Mi = M.real.reshape(NB, n), M.imag.reshape(NB, n)
    out = np.zeros((2, 2 * NB, n), np.float32)
    out[0, 0::2] = Mr
    out[0, 1::2] = -Mi
    out[1, 0::2] = -Mi
    out[1, 1::2] = -Mr
    return out


def _pack_weights(cfg, fourier_w, attn_w_qkv, attn_w_o, ffn_w_gate, ffn_w_up, ffn_w_down):
    wall = np.zeros((cfg.nslots, 128, 2048), np.float32)
    jF = jA = 0
    for l, t in enumerate(cfg.layers):
        if t == "F":
            w = np.asarray(fourier_w[jF], np.float32)
            s0 = cfg.slot[("wf", l)]
            wall[s0:s0 + 4] = w.reshape(4, 2, 128, D).transpose(0, 2, 1, 3).reshape(4, 128, 2048)
            jF += 1
        else:
            wqkv = np.asarray(attn_w_qkv[jA], np.float32)
            s0 = cfg.slot[("wq", l)]
            wall[s0:s0 + 4] = wqkv[:, 0:1024].reshape(4, 2, 128, 1024).transpose(0, 2, 1, 3).reshape(4, 128, 2048)
            s0 = cfg.slot[("wkv", l)]
            wall[s0:s0 + 2] = wqkv[:, 1024:1536].reshape(2, 4, 128, 512).transpose(0, 2, 1, 3).reshape(2, 128, 2048)
            s0 = cfg.slot[("wo", l)]
            wo = np.asarray(attn_w_o[jA], np.float32)
            wall[s0:s0 + 4] = wo.reshape(4, 2, 128, D).transpose(0, 2, 1, 3).reshape(4, 128, 2048)
            jA += 1
        FC = cfg.FC
        wg = np.asarray(ffn_w_gate[l], np.float32).reshape(KC, 128, FC, 128)
        wu = np.asarray(ffn_w_up[l], np.float32).reshape(KC, 128, FC, 128)
        gu = np.stack([wg, wu], 0)
        s0 = cfg.slot[("wgu", l)]
        wall[s0:s0 + FC] = gu.transpose(3, 2, 0, 1, 4).reshape(FC, 128, 2048)
        wd = np.asarray(ffn_w_down[l], np.float32)
        s0 = cfg.slot[("wd", l)]
        wall[s0:s0 + FC // 2] = wd.reshape(FC // 2, 2, 128, D).transpose(0, 2, 1, 3).reshape(FC // 2, 128, 2048)
    return wall.reshape(cfg.nslots * 128, 2048)


def _common_inputs(cfg, norm_mix, norm_ffn, attn_q_norm, attn_k_norm):
    L = cfg.L
    nm = np.asarray(norm_mix, np.float32)[:L]
    nf = np.asarray(norm_ffn, np.float32)[:L]
    gmixT = nm.reshape(L, KC, 128).transpose(2, 0, 1).reshape(128, L * KC).copy()
    gffnT = nf.reshape(L, KC, 128).transpose(2, 0, 1).reshape(128, L * KC).copy()
    gmixrep = np.broadcast_to(nm[None], (128, L, D)).copy()
    qkg = np.ones((128, L, 2, 128), np.float32)
    jA = 0
    for l, t in enumerate(cfg.layers):
        if t == "A":
            qkg[:, l, 0, :] = np.asarray(attn_q_norm[jA], np.float32)[None]
            qkg[:, l, 1, :] = np.asarray(attn_k_norm[jA], np.float32)[None]
            jA += 1
    dft = _dft_common()
    return dict(gmixT=gmixT, gffnT=gffnT, gmixrep=gmixrep, qkg=qkg, ident=np.eye(128, dtype=np.float32),
                dftA=dft, dftC=dft.copy(),
                dftBS=_dft_B(cfg.NBS, cfg.RS, np.arange(cfg.NBS)),
                csS=_cs_tiles(np.arange(cfg.RS)))


_NC_CACHE = {}


def run_cores(cfg, xp_list, xs_list, weights):
    key = (cfg.RP, cfg.RS, cfg.DFF, cfg.layers, cfg.ring)
    if key not in _NC_CACHE:
        _NC_CACHE[key] = Builder(cfg).build()
    nc = _NC_CACHE[key]
    in_maps = []
    for c in range(8):
        r = c % 4
        m = dict(xp=np.ascontiguousarray(xp_list[c], np.float32), xs=np.ascontiguousarray(xs_list[c], np.float32),
                 wall=weights["wall"])
        m.update(weights["common"])
        m["csP"] = _cs_tiles(cfg.RP * r + np.arange(cfg.RP))
        m["dftBP"] = _dft_B(cfg.NBP, 4 * cfg.RP, cfg.KBP * r + np.arange(cfg.KBP))
        in_maps.append(m)
    res = run_bass_kernel_spmd(nc, in_maps, core_ids=list(range(8)))
    return [(r["yp"], r["ys"]) for r in res.results]


def kernel(x_prompt, x_sample, norm_mix, norm_ffn, fourier_w, attn_w_qkv, attn_q_norm, attn_k_norm,
           attn_w_o, ffn_w_gate, ffn_w_up, ffn_w_down):
    cfg = Cfg()
    xp = np.asarray(x_prompt, np.float32)
    xs = np.asarray(x_sample, np.float32)
    weights = dict(
        wall=_pack_weights(cfg, fourier_w, attn_w_qkv, attn_w_o, ffn_w_gate, ffn_w_up, ffn_w_down),
        common=_common_inputs(cfg, norm_mix, norm_ffn, attn_q_norm, attn_k_norm))
    RP = cfg.RP
    xp_list = [xp[c // 4, RP * (c % 4):RP * (c % 4 + 1)] for c in range(8)]
    xs_list = [xs[c] for c in range(8)]
    outs = run_cores(cfg, xp_list, xs_list, weights)
    y_prompt = np.zeros_like(xp)
    y_sample = np.zeros_like(xs)
    for c in range(8):
        y_prompt[c // 4, RP * (c % 4):RP * (c % 4 + 1)] = outs[c][0]
        y_sample[c] = outs[c][1]
    return (y_prompt, y_sample)
```

```python
import bisect
import contextlib
import numpy as np
import ml_dtypes
import concourse.bass as bass
import concourse.mybir as mybir
from concourse.bass_utils import run_bass_kernel_spmd

F32 = mybir.dt.float32
BF16 = mybir.dt.bfloat16
AF = mybir.ActivationFunctionType
ALU = mybir.AluOpType
AX = mybir.AxisListType

D = 1024
KC = 8
HD = 128
NH = 8
NKV = 2
EPS = 1e-6
NEG = -30000.0


class Prog:
    ENGS = ("sync", "scalar", "vector", "gpsimd", "tensor")

    def __init__(self, nc, stack):
        self.nc = nc
        self.stack = stack
        self.ops = {e: [] for e in self.ENGS}
        self.marked = {e: [] for e in self.ENGS}
        self.cnt = {e: 0 for e in self.ENGS}
        self.seen = {e: {} for e in self.ENGS}
        self.sems = {}
        self.dmacnt = {}
        self.last_write = {}
        self.readers = {}

    def sem(self, name):
        if name not in self.sems:
            self.sems[name] = self.stack.enter_context(self.nc.semaphore(name))
        return self.sems[name]

    def _ticket(self, ref):
        if ref[0] == "dma":
            return (ref[1], ref[2])
        eng, pos = ref
        m = self.marked[eng]
        i = bisect.bisect_left(m, pos)
        if i < len(m):
            p = m[i]
        else:
            p = pos
            self.cnt[eng] += 1
            self.ops[eng][p][1] = self.cnt[eng]
            m.append(p)
        return ("E_" + eng, self.ops[eng][p][1])

    def _wait(self, eng, ticket):
        name, val = ticket
        if self.seen[eng].get(name, 0) >= val:
            return
        self.seen[eng][name] = val
        semh = self.sem(name)
        self.ops[eng].append([lambda e, s=semh, v=val: e.wait_ge(s, v), None, True])

    def _deps(self, eng, reads, writes):
        refs = []
        for r in reads:
            w = self.last_write.get(r)
            if w is not None:
                refs.append(w)
        for w_ in writes:
            w = self.last_write.get(w_)
            if w is not None:
                refs.append(w)
            refs.extend(self.readers.get(w_, {}).values())
        return [r for r in refs if not (r[0] == "tensor" and eng == "tensor")]

    def _update(self, ref, rkey, reads, writes):
        for r in reads:
            self.readers.setdefault(r, {})[rkey] = ref
        for w in writes:
            self.last_write[w] = ref
            self.readers[w] = {}

    def op(self, eng, fn, reads=(), writes=()):
        for ref in self._deps(eng, reads, writes):
            self._wait(eng, self._ticket(ref))
        self.ops[eng].append([fn, None, False])
        ref = (eng, len(self.ops[eng]) - 1)
        self._update(ref, eng, reads, writes)
        return ref

    def dma(self, q, out, in_, semkey, reads=(), writes=(), **kw):
        for ref in self._deps("dma:" + q, reads, writes):
            self._wait(q, self._ticket(ref))
        name = "D_" + semkey
        self.dmacnt[name] = self.dmacnt.get(name, 0) + 16
        val = self.dmacnt[name]
        semh = self.sem(name)
        self.ops[q].append([lambda e, o=out, i=in_, s=semh, k=kw: e.dma_start(out=o, in_=i, **k).then_inc(s, 16),
                            None, True])
        ref = ("dma", name, val)
        self._update(ref, name, reads, writes)
        return ref

    def barrier(self, skip=()):
        tickets = []
        for e in self.ENGS:
            pos = len(self.ops[e]) - 1
            while pos >= 0 and self.ops[e][pos][2]:
                pos -= 1
            if pos >= 0:
                tickets.append(self._ticket((e, pos)))
        for name, val in self.dmacnt.items():
            if name not in skip:
                tickets.append((name, val))
        for e in self.ENGS:
            for t in tickets:
                self._wait(e, t)

    def replay(self, eng, e):
        semh = self.sem("E_" + eng) if self.cnt[eng] else None
        for o in self.ops[eng]:
            ins = o[0](e)
            if o[1] is not None:
                ins.then_inc(semh, 1)


def _coll(P, ins_ap, outs_ap, semkey, reads, writes, groups):
    for ref in P._deps("dma:gpsimd", reads, writes):
        P._wait("gpsimd", P._ticket(ref))
    name = "C_" + semkey
    P.dmacnt[name] = P.dmacnt.get(name, 0) + 1
    val = P.dmacnt[name]
    semh = P.sem(name)
    P.ops["gpsimd"].append([lambda e: e.collective_compute(
        "AllGather", ALU.bypass, replica_groups=groups, ins=[ins_ap.opt()], outs=[outs_ap.opt()]).then_inc(semh),
        None, True])
    ref = ("dma", name, val)
    P._update(ref, name, reads, writes)
    return ref


class Cfg:
    def __init__(self, RP=2048, RS=4096, DFF=2816, layers=("F", "A", "F", "A"), ring=6):
        self.RP, self.RS = RP, RS
        self.NTP, self.NTS = RP // 512, RS // 512
        self.NBP = 4 * RP // 128
        self.KBP = self.NBP // 4
        self.NBS = RS // 128
        assert self.NBS % 16 == 0 and self.NBP % 16 == 0
        self.DFF = DFF
        self.FC = DFF // 128
        assert self.FC % 2 == 0
        self.layers = tuple(layers)
        self.L = len(layers)
        self.ring = ring
        self.slot = {}
        n = 0
        for l, t in enumerate(self.layers):
            if t == "F":
                self.slot[("wf", l)] = n; n += 4
            else:
                self.slot[("wq", l)] = n; n += 4
                self.slot[("wkv", l)] = n; n += 2
                self.slot[("wo", l)] = n; n += 4
            self.slot[("wgu", l)] = n; n += self.FC
            self.slot[("wd", l)] = n; n += self.FC // 2
        self.nslots = n
        self.groups = [[0, 1, 2, 3], [4, 5, 6, 7]]


class Builder:
    def __init__(self, cfg):
        self.c = cfg
        self.nc = bass.Bass("TRN2", target_bir_lowering=False)

    def build(self):
        c, nc = self.c, self.nc
        RP, RS = c.RP, c.RS
        dt = nc.dram_tensor
        ein = lambda name, shape, dtype=F32: dt(name, shape, dtype, kind="ExternalInput").ap()
        itn = lambda name, shape, dtype: dt(name, shape, dtype).ap()
        self.xin = {"P": ein("xp", [RP, D]), "S": ein("xs", [RS, D])}
        self.yout = {"P": dt("yp", [RP, D], F32, kind="ExternalOutput").ap(),
                     "S": dt("ys", [RS, D], F32, kind="ExternalOutput").ap()}
        self.xscr = {"P": itn("xscrp", [RP, D], F32), "S": itn("xscrs", [RS, D], F32)}
        self.wall = ein("wall", [c.nslots * 128, 2048])
        self.gmixT_d = ein("gmixT", [128, c.L * 8])
        self.gffnT_d = ein("gffnT", [128, c.L * 8])
        self.gmixrep_d = ein("gmixrep", [128, c.L, D])
        self.qkg_d = ein("qkg", [128, c.L, 2, 128])
        self.cs_d = {"P": ein("csP", [c.NTP, 128, 4, 128]), "S": ein("csS", [c.NTS, 128, 4, 128])}
        self.dftA_d = ein("dftA", [128, 256])
        self.dftC_d = ein("dftC", [128, 256])
        self.dftB_d = {"P": ein("dftBP", [2, 2 * c.NBP, 128 * c.KBP]), "S": ein("dftBS", [2, 2 * c.NBS, 128 * c.NBS])}
        self.ident_d = ein("ident", [128, 128])
        self.wbf = itn("wbf", [c.nslots * 128, 2048], BF16)
        self.mixT = {"P": itn("mixTP", [8, 128, RP], BF16), "S": itn("mixTS", [8, 128, RS], BF16)}
        self.hloc = [itn("hloc%d" % t, [512, D], BF16) for t in range(c.NTP)]
        self.hall = [itn("hall%d" % t, [4 * 512, D], BF16) for t in range(c.NTP)]
        self.kloc = itn("kloc", [256, RP], BF16)
        self.kall = itn("kall", [1024, RP], BF16)
        self.vloc = itn("vloc", [RP, 256], BF16)
        self.vall = itn("vall", [4 * RP, 256], BF16)
        self.NT = {"P": c.NTP, "S": c.NTS}

        with contextlib.ExitStack() as st:
            self.st = st
            self.P = P = Prog(nc, st)
            sb = lambda name, shape, dtype: st.enter_context(nc.sbuf_tensor("s_" + name, shape, dtype))
            self.pp = [st.enter_context(nc.psum_tensor("pp%d" % i, [128, 1024], F32)) for i in range(4)]
            self.ps = [self.pp[i // 2][:, (i % 2) * 512:(i % 2 + 1) * 512] for i in range(8)]
            self.ident = sb("ident", [128, 128], F32)
            self.identb = sb("identb", [128, 128], BF16)
            self.onesb = sb("onesb", [128, 128], BF16)
            self.gmixT = sb("gmixT", [128, c.L * 8], F32)
            self.gffnT = sb("gffnT", [128, c.L * 8], F32)
            self.xtall = sb("xtall", [128, 8 * D], F32)
            self.xt = [self.xtall[:, i * 4 * D:(i + 1) * 4 * D].rearrange("p (s c) -> p s c", c=D) for i in range(2)]
            self.hn = [sb("hn%d" % i, [128, D], F32) for i in range(2)]
            self.hT = sb("hT", [128, KC, 512], BF16)
            self.act = sb("act", [128, c.FC, 512], BF16)
            self.sg = [sb("sg%d" % i, [128, 512], F32) for i in range(2)]
            self.ringb = [sb("ring%d" % i, [128, 2048], BF16) for i in range(c.ring)]
            self.junk = sb("junk", [128, D], BF16)
            self.ss = sb("ss", [128, 8], F32)
            self.rt = sb("rt", [128, 8], F32)
            self.rstd = sb("rstd", [128, 8], F32)
            self.epsb = sb("epsb", [128, 1], F32)
            self.ring_pos = 0
            self.conv_pos = 0
            self.conv_i = 0
            self.tilecnt = 0

            self.load_consts()
            for l, t in enumerate(c.layers):
                if t == "F":
                    self.fourier_layer(l)
                else:
                    self.attn_layer(l)
                P.barrier()
            P.barrier()

            with nc.Block() as block:
                @block.sync
                def _(e):
                    P.replay("sync", e)

                @block.scalar
                def _(e):
                    P.replay("scalar", e)

                @block.vector
                def _(e):
                    P.replay("vector", e)

                @block.gpsimd
                def _(e):
                    P.replay("gpsimd", e)

                @block.tensor
                def _(e):
                    P.replay("tensor", e)
        return nc

    def src(self, l, part):
        return self.xin[part] if l == 0 else self.xscr[part]

    def dst(self, l, part):
        return self.yout[part] if l == self.c.L - 1 else self.xscr[part]

    def load_consts(self):
        P = self.P
        P.dma("sync", self.ident[:, :], self.ident_d[:, :], "c0", writes=["ident"])
        P.dma("sync", self.gmixT[:, :], self.gmixT_d[:, :], "c1", writes=["gmixT"])
        P.dma("sync", self.gffnT[:, :], self.gffnT_d[:, :], "c2", writes=["gffnT"])
        P.op("vector", lambda e: e.tensor_copy(out=self.identb[:, :], in_=self.ident[:, :]),
             reads=["ident"], writes=["identb"])
        P.op("vector", lambda e: e.memset(self.onesb[:, :], 1.0), writes=["onesb"])
        P.op("vector", lambda e: e.memset(self.epsb[:, :], EPS), writes=["epsb"])

    def conv_some(self, k, after=()):
        c, P = self.c, self.P
        while k > 0 and self.conv_pos < c.nslots:
            s = self.conv_pos
            n = min(4, c.nslots - s)
            P.dma("gpsimd", self.wbf[s * 128:(s + n) * 128, :], self.wall[s * 128:(s + n) * 128, :],
                  "cv%d" % (self.conv_i % 4), reads=list(after), writes=[("wbf", j) for j in range(s, s + n)])
            self.conv_pos += n
            self.conv_i += 1
            k -= 1

    def conv_until(self, slot_end):
        while self.conv_pos < min(slot_end, self.c.nslots):
            self.conv_some(1)

    def convert_weights(self):
        self.conv_until(self.c.nslots)

    def next_xt(self):
        b = self.tilecnt % 2
        self.tilecnt += 1
        return b

    def load_tile(self, l, part, t, b):
        src = self.src(l, part)
        self.P.dma("sync", self.xt[b][:, :, :],
                   src[t * 512:(t + 1) * 512, :].rearrange("(s p) c -> p s c", p=128),
                   "xt%d" % b, reads=[("X", part, l, t)], writes=[(("xt", b), s) for s in range(4)])

    def store_tile(self, l, part, t, b):
        dst = self.dst(l, part)
        self.P.dma("scalar", dst[t * 512:(t + 1) * 512, :].rearrange("(s p) c -> p s c", p=128),
                   self.xt[b][:, :, :], "st%d" % b,
                   reads=[(("xt", b), s) for s in range(4)], writes=[("X", part, l + 1, t)])

    def ring_load(self, slot):
        P = self.P
        b = self.ring_pos % self.c.ring
        self.ring_pos += 1
        key = ("ring", b)
        P.dma("sync", self.ringb[b][:, :], self.wbf[slot * 128:(slot + 1) * 128, :], "ring%d" % b,
              reads=[("wbf", slot)], writes=[key])
        return self.ringb[b], key

    def norm_to_hT(self, xt, xkey, gT, gkey, gcol):
        P = self.P
        for s in range(4):
            P.op("scalar", lambda e, s=s: e.activation(out=self.hn[s % 2][:, :], in_=xt[:, s, :], func=AF.Square,
                                                      accum_out=self.ss[:, s:s + 1]),
                 reads=[(xkey, s)], writes=[("ss", s), ("hn", s % 2)])
        P.op("scalar", lambda e: e.activation(out=self.rt[:, 0:4], in_=self.ss[:, 0:4], func=AF.Sqrt,
                                               bias=self.epsb[:, 0:1], scale=1.0 / D),
             reads=[("ss", s) for s in range(4)] + ["epsb"], writes=["rt"])
        P.op("vector", lambda e: e.reciprocal(out=self.rstd[:, 0:4], in_=self.rt[:, 0:4]),
             reads=["rt"], writes=["rstd"])
        def scale(s):
            hb = self.hn[s % 2]
            P.op("vector", lambda e, s=s, hb=hb: e.tensor_scalar(out=hb[:, :], in0=xt[:, s, :],
                                                                scalar1=self.rstd[:, s:s + 1], scalar2=None,
                                                                op0=ALU.mult),
                 reads=[(xkey, s), "rstd"], writes=[("hn", s % 2)])

        def trans(s):
            hb = self.hn[s % 2]
            hk = ("hn", s % 2)
            for half in range(2):
                bank = 2 * (s % 2) + half
                for j in range(4):
                    kc = half * 4 + j
                    P.op("tensor", lambda e, hb=hb, kc=kc, j=j, bank=bank: e.transpose(
                        out=self.ps[bank][:, j * 128:(j + 1) * 128], in_=hb[:, kc * 128:(kc + 1) * 128],
                        identity=self.ident[:, :]),
                        reads=[hk, "ident"], writes=[("ps", bank)])

        def evac(s):
            for half in range(2):
                bank = 2 * (s % 2) + half
                g0 = gcol + half * 4
                P.op("vector", lambda e, s=s, half=half, bank=bank, g0=g0: e.tensor_tensor(
                    out=self.hT[:, half * 4:half * 4 + 4, s * 128:(s + 1) * 128],
                    in0=self.ps[bank][:, :].rearrange("p (j t) -> p j t", j=4),
                    in1=gT[:, g0:g0 + 4].unsqueeze(2).broadcast_to([128, 4, 128]),
                    op=ALU.mult),
                    reads=[("ps", bank), gkey], writes=[("hT", s)])

        scale(0)
        for s in range(4):
            if s + 1 < 4:
                scale(s + 1)
            trans(s)
            evac(s)

    def ffn_tile(self, l, xt, xkey):
        c, P = self.c, self.P
        self.norm_to_hT(xt, xkey, self.gffnT, "gffnT", l * 8)
        hTr = [("hT", s) for s in range(4)]
        s_gu = c.slot[("wgu", l)]
        s_d = c.slot[("wd", l)]
        for fc in range(c.FC):
            w, wk = self.ring_load(s_gu + fc)
            par = fc % 2
            pg, pu = self.ps[4 + 2 * par], self.ps[5 + 2 * par]
            kg, ku = ("ps", 4 + 2 * par), ("ps", 5 + 2 * par)
            for kc in range(KC):
                P.op("tensor", lambda e, w=w, kc=kc, pg=pg: e.matmul(
                    pg[:, :], lhsT=w[:, kc * 128:(kc + 1) * 128], rhs=self.hT[:, kc, :],
                    start=(kc == 0), stop=(kc == KC - 1)), reads=[wk] + hTr, writes=[kg])
            for kc in range(KC):
                P.op("tensor", lambda e, w=w, kc=kc, pu=pu: e.matmul(
                    pu[:, :], lhsT=w[:, 1024 + kc * 128:1024 + (kc + 1) * 128], rhs=self.hT[:, kc, :],
                    start=(kc == 0), stop=(kc == KC - 1)), reads=[wk] + hTr, writes=[ku])
            sgb = self.sg[par]
            P.op("scalar", lambda e, pg=pg, sgb=sgb: e.activation(out=sgb[:, :], in_=pg[:, :], func=AF.Silu),
                 reads=[kg], writes=[("sg", par)])
            P.op("vector", lambda e, pu=pu, sgb=sgb, fc=fc: e.tensor_tensor(
                out=self.act[:, fc, :], in0=pu[:, :], in1=sgb[:, :], op=ALU.mult),
                reads=[ku, ("sg", par)], writes=[("act", fc)])
        for j in range(c.FC // 2):
            w, wk = self.ring_load(s_d + j)
            for i in range(2):
                fc = 2 * j + i
                for s in range(4):
                    for half in range(2):
                        bank = 2 * s + half
                        P.op("tensor", lambda e, w=w, i=i, fc=fc, s=s, half=half, bank=bank: e.matmul(
                            self.ps[bank][:, :], lhsT=self.act[:, fc, s * 128:(s + 1) * 128],
                            rhs=w[:, i * 1024 + half * 512:i * 1024 + (half + 1) * 512],
                            start=(fc == 0), stop=(fc == c.FC - 1)),
                            reads=[wk, ("act", fc)], writes=[("ps", bank)])
        for s in range(4):
            for half in range(2):
                bank = 2 * s + half
                P.op("vector", lambda e, s=s, half=half, bank=bank: e.tensor_tensor(
                    out=xt[:, s, half * 512:(half + 1) * 512], in0=self.ps[bank][:, :],
                    in1=xt[:, s, half * 512:(half + 1) * 512], op=ALU.add),
                    reads=[("ps", bank), (xkey, s)], writes=[(xkey, s)])

    def add_psum_to_xt(self, xt, xkey):
        for s in range(4):
            for half in range(2):
                bank = 2 * s + half
                self.P.op("vector", lambda e, s=s, half=half, bank=bank: e.tensor_tensor(
                    out=xt[:, s, half * 512:(half + 1) * 512], in0=self.ps[bank][:, :],
                    in1=xt[:, s, half * 512:(half + 1) * 512], op=ALU.add),
                    reads=[("ps", bank), (xkey, s)], writes=[(xkey, s)])


    def fft_part(self, l, part, fb):
        c, P = self.c, self.P
        NB = c.NBP if part == "P" else c.NBS
        KB = c.KBP if part == "P" else c.NBS
        NT = self.NT[part]
        A, C, B, T2, hb, xg = fb["A"], fb["C"], fb["B" + part], fb["T2"], fb["hb"], fb["xg"]
        Bkey = "B" + part
        T1 = self.xtall[:, :].bitcast(BF16)[:, 0:NB * 256].rearrange("p (b k) -> p b k", k=256)
        T1f = self.xtall[:, :].bitcast(BF16)
        YT = self.act[:, :, :].rearrange("p f t -> p (f t)")[:, 0:128 * KB]
        KG = 512 // KB
        if part == "S":
            src = self.src(l, "S")
            allX = [("X", "S", l, t) for t in range(NT)]
            srcv = src.rearrange("(a b) c -> a b c", b=NB)
            ssq, rtq, rsq, grep = fb["ssq"], fb["rtq"], fb["rsq"], fb["grep"]
            nbs = NB // 16
            hbv = lambda h: h[:, :, :].rearrange("p b c -> p (b c)").bitcast(F32)[:, 0:NB * 64].rearrange("p (b c) -> p b c", c=D)
            bufs = [xg[:, :].rearrange("p (b c) -> p b c", c=D), fb["xg2"][:, :].rearrange("p (b c) -> p b c", c=D),
                    hbv(hb[0]), hbv(hb[1])]
            bkeys = [("xg", 0), ("xg", 1), ("hb", 0), ("hb", 1)]
            for i in range(16):
                bf, bk = bufs[i % 4], bkeys[i % 4]
                P.dma("sync", bf[:, 0:nbs, :], srcv[:, i * nbs:(i + 1) * nbs, :], "fs%d" % (i % 4),
                      reads=allX, writes=[bk])
                for q in range(nbs):
                    b = i * nbs + q
                    jk = b % 4
                    P.op("scalar", lambda e, bf=bf, q=q, b=b, jk=jk: e.activation(
                        out=T1f[:, jk * D:(jk + 1) * D], in_=bf[:, q, :], func=AF.Square, accum_out=ssq[:, b:b + 1]),
                        reads=[bk], writes=[("ssq", b), ("jk", jk)])
            P.op("scalar", lambda e: e.activation(out=rtq[:, 0:NB], in_=ssq[:, 0:NB], func=AF.Sqrt,
                                                   bias=self.epsb[:, 0:1], scale=1.0 / D),
                 reads=[("ssq", b) for b in range(NB)] + ["epsb"], writes=["rtq"])
            P.op("vector", lambda e: e.reciprocal(out=rsq[:, 0:NB], in_=rtq[:, 0:NB]), reads=["rtq"], writes=["rsq"])
            hnb = NB // 2
            xgv2 = [xg[:, :].rearrange("p (b c) -> p b c", c=128), fb["xg2"][:, :].rearrange("p (b c) -> p b c", c=128)]
        else:
            npiece = 4 * c.NTP
        def prep(g):
            hbg = hb[g % 2]
            hk = ("hb", g % 2)
            if part == "S":
                for half in range(2):
                    xv = xgv2[half]
                    xk = ("xg", half)
                    P.dma("sync", xv[:, :, :], srcv[:, half * hnb:(half + 1) * hnb, g * 128:(g + 1) * 128],
                          "fx%d" % half, reads=allX, writes=[xk])
                    P.op("vector", lambda e, half=half, xv=xv: e.tensor_tensor(
                        out=xv[:, :, :], in0=xv[:, :, :],
                        in1=rsq[:, half * hnb:(half + 1) * hnb].unsqueeze(2).broadcast_to([128, hnb, 128]),
                        op=ALU.mult), reads=[xk, "rsq"], writes=[xk])
                    P.op("gpsimd", lambda e, half=half, g=g, hbg=hbg, xv=xv: e.tensor_tensor(
                        out=hbg[:, half * hnb:(half + 1) * hnb, :], in0=xv[:, :, :],
                        in1=grep[:, g * 128:(g + 1) * 128].unsqueeze(1).broadcast_to([128, hnb, 128]),
                        op=ALU.mult), reads=[xk, "grep"], writes=[hk])
            else:
                apt, apr = 512 // NB, c.RP // NB
                k = 0
                for r in range(4):
                    for t in range(c.NTP):
                        p0 = r * apr + t * apt
                        P.dma("sync", hbg[p0:p0 + apt, 0:NB, :],
                              self.hall[t][r * 512:(r + 1) * 512, g * 128:(g + 1) * 128].rearrange("(a b) c -> a b c", b=NB),
                              "fq%d" % (k % 4), reads=[("hall", tt) for tt in range(c.NTP)], writes=[(hk, k)])
                        k += 1

        prep(0)
        for g in range(8):
            hbg = hb[g % 2]
            hk = ("hb", g % 2)
            hbreads = [hk] if part == "S" else [(hk, k) for k in range(npiece)]
            for b2 in range(NB // 2):
                bank = b2 % 2
                for q in range(2):
                    b = 2 * b2 + q
                    P.op("tensor", lambda e, b=b, q=q, bank=bank, hbg=hbg: e.matmul(
                        self.ps[bank][:, q * 256:(q + 1) * 256], lhsT=hbg[:, b, :], rhs=A[:, :],
                        start=True, stop=True), reads=hbreads + ["A"], writes=[("ps", bank)])
                dstap = T1[:, 2 * b2:2 * b2 + 2, :]
                srcap = self.ps[bank].rearrange("p (q k) -> p q k", q=2)
                if b2 % 2 == 0:
                    P.op("scalar", lambda e, d=dstap, s_=srcap: e.copy(out=d, in_=s_),
                         reads=[("ps", bank)], writes=["T1"])
                else:
                    P.op("vector", lambda e, d=dstap, s_=srcap: e.tensor_copy(out=d, in_=s_),
                         reads=[("ps", bank)], writes=["T1"])
            if g + 1 < 8:
                prep(g + 1)
            self.conv_some(3, after=[("mixT", part, g - 1)] if g > 0 else [])
            T1v = T1.rearrange("p b (ri k) -> p b ri k", ri=2)
            YTv = YT.rearrange("p (kb ka) -> p ka kb", ka=128)
            def emit_s2(ka2):
                bank = 2 + ka2 % 2
                t2 = T2[ka2 % 2]
                t2k = ("T2", ka2 % 2)
                for q in range(2):
                    ka = 2 * ka2 + q
                    P.op("tensor", lambda e, ka=ka, q=q, bank=bank: e.matmul(
                        self.ps[bank][0:2 * NB, q * 256:(q + 1) * 256], lhsT=T1v[:, :, :, ka], rhs=C[:, :],
                        start=True, stop=True), reads=["T1", "C"], writes=[("ps", bank)])
                if ka2 % 2 == 0:
                    P.op("scalar", lambda e, t2=t2, bank=bank: e.copy(out=t2[0:2 * NB, :], in_=self.ps[bank][0:2 * NB, :]),
                         reads=[("ps", bank)], writes=[t2k])
                else:
                    P.op("vector", lambda e, t2=t2, bank=bank: e.tensor_copy(out=t2[0:2 * NB, :], in_=self.ps[bank][0:2 * NB, :]),
                         reads=[("ps", bank)], writes=[t2k])

            def emit_s3(ka2):
                t2 = T2[ka2 % 2]
                t2k = ("T2", ka2 % 2)
                for q in range(2):
                    ka = 2 * ka2 + q
                    kg = ka // KG
                    bank3 = 4 + kg % 2
                    col = (ka % KG) * KB
                    for w in range(2):
                        P.op("tensor", lambda e, t2=t2, q=q, w=w, ka=ka, bank3=bank3, col=col: e.matmul(
                            self.ps[bank3][:, col:col + KB],
                            lhsT=t2[0:2 * NB, q * 256 + w * 128:q * 256 + (w + 1) * 128],
                            rhs=B[0:2 * NB, w, ka * KB:(ka + 1) * KB],
                            start=(w == 0), stop=(w == 1)),
                            reads=[t2k, Bkey], writes=[("ps", bank3)])
                    if ka % KG == KG - 1:
                        ka0 = ka - KG + 1
                        dstap = YTv[:, ka0:ka0 + KG, :]
                        srcap = self.ps[bank3].rearrange("p (k b) -> p k b", b=KB)
                        if kg % 2 == 0:
                            P.op("vector", lambda e, d=dstap, s_=srcap: e.tensor_copy(out=d, in_=s_),
                                 reads=[("ps", bank3)], writes=["YT"])
                        else:
                            P.op("scalar", lambda e, d=dstap, s_=srcap: e.copy(out=d, in_=s_),
                                 reads=[("ps", bank3)], writes=["YT"])

            emit_s2(0)
            for ka2 in range(64):
                if ka2 + 1 < 64:
                    emit_s2(ka2 + 1)
                emit_s3(ka2)
            P.dma("sync", self.mixT[part][g, :, :], YT, "fy", reads=["YT"], writes=[("mixT", part, g)])

    def fourier_layer(self, l):
        c, P, nc = self.c, self.P, self.nc
        with contextlib.ExitStack() as ls:
            sb = lambda name, shape, dtype: ls.enter_context(nc.sbuf_tensor("f%d_" % l + name, shape, dtype))
            fb = {}
            fb["grep"] = grep = sb("grep", [128, D], F32)
            fb["A"] = A = sb("A", [128, 256], BF16)
            fb["C"] = C = sb("C", [128, 256], BF16)
            fb["BP"] = sb("BP", [128, 2, 128 * c.KBP], BF16)
            fb["BS"] = sb("BS", [128, 2, 128 * c.NBS], BF16)
            fb["xg"] = sb("xg", [128, c.NBS * 64], F32)
            fb["xg2"] = sb("xg2", [128, c.NBS * 64], F32)
            fb["hb"] = [sb("hb%d" % i, [128, max(c.NBP, c.NBS), 128], BF16) for i in range(2)]
            fb["T2"] = [sb("T2_%d" % i, [128, 512], BF16) for i in range(2)]
            fb["ssq"] = sb("ssq", [128, c.NBS], F32)
            fb["rtq"] = sb("rtq", [128, c.NBS], F32)
            fb["rsq"] = sb("rsq", [128, c.NBS], F32)
            mT = sb("mT", [128, 8, 512], BF16)

            P.dma("sync", grep[:, :], self.gmixrep_d[:, l, :], "f0", writes=["grep"])
            P.dma("gpsimd", A[:, :], self.dftA_d[:, :], "f1", writes=["A"])
            P.dma("gpsimd", C[:, :], self.dftC_d[:, :], "f2", writes=["C"])
            k = 0
            for part, nbp, ncol in (("S", c.NBS, 128 * c.NBS), ("P", c.NBP, 128 * c.KBP)):
                ch = min(2048, ncol)
                for w in range(2):
                    for i in range(ncol // ch):
                        P.dma("gpsimd", fb["B" + part][0:2 * nbp, w, i * ch:(i + 1) * ch],
                              self.dftB_d[part][w, :, i * ch:(i + 1) * ch], "fb%d" % (k % 4),
                              writes=["B" + part])
                        k += 1

            hst = mT[:, :, :].rearrange("p g t -> p (g t)").rearrange("p (s c) -> p s c", c=D)
            for t in range(c.NTP):
                b = self.next_xt()
                xt, xkey = self.xt[b], ("xt", b)
                self.load_tile(l, "P", t, b)
                for s in range(4):
                    P.op("scalar", lambda e, s=s, xt=xt: e.activation(out=self.hn[s % 2][:, :], in_=xt[:, s, :], func=AF.Square,
                                                                     accum_out=self.ss[:, s:s + 1]),
                         reads=[(xkey, s)], writes=[("ss", s), ("hn", s % 2)])
                P.op("scalar", lambda e: e.activation(out=self.rt[:, 0:4], in_=self.ss[:, 0:4], func=AF.Sqrt,
                                                       bias=self.epsb[:, 0:1], scale=1.0 / D),
                     reads=[("ss", s) for s in range(4)] + ["epsb"], writes=["rt"])
                P.op("vector", lambda e: e.reciprocal(out=self.rstd[:, 0:4], in_=self.rt[:, 0:4]),
                     reads=["rt"], writes=["rstd"])
                for s in range(4):
                    hb_, hk = self.hn[s % 2], ("hn", s % 2)
                    P.op("vector", lambda e, s=s, hb_=hb_, xt=xt: e.tensor_scalar(
                        out=hb_[:, :], in0=xt[:, s, :], scalar1=self.rstd[:, s:s + 1], scalar2=None, op0=ALU.mult),
                        reads=[(xkey, s), "rstd"], writes=[hk])
                    P.op("vector", lambda e, s=s, hb_=hb_: e.tensor_tensor(
                        out=hst[:, s, :], in0=hb_[:, :], in1=grep[:, :], op=ALU.mult),
                        reads=[hk, "grep"], writes=[("hst", s)])
                P.dma("sync", self.hloc[t].rearrange("(s p) c -> p s c", p=128), hst,
                      "fh", reads=[("hst", s) for s in range(4)], writes=[("hloc", t)])
                _coll(P, self.hloc[t], self.hall[t], "ag", [("hloc", t)], [("hall", t)], c.groups)
            P.barrier(skip=("C_ag",) + tuple("D_cv%d" % i for i in range(4)))

            self.fft_part(l, "S", fb)
            P.barrier(skip=("C_ag",) + tuple("D_cv%d" % i for i in range(4)))
            self.fft_part(l, "P", fb)
            P.barrier(skip=tuple("D_cv%d" % i for i in range(4)))

            self.convert_weights()
            s_wf = c.slot[("wf", l)]
            for part in ("P", "S"):
                mixv = self.mixT[part].rearrange("g k r -> k g r")
                for t in range(self.NT[part]):
                    b = self.next_xt()
                    xt, xkey = self.xt[b], ("xt", b)
                    self.load_tile(l, part, t, b)
                    P.dma("sync", mT[:, :, :], mixv[:, :, t * 512:(t + 1) * 512], "fm",
                          reads=[("mixT", part, g) for g in range(8)], writes=["mT"])
                    ws = [self.ring_load(s_wf + j) for j in range(4)]
                    for s in range(4):
                        for half in range(2):
                            bank = 2 * s + half
                            for g in range(8):
                                w, wk = ws[g // 2]
                                P.op("tensor", lambda e, w=w, g=g, s=s, half=half, bank=bank: e.matmul(
                                    self.ps[bank], lhsT=mT[:, g, s * 128:(s + 1) * 128],
                                    rhs=w[:, (g % 2) * 1024 + half * 512:(g % 2) * 1024 + (half + 1) * 512],
                                    start=(g == 0), stop=(g == 7)),
                                    reads=[wk, "mT"], writes=[("ps", bank)])
                    self.add_psum_to_xt(xt, xkey)
                    self.ffn_tile(l, xt, xkey)
                    self.store_tile(l, part, t, b)
    def qk_norm_rope(self, tmp, nh, pss, gain, gkey, cs, cskey, s):
        P = self.P
        qn, qr = tmp["qn"], tmp["qr"]
        t1, t2, t3, t4 = tmp["t"]
        h0 = 0
        for (psap, bank, n) in pss:
            for i in range(n):
                P.op("scalar", lambda e, psap=psap, i=i, h0=h0: e.activation(
                    out=self.junk[:, (h0 + i) * 128:(h0 + i + 1) * 128], in_=psap[:, i * 128:(i + 1) * 128],
                    func=AF.Square, accum_out=self.ss[:, h0 + i:h0 + i + 1]),
                    reads=[("ps", bank)], writes=[("ssh", h0 + i), ("junk", h0 + i)])
            h0 += n
        P.op("scalar", lambda e: e.activation(out=self.rt[:, 0:nh], in_=self.ss[:, 0:nh], func=AF.Sqrt,
                                               bias=self.epsb[:, 0:1], scale=1.0 / HD),
             reads=[("ssh", i) for i in range(nh)] + ["epsb"], writes=["rth"])
        P.op("vector", lambda e: e.reciprocal(out=self.rstd[:, 0:nh], in_=self.rt[:, 0:nh]),
             reads=["rth"], writes=["rstdh"])
        h0 = 0
        for (psap, bank, n) in pss:
            P.op("vector", lambda e, psap=psap, n=n, h0=h0: e.tensor_tensor(
                out=qn[:, h0:h0 + n, :], in0=psap.rearrange("p (h d) -> p h d", d=128),
                in1=self.rstd[:, h0:h0 + n].unsqueeze(2).broadcast_to([128, n, 128]), op=ALU.mult),
                reads=[("ps", bank), "rstdh"], writes=["qn"] + tmp.get("fence", []))
            h0 += n
        P.op("gpsimd", lambda e: e.tensor_tensor(
            out=qn[:, 0:nh, :], in0=qn[:, 0:nh, :],
            in1=gain.unsqueeze(1).broadcast_to([128, nh, 128]), op=ALU.mult),
            reads=["qn", gkey], writes=["qn"])
        qv = qn[:, 0:nh, :].rearrange("p h (i two) -> p h i two", two=2)
        rv = qr[:, 0:nh, :].rearrange("p h (i two) -> p h i two", two=2)
        x0, x1 = qv[:, :, :, 0], qv[:, :, :, 1]
        cc = cs[:, s, 0:64].unsqueeze(1).broadcast_to([128, nh, 64])
        sn = cs[:, s, 64:128].unsqueeze(1).broadcast_to([128, nh, 64])
        P.op("gpsimd", lambda e: e.tensor_tensor(out=t1[:, 0:nh, :], in0=x0, in1=cc, op=ALU.mult),
             reads=["qn", cskey], writes=["t1"])
        P.op("gpsimd", lambda e: e.tensor_tensor(out=t2[:, 0:nh, :], in0=x1, in1=sn, op=ALU.mult),
             reads=["qn", cskey], writes=["t2"])
        P.op("gpsimd", lambda e: e.tensor_tensor(out=rv[:, :, :, 0], in0=t1[:, 0:nh, :], in1=t2[:, 0:nh, :],
                                                 op=ALU.subtract), reads=["t1", "t2"], writes=["qr0"])
        P.op("vector", lambda e: e.tensor_tensor(out=t3[:, 0:nh, :], in0=x0, in1=sn, op=ALU.mult),
             reads=["qn", cskey], writes=["t3"])
        P.op("vector", lambda e: e.tensor_tensor(out=t4[:, 0:nh, :], in0=x1, in1=cc, op=ALU.mult),
             reads=["qn", cskey], writes=["t4"])
        P.op("vector", lambda e: e.tensor_tensor(out=rv[:, :, :, 1], in0=t3[:, 0:nh, :], in1=t4[:, 0:nh, :],
                                                 op=ALU.add), reads=["t3", "t4"], writes=["qr1"])


    def attn_layer(self, l):
        c, P, nc = self.c, self.P, self.nc
        scale = float(HD) ** -0.5
        NBmax = max(c.NBP, c.NBS)
        with contextlib.ExitStack() as ls:
            sb = lambda name, shape, dtype: ls.enter_context(nc.sbuf_tensor("a%d_" % l + name, shape, dtype))
            KT = sb("KT", [128, 2, NBmax * 128], BF16)
            Vb = sb("Vb", [128, NBmax, 256], BF16)
            QT = sb("QT", [128, 8, 512], BF16)
            OT = sb("OT", [128, 8, 512], BF16)
            NPT = 4
            pT = [sb("pT%d" % i, [128, 1024], BF16) for i in range(NPT)]
            s2 = [sb("s2_%d" % i, [128, 512], BF16) for i in range(NPT)]
            rinv = sb("rinv", [128, 512], F32)
            ou = sb("ou", [128, 512], F32)
            cs = sb("cs", [128, 4, 128], F32)
            qkg = sb("qkg", [128, 2, 128], F32)
            flat = self.act[:, :, :].rearrange("p f t -> p (f t)").bitcast(F32)
            assert c.FC * 256 >= 3584
            tmp = {
                "fence": [("act", fc) for fc in range(c.FC)],
                "qn": flat[:, 0:1024].rearrange("p (h d) -> p h d", d=128),
                "t": [flat[:, 1024 + 512 * i:1024 + 512 * (i + 1)].rearrange("p (h d) -> p h d", d=64) for i in range(4)],
                "qr": flat[:, 3072:3584].bitcast(BF16).rearrange("p (h d) -> p h d", d=128),
            }
            P.dma("sync", qkg[:, :, :], self.qkg_d[:, l, :, :], "a0", writes=["qkg"])
            self.convert_weights()
            s_q, s_kv, s_o = c.slot[("wq", l)], c.slot[("wkv", l)], c.slot[("wo", l)]

            def pass1(part):
                for ta in range(self.NT[part]):
                    b = self.next_xt()
                    xt, xkey = self.xt[b], ("xt", b)
                    self.load_tile(l, part, ta, b)
                    P.dma("sync", cs[:, :, :], self.cs_d[part][ta], "a1", writes=["cs"])
                    self.norm_to_hT(xt, xkey, self.gmixT, "gmixT", l * 8)
                    ws = [self.ring_load(s_kv + j) for j in range(2)]
                    def mm1(s, ws=ws):
                        bank = 4 + s % 2
                        for kc in range(KC):
                            w, wk = ws[kc // 4]
                            P.op("tensor", lambda e, w=w, kc=kc, s=s, bank=bank: e.matmul(
                                self.ps[bank], lhsT=self.hT[:, kc, s * 128:(s + 1) * 128],
                                rhs=w[:, (kc % 4) * 512:(kc % 4 + 1) * 512],
                                start=(kc == 0), stop=(kc == KC - 1)),
                                reads=[wk, ("hT", s)], writes=[("ps", bank)])
                    mm1(0)
                    for s in range(4):
                        sa = 4 * ta + s
                        bank = 4 + s % 2
                        if s + 1 < 4:
                            mm1(s + 1)
                        P.op("scalar", lambda e, sa=sa, bank=bank: e.copy(out=Vb[:, sa, :], in_=self.ps[bank][:, 256:512]),
                             reads=[("ps", bank)], writes=[("V", sa)])
                        self.qk_norm_rope(tmp, 2, [(self.ps[bank][:, 0:256], bank, 2)], qkg[:, 1, :], "qkg", cs, "cs", s)
                        pb = self.pp[3][:, (s % 2) * 512:(s % 2 + 1) * 512].bitcast(BF16)
                        for kvh in range(2):
                            P.op("tensor", lambda e, kvh=kvh, pb=pb: e.transpose(
                                out=pb[:, kvh * 128:(kvh + 1) * 128], in_=tmp["qr"][:, kvh, :], identity=self.identb[:, :]),
                                reads=["qr0", "qr1", "identb"], writes=[("ps", 6 + s % 2)])
                        P.op("scalar", lambda e, sa=sa, pb=pb: e.copy(
                            out=KT[:, :, sa * 128:(sa + 1) * 128], in_=pb[:, 0:256].rearrange("p (h t) -> p h t", h=2)),
                            reads=[("ps", 6 + s % 2)], writes=[("K", sa)])

            def pass2(part, nbk):
                allK = [("K", sa) for sa in range(nbk)]
                allV = [("V", sa) for sa in range(nbk)]
                for ta in range(self.NT[part]):
                    b = self.next_xt()
                    xt, xkey = self.xt[b], ("xt", b)
                    self.load_tile(l, part, ta, b)
                    P.dma("sync", cs[:, :, :], self.cs_d[part][ta], "a1", writes=["cs"])
                    self.norm_to_hT(xt, xkey, self.gmixT, "gmixT", l * 8)
                    ws = [self.ring_load(s_q + j) for j in range(4)]
                    def mm2(s, ws=ws):
                        banks = (4, 5) if s % 2 == 0 else (2, 3)
                        for half in range(2):
                            for kc in range(KC):
                                w, wk = ws[kc // 2]
                                P.op("tensor", lambda e, w=w, kc=kc, s=s, half=half, banks=banks: e.matmul(
                                    self.ps[banks[half]], lhsT=self.hT[:, kc, s * 128:(s + 1) * 128],
                                    rhs=w[:, (kc % 2) * 1024 + half * 512:(kc % 2) * 1024 + (half + 1) * 512],
                                    start=(kc == 0), stop=(kc == KC - 1)),
                                    reads=[wk, ("hT", s)], writes=[("ps", banks[half])])
                    mm2(0)
                    for s in range(4):
                        banks = (4, 5) if s % 2 == 0 else (2, 3)
                        if s + 1 < 4:
                            mm2(s + 1)
                        self.qk_norm_rope(tmp, 8, [(self.ps[banks[0]], banks[0], 4), (self.ps[banks[1]], banks[1], 4)],
                                          qkg[:, 0, :], "qkg", cs, "cs", s)
                        pb = self.pp[3][:, (s % 2) * 512:(s % 2 + 1) * 512].bitcast(BF16)
                        for hd in range(8):
                            P.op("tensor", lambda e, hd=hd, pb=pb: e.transpose(
                                out=pb[:, hd * 128:(hd + 1) * 128], in_=tmp["qr"][:, hd, :], identity=self.identb[:, :]),
                                reads=["qr0", "qr1", "identb"], writes=[("ps", 6 + s % 2)])
                        P.op("vector", lambda e, s=s, pb=pb: e.tensor_copy(
                            out=QT[:, :, s * 128:(s + 1) * 128], in_=pb[:, :].rearrange("p (h t) -> p h t", h=8)),
                            reads=[("ps", 6 + s % 2)], writes=["QT"])
                    npair = nbk // 2
                    items = [(hd, kp) for hd in range(8) for kp in range(npair)]

                    def emit_score(i):
                        hd, kp = items[i]
                        kvh = hd // 4
                        j = i % 3
                        for q in range(2):
                            kt = 2 * kp + q
                            P.op("tensor", lambda e, hd=hd, kt=kt, kvh=kvh, j=j, q=q: e.matmul(
                                self.ps[2 * j + q], lhsT=KT[:, kvh, kt * 128:(kt + 1) * 128], rhs=QT[:, hd, :],
                                start=True, stop=True), reads=["QT"] + allK, writes=[("ps", 2 * j + q)])
                        jb = i % NPT
                        P.op("scalar", lambda e, j=j, jb=jb: e.activation(
                            out=pT[jb][:, :], in_=self.pp[j][:, :], func=AF.Exp, scale=scale),
                            reads=[("ps", 2 * j), ("ps", 2 * j + 1)], writes=[("pT", jb)])

                    emit_score(0)
                    emit_score(1)
                    pending = []
                    for i, (hd, kp) in enumerate(items):
                        if i + 2 < len(items):
                            emit_score(i + 2)
                        kvh = hd // 4
                        jb = i % NPT
                        par = hd % 2
                        po, pr = 6, 7
                        for q in range(2):
                            kt = 2 * kp + q
                            P.op("tensor", lambda e, kt=kt, kvh=kvh, jb=jb, po=po, q=q: e.matmul(
                                self.ps[po], lhsT=Vb[:, kt, kvh * 128:(kvh + 1) * 128],
                                rhs=pT[jb][:, q * 512:(q + 1) * 512],
                                start=(kt == 0), stop=(kt == nbk - 1)),
                                reads=[("pT", jb)] + allV, writes=[("ps", po)])
                        GS = 4 if npair % 4 == 0 else 2
                        sj = (i // GS) % NPT
                        if kp % GS == 0:
                            P.op("vector", lambda e, jb=jb, sj=sj: e.tensor_tensor(
                                out=s2[sj][:, :], in0=pT[jb][:, 0:512], in1=pT[jb][:, 512:1024], op=ALU.add),
                                reads=[("pT", jb)], writes=[("s2", sj)])
                        else:
                            P.op("vector", lambda e, jb=jb: e.tensor_tensor(
                                out=rinv[:, :].bitcast(BF16)[:, 0:512], in0=pT[jb][:, 0:512], in1=pT[jb][:, 512:1024], op=ALU.add),
                                reads=[("pT", jb)], writes=["s2tmp", "rinv"])
                            P.op("vector", lambda e, sj=sj: e.tensor_tensor(
                                out=s2[sj][:, :], in0=s2[sj][:, :], in1=rinv[:, :].bitcast(BF16)[:, 0:512], op=ALU.add),
                                reads=["s2tmp", ("s2", sj)], writes=[("s2", sj)])

                            def emit_rs(sj=sj, pr=pr, kp=kp, GS=GS):
                                P.op("tensor", lambda e: e.matmul(
                                    self.ps[pr], lhsT=self.onesb[:, :], rhs=s2[sj][:, :],
                                    start=(kp == GS - 1), stop=(kp == npair - 1)),
                                    reads=[("s2", sj), "onesb"], writes=[("ps", pr)])
                            if kp % GS == GS - 1:
                                if pending:
                                    pending.pop()()
                                pending.append(emit_rs)
                        if kp == npair - 1:
                            while pending:
                                pending.pop()()
                            P.op("vector", lambda e, po=po: e.tensor_copy(out=ou[:, :], in_=self.ps[po]),
                                 reads=[("ps", po)], writes=["ou"])
                            P.op("vector", lambda e, pr=pr: e.reciprocal(out=rinv[:, :], in_=self.ps[pr]),
                                 reads=[("ps", pr), "s2tmp"], writes=["rinv", "s2tmp"])
                            P.op("vector", lambda e, hd=hd: e.tensor_tensor(
                                out=OT[:, hd, :], in0=ou[:, :], in1=rinv[:, :], op=ALU.mult),
                                reads=["ou", "rinv"], writes=[("OT", hd)])
                    ws = [self.ring_load(s_o + j) for j in range(4)]
                    for s in range(4):
                        for half in range(2):
                            bank = 2 * s + half
                            for hd in range(8):
                                w, wk = ws[hd // 2]
                                P.op("tensor", lambda e, w=w, hd=hd, s=s, half=half, bank=bank: e.matmul(
                                    self.ps[bank], lhsT=OT[:, hd, s * 128:(s + 1) * 128],
                                    rhs=w[:, (hd % 2) * 1024 + half * 512:(hd % 2) * 1024 + (half + 1) * 512],
                                    start=(hd == 0), stop=(hd == 7)),
                                    reads=[wk, ("OT", hd)], writes=[("ps", bank)])
                    self.add_psum_to_xt(xt, xkey)
                    self.ffn_tile(l, xt, xkey)
                    self.store_tile(l, part, ta, b)

            nloc = c.RP // 128
            pass1("P")
            P.dma("sync", self.kloc.rearrange("(h d) k -> d h k", h=2), KT[:, :, 0:c.RP], "ak",
                  reads=[("K", sa) for sa in range(nloc)], writes=["kloc"])
            P.dma("sync", self.vloc.rearrange("(s p) c -> p s c", p=128), Vb[:, 0:nloc, :], "av",
                  reads=[("V", sa) for sa in range(nloc)], writes=["vloc"])
            _coll(P, self.kloc, self.kall, "agk", ["kloc"], ["kall"], c.groups)
            _coll(P, self.vloc, self.vall, "agv", ["vloc"], ["vall"], c.groups)
            pass1("S")
            pass2("S", c.NBS)
            kallv = self.kall.rearrange("(r h d) k -> h d r k", r=4, h=2)
            for h in range(2):
                P.dma("sync", KT[:, h, 0:4 * c.RP].rearrange("d (r k) -> d r k", r=4), kallv[h], "ak%d" % h,
                      reads=["kall"], writes=[("K", sa) for sa in range(c.NBP)])
            P.dma("sync", Vb[:, 0:c.NBP, :], self.vall.rearrange("(s p) c -> p s c", p=128), "av2",
                  reads=["vall"], writes=[("V", sa) for sa in range(c.NBP)])
            pass2("P", c.NBP)


def _rope_rows(pos):
    inv = (np.float32(10000.0) ** (-np.arange(0, 64, 2, dtype=np.float32) / np.float32(64))).astype(np.float32)
    rowp = (pos // 64).astype(np.float32)
    colp = (pos % 64).astype(np.float32)
    ang = np.concatenate([rowp[:, None] * inv, colp[:, None] * inv], -1).astype(np.float32)
    return np.concatenate([np.cos(ang), np.sin(ang)], -1).astype(np.float32)


def _cs_tiles(pos):
    cs = _rope_rows(pos)
    nt = cs.shape[0] // 512
    return np.ascontiguousarray(cs.reshape(nt, 4, 128, 128).transpose(0, 2, 1, 3))


def _dft_common():
    a = np.arange(128)
    A = np.exp(-2j * np.pi * np.outer(a, a) / 128.0)
    m = np.concatenate([A.real, A.imag], 1).astype(np.float32)
    return m


def _dft_B(NB, R, kbs):
    b = np.arange(NB).astype(np.float64)
    ka = np.arange(128).astype(np.float64)
    kb = np.asarray(kbs).astype(np.float64)
    M = (np.exp(-2j * np.pi * b[:, None, None] * kb[None, None, :] / NB)
         * np.exp(-2j * np.pi * b[:, None, None] * ka[None, :, None] / float(R))) / np.sqrt(R * 128.0)
    n = 128 * len(kbs)
    Mr, Mi = M.real.reshape(NB, n), M.imag.reshape(NB, n)
    out = np.zeros((2, 2 * NB, n), np.float32)
    out[0, 0::2] = Mr
    out[0, 1::2] = -Mi
    out[1, 0::2] = -Mi
    out[1, 1::2] = -Mr
    return out


def _pack_weights(cfg, fourier_w, attn_w_qkv, attn_w_o, ffn_w_gate, ffn_w_up, ffn_w_down):
    wall = np.zeros((cfg.nslots, 128, 2048), np.float32)
    jF = jA = 0
    for l, t in enumerate(cfg.layers):
        if t == "F":
            w = np.asarray(fourier_w[jF], np.float32)
            s0 = cfg.slot[("wf", l)]
            wall[s0:s0 + 4] = w.reshape(4, 2, 128, D).transpose(0, 2, 1, 3).reshape(4, 128, 2048)
            jF += 1
        else:
            wqkv = np.asarray(attn_w_qkv[jA], np.float32)
            s0 = cfg.slot[("wq", l)]
            wall[s0:s0 + 4] = wqkv[:, 0:1024].reshape(4, 2, 128, 1024).transpose(0, 2, 1, 3).reshape(4, 128, 2048)
            s0 = cfg.slot[("wkv", l)]
            wall[s0:s0 + 2] = wqkv[:, 1024:1536].reshape(2, 4, 128, 512).transpose(0, 2, 1, 3).reshape(2, 128, 2048)
            s0 = cfg.slot[("wo", l)]
            wo = np.asarray(attn_w_o[jA], np.float32)
            wall[s0:s0 + 4] = wo.reshape(4, 2, 128, D).transpose(0, 2, 1, 3).reshape(4, 128, 2048)
            jA += 1
        FC = cfg.FC
        wg = np.asarray(ffn_w_gate[l], np.float32).reshape(KC, 128, FC, 128)
        wu = np.asarray(ffn_w_up[l], np.float32).reshape(KC, 128, FC, 128)
        gu = np.stack([wg, wu], 0)
        s0 = cfg.slot[("wgu", l)]
        wall[s0:s0 + FC] = gu.transpose(3, 2, 0, 1, 4).reshape(FC, 128, 2048)
        wd = np.asarray(ffn_w_down[l], np.float32)
        s0 = cfg.slot[("wd", l)]
        wall[s0:s0 + FC // 2] = wd.reshape(FC // 2, 2, 128, D).transpose(0, 2, 1, 3).reshape(FC // 2, 128, 2048)
    return wall.reshape(cfg.nslots * 128, 2048)


def _common_inputs(cfg, norm_mix, norm_ffn, attn_q_norm, attn_k_norm):
    L = cfg.L
    nm = np.asarray(norm_mix, np.float32)[:L]
    nf = np.asarray(norm_ffn, np.float32)[:L]
    gmixT = nm.reshape(L, KC, 128).transpose(2, 0, 1).reshape(128, L * KC).copy()
    gffnT = nf.reshape(L, KC, 128).transpose(2, 0, 1).reshape(128, L * KC).copy()
    gmixrep = np.broadcast_to(nm[None], (128, L, D)).copy()
    qkg = np.ones((128, L, 2, 128), np.float32)
    jA = 0
    for l, t in enumerate(cfg.layers):
        if t == "A":
            qkg[:, l, 0, :] = np.asarray(attn_q_norm[jA], np.float32)[None]
            qkg[:, l, 1, :] = np.asarray(attn_k_norm[jA], np.float32)[None]
            jA += 1
    dft = _dft_common()
    return dict(gmixT=gmixT, gffnT=gffnT, gmixrep=gmixrep, qkg=qkg, ident=np.eye(128, dtype=np.float32),
                dftA=dft, dftC=dft.copy(),
                dftBS=_dft_B(cfg.NBS, cfg.RS, np.arange(cfg.NBS)),
                csS=_cs_tiles(np.arange(cfg.RS)))


_NC_CACHE = {}


def run_cores(cfg, xp_list, xs_list, weights):
    key = (cfg.RP, cfg.RS, cfg.DFF, cfg.layers, cfg.ring)
    if key not in _NC_CACHE:
        _NC_CACHE[key] = Builder(cfg).build()
    nc = _NC_CACHE[key]
    in_maps = []
    for c in range(8):
        r = c % 4
        m = dict(xp=np.ascontiguousarray(xp_list[c], np.float32), xs=np.ascontiguousarray(xs_list[c], np.float32),
                 wall=weights["wall"])
        m.update(weights["common"])
        m["csP"] = _cs_tiles(cfg.RP * r + np.arange(cfg.RP))
        m["dftBP"] = _dft_B(cfg.NBP, 4 * cfg.RP, cfg.KBP * r + np.arange(cfg.KBP))
        in_maps.append(m)
    res = run_bass_kernel_spmd(nc, in_maps, core_ids=list(range(8)))
    return [(r["yp"], r["ys"]) for r in res.results]


def kernel(x_prompt, x_sample, norm_mix, norm_ffn, fourier_w, attn_w_qkv, attn_q_norm, attn_k_norm,
           attn_w_o, ffn_w_gate, ffn_w_up, ffn_w_down):
    cfg = Cfg()
    xp = np.asarray(x_prompt, np.float32)
    xs = np.asarray(x_sample, np.float32)
    weights = dict(
        wall=_pack_weights(cfg, fourier_w, attn_w_qkv, attn_w_o, ffn_w_gate, ffn_w_up, ffn_w_down),
        common=_common_inputs(cfg, norm_mix, norm_ffn, attn_q_norm, attn_k_norm))
    RP = cfg.RP
    xp_list = [xp[c // 4, RP * (c % 4):RP * (c % 4 + 1)] for c in range(8)]
    xs_list = [xs[c] for c in range(8)]
    outs = run_cores(cfg, xp_list, xs_list, weights)
    y_prompt = np.zeros_like(xp)
    y_sample = np.zeros_like(xs)
    for c in range(8):
        y_prompt[c // 4, RP * (c % 4):RP * (c % 4 + 1)] = outs[c][0]
        y_sample[c] = outs[c][1]
    return (y_prompt, y_sample)
```

```python
import bisect
import contextlib
import numpy as np
import ml_dtypes
import concourse.bass as bass
import concourse.mybir as mybir
from concourse.bass_utils import run_bass_kernel_spmd

F32 = mybir.dt.float32
BF16 = mybir.dt.bfloat16
AF = mybir.ActivationFunctionType
ALU = mybir.AluOpType
AX = mybir.AxisListType

D = 1024
KC = 8
HD = 128
NH = 8
NKV = 2
EPS = 1e-6
NEG = -30000.0


class Prog:
    ENGS = ("sync", "scalar", "vector", "gpsimd", "tensor")

    def __init__(self, nc, stack):
        self.nc = nc
        self.stack = stack
        self.ops = {e: [] for e in self.ENGS}
        self.marked = {e: [] for e in self.ENGS}
        self.cnt = {e: 0 for e in self.ENGS}
        self.seen = {e: {} for e in self.ENGS}
        self.sems = {}
        self.dmacnt = {}
        self.last_write = {}
        self.readers = {}

    def sem(self, name):
        if name not in self.sems:
            self.sems[name] = self.stack.enter_context(self.nc.semaphore(name))
        return self.sems[name]

    def _ticket(self, ref):
        if ref[0] == "dma":
            return (ref[1], ref[2])
        eng, pos = ref
        m = self.marked[eng]
        i = bisect.bisect_left(m, pos)
        if i < len(m):
            p = m[i]
        else:
            p = pos
            self.cnt[eng] += 1
            self.ops[eng][p][1] = self.cnt[eng]
            m.append(p)
        return ("E_" + eng, self.ops[eng][p][1])

    def _wait(self, eng, ticket):
        name, val = ticket
        if self.seen[eng].get(name, 0) >= val:
            return
        self.seen[eng][name] = val
        semh = self.sem(name)
        self.ops[eng].append([lambda e, s=semh, v=val: e.wait_ge(s, v), None, True])

    def _deps(self, eng, reads, writes):
        refs = []
        for r in reads:
            w = self.last_write.get(r)
            if w is not None:
                refs.append(w)
        for w_ in writes:
            w = self.last_write.get(w_)
            if w is not None:
                refs.append(w)
            refs.extend(self.readers.get(w_, {}).values())
        return [r for r in refs if not (r[0] == "tensor" and eng == "tensor")]

    def _update(self, ref, rkey, reads, writes):
        for r in reads:
            self.readers.setdefault(r, {})[rkey] = ref
        for w in writes:
            self.last_write[w] = ref
            self.readers[w] = {}

    def op(self, eng, fn, reads=(), writes=()):
        for ref in self._deps(eng, reads, writes):
            self._wait(eng, self._ticket(ref))
        self.ops[eng].append([fn, None, False])
        ref = (eng, len(self.ops[eng]) - 1)
        self._update(ref, eng, reads, writes)
        return ref

    def dma(self, q, out, in_, semkey, reads=(), writes=(), **kw):
        for ref in self._deps("dma:" + q, reads, writes):
            self._wait(q, self._ticket(ref))
        name = "D_" + semkey
        self.dmacnt[name] = self.dmacnt.get(name, 0) + 16
        val = self.dmacnt[name]
        semh = self.sem(name)
        self.ops[q].append([lambda e, o=out, i=in_, s=semh, k=kw: e.dma_start(out=o, in_=i, **k).then_inc(s, 16),
                            None, True])
        ref = ("dma", name, val)
        self._update(ref, name, reads, writes)
        return ref

    def barrier(self, skip=()):
        tickets = []
        for e in self.ENGS:
            pos = len(self.ops[e]) - 1
            while pos >= 0 and self.ops[e][pos][2]:
                pos -= 1
            if pos >= 0:
                tickets.append(self._ticket((e, pos)))
        for name, val in self.dmacnt.items():
            if name not in skip:
                tickets.append((name, val))
        for e in self.ENGS:
            for t in tickets:
                self._wait(e, t)

    def replay(self, eng, e):
        semh = self.sem("E_" + eng) if self.cnt[eng] else None
        for o in self.ops[eng]:
            ins = o[0](e)
            if o[1] is not None:
                ins.then_inc(semh, 1)


def _coll(P, ins_ap, outs_ap, semkey, reads, writes, groups):
    for ref in P._deps("dma:gpsimd", reads, writes):
        P._wait("gpsimd", P._ticket(ref))
    name = "C_" + semkey
    P.dmacnt[name] = P.dmacnt.get(name, 0) + 1
    val = P.dmacnt[name]
    semh = P.sem(name)
    P.ops["gpsimd"].append([lambda e: e.collective_compute(
        "AllGather", ALU.bypass, replica_groups=groups, ins=[ins_ap.opt()], outs=[outs_ap.opt()]).then_inc(semh),
        None, True])
    ref = ("dma", name, val)
    P._update(ref, name, reads, writes)
    return ref


class Cfg:
    def __init__(self, RP=2048, RS=4096, DFF=2816, layers=("F", "A", "F", "A"), ring=6):
        self.RP, self.RS = RP, RS
        self.NTP, self.NTS = RP // 512, RS // 512
        self.NBP = 4 * RP // 128
        self.KBP = self.NBP // 4
        self.NBS = RS // 128
        assert self.NBS % 16 == 0 and self.NBP % 16 == 0
        self.DFF = DFF
        self.FC = DFF // 128
        assert self.FC % 2 == 0
        self.layers = tuple(layers)
        self.L = len(layers)
        self.ring = ring
        self.slot = {}
        n = 0
        for l, t in enumerate(self.layers):
            if t == "F":
                self.slot[("wf", l)] = n; n += 4
            else:
                self.slot[("wq", l)] = n; n += 4
                self.slot[("wkv", l)] = n; n += 2
                self.slot[("wo", l)] = n; n += 4
            self.slot[("wgu", l)] = n; n += self.FC
            self.slot[("wd", l)] = n; n += self.FC // 2
        self.nslots = n
        self.groups = [[0, 1, 2, 3], [4, 5, 6, 7]]


class Builder:
    def __init__(self, cfg):
        self.c = cfg
        self.nc = bass.Bass("TRN2", target_bir_lowering=False)

    def build(self):
        c, nc = self.c, self.nc
        RP, RS = c.RP, c.RS
        dt = nc.dram_tensor
        ein = lambda name, shape, dtype=F32: dt(name, shape, dtype, kind="ExternalInput").ap()
        itn = lambda name, shape, dtype: dt(name, shape, dtype).ap()
        self.xin = {"P": ein("xp", [RP, D]), "S": ein("xs", [RS, D])}
        self.yout = {"P": dt("yp", [RP, D], F32, kind="ExternalOutput").ap(),
                     "S": dt("ys", [RS, D], F32, kind="ExternalOutput").ap()}
        self.xscr = {"P": itn("xscrp", [RP, D], F32), "S": itn("xscrs", [RS, D], F32)}
        self.wall = ein("wall", [c.nslots * 128, 2048])
        self.gmixT_d = ein("gmixT", [128, c.L * 8])
        self.gffnT_d = ein("gffnT", [128, c.L * 8])
        self.gmixrep_d = ein("gmixrep", [128, c.L, D])
        self.qkg_d = ein("qkg", [128, c.L, 2, 128])
        self.cs_d = {"P": ein("csP", [c.NTP, 128, 4, 128]), "S": ein("csS", [c.NTS, 128, 4, 128])}
        self.dftA_d = ein("dftA", [128, 256])
        self.dftC_d = ein("dftC", [128, 256])
        self.dftB_d = {"P": ein("dftBP", [2, 2 * c.NBP, 128 * c.KBP]), "S": ein("dftBS", [2, 2 * c.NBS, 128 * c.NBS])}
        self.ident_d = ein("ident", [128, 128])
        self.wbf = itn("wbf", [c.nslots * 128, 2048], BF16)
        self.mixT = {"P": itn("mixTP", [8, 128, RP], BF16), "S": itn("mixTS", [8, 128, RS], BF16)}
        self.hloc = [itn("hloc%d" % t, [512, D], BF16) for t in range(c.NTP)]
        self.hall = [itn("hall%d" % t, [4 * 512, D], BF16) for t in range(c.NTP)]
        self.kloc = itn("kloc", [256, RP], BF16)
        self.kall = itn("kall", [1024, RP], BF16)
        self.vloc = itn("vloc", [RP, 256], BF16)
        self.vall = itn("vall", [4 * RP, 256], BF16)
        self.NT = {"P": c.NTP, "S": c.NTS}

        with contextlib.ExitStack() as st:
            self.st = st
            self.P = P = Prog(nc, st)
            sb = lambda name, shape, dtype: st.enter_context(nc.sbuf_tensor("s_" + name, shape, dtype))
            self.pp = [st.enter_context(nc.psum_tensor("pp%d" % i, [128, 1024], F32)) for i in range(4)]
            self.ps = [self.pp[i // 2][:, (i % 2) * 512:(i % 2 + 1) * 512] for i in range(8)]
            self.ident = sb("ident", [128, 128], F32)
            self.identb = sb("identb", [128, 128], BF16)
            self.onesb = sb("onesb", [128, 128], BF16)
            self.gmixT = sb("gmixT", [128, c.L * 8], F32)
            self.gffnT = sb("gffnT", [128, c.L * 8], F32)
            self.xtall = sb("xtall", [128, 8 * D], F32)
            self.xt = [self.xtall[:, i * 4 * D:(i + 1) * 4 * D].rearrange("p (s c) -> p s c", c=D) for i in range(2)]
            self.hn = [sb("hn%d" % i, [128, D], F32) for i in range(2)]
            self.hT = sb("hT", [128, KC, 512], BF16)
            self.act = sb("act", [128, c.FC, 512], BF16)
            self.sg = [sb("sg%d" % i, [128, 512], F32) for i in range(2)]
            self.ringb = [sb("ring%d" % i, [128, 2048], BF16) for i in range(c.ring)]
            self.junk = sb("junk", [128, D], BF16)
            self.ss = sb("ss", [128, 8], F32)
            self.rt = sb("rt", [128, 8], F32)
            self.rstd = sb("rstd", [128, 8], F32)
            self.epsb = sb("epsb", [128, 1], F32)
            self.ring_pos = 0
            self.conv_pos = 0
            self.conv_i = 0
            self.tilecnt = 0

            self.load_consts()
            for l, t in enumerate(c.layers):
                if t == "F":
                    self.fourier_layer(l)
                else:
                    self.attn_layer(l)
                P.barrier()
            P.barrier()

            with nc.Block() as block:
                @block.sync
                def _(e):
                    P.replay("sync", e)

                @block.scalar
                def _(e):
                    P.replay("scalar", e)

                @block.vector
                def _(e):
                    P.replay("vector", e)

                @block.gpsimd
                def _(e):
                    P.replay("gpsimd", e)

                @block.tensor
                def _(e):
                    P.replay("tensor", e)
        return nc

    def src(self, l, part):
        return self.xin[part] if l == 0 else self.xscr[part]

    def dst(self, l, part):
        return self.yout[part] if l == self.c.L - 1 else self.xscr[part]

    def load_consts(self):
        P = self.P
        P.dma("sync", self.ident[:, :], self.ident_d[:, :], "c0", writes=["ident"])
        P.dma("sync", self.gmixT[:, :], self.gmixT_d[:, :], "c1", writes=["gmixT"])
        P.dma("sync", self.gffnT[:, :], self.gffnT_d[:, :], "c2", writes=["gffnT"])
        P.op("vector", lambda e: e.tensor_copy(out=self.identb[:, :], in_=self.ident[:, :]),
             reads=["ident"], writes=["identb"])
        P.op("vector", lambda e: e.memset(self.onesb[:, :], 1.0), writes=["onesb"])
        P.op("vector", lambda e: e.memset(self.epsb[:, :], EPS), writes=["epsb"])

    def conv_some(self, k, after=()):
        c, P = self.c, self.P
        while k > 0 and self.conv_pos < c.nslots:
            s = self.conv_pos
            n = min(4, c.nslots - s)
            P.dma("gpsimd", self.wbf[s * 128:(s + n) * 128, :], self.wall[s * 128:(s + n) * 128, :],
                  "cv%d" % (self.conv_i % 4), reads=list(after), writes=[("wbf", j) for j in range(s, s + n)])
            self.conv_pos += n
            self.conv_i += 1
            k -= 1

    def conv_until(self, slot_end):
        while self.conv_pos < min(slot_end, self.c.nslots):
            self.conv_some(1)

    def convert_weights(self):
        self.conv_until(self.c.nslots)

    def next_xt(self):
        b = self.tilecnt % 2
        self.tilecnt += 1
        return b

    def load_tile(self, l, part, t, b):
        src = self.src(l, part)
        self.P.dma("sync", self.xt[b][:, :, :],
                   src[t * 512:(t + 1) * 512, :].rearrange("(s p) c -> p s c", p=128),
                   "xt%d" % b, reads=[("X", part, l, t)], writes=[(("xt", b), s) for s in range(4)])

    def store_tile(self, l, part, t, b):
        dst = self.dst(l, part)
        self.P.dma("scalar", dst[t * 512:(t + 1) * 512, :].rearrange("(s p) c -> p s c", p=128),
                   self.xt[b][:, :, :], "st%d" % b,
                   reads=[(("xt", b), s) for s in range(4)], writes=[("X", part, l + 1, t)])

    def ring_load(self, slot):
        P = self.P
        b = self.ring_pos % self.c.ring
        self.ring_pos += 1
        key = ("ring", b)
        P.dma("sync", self.ringb[b][:, :], self.wbf[slot * 128:(slot + 1) * 128, :], "ring%d" % b,
              reads=[("wbf", slot)], writes=[key])
        return self.ringb[b], key

    def norm_to_hT(self, xt, xkey, gT, gkey, gcol):
        P = self.P
        for s in range(4):
            P.op("scalar", lambda e, s=s: e.activation(out=self.hn[s % 2][:, :], in_=xt[:, s, :], func=AF.Square,
                                                      accum_out=self.ss[:, s:s + 1]),
                 reads=[(xkey, s)], writes=[("ss", s), ("hn", s % 2)])
        P.op("scalar", lambda e: e.activation(out=self.rt[:, 0:4], in_=self.ss[:, 0:4], func=AF.Sqrt,
                                               bias=self.epsb[:, 0:1], scale=1.0 / D),
             reads=[("ss", s) for s in range(4)] + ["epsb"], writes=["rt"])
        P.op("vector", lambda e: e.reciprocal(out=self.rstd[:, 0:4], in_=self.rt[:, 0:4]),
             reads=["rt"], writes=["rstd"])
        def scale(s):
            hb = self.hn[s % 2]
            P.op("vector", lambda e, s=s, hb=hb: e.tensor_scalar(out=hb[:, :], in0=xt[:, s, :],
                                                                scalar1=self.rstd[:, s:s + 1], scalar2=None,
                                                                op0=ALU.mult),
                 reads=[(xkey, s), "rstd"], writes=[("hn", s % 2)])

        def trans(s):
            hb = self.hn[s % 2]
            hk = ("hn", s % 2)
            for half in range(2):
                bank = 2 * (s % 2) + half
                for j in range(4):
                    kc = half * 4 + j
                    P.op("tensor", lambda e, hb=hb, kc=kc, j=j, bank=bank: e.transpose(
                        out=self.ps[bank][:, j * 128:(j + 1) * 128], in_=hb[:, kc * 128:(kc + 1) * 128],
                        identity=self.ident[:, :]),
                        reads=[hk, "ident"], writes=[("ps", bank)])

        def evac(s):
            for half in range(2):
                bank = 2 * (s % 2) + half
                g0 = gcol + half * 4
                P.op("vector", lambda e, s=s, half=half, bank=bank, g0=g0: e.tensor_tensor(
                    out=self.hT[:, half * 4:half * 4 + 4, s * 128:(s + 1) * 128],
                    in0=self.ps[bank][:, :].rearrange("p (j t) -> p j t", j=4),
                    in1=gT[:, g0:g0 + 4].unsqueeze(2).broadcast_to([128, 4, 128]),
                    op=ALU.mult),
                    reads=[("ps", bank), gkey], writes=[("hT", s)])

        scale(0)
        for s in range(4):
            if s + 1 < 4:
                scale(s + 1)
            trans(s)
            evac(s)

    def ffn_tile(self, l, xt, xkey):
        c, P = self.c, self.P
        self.norm_to_hT(xt, xkey, self.gffnT, "gffnT", l * 8)
        hTr = [("hT", s) for s in range(4)]
        s_gu = c.slot[("wgu", l)]
        s_d = c.slot[("wd", l)]
        for fc in range(c.FC):
            w, wk = self.ring_load(s_gu + fc)
            par = fc % 2
            pg, pu = self.ps[4 + 2 * par], self.ps[5 + 2 * par]
            kg, ku = ("ps", 4 + 2 * par), ("ps", 5 + 2 * par)
            for kc in range(KC):
                P.op("tensor", lambda e, w=w, kc=kc, pg=pg: e.matmul(
                    pg[:, :], lhsT=w[:, kc * 128:(kc + 1) * 128], rhs=self.hT[:, kc, :],
                    start=(kc == 0), stop=(kc == KC - 1)), reads=[wk] + hTr, writes=[kg])
            for kc in range(KC):
                P.op("tensor", lambda e, w=w, kc=kc, pu=pu: e.matmul(
                    pu[:, :], lhsT=w[:, 1024 + kc * 128:1024 + (kc + 1) * 128], rhs=self.hT[:, kc, :],
                    start=(kc == 0), stop=(kc == KC - 1)), reads=[wk] + hTr, writes=[ku])
            sgb = self.sg[par]
            P.op("scalar", lambda e, pg=pg, sgb=sgb: e.activation(out=sgb[:, :], in_=pg[:, :], func=AF.Silu),
                 reads=[kg], writes=[("sg", par)])
            P.op("vector", lambda e, pu=pu, sgb=sgb, fc=fc: e.tensor_tensor(
                out=self.act[:, fc, :], in0=pu[:, :], in1=sgb[:, :], op=ALU.mult),
                reads=[ku, ("sg", par)], writes=[("act", fc)])
        for j in range(c.FC // 2):
            w, wk = self.ring_load(s_d + j)
            for i in range(2):
                fc = 2 * j + i
                for s in range(4):
                    for half in range(2):
                        bank = 2 * s + half
                        P.op("tensor", lambda e, w=w, i=i, fc=fc, s=s, half=half, bank=bank: e.matmul(
                            self.ps[bank][:, :], lhsT=self.act[:, fc, s * 128:(s + 1) * 128],
                            rhs=w[:, i * 1024 + half * 512:i * 1024 + (half + 1) * 512],
                            start=(fc == 0), stop=(fc == c.FC - 1)),
                            reads=[wk, ("act", fc)], writes=[("ps", bank)])
        for s in range(4):
            for half in range(2):
                bank = 2 * s + half
                P.op("vector", lambda e, s=s, half=half, bank=bank: e.tensor_tensor(
                    out=xt[:, s, half * 512:(half + 1) * 512], in0=self.ps[bank][:, :],
                    in1=xt[:, s, half * 512:(half + 1) * 512], op=ALU.add),
                    reads=[("ps", bank), (xkey, s)], writes=[(xkey, s)])

    def add_psum_to_xt(self, xt, xkey):
        for s in range(4):
            for half in range(2):
                bank = 2 * s + half
                self.P.op("vector", lambda e, s=s, half=half, bank=bank: e.tensor_tensor(
                    out=xt[:, s, half * 512:(half + 1) * 512], in0=self.ps[bank][:, :],
                    in1=xt[:, s, half * 512:(half + 1) * 512], op=ALU.add),
                    reads=[("ps", bank), (xkey, s)], writes=[(xkey, s)])


    def fft_part(self, l, part, fb):
        c, P = self.c, self.P
        NB = c.NBP if part == "P" else c.NBS
        KB = c.KBP if part == "P" else c.NBS
        NT = self.NT[part]
        A, C, B, T2, hb, xg = fb["A"], fb["C"], fb["B" + part], fb["T2"], fb["hb"], fb["xg"]
        Bkey = "B" + part
        T1 = self.xtall[:, :].bitcast(BF16)[:, 0:NB * 256].rearrange("p (b k) -> p b k", k=256)
        T1f = self.xtall[:, :].bitcast(BF16)
        YT = self.act[:, :, :].rearrange("p f t -> p (f t)")[:, 0:128 * KB]
        KG = 512 // KB
        if part == "S":
            src = self.src(l, "S")
            allX = [("X", "S", l, t) for t in range(NT)]
            srcv = src.rearrange("(a b) c -> a b c", b=NB)
            ssq, rtq, rsq, grep = fb["ssq"], fb["rtq"], fb["rsq"], fb["grep"]
            nbs = NB // 16
            hbv = lambda h: h[:, :, :].rearrange("p b c -> p (b c)").bitcast(F32)[:, 0:NB * 64].rearrange("p (b c) -> p b c", c=D)
            bufs = [xg[:, :].rearrange("p (b c) -> p b c", c=D), fb["xg2"][:, :].rearrange("p (b c) -> p b c", c=D),
                    hbv(hb[0]), hbv(hb[1])]
            bkeys = [("xg", 0), ("xg", 1), ("hb", 0), ("hb", 1)]
            for i in range(16):
                bf, bk = bufs[i % 4], bkeys[i % 4]
                P.dma("sync", bf[:, 0:nbs, :], srcv[:, i * nbs:(i + 1) * nbs, :], "fs%d" % (i % 4),
                      reads=allX, writes=[bk])
                for q in range(nbs):
                    b = i * nbs + q
                    jk = b % 4
                    P.op("scalar", lambda e, bf=bf, q=q, b=b, jk=jk: e.activation(
                        out=T1f[:, jk * D:(jk + 1) * D], in_=bf[:, q, :], func=AF.Square, accum_out=ssq[:, b:b + 1]),
                        reads=[bk], writes=[("ssq", b), ("jk", jk)])
            P.op("scalar", lambda e: e.activation(out=rtq[:, 0:NB], in_=ssq[:, 0:NB], func=AF.Sqrt,
                                                   bias=self.epsb[:, 0:1], scale=1.0 / D),
                 reads=[("ssq", b) for b in range(NB)] + ["epsb"], writes=["rtq"])
            P.op("vector", lambda e: e.reciprocal(out=rsq[:, 0:NB], in_=rtq[:, 0:NB]), reads=["rtq"], writes=["rsq"])
            hnb = NB // 2
            xgv2 = [xg[:, :].rearrange("p (b c) -> p b c", c=128), fb["xg2"][:, :].rearrange("p (b c) -> p b c", c=128)]
        else:
            npiece = 4 * c.NTP
        def prep(g):
            hbg = hb[g % 2]
            hk = ("hb", g % 2)
            if part == "S":
                for half in range(2):
                    xv = xgv2[half]
                    xk = ("xg", half)
                    P.dma("sync", xv[:, :, :], srcv[:, half * hnb:(half + 1) * hnb, g * 128:(g + 1) * 128],
                          "fx%d" % half, reads=allX, writes=[xk])
                    P.op("vector", lambda e, half=half, xv=xv: e.tensor_tensor(
                        out=xv[:, :, :], in0=xv[:, :, :],
                        in1=rsq[:, half * hnb:(half + 1) * hnb].unsqueeze(2).broadcast_to([128, hnb, 128]),
                        op=ALU.mult), reads=[xk, "rsq"], writes=[xk])
                    P.op("gpsimd", lambda e, half=half, g=g, hbg=hbg, xv=xv: e.tensor_tensor(
                        out=hbg[:, half * hnb:(half + 1) * hnb, :], in0=xv[:, :, :],
                        in1=grep[:, g * 128:(g + 1) * 128].unsqueeze(1).broadcast_to([128, hnb, 128]),
                        op=ALU.mult), reads=[xk, "grep"], writes=[hk])
            else:
                apt, apr = 512 // NB, c.RP // NB
                k = 0
                for r in range(4):
                    for t in range(c.NTP):
                        p0 = r * apr + t * apt
                        P.dma("sync", hbg[p0:p0 + apt, 0:NB, :],
                              self.hall[t][r * 512:(r + 1) * 512, g * 128:(g + 1) * 128].rearrange("(a b) c -> a b c", b=NB),
                              "fq%d" % (k % 4), reads=[("hall", tt) for tt in range(c.NTP)], writes=[(hk, k)])
                        k += 1

        prep(0)
        for g in range(8):
            hbg = hb[g % 2]
            hk = ("hb", g % 2)
            hbreads = [hk] if part == "S" else [(hk, k) for k in range(npiece)]
            for b2 in range(NB // 2):
                bank = b2 % 2
                for q in range(2):
                    b = 2 * b2 + q
                    P.op("tensor", lambda e, b=b, q=q, bank=bank, hbg=hbg: e.matmul(
                        self.ps[bank][:, q * 256:(q + 1) * 256], lhsT=hbg[:, b, :], rhs=A[:, :],
                        start=True, stop=True), reads=hbreads + ["A"], writes=[("ps", bank)])
                dstap = T1[:, 2 * b2:2 * b2 + 2, :]
                srcap = self.ps[bank].rearrange("p (q k) -> p q k", q=2)
                if b2 % 2 == 0:
                    P.op("scalar", lambda e, d=dstap, s_=srcap: e.copy(out=d, in_=s_),
                         reads=[("ps", bank)], writes=["T1"])
                else:
                    P.op("vector", lambda e, d=dstap, s_=srcap: e.tensor_copy(out=d, in_=s_),
                         reads=[("ps", bank)], writes=["T1"])
            if g + 1 < 8:
                prep(g + 1)
            self.conv_some(3, after=[("mixT", part, g - 1)] if g > 0 else [])
            T1v = T1.rearrange("p b (ri k) -> p b ri k", ri=2)
            YTv = YT.rearrange("p (kb ka) -> p ka kb", ka=128)
            def emit_s2(ka2):
                bank = 2 + ka2 % 2
                t2 = T2[ka2 % 2]
                t2k = ("T2", ka2 % 2)
                for q in range(2):
                    ka = 2 * ka2 + q
                    P.op("tensor", lambda e, ka=ka, q=q, bank=bank: e.matmul(
                        self.ps[bank][0:2 * NB, q * 256:(q + 1) * 256], lhsT=T1v[:, :, :, ka], rhs=C[:, :],
                        start=True, stop=True), reads=["T1", "C"], writes=[("ps", bank)])
                if ka2 % 2 == 0:
                    P.op("scalar", lambda e, t2=t2, bank=bank: e.copy(out=t2[0:2 * NB, :], in_=self.ps[bank][0:2 * NB, :]),
                         reads=[("ps", bank)], writes=[t2k])
                else:
                    P.op("vector", lambda e, t2=t2, bank=bank: e.tensor_copy(out=t2[0:2 * NB, :], in_=self.ps[bank][0:2 * NB, :]),
                         reads=[("ps", bank)], writes=[t2k])

            def emit_s3(ka2):
                t2 = T2[ka2 % 2]
                t2k = ("T2", ka2 % 2)
                for q in range(2):
                    ka = 2 * ka2 + q
                    kg = ka // KG
                    bank3 = 4 + kg % 2
                    col = (ka % KG) * KB
                    for w in range(2):
                        P.op("tensor", lambda e, t2=t2, q=q, w=w, ka=ka, bank3=bank3, col=col: e.matmul(
                            self.ps[bank3][:, col:col + KB],
                            lhsT=t2[0:2 * NB, q * 256 + w * 128:q * 256 + (w + 1) * 128],
                            rhs=B[0:2 * NB, w, ka * KB:(ka + 1) * KB],
                            start=(w == 0), stop=(w == 1)),
                            reads=[t2k, Bkey], writes=[("ps", bank3)])
                    if ka % KG == KG - 1:
                        ka0 = ka - KG + 1
                        dstap = YTv[:, ka0:ka0 + KG, :]
                        srcap = self.ps[bank3].rearrange("p (k b) -> p k b", b=KB)
                        if kg % 2 == 0:
                            P.op("vector", lambda e, d=dstap, s_=srcap: e.tensor_copy(out=d, in_=s_),
                                 reads=[("ps", bank3)], writes=["YT"])
                        else:
                            P.op("scalar", lambda e, d=dstap, s_=srcap: e.copy(out=d, in_=s_),
                                 reads=[("ps", bank3)], writes=["YT"])

            emit_s2(0)
            for ka2 in range(64):
                if ka2 + 1 < 64:
                    emit_s2(ka2 + 1)
                emit_s3(ka2)
            P.dma("sync", self.mixT[part][g, :, :], YT, "fy", reads=["YT"], writes=[("mixT", part, g)])

    def fourier_layer(self, l):
        c, P, nc = self.c, self.P, self.nc
        with contextlib.ExitStack() as ls:
            sb = lambda name, shape, dtype: ls.enter_context(nc.sbuf_tensor("f%d_" % l + name, shape, dtype))
            fb = {}
            fb["grep"] = grep = sb("grep", [128, D], F32)
            fb["A"] = A = sb("A", [128, 256], BF16)
            fb["C"] = C = sb("C", [128, 256], BF16)
            fb["BP"] = sb("BP", [128, 2, 128 * c.KBP], BF16)
            fb["BS"] = sb("BS", [128, 2, 128 * c.NBS], BF16)
            fb["xg"] = sb("xg", [128, c.NBS * 64], F32)
            fb["xg2"] = sb("xg2", [128, c.NBS * 64], F32)
            fb["hb"] = [sb("hb%d" % i, [128, max(c.NBP, c.NBS), 128], BF16) for i in range(2)]
            fb["T2"] = [sb("T2_%d" % i, [128, 512], BF16) for i in range(2)]
            fb["ssq"] = sb("ssq", [128, c.NBS], F32)
            fb["rtq"] = sb("rtq", [128, c.NBS], F32)
            fb["rsq"] = sb("rsq", [128, c.NBS], F32)
            mT = sb("mT", [128, 8, 512], BF16)

            P.dma("sync", grep[:, :], self.gmixrep_d[:, l, :], "f0", writes=["grep"])
            P.dma("gpsimd", A[:, :], self.dftA_d[:, :], "f1", writes=["A"])
            P.dma("gpsimd", C[:, :], self.dftC_d[:, :], "f2", writes=["C"])
            k = 0
            for part, nbp, ncol in (("S", c.NBS, 128 * c.NBS), ("P", c.NBP, 128 * c.KBP)):
                ch = min(2048, ncol)
                for w in range(2):
                    for i in range(ncol // ch):
                        P.dma("gpsimd", fb["B" + part][0:2 * nbp, w, i * ch:(i + 1) * ch],
                              self.dftB_d[part][w, :, i * ch:(i + 1) * ch], "fb%d" % (k % 4),
                              writes=["B" + part])
                        k += 1

            hst = mT[:, :, :].rearrange("p g t -> p (g t)").rearrange("p (s c) -> p s c", c=D)
            for t in range(c.NTP):
                b = self.next_xt()
                xt, xkey = self.xt[b], ("xt", b)
                self.load_tile(l, "P", t, b)
                for s in range(4):
                    P.op("scalar", lambda e, s=s, xt=xt: e.activation(out=self.hn[s % 2][:, :], in_=xt[:, s, :], func=AF.Square,
                                                                     accum_out=self.ss[:, s:s + 1]),
                         reads=[(xkey, s)], writes=[("ss", s), ("hn", s % 2)])
                P.op("scalar", lambda e: e.activation(out=self.rt[:, 0:4], in_=self.ss[:, 0:4], func=AF.Sqrt,
                                                       bias=self.epsb[:, 0:1], scale=1.0 / D),
                     reads=[("ss", s) for s in range(4)] + ["epsb"], writes=["rt"])
                P.op("vector", lambda e: e.reciprocal(out=self.rstd[:, 0:4], in_=self.rt[:, 0:4]),
                     reads=["rt"], writes=["rstd"])
                for s in range(4):
                    hb_, hk = self.hn[s % 2], ("hn", s % 2)
                    P.op("vector", lambda e, s=s, hb_=hb_, xt=xt: e.tensor_scalar(
                        out=hb_[:, :], in0=xt[:, s, :], scalar1=self.rstd[:, s:s + 1], scalar2=None, op0=ALU.mult),
                        reads=[(xkey, s), "rstd"], writes=[hk])
                    P.op("vector", lambda e, s=s, hb_=hb_: e.tensor_tensor(
                        out=hst[:, s, :], in0=hb_[:, :], in1=grep[:, :], op=ALU.mult),
                        reads=[hk, "grep"], writes=[("hst", s)])
                P.dma("sync", self.hloc[t].rearrange("(s p) c -> p s c", p=128), hst,
                      "fh", reads=[("hst", s) for s in range(4)], writes=[("hloc", t)])
                _coll(P, self.hloc[t], self.hall[t], "ag", [("hloc", t)], [("hall", t)], c.groups)
            P.barrier(skip=("C_ag",) + tuple("D_cv%d" % i for i in range(4)))

            self.fft_part(l, "S", fb)
            P.barrier(skip=("C_ag",) + tuple("D_cv%d" % i for i in range(4)))
            self.fft_part(l, "P", fb)
            P.barrier(skip=tuple("D_cv%d" % i for i in range(4)))

            self.convert_weights()
            s_wf = c.slot[("wf", l)]
            for part in ("P", "S"):
                mixv = self.mixT[part].rearrange("g k r -> k g r")
                for t in range(self.NT[part]):
                    b = self.next_xt()
                    xt, xkey = self.xt[b], ("xt", b)
                    self.load_tile(l, part, t, b)
                    P.dma("sync", mT[:, :, :], mixv[:, :, t * 512:(t + 1) * 512], "fm",
                          reads=[("mixT", part, g) for g in range(8)], writes=["mT"])
                    ws = [self.ring_load(s_wf + j) for j in range(4)]
                    for s in range(4):
                        for half in range(2):
                            bank = 2 * s + half
                            for g in range(8):
                                w, wk = ws[g // 2]
                                P.op("tensor", lambda e, w=w, g=g, s=s, half=half, bank=bank: e.matmul(
                                    self.ps[bank], lhsT=mT[:, g, s * 128:(s + 1) * 128],
                                    rhs=w[:, (g % 2) * 1024 + half * 512:(g % 2) * 1024 + (half + 1) * 512],
                                    start=(g == 0), stop=(g == 7)),
                                    reads=[wk, "mT"], writes=[("ps", bank)])
                    self.add_psum_to_xt(xt, xkey)
                    self.ffn_tile(l, xt, xkey)
                    self.store_tile(l, part, t, b)
    def qk_norm(self, tmp, nh, pss, gain, gkey):
        P = self.P
        qn, qr = tmp["qn"], tmp["qr"]
        t1, t2, t3, t4 = tmp["t"]
        h0 = 0
        for (psap, bank, n) in pss:
            for i in range(n):
                P.op("scalar", lambda e, psap=psap, i=i, h0=h0: e.activation(
                    out=self.junk[:, (h0 + i) * 128:(h0 + i + 1) * 128], in_=psap[:, i * 128:(i + 1) * 128],
                    func=AF.Square, accum_out=self.ss[:, h0 + i:h0 + i + 1]),
                    reads=[("ps", bank)], writes=[("ssh", h0 + i), ("junk", h0 + i)])
            h0 += n
        P.op("scalar", lambda e: e.activation(out=self.rt[:, 0:nh], in_=self.ss[:, 0:nh], func=AF.Sqrt,
                                               bias=self.epsb[:, 0:1], scale=1.0 / HD),
             reads=[("ssh", i) for i in range(nh)] + ["epsb"], writes=["rth"])
        P.op("vector", lambda e: e.reciprocal(out=self.rstd[:, 0:nh], in_=self.rt[:, 0:nh]),
             reads=["rth"], writes=["rstdh"])
        h0 = 0
        for (psap, bank, n) in pss:
            P.op("vector", lambda e, psap=psap, n=n, h0=h0: e.tensor_tensor(
                out=qn[:, h0:h0 + n, :], in0=psap.rearrange("p (h d) -> p h d", d=128),
                in1=self.rstd[:, h0:h0 + n].unsqueeze(2).broadcast_to([128, n, 128]), op=ALU.mult),
                reads=[("ps", bank), "rstdh"], writes=["qn"] + tmp.get("fence", []))
            h0 += n
        P.op("gpsimd", lambda e: e.tensor_tensor(
            out=qn[:, 0:nh, :], in0=qn[:, 0:nh, :],
            in1=gain.unsqueeze(1).broadcast_to([128, nh, 128]), op=ALU.mult),
            reads=["qn", gkey], writes=["qn"])

    def qk_rope(self, tmp, nh, cs, cskey, s):
        P = self.P
        qn, qr = tmp["qn"], tmp["qr"]
        t1, t2, t3, t4 = tmp["t"]
        qv = qn[:, 0:nh, :].rearrange("p h (i two) -> p h i two", two=2)
        rv = qr[:, 0:nh, :].rearrange("p h (i two) -> p h i two", two=2)
        x0, x1 = qv[:, :, :, 0], qv[:, :, :, 1]
        cc = cs[:, s, 0:64].unsqueeze(1).broadcast_to([128, nh, 64])
        sn = cs[:, s, 64:128].unsqueeze(1).broadcast_to([128, nh, 64])
        P.op("gpsimd", lambda e: e.tensor_tensor(out=t1[:, 0:nh, :], in0=x0, in1=cc, op=ALU.mult),
             reads=["qn", cskey], writes=["t1"])
        P.op("gpsimd", lambda e: e.tensor_tensor(out=t2[:, 0:nh, :], in0=x1, in1=sn, op=ALU.mult),
             reads=["qn", cskey], writes=["t2"])
        P.op("gpsimd", lambda e: e.tensor_tensor(out=rv[:, :, :, 0], in0=t1[:, 0:nh, :], in1=t2[:, 0:nh, :],
                                                 op=ALU.subtract), reads=["t1", "t2"], writes=["qr0"])
        P.op("vector", lambda e: e.tensor_tensor(out=t3[:, 0:nh, :], in0=x0, in1=sn, op=ALU.mult),
             reads=["qn", cskey], writes=["t3"])
        P.op("vector", lambda e: e.tensor_tensor(out=t4[:, 0:nh, :], in0=x1, in1=cc, op=ALU.mult),
             reads=["qn", cskey], writes=["t4"])
        P.op("vector", lambda e: e.tensor_tensor(out=rv[:, :, :, 1], in0=t3[:, 0:nh, :], in1=t4[:, 0:nh, :],
                                                 op=ALU.add), reads=["t3", "t4"], writes=["qr1"])


    def attn_layer(self, l):
        c, P, nc = self.c, self.P, self.nc
        scale = float(HD) ** -0.5
        NBmax = max(c.NBP, c.NBS)
        with contextlib.ExitStack() as ls:
            sb = lambda name, shape, dtype: ls.enter_context(nc.sbuf_tensor("a%d_" % l + name, shape, dtype))
            KT = sb("KT", [128, 2, NBmax * 128], BF16)
            Vb = sb("Vb", [128, NBmax, 256], BF16)
            QT = sb("QT", [128, 8, 512], BF16)
            OT = sb("OT", [128, 8, 512], BF16)
            NPT = 4
            pT = [sb("pT%d" % i, [128, 1024], BF16) for i in range(NPT)]
            s2 = [sb("s2_%d" % i, [128, 512], BF16) for i in range(NPT)]
            rinv = sb("rinv", [128, 512], F32)
            ou = sb("ou", [128, 512], F32)
            cs = sb("cs", [128, 4, 128], F32)
            qkg = sb("qkg", [128, 2, 128], F32)
            flat = self.act[:, :, :].rearrange("p f t -> p (f t)").bitcast(F32)
            assert c.FC * 256 >= 3584
            tmp = {
                "fence": [("act", fc) for fc in range(c.FC)],
                "qn": flat[:, 0:1024].rearrange("p (h d) -> p h d", d=128),
                "t": [flat[:, 1024 + 512 * i:1024 + 512 * (i + 1)].rearrange("p (h d) -> p h d", d=64) for i in range(4)],
                "qr": flat[:, 3072:3584].bitcast(BF16).rearrange("p (h d) -> p h d", d=128),
            }
            P.dma("sync", qkg[:, :, :], self.qkg_d[:, l, :, :], "a0", writes=["qkg"])
            self.convert_weights()
            s_q, s_kv, s_o = c.slot[("wq", l)], c.slot[("wkv", l)], c.slot[("wo", l)]

            def pass1(part):
                for ta in range(self.NT[part]):
                    b = self.next_xt()
                    xt, xkey = self.xt[b], ("xt", b)
                    self.load_tile(l, part, ta, b)
                    P.dma("sync", cs[:, :, :], self.cs_d[part][ta], "a1", writes=["cs"])
                    self.norm_to_hT(xt, xkey, self.gmixT, "gmixT", l * 8)
                    ws = [self.ring_load(s_kv + j) for j in range(2)]
                    def mm1(s, ws=ws):
                        bank = 4 + s % 2
                        for kc in range(KC):
                            w, wk = ws[kc // 4]
                            P.op("tensor", lambda e, w=w, kc=kc, s=s, bank=bank: e.matmul(
                                self.ps[bank], lhsT=self.hT[:, kc, s * 128:(s + 1) * 128],
                                rhs=w[:, (kc % 4) * 512:(kc % 4 + 1) * 512],
                                start=(kc == 0), stop=(kc == KC - 1)),
                                reads=[wk, ("hT", s)], writes=[("ps", bank)])
                    def norm1(s):
                        bank = 4 + s % 2
                        self.qk_norm(tmp, 2, [(self.ps[bank][:, 0:256], bank, 2)], qkg[:, 1, :], "qkg")
                    mm1(0)
                    norm1(0)
                    for s in range(4):
                        sa = 4 * ta + s
                        bank = 4 + s % 2
                        if s + 1 < 4:
                            mm1(s + 1)
                        P.op("scalar", lambda e, sa=sa, bank=bank: e.copy(out=Vb[:, sa, :], in_=self.ps[bank][:, 256:512]),
                             reads=[("ps", bank)], writes=[("V", sa)])
                        self.qk_rope(tmp, 2, cs, "cs", s)
                        if s + 1 < 4:
                            norm1(s + 1)
                        pb = self.pp[3][:, (s % 2) * 512:(s % 2 + 1) * 512].bitcast(BF16)
                        for kvh in range(2):
                            P.op("tensor", lambda e, kvh=kvh, pb=pb: e.transpose(
                                out=pb[:, kvh * 128:(kvh + 1) * 128], in_=tmp["qr"][:, kvh, :], identity=self.identb[:, :]),
                                reads=["qr0", "qr1", "identb"], writes=[("ps", 6 + s % 2)])
                        P.op("scalar", lambda e, sa=sa, pb=pb: e.copy(
                            out=KT[:, :, sa * 128:(sa + 1) * 128], in_=pb[:, 0:256].rearrange("p (h t) -> p h t", h=2)),
                            reads=[("ps", 6 + s % 2)], writes=[("K", sa)])

            def pass2(part, nbk):
                allK = [("K", sa) for sa in range(nbk)]
                allV = [("V", sa) for sa in range(nbk)]
                for ta in range(self.NT[part]):
                    b = self.next_xt()
                    xt, xkey = self.xt[b], ("xt", b)
                    self.load_tile(l, part, ta, b)
                    P.dma("sync", cs[:, :, :], self.cs_d[part][ta], "a1", writes=["cs"])
                    self.norm_to_hT(xt, xkey, self.gmixT, "gmixT", l * 8)
                    ws = [self.ring_load(s_q + j) for j in range(4)]
                    def mm2(s, ws=ws):
                        banks = (4, 5) if s % 2 == 0 else (2, 3)
                        for half in range(2):
                            for kc in range(KC):
                                w, wk = ws[kc // 2]
                                P.op("tensor", lambda e, w=w, kc=kc, s=s, half=half, banks=banks: e.matmul(
                                    self.ps[banks[half]], lhsT=self.hT[:, kc, s * 128:(s + 1) * 128],
                                    rhs=w[:, (kc % 2) * 1024 + half * 512:(kc % 2) * 1024 + (half + 1) * 512],
                                    start=(kc == 0), stop=(kc == KC - 1)),
                                    reads=[wk, ("hT", s)], writes=[("ps", banks[half])])
                    def norm2(s):
                        banks = (4, 5) if s % 2 == 0 else (2, 3)
                        self.qk_norm(tmp, 8, [(self.ps[banks[0]], banks[0], 4), (self.ps[banks[1]], banks[1], 4)],
                                     qkg[:, 0, :], "qkg")
                    mm2(0)
                    norm2(0)
                    for s in range(4):
                        if s + 1 < 4:
                            mm2(s + 1)
                        self.qk_rope(tmp, 8, cs, "cs", s)
                        if s + 1 < 4:
                            norm2(s + 1)
                        pb = self.pp[3][:, (s % 2) * 512:(s % 2 + 1) * 512].bitcast(BF16)
                        for hd in range(8):
                            P.op("tensor", lambda e, hd=hd, pb=pb: e.transpose(
                                out=pb[:, hd * 128:(hd + 1) * 128], in_=tmp["qr"][:, hd, :], identity=self.identb[:, :]),
                                reads=["qr0", "qr1", "identb"], writes=[("ps", 6 + s % 2)])
                        P.op("vector", lambda e, s=s, pb=pb: e.tensor_copy(
                            out=QT[:, :, s * 128:(s + 1) * 128], in_=pb[:, :].rearrange("p (h t) -> p h t", h=8)),
                            reads=[("ps", 6 + s % 2)], writes=["QT"])
                    npair = nbk // 2
                    items = [(hd, kp) for hd in range(8) for kp in range(npair)]

                    def emit_score(i):
                        hd, kp = items[i]
                        kvh = hd // 4
                        j = i % 3
                        for q in range(2):
                            kt = 2 * kp + q
                            P.op("tensor", lambda e, hd=hd, kt=kt, kvh=kvh, j=j, q=q: e.matmul(
                                self.ps[2 * j + q], lhsT=KT[:, kvh, kt * 128:(kt + 1) * 128], rhs=QT[:, hd, :],
                                start=True, stop=True), reads=["QT"] + allK, writes=[("ps", 2 * j + q)])
                        jb = i % NPT
                        P.op("scalar", lambda e, j=j, jb=jb: e.activation(
                            out=pT[jb][:, :], in_=self.pp[j][:, :], func=AF.Exp, scale=scale),
                            reads=[("ps", 2 * j), ("ps", 2 * j + 1)], writes=[("pT", jb)])

                    emit_score(0)
                    emit_score(1)
                    pending = []
                    for i, (hd, kp) in enumerate(items):
                        if i + 2 < len(items):
                            emit_score(i + 2)
                        kvh = hd // 4
                        jb = i % NPT
                        par = hd % 2
                        po, pr = 6, 7
                        for q in range(2):
                            kt = 2 * kp + q
                            P.op("tensor", lambda e, kt=kt, kvh=kvh, jb=jb, po=po, q=q: e.matmul(
                                self.ps[po], lhsT=Vb[:, kt, kvh * 128:(kvh + 1) * 128],
                                rhs=pT[jb][:, q * 512:(q + 1) * 512],
                                start=(kt == 0), stop=(kt == nbk - 1)),
                                reads=[("pT", jb)] + allV, writes=[("ps", po)])
                        GS = 4 if npair % 4 == 0 else 2
                        sj = (i // GS) % NPT
                        if kp % GS == 0:
                            P.op("vector", lambda e, jb=jb, sj=sj: e.tensor_tensor(
                                out=s2[sj][:, :], in0=pT[jb][:, 0:512], in1=pT[jb][:, 512:1024], op=ALU.add),
                                reads=[("pT", jb)], writes=[("s2", sj)])
                        else:
                            P.op("vector", lambda e, jb=jb: e.tensor_tensor(
                                out=rinv[:, :].bitcast(BF16)[:, 0:512], in0=pT[jb][:, 0:512], in1=pT[jb][:, 512:1024], op=ALU.add),
                                reads=[("pT", jb)], writes=["s2tmp", "rinv"])
                            P.op("vector", lambda e, sj=sj: e.tensor_tensor(
                                out=s2[sj][:, :], in0=s2[sj][:, :], in1=rinv[:, :].bitcast(BF16)[:, 0:512], op=ALU.add),
                                reads=["s2tmp", ("s2", sj)], writes=[("s2", sj)])

                            def emit_rs(sj=sj, pr=pr, kp=kp, GS=GS):
                                P.op("tensor", lambda e: e.matmul(
                                    self.ps[pr], lhsT=self.onesb[:, :], rhs=s2[sj][:, :],
                                    start=(kp == GS - 1), stop=(kp == npair - 1)),
                                    reads=[("s2", sj), "onesb"], writes=[("ps", pr)])
                            if kp % GS == GS - 1:
                                if pending:
                                    pending.pop()()
                                pending.append(emit_rs)
                        if kp == npair - 1:
                            while pending:
                                pending.pop()()
                            P.op("vector", lambda e, po=po: e.tensor_copy(out=ou[:, :], in_=self.ps[po]),
                                 reads=[("ps", po)], writes=["ou"])
                            P.op("vector", lambda e, pr=pr: e.reciprocal(out=rinv[:, :], in_=self.ps[pr]),
                                 reads=[("ps", pr), "s2tmp"], writes=["rinv", "s2tmp"])
                            P.op("vector", lambda e, hd=hd: e.tensor_tensor(
                                out=OT[:, hd, :], in0=ou[:, :], in1=rinv[:, :], op=ALU.mult),
                                reads=["ou", "rinv"], writes=[("OT", hd)])
                    ws = [self.ring_load(s_o + j) for j in range(4)]
                    for s in range(4):
                        for half in range(2):
                            bank = 2 * s + half
                            for hd in range(8):
                                w, wk = ws[hd // 2]
                                P.op("tensor", lambda e, w=w, hd=hd, s=s, half=half, bank=bank: e.matmul(
                                    self.ps[bank], lhsT=OT[:, hd, s * 128:(s + 1) * 128],
                                    rhs=w[:, (hd % 2) * 1024 + half * 512:(hd % 2) * 1024 + (half + 1) * 512],
                                    start=(hd == 0), stop=(hd == 7)),
                                    reads=[wk, ("OT", hd)], writes=[("ps", bank)])
                    self.add_psum_to_xt(xt, xkey)
                    self.ffn_tile(l, xt, xkey)
                    self.store_tile(l, part, ta, b)

            nloc = c.RP // 128
            pass1("P")
            P.dma("sync", self.kloc.rearrange("(h d) k -> d h k", h=2), KT[:, :, 0:c.RP], "ak",
                  reads=[("K", sa) for sa in range(nloc)], writes=["kloc"])
            P.dma("sync", self.vloc.rearrange("(s p) c -> p s c", p=128), Vb[:, 0:nloc, :], "av",
                  reads=[("V", sa) for sa in range(nloc)], writes=["vloc"])
            _coll(P, self.kloc, self.kall, "agk", ["kloc"], ["kall"], c.groups)
            _coll(P, self.vloc, self.vall, "agv", ["vloc"], ["vall"], c.groups)
            pass1("S")
            pass2("S", c.NBS)
            kallv = self.kall.rearrange("(r h d) k -> h d r k", r=4, h=2)
            for h in range(2):
                P.dma("sync", KT[:, h, 0:4 * c.RP].rearrange("d (r k) -> d r k", r=4), kallv[h], "ak%d" % h,
                      reads=["kall"], writes=[("K", sa) for sa in range(c.NBP)])
            P.dma("sync", Vb[:, 0:c.NBP, :], self.vall.rearrange("(s p) c -> p s c", p=128), "av2",
                  reads=["vall"], writes=[("V", sa) for sa in range(c.NBP)])
            pass2("P", c.NBP)


def _rope_rows(pos):
    inv = (np.float32(10000.0) ** (-np.arange(0, 64, 2, dtype=np.float32) / np.float32(64))).astype(np.float32)
    rowp = (pos // 64).astype(np.float32)
    colp = (pos % 64).astype(np.float32)
    ang = np.concatenate([rowp[:, None] * inv, colp[:, None] * inv], -1).astype(np.float32)
    return np.concatenate([np.cos(ang), np.sin(ang)], -1).astype(np.float32)


def _cs_tiles(pos):
    cs = _rope_rows(pos)
    nt = cs.shape[0] // 512
    return np.ascontiguousarray(cs.reshape(nt, 4, 128, 128).transpose(0, 2, 1, 3))


def _dft_common():
    a = np.arange(128)
    A = np.exp(-2j * np.pi * np.outer(a, a) / 128.0)
    m = np.concatenate([A.real, A.imag], 1).astype(np.float32)
    return m


def _dft_B(NB, R, kbs):
    b = np.arange(NB).astype(np.float64)
    ka = np.arange(128).astype(np.float64)
    kb = np.asarray(kbs).astype(np.float64)
    M = (np.exp(-2j * np.pi * b[:, None, None] * kb[None, None, :] / NB)
         * np.exp(-2j * np.pi * b[:, None, None] * ka[None, :, None] / float(R))) / np.sqrt(R * 128.0)
    n = 128 * len(kbs)
    Mr, Mi = M.real.reshape(NB, n), M.imag.reshape(NB, n)
    out = np.zeros((2, 2 * NB, n), np.float32)
    out[0, 0::2] = Mr
    out[0, 1::2] = -Mi
    out[1, 0::2] = -Mi
    out[1, 1::2] = -Mr
    return out


def _pack_weights(cfg, fourier_w, attn_w_qkv, attn_w_o, ffn_w_gate, ffn_w_up, ffn_w_down):
    wall = np.zeros((cfg.nslots, 128, 2048), np.float32)
    jF = jA = 0
    for l, t in enumerate(cfg.layers):
        if t == "F":
            w = np.asarray(fourier_w[jF], np.float32)
            s0 = cfg.slot[("wf", l)]
            wall[s0:s0 + 4] = w.reshape(4, 2, 128, D).transpose(0, 2, 1, 3).reshape(4, 128, 2048)
            jF += 1
        else:
            wqkv = np.asarray(attn_w_qkv[jA], np.float32)
            s0 = cfg.slot[("wq", l)]
            wall[s0:s0 + 4] = wqkv[:, 0:1024].reshape(4, 2, 128, 1024).transpose(0, 2, 1, 3).reshape(4, 128, 2048)
            s0 = cfg.slot[("wkv", l)]
            wall[s0:s0 + 2] = wqkv[:, 1024:1536].reshape(2, 4, 128, 512).transpose(0, 2, 1, 3).reshape(2, 128, 2048)
            s0 = cfg.slot[("wo", l)]
            wo = np.asarray(attn_w_o[jA], np.float32)
            wall[s0:s0 + 4] = wo.reshape(4, 2, 128, D).transpose(0, 2, 1, 3).reshape(4, 128, 2048)
            jA += 1
        FC = cfg.FC
        wg = np.asarray(ffn_w_gate[l], np.float32).reshape(KC, 128, FC, 128)
        wu = np.asarray(ffn_w_up[l], np.float32).reshape(KC, 128, FC, 128)
        gu = np.stack([wg, wu], 0)
        s0 = cfg.slot[("wgu", l)]
        wall[s0:s0 + FC] = gu.transpose(3, 2, 0, 1, 4).reshape(FC, 128, 2048)
        wd = np.asarray(ffn_w_down[l], np.float32)
        s0 = cfg.slot[("wd", l)]
        wall[s0:s0 + FC // 2] = wd.reshape(FC // 2, 2, 128, D).transpose(0, 2, 1, 3).reshape(FC // 2, 128, 2048)
    return wall.reshape(cfg.nslots * 128, 2048)


def _common_inputs(cfg, norm_mix, norm_ffn, attn_q_norm, attn_k_norm):
    L = cfg.L
    nm = np.asarray(norm_mix, np.float32)[:L]
    nf = np.asarray(norm_ffn, np.float32)[:L]
    gmixT = nm.reshape(L, KC, 128).transpose(2, 0, 1).reshape(128, L * KC).copy()
    gffnT = nf.reshape(L, KC, 128).transpose(2, 0, 1).reshape(128, L * KC).copy()
    gmixrep = np.broadcast_to(nm[None], (128, L, D)).copy()
    qkg = np.ones((128, L, 2, 128), np.float32)
    jA = 0
    for l, t in enumerate(cfg.layers):
        if t == "A":
            qkg[:, l, 0, :] = np.asarray(attn_q_norm[jA], np.float32)[None]
            qkg[:, l, 1, :] = np.asarray(attn_k_norm[jA], np.float32)[None]
            jA += 1
    dft = _dft_common()
    return dict(gmixT=gmixT, gffnT=gffnT, gmixrep=gmixrep, qkg=qkg, ident=np.eye(128, dtype=np.float32),
                dftA=dft, dftC=dft.copy(),
                dftBS=_dft_B(cfg.NBS, cfg.RS, np.arange(cfg.NBS)),
                csS=_cs_tiles(np.arange(cfg.RS)))


_NC_CACHE = {}


def run_cores(cfg, xp_list, xs_list, weights):
    key = (cfg.RP, cfg.RS, cfg.DFF, cfg.layers, cfg.ring)
    if key not in _NC_CACHE:
        _NC_CACHE[key] = Builder(cfg).build()
    nc = _NC_CACHE[key]
    in_maps = []
    for c in range(8):
        r = c % 4
        m = dict(xp=np.ascontiguousarray(xp_list[c], np.float32), xs=np.ascontiguousarray(xs_list[c], np.float32),
                 wall=weights["wall"])
        m.update(weights["common"])
        m["csP"] = _cs_tiles(cfg.RP * r + np.arange(cfg.RP))
        m["dftBP"] = _dft_B(cfg.NBP, 4 * cfg.RP, cfg.KBP * r + np.arange(cfg.KBP))
        in_maps.append(m)
    res = run_bass_kernel_spmd(nc, in_maps, core_ids=list(range(8)))
    return [(r["yp"], r["ys"]) for r in res.results]


def kernel(x_prompt, x_sample, norm_mix, norm_ffn, fourier_w, attn_w_qkv, attn_q_norm, attn_k_norm,
           attn_w_o, ffn_w_gate, ffn_w_up, ffn_w_down):
    cfg = Cfg()
    xp = np.asarray(x_prompt, np.float32)
    xs = np.asarray(x_sample, np.float32)
    weights = dict(
        wall=_pack_weights(cfg, fourier_w, attn_w_qkv, attn_w_o, ffn_w_gate, ffn_w_up, ffn_w_down),
        common=_common_inputs(cfg, norm_mix, norm_ffn, attn_q_norm, attn_k_norm))
    RP = cfg.RP
    xp_list = [xp[c // 4, RP * (c % 4):RP * (c % 4 + 1)] for c in range(8)]
    xs_list = [xs[c] for c in range(8)]
    outs = run_cores(cfg, xp_list, xs_list, weights)
    y_prompt = np.zeros_like(xp)
    y_sample = np.zeros_like(xs)
    for c in range(8):
        y_prompt[c // 4, RP * (c % 4):RP * (c % 4 + 1)] = outs[c][0]
        y_sample[c] = outs[c][1]
    return (y_prompt, y_sample)
```

```python
import bisect
import contextlib
import numpy as np
import ml_dtypes
import concourse.bass as bass
import concourse.mybir as mybir
from concourse.bass_utils import run_bass_kernel_spmd

F32 = mybir.dt.float32
BF16 = mybir.dt.bfloat16
AF = mybir.ActivationFunctionType
ALU = mybir.AluOpType
AX = mybir.AxisListType

D = 1024
KC = 8
HD = 128
NH = 8
NKV = 2
EPS = 1e-6
NEG = -30000.0


class Prog:
    ENGS = ("sync", "scalar", "vector", "gpsimd", "tensor")

    def __init__(self, nc, stack):
        self.nc = nc
        self.stack = stack
        self.ops = {e: [] for e in self.ENGS}
        self.marked = {e: [] for e in self.ENGS}
        self.cnt = {e: 0 for e in self.ENGS}
        self.seen = {e: {} for e in self.ENGS}
        self.sems = {}
        self.dmacnt = {}
        self.last_write = {}
        self.readers = {}

    def sem(self, name):
        if name not in self.sems:
            self.sems[name] = self.stack.enter_context(self.nc.semaphore(name))
        return self.sems[name]

    def _ticket(self, ref):
        if ref[0] == "dma":
            return (ref[1], ref[2])
        eng, pos = ref
        m = self.marked[eng]
        i = bisect.bisect_left(m, pos)
        if i < len(m):
            p = m[i]
        else:
            p = pos
            self.cnt[eng] += 1
            self.ops[eng][p][1] = self.cnt[eng]
            m.append(p)
        return ("E_" + eng, self.ops[eng][p][1])

    def _wait(self, eng, ticket):
        name, val = ticket
        if self.seen[eng].get(name, 0) >= val:
            return
        self.seen[eng][name] = val
        semh = self.sem(name)
        self.ops[eng].append([lambda e, s=semh, v=val: e.wait_ge(s, v), None, True])

    def _deps(self, eng, reads, writes):
        refs = []
        for r in reads:
            w = self.last_write.get(r)
            if w is not None:
                refs.append(w)
        for w_ in writes:
            w = self.last_write.get(w_)
            if w is not None:
                refs.append(w)
            refs.extend(self.readers.get(w_, {}).values())
        return [r for r in refs if not (r[0] == "tensor" and eng == "tensor")]

    def _update(self, ref, rkey, reads, writes):
        for r in reads:
            self.readers.setdefault(r, {})[rkey] = ref
        for w in writes:
            self.last_write[w] = ref
            self.readers[w] = {}

    def op(self, eng, fn, reads=(), writes=()):
        for ref in self._deps(eng, reads, writes):
            self._wait(eng, self._ticket(ref))
        self.ops[eng].append([fn, None, False])
        ref = (eng, len(self.ops[eng]) - 1)
        self._update(ref, eng, reads, writes)
        return ref

    def dma(self, q, out, in_, semkey, reads=(), writes=(), **kw):
        for ref in self._deps("dma:" + q, reads, writes):
            self._wait(q, self._ticket(ref))
        name = "D_" + semkey
        self.dmacnt[name] = self.dmacnt.get(name, 0) + 16
        val = self.dmacnt[name]
        semh = self.sem(name)
        self.ops[q].append([lambda e, o=out, i=in_, s=semh, k=kw: e.dma_start(out=o, in_=i, **k).then_inc(s, 16),
                            None, True])
        ref = ("dma", name, val)
        self._update(ref, name, reads, writes)
        return ref

    def barrier(self, skip=()):
        tickets = []
        for e in self.ENGS:
            pos = len(self.ops[e]) - 1
            while pos >= 0 and self.ops[e][pos][2]:
                pos -= 1
            if pos >= 0:
                tickets.append(self._ticket((e, pos)))
        for name, val in self.dmacnt.items():
            if name not in skip:
                tickets.append((name, val))
        for e in self.ENGS:
            for t in tickets:
                self._wait(e, t)

    def replay(self, eng, e):
        semh = self.sem("E_" + eng) if self.cnt[eng] else None
        for o in self.ops[eng]:
            ins = o[0](e)
            if o[1] is not None:
                ins.then_inc(semh, 1)


def _coll(P, ins_ap, outs_ap, semkey, reads, writes, groups):
    for ref in P._deps("dma:gpsimd", reads, writes):
        P._wait("gpsimd", P._ticket(ref))
    name = "C_" + semkey
    P.dmacnt[name] = P.dmacnt.get(name, 0) + 1
    val = P.dmacnt[name]
    semh = P.sem(name)
    P.ops["gpsimd"].append([lambda e: e.collective_compute(
        "AllGather", ALU.bypass, replica_groups=groups, ins=[ins_ap.opt()], outs=[outs_ap.opt()]).then_inc(semh),
        None, True])
    ref = ("dma", name, val)
    P._update(ref, name, reads, writes)
    return ref


class Cfg:
    def __init__(self, RP=2048, RS=4096, DFF=2816, layers=("F", "A", "F", "A"), ring=6):
        self.RP, self.RS = RP, RS
        self.NTP, self.NTS = RP // 512, RS // 512
        self.NBP = 4 * RP // 128
        self.KBP = self.NBP // 4
        self.NBS = RS // 128
        assert self.NBS % 16 == 0 and self.NBP % 16 == 0
        self.DFF = DFF
        self.FC = DFF // 128
        assert self.FC % 2 == 0
        self.layers = tuple(layers)
        self.L = len(layers)
        self.ring = ring
        self.slot = {}
        n = 0
        for l, t in enumerate(self.layers):
            if t == "F":
                self.slot[("wf", l)] = n; n += 4
            else:
                self.slot[("wq", l)] = n; n += 4
                self.slot[("wkv", l)] = n; n += 2
                self.slot[("wo", l)] = n; n += 4
            self.slot[("wgu", l)] = n; n += self.FC
            self.slot[("wd", l)] = n; n += self.FC // 2
        self.nslots = n
        self.groups = [[0, 1, 2, 3], [4, 5, 6, 7]]


class Builder:
    def __init__(self, cfg):
        self.c = cfg
        self.nc = bass.Bass("TRN2", target_bir_lowering=False)

    def build(self):
        c, nc = self.c, self.nc
        RP, RS = c.RP, c.RS
        dt = nc.dram_tensor
        ein = lambda name, shape, dtype=F32: dt(name, shape, dtype, kind="ExternalInput").ap()
        itn = lambda name, shape, dtype: dt(name, shape, dtype).ap()
        self.xin = {"P": ein("xp", [RP, D]), "S": ein("xs", [RS, D])}
        self.yout = {"P": dt("yp", [RP, D], F32, kind="ExternalOutput").ap(),
                     "S": dt("ys", [RS, D], F32, kind="ExternalOutput").ap()}
        self.xscr = {"P": itn("xscrp", [RP, D], F32), "S": itn("xscrs", [RS, D], F32)}
        self.wall = ein("wall", [c.nslots * 128, 2048])
        self.gmixT_d = ein("gmixT", [128, c.L * 8])
        self.gffnT_d = ein("gffnT", [128, c.L * 8])
        self.gmixrep_d = ein("gmixrep", [128, c.L, D])
        self.qkg_d = ein("qkg", [128, c.L, 2, 128])
        self.cs_d = {"P": ein("csP", [c.NTP, 128, 4, 128]), "S": ein("csS", [c.NTS, 128, 4, 128])}
        self.dftA_d = ein("dftA", [128, 256])
        self.dftC_d = ein("dftC", [128, 256])
        self.dftB_d = {"P": ein("dftBP", [2, 2 * c.NBP, 128 * c.KBP]), "S": ein("dftBS", [2, 2 * c.NBS, 128 * c.NBS])}
        self.ident_d = ein("ident", [128, 128])
        self.wbf = itn("wbf", [c.nslots * 128, 2048], BF16)
        self.mixT = {"P": itn("mixTP", [8, 128, RP], BF16), "S": itn("mixTS", [8, 128, RS], BF16)}
        self.hloc = [itn("hloc%d" % t, [512, D], BF16) for t in range(c.NTP)]
        self.hall = [itn("hall%d" % t, [4 * 512, D], BF16) for t in range(c.NTP)]
        self.kloc = itn("kloc", [256, RP], BF16)
        self.kall = itn("kall", [1024, RP], BF16)
        self.vloc = itn("vloc", [RP, 256], BF16)
        self.vall = itn("vall", [4 * RP, 256], BF16)
        self.NT = {"P": c.NTP, "S": c.NTS}

        with contextlib.ExitStack() as st:
            self.st = st
            self.P = P = Prog(nc, st)
            sb = lambda name, shape, dtype: st.enter_context(nc.sbuf_tensor("s_" + name, shape, dtype))
            self.pp = [st.enter_context(nc.psum_tensor("pp%d" % i, [128, 1024], F32)) for i in range(4)]
            self.ps = [self.pp[i // 2][:, (i % 2) * 512:(i % 2 + 1) * 512] for i in range(8)]
            self.ident = sb("ident", [128, 128], F32)
            self.identb = sb("identb", [128, 128], BF16)
            self.onesb = sb("onesb", [128, 128], BF16)
            self.gmixT = sb("gmixT", [128, c.L * 8], F32)
            self.gffnT = sb("gffnT", [128, c.L * 8], F32)
            self.xtall = sb("xtall", [128, 8 * D], F32)
            self.xt = [self.xtall[:, i * 4 * D:(i + 1) * 4 * D].rearrange("p (s c) -> p s c", c=D) for i in range(2)]
            self.hn = [sb("hn%d" % i, [128, D], F32) for i in range(2)]
            self.hT = sb("hT", [128, KC, 512], BF16)
            self.act = sb("act", [128, c.FC, 512], BF16)
            self.sg = [sb("sg%d" % i, [128, 512], F32) for i in range(2)]
            self.ringb = [sb("ring%d" % i, [128, 2048], BF16) for i in range(c.ring)]
            self.junk = sb("junk", [128, D], BF16)
            self.ss = sb("ss", [128, 8], F32)
            self.rt = sb("rt", [128, 8], F32)
            self.rstd = sb("rstd", [128, 8], F32)
            self.epsb = sb("epsb", [128, 1], F32)
            self.ring_pos = 0
            self.conv_pos = 0
            self.conv_i = 0
            self.tilecnt = 0

            self.load_consts()
            for l, t in enumerate(c.layers):
                if t == "F":
                    self.fourier_layer(l)
                else:
                    self.attn_layer(l)
                P.barrier()
            P.barrier()

            with nc.Block() as block:
                @block.sync
                def _(e):
                    P.replay("sync", e)

                @block.scalar
                def _(e):
                    P.replay("scalar", e)

                @block.vector
                def _(e):
                    P.replay("vector", e)

                @block.gpsimd
                def _(e):
                    P.replay("gpsimd", e)

                @block.tensor
                def _(e):
                    P.replay("tensor", e)
        return nc

    def src(self, l, part):
        return self.xin[part] if l == 0 else self.xscr[part]

    def dst(self, l, part):
        return self.yout[part] if l == self.c.L - 1 else self.xscr[part]

    def load_consts(self):
        P = self.P
        P.dma("sync", self.ident[:, :], self.ident_d[:, :], "c0", writes=["ident"])
        P.dma("sync", self.gmixT[:, :], self.gmixT_d[:, :], "c1", writes=["gmixT"])
        P.dma("sync", self.gffnT[:, :], self.gffnT_d[:, :], "c2", writes=["gffnT"])
        P.op("vector", lambda e: e.tensor_copy(out=self.identb[:, :], in_=self.ident[:, :]),
             reads=["ident"], writes=["identb"])
        P.op("vector", lambda e: e.memset(self.onesb[:, :], 1.0), writes=["onesb"])
        P.op("vector", lambda e: e.memset(self.epsb[:, :], EPS), writes=["epsb"])

    def conv_some(self, k, after=()):
        c, P = self.c, self.P
        while k > 0 and self.conv_pos < c.nslots:
            s = self.conv_pos
            n = min(4, c.nslots - s)
            P.dma("gpsimd", self.wbf[s * 128:(s + n) * 128, :], self.wall[s * 128:(s + n) * 128, :],
                  "cv%d" % (self.conv_i % 4), reads=list(after), writes=[("wbf", j) for j in range(s, s + n)])
            self.conv_pos += n
            self.conv_i += 1
            k -= 1

    def conv_until(self, slot_end):
        while self.conv_pos < min(slot_end, self.c.nslots):
            self.conv_some(1)

    def convert_weights(self):
        self.conv_until(self.c.nslots)

    def next_xt(self):
        b = self.tilecnt % 2
        self.tilecnt += 1
        return b

    def load_tile(self, l, part, t, b):
        src = self.src(l, part)
        self.P.dma("sync", self.xt[b][:, :, :],
                   src[t * 512:(t + 1) * 512, :].rearrange("(s p) c -> p s c", p=128),
                   "xt%d" % b, reads=[("X", part, l, t)], writes=[(("xt", b), s) for s in range(4)])

    def store_tile(self, l, part, t, b):
        dst = self.dst(l, part)
        self.P.dma("scalar", dst[t * 512:(t + 1) * 512, :].rearrange("(s p) c -> p s c", p=128),
                   self.xt[b][:, :, :], "st%d" % b,
                   reads=[(("xt", b), s) for s in range(4)], writes=[("X", part, l + 1, t)])

    def ring_load(self, slot):
        P = self.P
        b = self.ring_pos % self.c.ring
        self.ring_pos += 1
        key = ("ring", b)
        P.dma("sync", self.ringb[b][:, :], self.wbf[slot * 128:(slot + 1) * 128, :], "ring%d" % b,
              reads=[("wbf", slot)], writes=[key])
        return self.ringb[b], key

    def norm_to_hT(self, xt, xkey, gT, gkey, gcol, pre=None):
        P = self.P

        def stat(s):
            if pre is not None:
                pre(s)
            P.op("scalar", lambda e, s=s: e.activation(out=self.junk[:, :], in_=xt[:, s, :], func=AF.Square,
                                                      accum_out=self.ss[:, s:s + 1]),
                 reads=[(xkey, s)], writes=[("ss", s), "junkN"])
            P.op("scalar", lambda e, s=s: e.activation(out=self.rt[:, s:s + 1], in_=self.ss[:, s:s + 1], func=AF.Sqrt,
                                                      bias=self.epsb[:, 0:1], scale=1.0 / D),
                 reads=[("ss", s), "epsb"], writes=[("rt", s)])

        def scale(s):
            hb = self.hn[s % 2]
            P.op("vector", lambda e, s=s: e.reciprocal(out=self.rstd[:, s:s + 1], in_=self.rt[:, s:s + 1]),
                 reads=[("rt", s)], writes=[("rstd", s)])
            P.op("vector", lambda e, s=s, hb=hb: e.tensor_scalar(out=hb[:, :], in0=xt[:, s, :],
                                                                scalar1=self.rstd[:, s:s + 1], scalar2=None,
                                                                op0=ALU.mult),
                 reads=[(xkey, s), ("rstd", s)], writes=[("hn", s % 2)])

        def trans(s):
            hb = self.hn[s % 2]
            hk = ("hn", s % 2)
            for half in range(2):
                bank = 2 * (s % 2) + half
                for j in range(4):
                    kc = half * 4 + j
                    P.op("tensor", lambda e, hb=hb, kc=kc, j=j, bank=bank: e.transpose(
                        out=self.ps[bank][:, j * 128:(j + 1) * 128], in_=hb[:, kc * 128:(kc + 1) * 128],
                        identity=self.ident[:, :]),
                        reads=[hk, "ident"], writes=[("ps", bank)])

        def evac(s):
            for half in range(2):
                bank = 2 * (s % 2) + half
                g0 = gcol + half * 4
                P.op("vector", lambda e, s=s, half=half, bank=bank, g0=g0: e.tensor_tensor(
                    out=self.hT[:, half * 4:half * 4 + 4, s * 128:(s + 1) * 128],
                    in0=self.ps[bank][:, :].rearrange("p (j t) -> p j t", j=4),
                    in1=gT[:, g0:g0 + 4].unsqueeze(2).broadcast_to([128, 4, 128]),
                    op=ALU.mult),
                    reads=[("ps", bank), gkey], writes=[("hT", s)])

        stat(0)
        stat(1)
        scale(0)
        for s in range(4):
            if s + 2 < 4:
                stat(s + 2)
            if s + 1 < 4:
                scale(s + 1)
            trans(s)
            evac(s)

    def ffn_tile(self, l, xt, xkey, pre=None):
        c, P = self.c, self.P
        self.norm_to_hT(xt, xkey, self.gffnT, "gffnT", l * 8, pre=pre)
        hTr = [("hT", s) for s in range(4)]
        s_gu = c.slot[("wgu", l)]
        s_d = c.slot[("wd", l)]
        for fc in range(c.FC):
            w, wk = self.ring_load(s_gu + fc)
            par = fc % 2
            pg, pu = self.ps[4 + 2 * par], self.ps[5 + 2 * par]
            kg, ku = ("ps", 4 + 2 * par), ("ps", 5 + 2 * par)
            for kc in range(KC):
                P.op("tensor", lambda e, w=w, kc=kc, pg=pg: e.matmul(
                    pg[:, :], lhsT=w[:, kc * 128:(kc + 1) * 128], rhs=self.hT[:, kc, :],
                    start=(kc == 0), stop=(kc == KC - 1)), reads=[wk] + hTr, writes=[kg])
            for kc in range(KC):
                P.op("tensor", lambda e, w=w, kc=kc, pu=pu: e.matmul(
                    pu[:, :], lhsT=w[:, 1024 + kc * 128:1024 + (kc + 1) * 128], rhs=self.hT[:, kc, :],
                    start=(kc == 0), stop=(kc == KC - 1)), reads=[wk] + hTr, writes=[ku])
            sgb = self.sg[par]
            P.op("scalar", lambda e, pg=pg, sgb=sgb: e.activation(out=sgb[:, :], in_=pg[:, :], func=AF.Silu),
                 reads=[kg], writes=[("sg", par)])
            P.op("vector", lambda e, pu=pu, sgb=sgb, fc=fc: e.tensor_tensor(
                out=self.act[:, fc, :], in0=pu[:, :], in1=sgb[:, :], op=ALU.mult),
                reads=[ku, ("sg", par)], writes=[("act", fc)])
        for j in range(c.FC // 2):
            w, wk = self.ring_load(s_d + j)
            for i in range(2):
                fc = 2 * j + i
                for s in range(4):
                    for half in range(2):
                        bank = 2 * s + half
                        P.op("tensor", lambda e, w=w, i=i, fc=fc, s=s, half=half, bank=bank: e.matmul(
                            self.ps[bank][:, :], lhsT=self.act[:, fc, s * 128:(s + 1) * 128],
                            rhs=w[:, i * 1024 + half * 512:i * 1024 + (half + 1) * 512],
                            start=(fc == 0), stop=(fc == c.FC - 1)),
                            reads=[wk, ("act", fc)], writes=[("ps", bank)])
        for s in range(4):
            for half in range(2):
                bank = 2 * s + half
                P.op("vector", lambda e, s=s, half=half, bank=bank: e.tensor_tensor(
                    out=xt[:, s, half * 512:(half + 1) * 512], in0=self.ps[bank][:, :],
                    in1=xt[:, s, half * 512:(half + 1) * 512], op=ALU.add),
                    reads=[("ps", bank), (xkey, s)], writes=[(xkey, s)])

    def add_psum_to_xt(self, xt, xkey, s):
        if True:
            for half in range(2):
                bank = 2 * s + half
                self.P.op("vector", lambda e, s=s, half=half, bank=bank: e.tensor_tensor(
                    out=xt[:, s, half * 512:(half + 1) * 512], in0=self.ps[bank][:, :],
                    in1=xt[:, s, half * 512:(half + 1) * 512], op=ALU.add),
                    reads=[("ps", bank), (xkey, s)], writes=[(xkey, s)])


    def fft_part(self, l, part, fb):
        c, P = self.c, self.P
        NB = c.NBP if part == "P" else c.NBS
        KB = c.KBP if part == "P" else c.NBS
        NT = self.NT[part]
        A, C, B, T2, hb, xg = fb["A"], fb["C"], fb["B" + part], fb["T2"], fb["hb"], fb["xg"]
        Bkey = "B" + part
        T1 = self.xtall[:, :].bitcast(BF16)[:, 0:NB * 256].rearrange("p (b k) -> p b k", k=256)
        T1f = self.xtall[:, :].bitcast(BF16)
        YT = self.act[:, :, :].rearrange("p f t -> p (f t)")[:, 0:128 * KB]
        KG = 512 // KB
        if part == "S":
            src = self.src(l, "S")
            allX = [("X", "S", l, t) for t in range(NT)]
            srcv = src.rearrange("(a b) c -> a b c", b=NB)
            ssq, rtq, rsq, grep = fb["ssq"], fb["rtq"], fb["rsq"], fb["grep"]
            nbs = NB // 16
            hbv = lambda h: h[:, :, :].rearrange("p b c -> p (b c)").bitcast(F32)[:, 0:NB * 64].rearrange("p (b c) -> p b c", c=D)
            bufs = [xg[:, :].rearrange("p (b c) -> p b c", c=D), fb["xg2"][:, :].rearrange("p (b c) -> p b c", c=D),
                    hbv(hb[0]), hbv(hb[1])]
            bkeys = [("xg", 0), ("xg", 1), ("hb", 0), ("hb", 1)]
            for i in range(16):
                bf, bk = bufs[i % 4], bkeys[i % 4]
                P.dma("sync", bf[:, 0:nbs, :], srcv[:, i * nbs:(i + 1) * nbs, :], "fs%d" % (i % 4),
                      reads=allX, writes=[bk])
                for q in range(nbs):
                    b = i * nbs + q
                    jk = b % 4
                    P.op("scalar", lambda e, bf=bf, q=q, b=b, jk=jk: e.activation(
                        out=T1f[:, jk * D:(jk + 1) * D], in_=bf[:, q, :], func=AF.Square, accum_out=ssq[:, b:b + 1]),
                        reads=[bk], writes=[("ssq", b), ("jk", jk)])
            P.op("scalar", lambda e: e.activation(out=rtq[:, 0:NB], in_=ssq[:, 0:NB], func=AF.Sqrt,
                                                   bias=self.epsb[:, 0:1], scale=1.0 / D),
                 reads=[("ssq", b) for b in range(NB)] + ["epsb"], writes=["rtq"])
            P.op("vector", lambda e: e.reciprocal(out=rsq[:, 0:NB], in_=rtq[:, 0:NB]), reads=["rtq"], writes=["rsq"])
            hnb = NB // 2
            xgv2 = [xg[:, :].rearrange("p (b c) -> p b c", c=128), fb["xg2"][:, :].rearrange("p (b c) -> p b c", c=128)]
        else:
            npiece = 4 * c.NTP
        def prep(g):
            hbg = hb[g % 2]
            hk = ("hb", g % 2)
            if part == "S":
                for half in range(2):
                    xv = xgv2[half]
                    xk = ("xg", half)
                    P.dma("sync", xv[:, :, :], srcv[:, half * hnb:(half + 1) * hnb, g * 128:(g + 1) * 128],
                          "fx%d" % half, reads=allX, writes=[xk])
                    P.op("vector", lambda e, half=half, xv=xv: e.tensor_tensor(
                        out=xv[:, :, :], in0=xv[:, :, :],
                        in1=rsq[:, half * hnb:(half + 1) * hnb].unsqueeze(2).broadcast_to([128, hnb, 128]),
                        op=ALU.mult), reads=[xk, "rsq"], writes=[xk])
                    P.op("gpsimd", lambda e, half=half, g=g, hbg=hbg, xv=xv: e.tensor_tensor(
                        out=hbg[:, half * hnb:(half + 1) * hnb, :], in0=xv[:, :, :],
                        in1=grep[:, g * 128:(g + 1) * 128].unsqueeze(1).broadcast_to([128, hnb, 128]),
                        op=ALU.mult), reads=[xk, "grep"], writes=[hk])
            else:
                apt, apr = 512 // NB, c.RP // NB
                k = 0
                for r in range(4):
                    for t in range(c.NTP):
                        p0 = r * apr + t * apt
                        P.dma("sync", hbg[p0:p0 + apt, 0:NB, :],
                              self.hall[t][r * 512:(r + 1) * 512, g * 128:(g + 1) * 128].rearrange("(a b) c -> a b c", b=NB),
                              "fq%d" % (k % 4), reads=[("hall", tt) for tt in range(c.NTP)], writes=[(hk, k)])
                        k += 1

        prep(0)
        for g in range(8):
            hbg = hb[g % 2]
            hk = ("hb", g % 2)
            hbreads = [hk] if part == "S" else [(hk, k) for k in range(npiece)]
            for b2 in range(NB // 2):
                bank = b2 % 2
                for q in range(2):
                    b = 2 * b2 + q
                    P.op("tensor", lambda e, b=b, q=q, bank=bank, hbg=hbg: e.matmul(
                        self.ps[bank][:, q * 256:(q + 1) * 256], lhsT=hbg[:, b, :], rhs=A[:, :],
                        start=True, stop=True), reads=hbreads + ["A"], writes=[("ps", bank)])
                dstap = T1[:, 2 * b2:2 * b2 + 2, :]
                srcap = self.ps[bank].rearrange("p (q k) -> p q k", q=2)
                if b2 % 2 == 0:
                    P.op("scalar", lambda e, d=dstap, s_=srcap: e.copy(out=d, in_=s_),
                         reads=[("ps", bank)], writes=["T1"])
                else:
                    P.op("vector", lambda e, d=dstap, s_=srcap: e.tensor_copy(out=d, in_=s_),
                         reads=[("ps", bank)], writes=["T1"])
            if g + 1 < 8:
                prep(g + 1)
            self.conv_some(3, after=[("mixT", part, g - 1)] if g > 0 else [])
            T1v = T1.rearrange("p b (ri k) -> p b ri k", ri=2)
            YTv = YT.rearrange("p (kb ka) -> p ka kb", ka=128)
            def emit_s2(ka2):
                bank = 2 + ka2 % 2
                t2 = T2[ka2 % 2]
                t2k = ("T2", ka2 % 2)
                for q in range(2):
                    ka = 2 * ka2 + q
                    P.op("tensor", lambda e, ka=ka, q=q, bank=bank: e.matmul(
                        self.ps[bank][0:2 * NB, q * 256:(q + 1) * 256], lhsT=T1v[:, :, :, ka], rhs=C[:, :],
                        start=True, stop=True), reads=["T1", "C"], writes=[("ps", bank)])
                if ka2 % 2 == 0:
                    P.op("scalar", lambda e, t2=t2, bank=bank: e.copy(out=t2[0:2 * NB, :], in_=self.ps[bank][0:2 * NB, :]),
                         reads=[("ps", bank)], writes=[t2k])
                else:
                    P.op("vector", lambda e, t2=t2, bank=bank: e.tensor_copy(out=t2[0:2 * NB, :], in_=self.ps[bank][0:2 * NB, :]),
                         reads=[("ps", bank)], writes=[t2k])

            def emit_s3(ka2):
                t2 = T2[ka2 % 2]
                t2k = ("T2", ka2 % 2)
                for q in range(2):
                    ka = 2 * ka2 + q
                    kg = ka // KG
                    bank3 = 4 + kg % 2
                    col = (ka % KG) * KB
                    for w in range(2):
                        P.op("tensor", lambda e, t2=t2, q=q, w=w, ka=ka, bank3=bank3, col=col: e.matmul(
                            self.ps[bank3][:, col:col + KB],
                            lhsT=t2[0:2 * NB, q * 256 + w * 128:q * 256 + (w + 1) * 128],
                            rhs=B[0:2 * NB, w, ka * KB:(ka + 1) * KB],
                            start=(w == 0), stop=(w == 1)),
                            reads=[t2k, Bkey], writes=[("ps", bank3)])
                    if ka % KG == KG - 1:
                        ka0 = ka - KG + 1
                        dstap = YTv[:, ka0:ka0 + KG, :]
                        srcap = self.ps[bank3].rearrange("p (k b) -> p k b", b=KB)
                        if kg % 2 == 0:
                            P.op("vector", lambda e, d=dstap, s_=srcap: e.tensor_copy(out=d, in_=s_),
                                 reads=[("ps", bank3)], writes=["YT"])
                        else:
                            P.op("scalar", lambda e, d=dstap, s_=srcap: e.copy(out=d, in_=s_),
                                 reads=[("ps", bank3)], writes=["YT"])

            emit_s2(0)
            for ka2 in range(64):
                if ka2 + 1 < 64:
                    emit_s2(ka2 + 1)
                emit_s3(ka2)
            P.dma("sync", self.mixT[part][g, :, :], YT, "fy", reads=["YT"], writes=[("mixT", part, g)])

    def fourier_layer(self, l):
        c, P, nc = self.c, self.P, self.nc
        with contextlib.ExitStack() as ls:
            sb = lambda name, shape, dtype: ls.enter_context(nc.sbuf_tensor("f%d_" % l + name, shape, dtype))
            fb = {}
            fb["grep"] = grep = sb("grep", [128, D], F32)
            fb["A"] = A = sb("A", [128, 256], BF16)
            fb["C"] = C = sb("C", [128, 256], BF16)
            fb["BP"] = sb("BP", [128, 2, 128 * c.KBP], BF16)
            fb["BS"] = sb("BS", [128, 2, 128 * c.NBS], BF16)
            fb["xg"] = sb("xg", [128, c.NBS * 64], F32)
            fb["xg2"] = sb("xg2", [128, c.NBS * 64], F32)
            fb["hb"] = [sb("hb%d" % i, [128, max(c.NBP, c.NBS), 128], BF16) for i in range(2)]
            fb["T2"] = [sb("T2_%d" % i, [128, 512], BF16) for i in range(2)]
            fb["ssq"] = sb("ssq", [128, c.NBS], F32)
            fb["rtq"] = sb("rtq", [128, c.NBS], F32)
            fb["rsq"] = sb("rsq", [128, c.NBS], F32)
            mT = sb("mT", [128, 8, 512], BF16)

            P.dma("sync", grep[:, :], self.gmixrep_d[:, l, :], "f0", writes=["grep"])
            P.dma("gpsimd", A[:, :], self.dftA_d[:, :], "f1", writes=["A"])
            P.dma("gpsimd", C[:, :], self.dftC_d[:, :], "f2", writes=["C"])
            k = 0
            for part, nbp, ncol in (("S", c.NBS, 128 * c.NBS), ("P", c.NBP, 128 * c.KBP)):
                ch = min(2048, ncol)
                for w in range(2):
                    for i in range(ncol // ch):
                        P.dma("gpsimd", fb["B" + part][0:2 * nbp, w, i * ch:(i + 1) * ch],
                              self.dftB_d[part][w, :, i * ch:(i + 1) * ch], "fb%d" % (k % 4),
                              writes=["B" + part])
                        k += 1

            hst = mT[:, :, :].rearrange("p g t -> p (g t)").rearrange("p (s c) -> p s c", c=D)
            for t in range(c.NTP):
                b = self.next_xt()
                xt, xkey = self.xt[b], ("xt", b)
                self.load_tile(l, "P", t, b)
                for s in range(4):
                    P.op("scalar", lambda e, s=s, xt=xt: e.activation(out=self.hn[s % 2][:, :], in_=xt[:, s, :], func=AF.Square,
                                                                     accum_out=self.ss[:, s:s + 1]),
                         reads=[(xkey, s)], writes=[("ss", s), ("hn", s % 2)])
                P.op("scalar", lambda e: e.activation(out=self.rt[:, 0:4], in_=self.ss[:, 0:4], func=AF.Sqrt,
                                                       bias=self.epsb[:, 0:1], scale=1.0 / D),
                     reads=[("ss", s) for s in range(4)] + ["epsb"], writes=["rt"])
                P.op("vector", lambda e: e.reciprocal(out=self.rstd[:, 0:4], in_=self.rt[:, 0:4]),
                     reads=["rt"], writes=["rstd"])
                for s in range(4):
                    hb_, hk = self.hn[s % 2], ("hn", s % 2)
                    P.op("vector", lambda e, s=s, hb_=hb_, xt=xt: e.tensor_scalar(
                        out=hb_[:, :], in0=xt[:, s, :], scalar1=self.rstd[:, s:s + 1], scalar2=None, op0=ALU.mult),
                        reads=[(xkey, s), "rstd"], writes=[hk])
                    P.op("vector", lambda e, s=s, hb_=hb_: e.tensor_tensor(
                        out=hst[:, s, :], in0=hb_[:, :], in1=grep[:, :], op=ALU.mult),
                        reads=[hk, "grep"], writes=[("hst", s)])
                P.dma("sync", self.hloc[t].rearrange("(s p) c -> p s c", p=128), hst,
                      "fh", reads=[("hst", s) for s in range(4)], writes=[("hloc", t)])
                _coll(P, self.hloc[t], self.hall[t], "ag", [("hloc", t)], [("hall", t)], c.groups)
            P.barrier(skip=("C_ag",) + tuple("D_cv%d" % i for i in range(4)))

            self.fft_part(l, "S", fb)
            P.barrier(skip=("C_ag",) + tuple("D_cv%d" % i for i in range(4)))
            self.fft_part(l, "P", fb)
            P.barrier(skip=tuple("D_cv%d" % i for i in range(4)))

            self.convert_weights()
            s_wf = c.slot[("wf", l)]
            for part in ("P", "S"):
                mixv = self.mixT[part].rearrange("g k r -> k g r")
                for t in range(self.NT[part]):
                    b = self.next_xt()
                    xt, xkey = self.xt[b], ("xt", b)
                    self.load_tile(l, part, t, b)
                    P.dma("sync", mT[:, :, :], mixv[:, :, t * 512:(t + 1) * 512], "fm",
                          reads=[("mixT", part, g) for g in range(8)], writes=["mT"])
                    ws = [self.ring_load(s_wf + j) for j in range(4)]
                    for s in range(4):
                        for half in range(2):
                            bank = 2 * s + half
                            for g in range(8):
                                w, wk = ws[g // 2]
                                P.op("tensor", lambda e, w=w, g=g, s=s, half=half, bank=bank: e.matmul(
                                    self.ps[bank], lhsT=mT[:, g, s * 128:(s + 1) * 128],
                                    rhs=w[:, (g % 2) * 1024 + half * 512:(g % 2) * 1024 + (half + 1) * 512],
                                    start=(g == 0), stop=(g == 7)),
                                    reads=[wk, "mT"], writes=[("ps", bank)])
                    self.ffn_tile(l, xt, xkey, pre=lambda s, xt=xt, xkey=xkey: self.add_psum_to_xt(xt, xkey, s))
                    self.store_tile(l, part, t, b)
    def qk_norm(self, tmp, nh, pss, gain, gkey):
        P = self.P
        qn, qr = tmp["qn"], tmp["qr"]
        t1, t2, t3, t4 = tmp["t"]
        h0 = 0
        for (psap, bank, n) in pss:
            for i in range(n):
                P.op("scalar", lambda e, psap=psap, i=i, h0=h0: e.activation(
                    out=self.junk[:, (h0 + i) * 128:(h0 + i + 1) * 128], in_=psap[:, i * 128:(i + 1) * 128],
                    func=AF.Square, accum_out=self.ss[:, h0 + i:h0 + i + 1]),
                    reads=[("ps", bank)], writes=[("ssh", h0 + i), ("junk", h0 + i)])
            h0 += n
        P.op("scalar", lambda e: e.activation(out=self.rt[:, 0:nh], in_=self.ss[:, 0:nh], func=AF.Sqrt,
                                               bias=self.epsb[:, 0:1], scale=1.0 / HD),
             reads=[("ssh", i) for i in range(nh)] + ["epsb"], writes=["rth"])
        P.op("vector", lambda e: e.reciprocal(out=self.rstd[:, 0:nh], in_=self.rt[:, 0:nh]),
             reads=["rth"], writes=["rstdh"])
        h0 = 0
        for (psap, bank, n) in pss:
            P.op("vector", lambda e, psap=psap, n=n, h0=h0: e.tensor_tensor(
                out=qn[:, h0:h0 + n, :], in0=psap.rearrange("p (h d) -> p h d", d=128),
                in1=self.rstd[:, h0:h0 + n].unsqueeze(2).broadcast_to([128, n, 128]), op=ALU.mult),
                reads=[("ps", bank), "rstdh"], writes=["qn"] + tmp.get("fence", []))
            h0 += n
        P.op("gpsimd", lambda e: e.tensor_tensor(
            out=qn[:, 0:nh, :], in0=qn[:, 0:nh, :],
            in1=gain.unsqueeze(1).broadcast_to([128, nh, 128]), op=ALU.mult),
            reads=["qn", gkey], writes=["qn"])

    def qk_rope(self, tmp, nh, cs, cskey, s):
        P = self.P
        qn, qr = tmp["qn"], tmp["qr"]
        t1, t2, t3, t4 = tmp["t"]
        qv = qn[:, 0:nh, :].rearrange("p h (i two) -> p h i two", two=2)
        rv = qr[:, 0:nh, :].rearrange("p h (i two) -> p h i two", two=2)
        x0, x1 = qv[:, :, :, 0], qv[:, :, :, 1]
        cc = cs[:, s, 0:64].unsqueeze(1).broadcast_to([128, nh, 64])
        sn = cs[:, s, 64:128].unsqueeze(1).broadcast_to([128, nh, 64])
        P.op("gpsimd", lambda e: e.tensor_tensor(out=t1[:, 0:nh, :], in0=x0, in1=cc, op=ALU.mult),
             reads=["qn", cskey], writes=["t1"])
        P.op("gpsimd", lambda e: e.tensor_tensor(out=t2[:, 0:nh, :], in0=x1, in1=sn, op=ALU.mult),
             reads=["qn", cskey], writes=["t2"])
        P.op("gpsimd", lambda e: e.tensor_tensor(out=rv[:, :, :, 0], in0=t1[:, 0:nh, :], in1=t2[:, 0:nh, :],
                                                 op=ALU.subtract), reads=["t1", "t2"], writes=["qr0"])
        P.op("vector", lambda e: e.tensor_tensor(out=t3[:, 0:nh, :], in0=x0, in1=sn, op=ALU.mult),
             reads=["qn", cskey], writes=["t3"])
        P.op("vector", lambda e: e.tensor_tensor(out=t4[:, 0:nh, :], in0=x1, in1=cc, op=ALU.mult),
             reads=["qn", cskey], writes=["t4"])
        P.op("vector", lambda e: e.tensor_tensor(out=rv[:, :, :, 1], in0=t3[:, 0:nh, :], in1=t4[:, 0:nh, :],
                                                 op=ALU.add), reads=["t3", "t4"], writes=["qr1"])


    def attn_layer(self, l):
        c, P, nc = self.c, self.P, self.nc
        scale = float(HD) ** -0.5
        NBmax = max(c.NBP, c.NBS)
        with contextlib.ExitStack() as ls:
            sb = lambda name, shape, dtype: ls.enter_context(nc.sbuf_tensor("a%d_" % l + name, shape, dtype))
            KT = sb("KT", [128, 2, NBmax * 128], BF16)
            Vb = sb("Vb", [128, NBmax, 256], BF16)
            QT = sb("QT", [128, 8, 512], BF16)
            OT = sb("OT", [128, 8, 512], BF16)
            NPT = 4
            pT = [sb("pT%d" % i, [128, 1024], BF16) for i in range(NPT)]
            s2 = [sb("s2_%d" % i, [128, 512], BF16) for i in range(NPT)]
            rinv = sb("rinv", [128, 512], F32)
            ou = sb("ou", [128, 512], F32)
            cs = sb("cs", [128, 4, 128], F32)
            qkg = sb("qkg", [128, 2, 128], F32)
            flat = self.act[:, :, :].rearrange("p f t -> p (f t)").bitcast(F32)
            assert c.FC * 256 >= 3584
            tmp = {
                "fence": [("act", fc) for fc in range(c.FC)],
                "qn": flat[:, 0:1024].rearrange("p (h d) -> p h d", d=128),
                "t": [flat[:, 1024 + 512 * i:1024 + 512 * (i + 1)].rearrange("p (h d) -> p h d", d=64) for i in range(4)],
                "qr": flat[:, 3072:3584].bitcast(BF16).rearrange("p (h d) -> p h d", d=128),
            }
            P.dma("sync", qkg[:, :, :], self.qkg_d[:, l, :, :], "a0", writes=["qkg"])
            self.convert_weights()
            s_q, s_kv, s_o = c.slot[("wq", l)], c.slot[("wkv", l)], c.slot[("wo", l)]

            def pass1(part):
                for ta in range(self.NT[part]):
                    b = self.next_xt()
                    xt, xkey = self.xt[b], ("xt", b)
                    self.load_tile(l, part, ta, b)
                    P.dma("sync", cs[:, :, :], self.cs_d[part][ta], "a1", writes=["cs"])
                    self.norm_to_hT(xt, xkey, self.gmixT, "gmixT", l * 8)
                    ws = [self.ring_load(s_kv + j) for j in range(2)]
                    def mm1(s, ws=ws):
                        bank = 4 + s % 2
                        for kc in range(KC):
                            w, wk = ws[kc // 4]
                            P.op("tensor", lambda e, w=w, kc=kc, s=s, bank=bank: e.matmul(
                                self.ps[bank], lhsT=self.hT[:, kc, s * 128:(s + 1) * 128],
                                rhs=w[:, (kc % 4) * 512:(kc % 4 + 1) * 512],
                                start=(kc == 0), stop=(kc == KC - 1)),
                                reads=[wk, ("hT", s)], writes=[("ps", bank)])
                    def norm1(s):
                        bank = 4 + s % 2
                        self.qk_norm(tmp, 2, [(self.ps[bank][:, 0:256], bank, 2)], qkg[:, 1, :], "qkg")
                    mm1(0)
                    norm1(0)
                    for s in range(4):
                        sa = 4 * ta + s
                        bank = 4 + s % 2
                        if s + 1 < 4:
                            mm1(s + 1)
                        P.op("scalar", lambda e, sa=sa, bank=bank: e.copy(out=Vb[:, sa, :], in_=self.ps[bank][:, 256:512]),
                             reads=[("ps", bank)], writes=[("V", sa)])
                        self.qk_rope(tmp, 2, cs, "cs", s)
                        if s + 1 < 4:
                            norm1(s + 1)
                        pb = self.pp[3][:, (s % 2) * 512:(s % 2 + 1) * 512].bitcast(BF16)
                        for kvh in range(2):
                            P.op("tensor", lambda e, kvh=kvh, pb=pb: e.transpose(
                                out=pb[:, kvh * 128:(kvh + 1) * 128], in_=tmp["qr"][:, kvh, :], identity=self.identb[:, :]),
                                reads=["qr0", "qr1", "identb"], writes=[("ps", 6 + s % 2)])
                        P.op("scalar", lambda e, sa=sa, pb=pb: e.copy(
                            out=KT[:, :, sa * 128:(sa + 1) * 128], in_=pb[:, 0:256].rearrange("p (h t) -> p h t", h=2)),
                            reads=[("ps", 6 + s % 2)], writes=[("K", sa)])

            def pass2(part, nbk):
                allK = [("K", sa) for sa in range(nbk)]
                allV = [("V", sa) for sa in range(nbk)]
                for ta in range(self.NT[part]):
                    b = self.next_xt()
                    xt, xkey = self.xt[b], ("xt", b)
                    self.load_tile(l, part, ta, b)
                    P.dma("sync", cs[:, :, :], self.cs_d[part][ta], "a1", writes=["cs"])
                    self.norm_to_hT(xt, xkey, self.gmixT, "gmixT", l * 8)
                    ws = [self.ring_load(s_q + j) for j in range(4)]
                    def mm2(s, ws=ws):
                        banks = (4, 5) if s % 2 == 0 else (2, 3)
                        for half in range(2):
                            for kc in range(KC):
                                w, wk = ws[kc // 2]
                                P.op("tensor", lambda e, w=w, kc=kc, s=s, half=half, banks=banks: e.matmul(
                                    self.ps[banks[half]], lhsT=self.hT[:, kc, s * 128:(s + 1) * 128],
                                    rhs=w[:, (kc % 2) * 1024 + half * 512:(kc % 2) * 1024 + (half + 1) * 512],
                                    start=(kc == 0), stop=(kc == KC - 1)),
                                    reads=[wk, ("hT", s)], writes=[("ps", banks[half])])
                    def norm2(s):
                        banks = (4, 5) if s % 2 == 0 else (2, 3)
                        self.qk_norm(tmp, 8, [(self.ps[banks[0]], banks[0], 4), (self.ps[banks[1]], banks[1], 4)],
                                     qkg[:, 0, :], "qkg")
                    mm2(0)
                    norm2(0)
                    for s in range(4):
                        if s + 1 < 4:
                            mm2(s + 1)
                        self.qk_rope(tmp, 8, cs, "cs", s)
                        if s + 1 < 4:
                            norm2(s + 1)
                        pb = self.pp[3][:, (s % 2) * 512:(s % 2 + 1) * 512].bitcast(BF16)
                        for hd in range(8):
                            P.op("tensor", lambda e, hd=hd, pb=pb: e.transpose(
                                out=pb[:, hd * 128:(hd + 1) * 128], in_=tmp["qr"][:, hd, :], identity=self.identb[:, :]),
                                reads=["qr0", "qr1", "identb"], writes=[("ps", 6 + s % 2)])
                        P.op("vector", lambda e, s=s, pb=pb: e.tensor_copy(
                            out=QT[:, :, s * 128:(s + 1) * 128], in_=pb[:, :].rearrange("p (h t) -> p h t", h=8)),
                            reads=[("ps", 6 + s % 2)], writes=["QT"])
                    npair = nbk // 2
                    items = [(hd, kp) for hd in range(8) for kp in range(npair)]

                    def emit_score(i):
                        hd, kp = items[i]
                        kvh = hd // 4
                        j = i % 3
                        for q in range(2):
                            kt = 2 * kp + q
                            P.op("tensor", lambda e, hd=hd, kt=kt, kvh=kvh, j=j, q=q: e.matmul(
                                self.ps[2 * j + q], lhsT=KT[:, kvh, kt * 128:(kt + 1) * 128], rhs=QT[:, hd, :],
                                start=True, stop=True), reads=["QT"] + allK, writes=[("ps", 2 * j + q)])
                        jb = i % NPT
                        P.op("scalar", lambda e, j=j, jb=jb: e.activation(
                            out=pT[jb][:, :], in_=self.pp[j][:, :], func=AF.Exp, scale=scale),
                            reads=[("ps", 2 * j), ("ps", 2 * j + 1)], writes=[("pT", jb)])

                    emit_score(0)
                    emit_score(1)
                    pending = []
                    for i, (hd, kp) in enumerate(items):
                        if i + 2 < len(items):
                            emit_score(i + 2)
                        kvh = hd // 4
                        jb = i % NPT
                        par = hd % 2
                        po, pr = 6, 7
                        for q in range(2):
                            kt = 2 * kp + q
                            P.op("tensor", lambda e, kt=kt, kvh=kvh, jb=jb, po=po, q=q: e.matmul(
                                self.ps[po], lhsT=Vb[:, kt, kvh * 128:(kvh + 1) * 128],
                                rhs=pT[jb][:, q * 512:(q + 1) * 512],
                                start=(kt == 0), stop=(kt == nbk - 1)),
                                reads=[("pT", jb)] + allV, writes=[("ps", po)])
                        GS = 4 if npair % 4 == 0 else 2
                        sj = (i // GS) % NPT
                        if kp % GS == 0:
                            P.op("vector", lambda e, jb=jb, sj=sj: e.tensor_tensor(
                                out=s2[sj][:, :], in0=pT[jb][:, 0:512], in1=pT[jb][:, 512:1024], op=ALU.add),
                                reads=[("pT", jb)], writes=[("s2", sj)])
                        else:
                            P.op("vector", lambda e, jb=jb: e.tensor_tensor(
                                out=rinv[:, :].bitcast(BF16)[:, 0:512], in0=pT[jb][:, 0:512], in1=pT[jb][:, 512:1024], op=ALU.add),
                                reads=[("pT", jb)], writes=["s2tmp", "rinv"])
                            P.op("vector", lambda e, sj=sj: e.tensor_tensor(
                                out=s2[sj][:, :], in0=s2[sj][:, :], in1=rinv[:, :].bitcast(BF16)[:, 0:512], op=ALU.add),
                                reads=["s2tmp", ("s2", sj)], writes=[("s2", sj)])

                            def emit_rs(sj=sj, pr=pr, kp=kp, GS=GS):
                                P.op("tensor", lambda e: e.matmul(
                                    self.ps[pr], lhsT=self.onesb[:, :], rhs=s2[sj][:, :],
                                    start=(kp == GS - 1), stop=(kp == npair - 1)),
                                    reads=[("s2", sj), "onesb"], writes=[("ps", pr)])
                            if kp % GS == GS - 1:
                                if pending:
                                    pending.pop()()
                                pending.append(emit_rs)
                        if kp == npair - 1:
                            while pending:
                                pending.pop()()
                            P.op("vector", lambda e, po=po: e.tensor_copy(out=ou[:, :], in_=self.ps[po]),
                                 reads=[("ps", po)], writes=["ou"])
                            P.op("vector", lambda e, pr=pr: e.reciprocal(out=rinv[:, :], in_=self.ps[pr]),
                                 reads=[("ps", pr), "s2tmp"], writes=["rinv", "s2tmp"])
                            P.op("vector", lambda e, hd=hd: e.tensor_tensor(
                                out=OT[:, hd, :], in0=ou[:, :], in1=rinv[:, :], op=ALU.mult),
                                reads=["ou", "rinv"], writes=[("OT", hd)])
                    ws = [self.ring_load(s_o + j) for j in range(4)]
                    for s in range(4):
                        for half in range(2):
                            bank = 2 * s + half
                            for hd in range(8):
                                w, wk = ws[hd // 2]
                                P.op("tensor", lambda e, w=w, hd=hd, s=s, half=half, bank=bank: e.matmul(
                                    self.ps[bank], lhsT=OT[:, hd, s * 128:(s + 1) * 128],
                                    rhs=w[:, (hd % 2) * 1024 + half * 512:(hd % 2) * 1024 + (half + 1) * 512],
                                    start=(hd == 0), stop=(hd == 7)),
                                    reads=[wk, ("OT", hd)], writes=[("ps", bank)])
                    self.ffn_tile(l, xt, xkey, pre=lambda s, xt=xt, xkey=xkey: self.add_psum_to_xt(xt, xkey, s))
                    self.store_tile(l, part, ta, b)

            nloc = c.RP // 128
            pass1("P")
            P.dma("sync", self.kloc.rearrange("(h d) k -> d h k", h=2), KT[:, :, 0:c.RP], "ak",
                  reads=[("K", sa) for sa in range(nloc)], writes=["kloc"])
            P.dma("sync", self.vloc.rearrange("(s p) c -> p s c", p=128), Vb[:, 0:nloc, :], "av",
                  reads=[("V", sa) for sa in range(nloc)], writes=["vloc"])
            _coll(P, self.kloc, self.kall, "agk", ["kloc"], ["kall"], c.groups)
            _coll(P, self.vloc, self.vall, "agv", ["vloc"], ["vall"], c.groups)
            pass1("S")
            pass2("S", c.NBS)
            kallv = self.kall.rearrange("(r h d) k -> h d r k", r=4, h=2)
            for h in range(2):
                P.dma("sync", KT[:, h, 0:4 * c.RP].rearrange("d (r k) -> d r k", r=4), kallv[h], "ak%d" % h,
                      reads=["kall"], writes=[("K", sa) for sa in range(c.NBP)])
            P.dma("sync", Vb[:, 0:c.NBP, :], self.vall.rearrange("(s p) c -> p s c", p=128), "av2",
                  reads=["vall"], writes=[("V", sa) for sa in range(c.NBP)])
            pass2("P", c.NBP)


def _rope_rows(pos):
    inv = (np.float32(10000.0) ** (-np.arange(0, 64, 2, dtype=np.float32) / np.float32(64))).astype(np.float32)
    rowp = (pos // 64).astype(np.float32)
    colp = (pos % 64).astype(np.float32)
    ang = np.concatenate([rowp[:, None] * inv, colp[:, None] * inv], -1).astype(np.float32)
    return np.concatenate([np.cos(ang), np.sin(ang)], -1).astype(np.float32)


def _cs_tiles(pos):
    cs = _rope_rows(pos)
    nt = cs.shape[0] // 512
    return np.ascontiguousarray(cs.reshape(nt, 4, 128, 128).transpose(0, 2, 1, 3))


def _dft_common():
    a = np.arange(128)
    A = np.exp(-2j * np.pi * np.outer(a, a) / 128.0)
    m = np.concatenate([A.real, A.imag], 1).astype(np.float32)
    return m


def _dft_B(NB, R, kbs):
    b = np.arange(NB).astype(np.float64)
    ka = np.arange(128).astype(np.float64)
    kb = np.asarray(kbs).astype(np.float64)
    M = (np.exp(-2j * np.pi * b[:, None, None] * kb[None, None, :] / NB)
         * np.exp(-2j * np.pi * b[:, None, None] * ka[None, :, None] / float(R))) / np.sqrt(R * 128.0)
    n = 128 * len(kbs)
    Mr, Mi = M.real.reshape(NB, n), M.imag.reshape(NB, n)
    out = np.zeros((2, 2 * NB, n), np.float32)
    out[0, 0::2] = Mr
    out[0, 1::2] = -Mi
    out[1, 0::2] = -Mi
    out[1, 1::2] = -Mr
    return out


def _pack_weights(cfg, fourier_w, attn_w_qkv, attn_w_o, ffn_w_gate, ffn_w_up, ffn_w_down):
    wall = np.zeros((cfg.nslots, 128, 2048), np.float32)
    jF = jA = 0
    for l, t in enumerate(cfg.layers):
        if t == "F":
            w = np.asarray(fourier_w[jF], np.float32)
            s0 = cfg.slot[("wf", l)]
            wall[s0:s0 + 4] = w.reshape(4, 2, 128, D).transpose(0, 2, 1, 3).reshape(4, 128, 2048)
            jF += 1
        else:
            wqkv = np.asarray(attn_w_qkv[jA], np.float32)
            s0 = cfg.slot[("wq", l)]
            wall[s0:s0 + 4] = wqkv[:, 0:1024].reshape(4, 2, 128, 1024).transpose(0, 2, 1, 3).reshape(4, 128, 2048)
            s0 = cfg.slot[("wkv", l)]
            wall[s0:s0 + 2] = wqkv[:, 1024:1536].reshape(2, 4, 128, 512).transpose(0, 2, 1, 3).reshape(2, 128, 2048)
            s0 = cfg.slot[("wo", l)]
            wo = np.asarray(attn_w_o[jA], np.float32)
            wall[s0:s0 + 4] = wo.reshape(4, 2, 128, D).transpose(0, 2, 1, 3).reshape(4, 128, 2048)
            jA += 1
        FC = cfg.FC
        wg = np.asarray(ffn_w_gate[l], np.float32).reshape(KC, 128, FC, 128)
        wu = np.asarray(ffn_w_up[l], np.float32).reshape(KC, 128, FC, 128)
        gu = np.stack([wg, wu], 0)
        s0 = cfg.slot[("wgu", l)]
        wall[s0:s0 + FC] = gu.transpose(3, 2, 0, 1, 4).reshape(FC, 128, 2048)
        wd = np.asarray(ffn_w_down[l], np.float32)
        s0 = cfg.slot[("wd", l)]
        wall[s0:s0 + FC // 2] = wd.reshape(FC // 2, 2, 128, D).transpose(0, 2, 1, 3).reshape(FC // 2, 128, 2048)
    return wall.reshape(cfg.nslots * 128, 2048)


def _common_inputs(cfg, norm_mix, norm_ffn, attn_q_norm, attn_k_norm):
    L = cfg.L
    nm = np.asarray(norm_mix, np.float32)[:L]
    nf = np.asarray(norm_ffn, np.float32)[:L]
    gmixT = nm.reshape(L, KC, 128).transpose(2, 0, 1).reshape(128, L * KC).copy()
    gffnT = nf.reshape(L, KC, 128).transpose(2, 0, 1).reshape(128, L * KC).copy()
    gmixrep = np.broadcast_to(nm[None], (128, L, D)).copy()
    qkg = np.ones((128, L, 2, 128), np.float32)
    jA = 0
    for l, t in enumerate(cfg.layers):
        if t == "A":
            qkg[:, l, 0, :] = np.asarray(attn_q_norm[jA], np.float32)[None]
            qkg[:, l, 1, :] = np.asarray(attn_k_norm[jA], np.float32)[None]
            jA += 1
    dft = _dft_common()
    return dict(gmixT=gmixT, gffnT=gffnT, gmixrep=gmixrep, qkg=qkg, ident=np.eye(128, dtype=np.float32),
                dftA=dft, dftC=dft.copy(),
                dftBS=_dft_B(cfg.NBS, cfg.RS, np.arange(cfg.NBS)),
                csS=_cs_tiles(np.arange(cfg.RS)))


_NC_CACHE = {}


def run_cores(cfg, xp_list, xs_list, weights):
    key = (cfg.RP, cfg.RS, cfg.DFF, cfg.layers, cfg.ring)
    if key not in _NC_CACHE:
        _NC_CACHE[key] = Builder(cfg).build()
    nc = _NC_CACHE[key]
    in_maps = []
    for c in range(8):
        r = c % 4
        m = dict(xp=np.ascontiguousarray(xp_list[c], np.float32), xs=np.ascontiguousarray(xs_list[c], np.float32),
                 wall=weights["wall"])
        m.update(weights["common"])
        m["csP"] = _cs_tiles(cfg.RP * r + np.arange(cfg.RP))
        m["dftBP"] = _dft_B(cfg.NBP, 4 * cfg.RP, cfg.KBP * r + np.arange(cfg.KBP))
        in_maps.append(m)
    res = run_bass_kernel_spmd(nc, in_maps, core_ids=list(range(8)))
    return [(r["yp"], r["ys"]) for r in res.results]


def kernel(x_prompt, x_sample, norm_mix, norm_ffn, fourier_w, attn_w_qkv, attn_q_norm, attn_k_norm,
           attn_w_o, ffn_w_gate, ffn_w_up, ffn_w_down):
    cfg = Cfg()
    xp = np.asarray(x_prompt, np.float32)
    xs = np.asarray(x_sample, np.float32)
    weights = dict(
        wall=_pack_weights(cfg, fourier_w, attn_w_qkv, attn_w_o, ffn_w_gate, ffn_w_up, ffn_w_down),
        common=_common_inputs(cfg, norm_mix, norm_ffn, attn_q_norm, attn_k_norm))
    RP = cfg.RP
    xp_list = [xp[c // 4, RP * (c % 4):RP * (c % 4 + 1)] for c in range(8)]
    xs_list = [xs[c] for c in range(8)]
    outs = run_cores(cfg, xp_list, xs_list, weights)
    y_prompt = np.zeros_like(xp)
    y_sample = np.zeros_like(xs)
    for c in range(8):
        y_prompt[c // 4, RP * (c % 4):RP * (c % 4 + 1)] = outs[c][0]
        y_sample[c] = outs[c][1]
    return (y_prompt, y_sample)
```
